# Optimizing a Trainium2 kernel written in Bass

```python
import math
import jax
import jax.numpy as jnp
from jax import lax
import numpy as np

D_MODEL = 1024
BATCH = 8
SEQ = 4096
DEPTH = 2

GRID_W = 64
CTX_LEN = 256
RMS_EPS = 1e-6
ROPE_THETA = 10000.0
Q_BLOCK = 128

POOL_WINDOWS = (2, 4, 8, 16)
POOL_GROUPS = 4
POOL_WIDTH = 512
POOL_GROUP_DIM = POOL_WIDTH // POOL_GROUPS

DIFF_HEADS = 4
DIFF_QK_DIM = 64
DIFF_V_DIM = 2 * DIFF_QK_DIM
DIFF_QK_WIDTH = 2 * DIFF_HEADS * DIFF_QK_DIM
DIFF_WIDTH = DIFF_HEADS * DIFF_V_DIM

MLA_HEADS = 4
MLA_Q_RANK = 512
MLA_KV_RANK = 256
MLA_NOPE_DIM = 128
MLA_ROPE_DIM = 64
MLA_V_DIM = 128
MLA_WIDTH = MLA_HEADS * MLA_V_DIM

HG_HEADS = 4
HG_K_DIM = 128
HG_V_DIM = 128
HG_QK_WIDTH = HG_HEADS * HG_K_DIM
HG_WIDTH = HG_HEADS * HG_V_DIM
HG_CHUNK = 64

D_FF = 2816
CONV_WIDTH = 3

MIX_WIDTH = POOL_WIDTH + DIFF_WIDTH
EVEN_IN = POOL_WIDTH + 2 * DIFF_QK_WIDTH + DIFF_WIDTH
ODD_IN = MLA_Q_RANK + MLA_KV_RANK + MLA_ROPE_DIM + 3 * HG_QK_WIDTH + 2 * HG_WIDTH
N_EVEN = (DEPTH + 1) // 2
N_ODD = DEPTH // 2

kernel_name = "hybrid_pool_diff_mla_hgrn2_dit_block"


def _rmsnorm(x, g):
    xf = x.astype(jnp.float32)
    y = xf * lax.rsqrt(jnp.mean(xf * xf, axis=-1, keepdims=True) + RMS_EPS)
    return (y * g.astype(jnp.float32)).astype(x.dtype)


def _modulation(cond, w, b):
    m = jax.nn.silu(cond) @ w + b
    m = m.reshape(m.shape[:-1] + (6, 1, D_MODEL))
    return jnp.moveaxis(m, -3, 0)


def _modulate(h, shift, scale):
    return h * (1 + scale) + shift


def _axial_rope_tables(n_tokens, rot_dim, dtype):
    rows = n_tokens // GRID_W
    pos_row = jnp.repeat(jnp.arange(rows), GRID_W)
    pos_col = jnp.tile(jnp.arange(GRID_W), rows)
    axis_dim = rot_dim // 2
    inv_freq = ROPE_THETA ** (-jnp.arange(0, axis_dim, 2, dtype=jnp.float32) / axis_dim)
    ang = jnp.stack([pos_row, pos_col], axis=-1).astype(jnp.float32)[..., None] * inv_freq
    return jnp.cos(ang).astype(dtype), jnp.sin(ang).astype(dtype)


def _apply_axial_rope(x, cos, sin):
    r = x.shape[-1]
    xr = x.reshape(x.shape[:-1] + (2, 2, r // 4))
    bshape = (1, cos.shape[0]) + (1,) * (x.ndim - 3) + (2, r // 4)
    cs = cos.reshape(bshape)
    sn = sin.reshape(bshape)
    x1 = xr[..., 0, :]
    x2 = xr[..., 1, :]
    out = jnp.stack([x1 * cs - x2 * sn, x2 * cs + x1 * sn], axis=-2)
    return out.reshape(x.shape)


def _query_blocks(a):
    b, n = a.shape[:2]
    return jnp.moveaxis(a.reshape((b, n // Q_BLOCK, Q_BLOCK) + a.shape[2:]), 1, 0)


def _merge_blocks(o):
    o = jnp.moveaxis(o, 0, 1)
    return o.reshape((o.shape[0], o.shape[1] * o.shape[2]) + o.shape[3:])


def _multiscale_pool(u):
    b, n, _ = u.shape
    ug = u.reshape(b, n, POOL_GROUPS, POOL_GROUP_DIM).astype(jnp.float32)
    csum = jnp.concatenate([jnp.zeros((b, 1, POOL_GROUPS, POOL_GROUP_DIM), jnp.float32),
                            jnp.cumsum(ug, axis=1)], axis=1)
    half = jnp.array(POOL_WINDOWS, jnp.int32) // 2
    t = jnp.arange(n, dtype=jnp.int32)[:, None]
    lo = jnp.clip(t - half, 0, n)
    hi = jnp.clip(t + half, 0, n)
    grp = jnp.arange(POOL_GROUPS, dtype=jnp.int32)[None, :]
    win_sum = csum[:, hi, grp] - csum[:, lo, grp]
    mean = win_sum / (hi - lo).astype(jnp.float32)[None, :, :, None]
    return (mean - ug).astype(u.dtype)


def _diff_attention(q, k, v, lam):
    scale = DIFF_QK_DIM ** -0.5

    def one(qblk):
        bsz = qblk.shape[0]
        s = jnp.einsum('bqhd,bkhd->bhqk', qblk, k).astype(jnp.float32) * scale
        p = jax.nn.softmax(s, axis=-1).reshape(bsz, DIFF_HEADS, 2, Q_BLOCK, -1)
        a = p[:, :, 0] - lam * p[:, :, 1]
        return jnp.einsum('bhqk,bkhe->bqhe', a.astype(v.dtype), v)

    return _merge_blocks(lax.map(one, _query_blocks(q)))


def _mla_attention(q_nope, q_rope, k_nope, k_rope, v):
    scale = (MLA_NOPE_DIM + MLA_ROPE_DIM) ** -0.5

    def one(blk):
        qn, qr = blk
        s = jnp.einsum('bqhd,bkhd->bhqk', qn, k_nope) + jnp.einsum('bqhr,bkr->bhqk', qr, k_rope)
        p = jax.nn.softmax(s.astype(jnp.float32) * scale, axis=-1)
        return jnp.einsum('bhqk,bkhv->bqhv', p.astype(v.dtype), v)

    return _merge_blocks(lax.map(one, (_query_blocks(q_nope), _query_blocks(q_rope))))


def _hgrn_scan(q, k, v, log_f, s0):
    b, n = q.shape[:2]
    n_chunks = n // HG_CHUNK

    def chunks(a):
        return jnp.moveaxis(a.reshape((b, n_chunks, HG_CHUNK) + a.shape[2:]), 1, 0)

    tri = jnp.tril(jnp.ones((HG_CHUNK, HG_CHUNK), bool))[None, :, :, None, None]

    def step(s, inp):
        qc, kc, vc, gc = inp
        cum = jnp.cumsum(gc, axis=1)
        o_inter = jnp.einsum('bthk,bhkv->bthv', qc * jnp.exp(cum), s)
        rel = jnp.where(tri, cum[:, :, None] - cum[:, None, :], -jnp.inf)
        scores = jnp.einsum('bthk,bshk,btshk->bhts', qc, kc, jnp.exp(rel))
        o_intra = jnp.einsum('bhts,bshv->bthv', scores, vc)
        last = cum[:, -1]
        s_new = jnp.exp(last)[..., None] * s + jnp.einsum(
            'bshk,bshv->bhkv', kc * jnp.exp(last[:, None] - cum), vc)
        return s_new, o_inter + o_intra

    s_fin, o = lax.scan(step, s0, (chunks(q), chunks(k), chunks(v), chunks(log_f)))
    return _merge_blocks(o), s_fin


def _hgrn_direction(q_c, q_l, i_c, i_l, fr_c, fr_l, lb, reverse):
    def gates(fr):
        f = lb + (1.0 - lb) * jax.nn.sigmoid(fr)
        return 1.0 - f, jnp.log(f)

    def orient(a):
        return jnp.flip(a, axis=1) if reverse else a

    k_c, lf_c = gates(fr_c)
    k_l, lf_l = gates(fr_l)
    s0 = jnp.zeros((q_l.shape[0], HG_HEADS, HG_K_DIM, HG_V_DIM), jnp.float32)
    o_c, s_c = _hgrn_scan(orient(q_c), orient(k_c), orient(i_c), orient(lf_c), s0)
    o_l, _ = _hgrn_scan(orient(q_l), orient(k_l), orient(i_l), orient(lf_l), s_c)
    return orient(o_c), orient(o_l)


def _even_mixer(h_c, h_l, w_in, pool_w, pool_scale, lam_vec, subln, w_out, lam_init, cos, sin, need_ctx):
    lam_vec = lam_vec.astype(jnp.float32)
    lam = (jnp.exp(jnp.sum(lam_vec[0] * lam_vec[1])) - jnp.exp(jnp.sum(lam_vec[2] * lam_vec[3]))
           + lam_init)
    split_at = [POOL_WIDTH, POOL_WIDTH + DIFF_QK_WIDTH, POOL_WIDTH + 2 * DIFF_QK_WIDTH]

    def project(h):
        b, n = h.shape[:2]
        u, q, k, v = jnp.split(h @ w_in, split_at, axis=-1)
        return (u, q.reshape(b, n, 2 * DIFF_HEADS, DIFF_QK_DIM), k.reshape(b, n, 2 * DIFF_HEADS, DIFF_QK_DIM),
                v.reshape(b, n, DIFF_HEADS, DIFF_V_DIM))

    def pool_branch(u):
        b, n = u.shape[:2]
        d = _multiscale_pool(u).reshape(b, n, POOL_GROUPS, POOL_GROUP_DIM)
        y = jnp.einsum('blgc,gcd->blgd', d, pool_w).reshape(b, n, POOL_WIDTH)
        return y * pool_scale

    def diff_branch(q, k, v):
        b, n = q.shape[:2]
        o = _diff_attention(q, k, v, lam)
        return (_rmsnorm(o, subln) * (1.0 - lam_init)).reshape(b, n, DIFF_WIDTH)

    u_c, q_c, k_c, v_c = project(h_c)
    u_l, q_l, k_l, v_l = project(h_l)
    q_l = _apply_axial_rope(q_l, cos, sin)
    k_l = _apply_axial_rope(k_l, cos, sin)
    k_all = jnp.concatenate([k_c, k_l], axis=1)
    v_all = jnp.concatenate([v_c, v_l], axis=1)
    y_l = jnp.concatenate([pool_branch(u_l), diff_branch(q_l, k_all, v_all)], axis=-1) @ w_out
    y_c = None
    if need_ctx:
        y_c = jnp.concatenate([pool_branch(u_c), diff_branch(q_c, k_c, v_c)], axis=-1) @ w_out
    return y_c, y_l


def _odd_mixer(h_c, h_l, w_in, q_norm, w_uq, kv_norm, w_ukv, hg_norm, lbs, w_out, cos, sin, need_ctx):
    split_at = np.cumsum([MLA_Q_RANK, MLA_KV_RANK, MLA_ROPE_DIM, HG_QK_WIDTH, HG_QK_WIDTH,
                          HG_QK_WIDTH, HG_WIDTH]).tolist()

    def project(h):
        b, n = h.shape[:2]
        cq, ckv, kr, hq, hf_fwd, hf_bwd, hi, hg = jnp.split(h @ w_in, split_at, axis=-1)
        q = (_rmsnorm(cq, q_norm) @ w_uq).reshape(b, n, MLA_HEADS, MLA_NOPE_DIM + MLA_ROPE_DIM)
        kv = (_rmsnorm(ckv, kv_norm) @ w_ukv).reshape(b, n, MLA_HEADS, MLA_NOPE_DIM + MLA_V_DIM)
        ks = (b, n, HG_HEADS, HG_K_DIM)
        vs = (b, n, HG_HEADS, HG_V_DIM)
        hgrn = (jax.nn.silu(hq).reshape(ks).astype(jnp.float32), hf_fwd.reshape(ks).astype(jnp.float32),
                hf_bwd.reshape(ks).astype(jnp.float32), hi.reshape(vs).astype(jnp.float32), hg.reshape(vs))
        return (q[..., :MLA_NOPE_DIM], q[..., MLA_NOPE_DIM:], kv[..., :MLA_NOPE_DIM],
                kv[..., MLA_NOPE_DIM:], kr, hgrn)

    qn_c, qr_c, kn_c, v_c, kr_c, hg_c = project(h_c)
    qn_l, qr_l, kn_l, v_l, kr_l, hg_l = project(h_l)
    qr_l = _apply_axial_rope(qr_l, cos, sin)
    kr_l = _apply_axial_rope(kr_l, cos, sin)
    mla_l = _mla_attention(qn_l, qr_l, jnp.concatenate([kn_c, kn_l], axis=1),
                           jnp.concatenate([kr_c, kr_l], axis=1), jnp.concatenate([v_c, v_l], axis=1))

    oc_f, ol_f = _hgrn_direction(hg_c[0], hg_l[0], hg_c[3], hg_l[3], hg_c[1], hg_l[1],
                                 lbs[0].reshape(HG_HEADS, HG_K_DIM), False)
    oc_b, ol_b = _hgrn_direction(hg_c[0], hg_l[0], hg_c[3], hg_l[3], hg_c[2], hg_l[2],
                                 lbs[1].reshape(HG_HEADS, HG_K_DIM), True)

    def hgrn_readout(o, gate):
        b, n = gate.shape[:2]
        return (_rmsnorm(o.astype(gate.dtype), hg_norm) * jax.nn.silu(gate)).reshape(b, n, HG_WIDTH)

    def merge(mla, hg_out):
        b, n = mla.shape[:2]
        return jnp.concatenate([mla.reshape(b, n, MLA_WIDTH), hg_out], axis=-1) @ w_out

    y_l = merge(mla_l, hgrn_readout(ol_f + ol_b, hg_l[4]))
    y_c = None
    if need_ctx:
        mla_c = _mla_attention(qn_c, qr_c, kn_c, kr_c, v_c)
        y_c = merge(mla_c, hgrn_readout(oc_f + oc_b, hg_c[4]))
    return y_c, y_l


def _conv_ffn(h, w_gate, w_up, conv_w, conv_b, w_down):
    a = h @ w_gate
    ap = jnp.pad(a, ((0, 0), (1, 1), (0, 0)))
    a = ap[:, :-2] * conv_w[0] + ap[:, 1:-1] * conv_w[1] + ap[:, 2:] * conv_w[2] + conv_b
    return (jax.nn.silu(a) * (h @ w_up)) @ w_down


def setup_inputs(seed: int = 0) -> dict:
    key = jax.random.key(seed)
    ks = jax.random.split(key, 25)

    def nrm(k, shape, scale):
        return jax.random.normal(k, shape, jnp.float32) * scale

    def gain(k, shape):
        return 1.0 + 0.05 * jax.random.normal(k, shape, jnp.float32)

    return {
        "x": nrm(ks[0], (BATCH, SEQ, D_MODEL), 1.0),
        "c": nrm(ks[1], (BATCH, D_MODEL), 1.0),
        "ctx": nrm(ks[2], (BATCH, CTX_LEN, D_MODEL), 1.0),
        "c_ctx": nrm(ks[3], (D_MODEL,), 1.0),
        "ada_w": nrm(ks[4], (DEPTH, D_MODEL, 6 * D_MODEL), 0.5 * D_MODEL ** -0.5),
        "ada_b": nrm(ks[5], (DEPTH, 6 * D_MODEL), 0.02),
        "norm_g": gain(ks[6], (DEPTH, 4, D_MODEL)),
        "mix_w_out": nrm(ks[7], (DEPTH, MIX_WIDTH, D_MODEL), MIX_WIDTH ** -0.5),
        "ffn_w_gate": nrm(ks[8], (DEPTH, D_MODEL, D_FF), D_MODEL ** -0.5),
        "ffn_w_up": nrm(ks[9], (DEPTH, D_MODEL, D_FF), D_MODEL ** -0.5),
        "ffn_conv_w": nrm(ks[10], (DEPTH, CONV_WIDTH, D_FF), CONV_WIDTH ** -0.5),
        "ffn_conv_b": nrm(ks[11], (DEPTH, D_FF), 0.02),
        "ffn_w_down": nrm(ks[12], (DEPTH, D_FF, D_MODEL), D_FF ** -0.5),
        "ev_w_in": nrm(ks[13], (N_EVEN, D_MODEL, EVEN_IN), D_MODEL ** -0.5),
        "pool_w": nrm(ks[14], (N_EVEN, POOL_GROUPS, POOL_GROUP_DIM, POOL_GROUP_DIM), POOL_GROUP_DIM ** -0.5),
        "pool_scale": 1.0 + 0.1 * jax.random.normal(ks[15], (N_EVEN, POOL_WIDTH), jnp.float32),
        "diff_lambda": nrm(ks[16], (N_EVEN, 4, DIFF_QK_DIM), 0.1),
        "diff_subln": gain(ks[17], (N_EVEN, DIFF_V_DIM)),
        "od_w_in": nrm(ks[18], (N_ODD, D_MODEL, ODD_IN), D_MODEL ** -0.5),
        "mla_q_norm": gain(ks[19], (N_ODD, MLA_Q_RANK)),
        "mla_w_uq": nrm(ks[20], (N_ODD, MLA_Q_RANK, MLA_HEADS * (MLA_NOPE_DIM + MLA_ROPE_DIM)), MLA_Q_RANK ** -0.5),
        "mla_kv_norm": gain(ks[21], (N_ODD, MLA_KV_RANK)),
        "mla_w_ukv": nrm(ks[22], (N_ODD, MLA_KV_RANK, MLA_HEADS * (MLA_NOPE_DIM + MLA_V_DIM)), MLA_KV_RANK ** -0.5),
        "hgrn_norm": gain(ks[23], (N_ODD, HG_V_DIM)),
        "hgrn_lb": nrm(ks[24], (2, DEPTH, HG_QK_WIDTH), 0.5),
    }


def reference(x, c, ctx, c_ctx, ada_w, ada_b, norm_g, mix_w_out, ffn_w_gate, ffn_w_up, ffn_conv_w,
              ffn_conv_b, ffn_w_down, ev_w_in, pool_w, pool_scale, diff_lambda, diff_subln, od_w_in,
              mla_q_norm, mla_w_uq, mla_kv_norm, mla_w_ukv, hgrn_norm, hgrn_lb):
    n_lat = x.shape[1]
    cos, sin = _axial_rope_tables(n_lat, DIFF_QK_DIM, x.dtype)
    probs = jax.nn.softmax(hgrn_lb.astype(jnp.float32), axis=1)
    lower_bounds = jnp.cumsum(probs, axis=1) - probs[:, :1]
    xc = ctx
    for layer in range(DEPTH):
        last = layer == DEPTH - 1
        j = layer // 2
        sh1, sc1, g1, sh2, sc2, g2 = _modulation(c, ada_w[layer], ada_b[layer])
        csh1, csc1, cg1, csh2, csc2, cg2 = _modulation(c_ctx, ada_w[layer], ada_b[layer])
        h_l = _modulate(_rmsnorm(x, norm_g[layer, 0]), sh1, sc1)
        h_c = _modulate(_rmsnorm(xc, norm_g[layer, 0]), csh1, csc1)
        if layer % 2 == 0:
            lam_init = 0.8 - 0.6 * math.exp(-0.3 * layer)
            y_c, y_l = _even_mixer(h_c, h_l, ev_w_in[j], pool_w[j], pool_scale[j], diff_lambda[j],
                                   diff_subln[j], mix_w_out[layer], lam_init, cos, sin, not last)
        else:
            y_c, y_l = _odd_mixer(h_c, h_l, od_w_in[j], mla_q_norm[j], mla_w_uq[j], mla_kv_norm[j],
                                  mla_w_ukv[j], hgrn_norm[j], lower_bounds[:, layer], mix_w_out[layer],
                                  cos, sin, not last)
        x = x + g1 * _rmsnorm(y_l, norm_g[layer, 1])
        f_l = _conv_ffn(_modulate(_rmsnorm(x, norm_g[layer, 2]), sh2, sc2), ffn_w_gate[layer],
                        ffn_w_up[layer], ffn_conv_w[layer], ffn_conv_b[layer], ffn_w_down[layer])
        x = x + g2 * _rmsnorm(f_l, norm_g[layer, 3])
        if not last:
            xc = xc + cg1 * _rmsnorm(y_c, norm_g[layer, 1])
            f_c = _conv_ffn(_modulate(_rmsnorm(xc, norm_g[layer, 2]), csh2, csc2), ffn_w_gate[layer],
                            ffn_w_up[layer], ffn_conv_w[layer], ffn_conv_b[layer], ffn_w_down[layer])
            xc = xc + cg2 * _rmsnorm(f_c, norm_g[layer, 3])
    return x
```

```python
import numpy as np
from contextlib import ExitStack
import concourse.bass as bass
import concourse.mybir as mybir
from concourse.bass_utils import run_bass_kernel_spmd

F32 = mybir.dt.float32
BF16 = mybir.dt.bfloat16
ALU = mybir.AluOpType
AF = mybir.ActivationFunctionType

NDSEM = 8
D = 1024
NT = 34
NTOK = 4352
DFF = 2816
NJ = 22
EPS = 1e-6


class Prog:
    ENGS = ("pe", "dve", "act", "pool", "sp")

    def __init__(self, nc):
        self.nc = nc
        self.ops = []
        self.lw = {}
        self.rd = {}
        self.cnt = {e: 0 for e in self.ENGS}
        self.dcnt = {e: 0 for e in self.ENGS}
        self.dslot_last = {e: [None] * NDSEM for e in self.ENGS}
        self.last_nd = {e: None for e in self.ENGS}
        self.pending_bar = {e: [] for e in self.ENGS}

    def op(self, eng, fn, r=(), w=(), dma=False):
        oid = len(self.ops)
        deps = []
        for k in r:
            y = self.lw.get(k)
            if y is not None:
                deps.append((y, "RAW"))
        for k in w:
            y = self.lw.get(k)
            if y is not None:
                deps.append((y, "WAW"))
            for y in self.rd.get(k, ()):
                deps.append((y, "WAR"))
        for y in self.pending_bar[eng]:
            deps.append((y, "RAW"))
        self.pending_bar[eng] = []
        o = dict(id=oid, eng=eng, fn=fn, deps=deps, dma=dma)
        if dma:
            i = self.dcnt[eng]
            self.dcnt[eng] += 1
            slot = i % NDSEM
            o["dslot"] = slot
            o["dval"] = 16 * (i // NDSEM + 1)
            prev = self.dslot_last[eng][slot]
            if prev is not None:
                deps.append((prev, "RAW"))
            self.dslot_last[eng][slot] = oid
        else:
            self.cnt[eng] += 1
            o["val"] = self.cnt[eng]
            self.last_nd[eng] = oid
        self.ops.append(o)
        for k in w:
            self.lw[k] = oid
            self.rd[k] = []
        for k in r:
            if k not in w:
                self.rd.setdefault(k, []).append(oid)
        return oid

    def barrier(self):
        snap = []
        for e in self.ENGS:
            if self.last_nd[e] is not None:
                snap.append(self.last_nd[e])
            for y in self.dslot_last[e]:
                if y is not None:
                    snap.append(y)
        for e in self.ENGS:
            self.pending_bar[e] = list(snap)
        self.lw = {}
        self.rd = {}

    def emit(self, st):
        nc = self.nc
        sems = {e: st.enter_context(nc.semaphore("s_" + e)) for e in self.ENGS}
        dsems = {e: [st.enter_context(nc.semaphore("d_%s%d" % (e, i))) for i in range(NDSEM)]
                 for e in ("sp", "pool", "act") if self.dcnt[e] > 0}
        block = st.enter_context(nc.Block())
        ops = self.ops

        def run(ename, eng):
            seen = {}
            for o in ops:
                if o["eng"] != ename:
                    continue
                need = {}
                for (y, kind) in o["deps"]:
                    Y = ops[y]
                    if Y["dma"]:
                        key = ("d", Y["eng"], Y["dslot"])
                        sem = dsems[Y["eng"]][Y["dslot"]]
                        val = Y["dval"]
                    else:
                        if Y["eng"] == ename and not o["dma"]:
                            if ename == "pe" or kind != "RAW":
                                continue
                        key = ("c", Y["eng"])
                        sem = sems[Y["eng"]]
                        val = Y["val"]
                    if seen.get(key, 0) >= val:
                        continue
                    if key not in need or need[key][1] < val:
                        need[key] = (sem, val)
                for key, (sem, val) in need.items():
                    eng.wait_ge(sem, val)
                    seen[key] = val
                ins = o["fn"](eng)
                if o["dma"]:
                    ins.then_inc(dsems[ename][o["dslot"]], 16)
                else:
                    ins.then_inc(sems[ename], 1)
            if ename in dsems:
                for slot in range(NDSEM):
                    y = self.dslot_last[ename][slot]
                    if y is not None:
                        Y = ops[y]
                        if seen.get(("d", ename, slot), 0) < Y["dval"]:
                            eng.wait_ge(dsems[ename][slot], Y["dval"])

        @block.tensor
        def _(e):
            run("pe", e)

        @block.vector
        def _(e):
            run("dve", e)

        @block.scalar
        def _(e):
            run("act", e)

        @block.gpsimd
        def _(e):
            run("pool", e)

        @block.sync
        def _(e):
            run("sp", e)


PW = 8 + 256 + 16 + 4096 + 8
PC0, PL0 = 8, 280


def _consts():
    c = {}
    c["ident"] = np.eye(128, dtype=np.float32)
    rot = np.zeros((128, 128), np.float32)
    for d in range(128):
        rot[d ^ 16, d] = 1.0
    c["rot"] = rot
    n = 4096
    pos_row = np.repeat(np.arange(n // 64), 64)
    pos_col = np.tile(np.arange(64), n // 64)
    inv_freq = (10000.0 ** (-np.arange(0, 32, 2, dtype=np.float32) / 32)).astype(np.float32)
    ang = np.stack([pos_row, pos_col], -1).astype(np.float32)[..., None] * inv_freq
    cs, sn = np.cos(ang).astype(np.float32), np.sin(ang).astype(np.float32)
    cos_t = np.zeros((128, n), np.float32)
    sin_t = np.zeros((128, n), np.float32)
    for d in range(128):
        dd = d % 64
        a, hf, i = dd // 32, (dd // 16) % 2, dd % 16
        cos_t[d] = cs[:, a, i]
        sin_t[d] = sn[:, a, i] * (-1.0 if hf == 0 else 1.0)
    c["cos_t"] = cos_t
    c["sin_t"] = sin_t
    inv = np.zeros((4, PW), np.float32)
    for g, w in enumerate((2, 4, 8, 16)):
        h = w // 2
        for (n_, off) in ((256, PC0), (4096, PL0)):
            t = np.arange(n_)
            lo = np.clip(t - h, 0, n_)
            hi = np.clip(t + h, 0, n_)
            inv[g, off:off + n_] = 1.0 / (hi - lo).astype(np.float32)
    c["invcnt"] = inv
    p = np.arange(128)[:, None] % 64
    t = np.arange(64)[None, :]
    c["mask_f"] = (p <= t).astype(np.float32)
    c["mask_b"] = (p >= t).astype(np.float32)
    return c


W_NAMES = ["ada_w", "ada_b", "norm_g", "mix_w_out", "ffn_w_gate", "ffn_w_up", "ffn_conv_w",
           "ffn_conv_b", "ffn_w_down", "ev_w_in", "pool_w", "pool_scale", "diff_lambda",
           "diff_subln", "od_w_in", "mla_q_norm", "mla_w_uq", "mla_kv_norm", "mla_w_ukv",
           "hgrn_norm", "hgrn_lb"]
W_SHAPES = {"ada_w": (2, 1024, 6144), "ada_b": (2, 6144), "norm_g": (2, 4, 1024),
            "mix_w_out": (2, 1024, 1024), "ffn_w_gate": (2, 1024, 2816), "ffn_w_up": (2, 1024, 2816),
            "ffn_conv_w": (2, 3, 2816), "ffn_conv_b": (2, 2816), "ffn_w_down": (2, 2816, 1024),
            "ev_w_in": (1, 1024, 2048), "pool_w": (1, 4, 128, 128), "pool_scale": (1, 512),
            "diff_lambda": (1, 4, 64), "diff_subln": (1, 128), "od_w_in": (1, 1024, 3392),
            "mla_q_norm": (1, 512), "mla_w_uq": (1, 512, 768), "mla_kv_norm": (1, 256),
            "mla_w_ukv": (1, 256, 1024), "hgrn_norm": (1, 128), "hgrn_lb": (2, 2, 512)}
C_SHAPES = {"ident": (128, 128), "rot": (128, 128), "cos_t": (128, 4096), "sin_t": (128, 4096),
            "invcnt": (4, PW), "mask_f": (128, 64), "mask_b": (128, 64)}

BLOCKS = [(0, 2)] + [(2 + 4 * i, 4) for i in range(8)]


def build(stop=None, dbg=False):
    nc = bass.Bass("TRN2", target_bir_lowering=False)
    din = {}
    din["x"] = nc.dram_tensor("x", [4096, D], F32, kind="ExternalInput").ap()
    din["ctx"] = nc.dram_tensor("ctx", [256, D], F32, kind="ExternalInput").ap()
    din["cvec"] = nc.dram_tensor("cvec", [2, D], F32, kind="ExternalInput").ap()
    for n in W_NAMES:
        din[n] = nc.dram_tensor(n, list(W_SHAPES[n]), F32, kind="ExternalInput").ap()
    for n in C_SHAPES:
        din[n] = nc.dram_tensor(n, list(C_SHAPES[n]), F32, kind="ExternalInput").ap()
    out = nc.dram_tensor("out", [4096, D], F32, kind="ExternalOutput").ap()
    XR = nc.dram_tensor("XR", [NTOK, D], F32, kind="ExternalOutput" if dbg else "Internal").ap()
    MODV = nc.dram_tensor("MODV", [2, 2, 6, D], F32, kind="Internal").ap()
    QT = nc.dram_tensor("QT", [9, 128, 4, 512], BF16, kind="Internal").ap()
    QR = nc.dram_tensor("QR", [9, 128, 2, 512], BF16, kind="Internal").ap()
    UT = nc.dram_tensor("UT", [4, 128, NTOK], F32, kind="Internal").ap()
    HT1 = nc.dram_tensor("HT1", [9, 128, 8, 512], BF16, kind="Internal").ap()
    H2D = nc.dram_tensor("H2D", [128, 8, 4355], BF16, kind="Internal").ap()
    MIXD = nc.dram_tensor("MIXD", [8, 128, NTOK], BF16, kind="ExternalOutput" if dbg else "Internal").ap()

    st = ExitStack()
    with st:
        P = Prog(nc)
        AW = 52000
        arena = st.enter_context(nc.sbuf_tensor("arena", [128, AW], F32))
        ps = [st.enter_context(nc.psum_tensor("ps%d" % i, [128, 512], F32)) for i in range(8)]
        ps = [p[:] for p in ps]
        psb = [p.bitcast(BF16) for p in ps]
        top = [0]

        def T(shape, dt=F32):
            n = int(np.prod(shape[1:]))
            cols = n if dt == F32 else (n + 1) // 2
            off = top[0]
            top[0] += cols
            assert top[0] <= AW, "SBUF arena overflow %d" % top[0]
            a = arena[0:shape[0], off:off + cols]
            if dt != F32:
                a = a.bitcast(dt)
            if len(shape) == 3:
                a = a.rearrange("p (a b) -> p a b", a=shape[1])
            elif len(shape) == 4:
                a = a.rearrange("p (a b c) -> p a b c", a=shape[1], b=shape[2])
            return a

        uid = [0]

        def K(s):
            uid[0] += 1
            return "%s#%d" % (s, uid[0])

        def dma(q, o, i, r=(), w=()):
            P.op(q, lambda e: e.dma_start(out=o, in_=i), r=r, w=w, dma=True)

        def dma_nc(q, o, i, r=(), w=()):
            P.op(q, lambda e: e.dma_start(out=o, in_=i, allow_slow_non_contiguous=True), r=r, w=w, dma=True)

        def mmg(o, pairs, r, w):
            def f(e):
                n = len(pairs)
                for i, (l, rh) in enumerate(pairs):
                    ins = e.matmul(o, lhsT=l, rhs=rh, start=(i == 0), stop=(i == n - 1))
                return ins
            P.op("pe", f, r=r, w=w)

        identb = T([128, 128], BF16)
        rotb = T([128, 128], BF16)
        onesb = T([128, 128], BF16)
        maskf = T([128, 64])
        maskb = T([128, 64])
        stg = [T([128, 4096]) for _ in range(2)]
        stgi = [0]
        PERSIST = None

        def wload(dst, src, q=None, ce="pool"):
            i = stgi[0] % 2
            stgi[0] += 1
            shp = list(dst.shape)
            n = int(np.prod(shp[1:]))
            assert n <= 4096
            s = stg[i][0:shp[0], 0:n]
            if len(shp) == 3:
                s = s.rearrange("p (a b) -> p a b", a=shp[1])
            qq = q or ("sp" if i == 0 else "pool")
            dma(qq, s, src, w=["stg%d" % i])
            kd = "W" + str(id(dst))
            if ce == "pool":
                P.op("pool", lambda e: e.tensor_copy(out=dst, in_=s), r=["stg%d" % i], w=["wgt"])
            elif ce == "dve":
                P.op("dve", lambda e: e.tensor_copy(out=dst, in_=s), r=["stg%d" % i], w=["wgt"])
            else:
                P.op("act", lambda e: e.copy(out=dst, in_=s), r=["stg%d" % i], w=["wgt"])

        for (dst, nm) in ((identb, "ident"), (rotb, "rot")):
            wload(dst, din[nm])
        P.op("pool", lambda e: e.memset(onesb, 1.0), w=["wgt"])
        dma("sp", maskf, din["mask_f"], w=["wgt"])
        dma("sp", maskb, din["mask_b"], w=["wgt"])
        PERSIST = top[0]

        def phase_reset():
            P.barrier()
            top[0] = PERSIST

        def phase_mod():
            cv = T([128, 2, 8])
            cvs = T([128, 2, 8])
            cvb = T([128, 8, 2], BF16)
            for j in range(2):
                dma_nc("sp", cv[:, j, :], din["cvec"][j].rearrange("(k p) -> p k", p=128), w=["cv"])
            P.op("act", lambda e: e.activation(out=cvs, in_=cv, func=AF.Silu), r=["cv"], w=["cvs"])
            P.op("dve", lambda e: e.tensor_copy(out=cvb, in_=cvs.rearrange("p j k -> p k j")), r=["cvs"], w=["cvb"])
            awb = [T([128, 8, 512], BF16) for _ in range(2)]
            Mt = T([2, 6 * D])
            bt = T([2, 6 * D])
            ng = T([2, 4, D])
            V = T([2, 6, D])
            for l in range(2):
                dma("sp", bt, din["ada_b"][l].partition_broadcast(2), w=["bt"])
                dma("sp", ng.rearrange("p a b -> p (a b)"),
                    din["norm_g"][l].rearrange("a b -> (a b)").partition_broadcast(2), w=["ng"])
                for nb in range(12):
                    ab = awb[nb % 2]
                    kab = "awb%d" % (nb % 2)
                    src = din["ada_w"][l].rearrange("(k p) n -> p k n", p=128)[:, :, nb * 512:(nb + 1) * 512]
                    i = stgi[0] % 2
                    stgi[0] += 1
                    s = stg[i][:, :].rearrange("p (a b) -> p a b", a=8)
                    dma("sp" if i == 0 else "pool", s, src, w=["stg%d" % i])
                    P.op("pool" if nb % 2 == 0 else "dve", (lambda e, ab=ab, s=s: e.tensor_copy(out=ab, in_=s)),
                         r=["stg%d" % i], w=[kab])
                    pb = ps[nb % 2][0:2, :]
                    mmg(pb, [(cvb[:, k, :], ab[:, k, :]) for k in range(8)], r=["cvb", kab], w=["ps%d" % (nb % 2)])
                    P.op("dve", (lambda e, pb=pb, nb=nb: e.tensor_tensor(out=Mt[:, nb * 512:(nb + 1) * 512], in0=pb,
                                                                       in1=bt[:, nb * 512:(nb + 1) * 512], op=ALU.add)),
                         r=["ps%d" % (nb % 2), "bt"], w=["Mt"])
                sl = lambda i: Mt[:, i * D:(i + 1) * D]
                P.op("dve", lambda e: e.scalar_tensor_tensor(out=V[:, 0, :], in0=sl(1), scalar=1.0, in1=ng[:, 0, :],
                                                             op0=ALU.add, op1=ALU.mult), r=["Mt", "ng"], w=["V"])
                P.op("dve", lambda e: e.tensor_copy(out=V[:, 1, :], in_=sl(0)), r=["Mt"], w=["V"])
                P.op("dve", lambda e: e.tensor_tensor(out=V[:, 2, :], in0=sl(2), in1=ng[:, 1, :], op=ALU.mult),
                     r=["Mt", "ng"], w=["V"])
                P.op("dve", lambda e: e.scalar_tensor_tensor(out=V[:, 3, :], in0=sl(4), scalar=1.0, in1=ng[:, 2, :],
                                                             op0=ALU.add, op1=ALU.mult), r=["Mt", "ng"], w=["V"])
                P.op("dve", lambda e: e.tensor_copy(out=V[:, 4, :], in_=sl(3)), r=["Mt"], w=["V"])
                P.op("dve", lambda e: e.tensor_tensor(out=V[:, 5, :], in0=sl(5), in1=ng[:, 3, :], op=ALU.mult),
                     r=["Mt", "ng"], w=["V"])
                dma("sp", MODV[l].rearrange("j a b -> j (a b)"), V.rearrange("p a b -> p (a b)"), r=["V"], w=["MODV"])

        def x_src(layer0_in, tt):
            if layer0_in:
                return din["ctx"][tt * 128:(tt + 1) * 128, :] if tt < 2 else din["x"][(tt - 2) * 128:(tt - 1) * 128, :]
            return XR[tt * 128:(tt + 1) * 128, :]

        class NM:
            def __init__(self, l, gi, si):
                self.xt = [T([128, D]) for _ in range(2)]
                self.junk = T([128, D], BF16)
                self.tmp = T([128, D])
                self.hb = [T([128, D], BF16) for _ in range(2)]
                self.ss = T([128, 2])
                self.rs = T([128, 2])
                self.G = [T([128, D]) for _ in range(2)]
                self.SH = [T([128, D]) for _ in range(2)]
                for j in range(2):
                    dma("sp", self.G[j], MODV[l, j, gi].partition_broadcast(128), r=["MODV"], w=["nmG"])
                    dma("sp", self.SH[j], MODV[l, j, si].partition_broadcast(128), r=["MODV"], w=["nmG"])
                self.i = 0

            def run(self, src, is_ctx, dst, dkey, bank):
                i = self.i % 2
                self.i += 1
                xt, hb = self.xt[i], self.hb[i]
                kx, kh = "nm_xt%d" % i, "nm_hb%d" % i
                ss, rs = self.ss[:, i:i + 1], self.rs[:, i:i + 1]
                G, SH = self.G[1 if is_ctx else 0], self.SH[1 if is_ctx else 0]
                dma("sp", xt, src, r=["XR"], w=[kx])
                P.op("act", lambda e: e.activation(out=self.junk, in_=xt, func=AF.Square, accum_out=ss),
                     r=[kx], w=["nm_junk", "nm_ss%d" % i])
                P.op("act", lambda e: e.activation(out=rs, in_=ss, func=AF.Sqrt, scale=1.0 / D, bias=EPS),
                     r=["nm_ss%d" % i], w=["nm_rs%d" % i])
                P.op("dve", lambda e: e.reciprocal(out=rs, in_=rs), r=["nm_rs%d" % i], w=["nm_rs%d" % i])
                P.op("dve", lambda e: e.scalar_tensor_tensor(out=self.tmp, in0=xt, scalar=rs, in1=G, op0=ALU.mult,
                                                             op1=ALU.mult), r=[kx, "nm_rs%d" % i, "nmG"], w=["nm_tmp"])
                P.op("pool", lambda e: e.tensor_tensor(out=hb, in0=self.tmp, in1=SH, op=ALU.add),
                     r=["nm_tmp", "nmG"], w=[kh])
                pb = psb[bank]

                def tr(e):
                    for k in range(8):
                        ins = e.transpose(out=pb[:, k * 128:(k + 1) * 128], in_=hb[:, k * 128:(k + 1) * 128],
                                          identity=identb)
                    return ins
                P.op("pe", tr, r=[kh], w=["ps%d" % bank])
                P.op("act", lambda e: e.copy(out=dst, in_=pb.rearrange("p (k n) -> p k n", k=8)),
                     r=["ps%d" % bank], w=[dkey])

        class RES:
            def __init__(self, l, gidx):
                self.GT = [T([128, D]) for _ in range(2)]
                for j in range(2):
                    dma("sp", self.GT[j], MODV[l, j, gidx].partition_broadcast(128), r=["MODV"], w=["resG"])
                self.xo = [T([128, D]) for _ in range(2)]
                self.t = T([128, D])
                self.junk = T([128, 512], BF16)
                self.ss = T([128, 4])
                self.rs = T([128, 2])
                self.i = 0

            def run(self, b0, b1, src, dstd, is_ctx, wkey):
                i = self.i % 2
                self.i += 1
                xo = self.xo[i]
                kx = "res_x%d" % i
                ss = self.ss[:, 2 * i:2 * i + 2]
                rs = self.rs[:, i:i + 1]
                GT = self.GT[1 if is_ctx else 0]
                dma("pool", xo, src, r=["XR"], w=[kx])
                P.op("act", lambda e: e.activation(out=self.junk, in_=ps[b0], func=AF.Square, accum_out=ss[:, 0:1]),
                     r=["ps%d" % b0], w=["res_junk", "res_ss%d" % i])
                P.op("act", lambda e: e.activation(out=self.junk, in_=ps[b1], func=AF.Square, accum_out=ss[:, 1:2]),
                     r=["ps%d" % b1], w=["res_junk", "res_ss%d" % i])
                P.op("dve", lambda e: e.tensor_tensor(out=rs, in0=ss[:, 0:1], in1=ss[:, 1:2], op=ALU.add),
                     r=["res_ss%d" % i], w=["res_rs%d" % i])
                P.op("act", lambda e: e.activation(out=rs, in_=rs, func=AF.Sqrt, scale=1.0 / D, bias=EPS),
                     r=["res_rs%d" % i], w=["res_rs%d" % i])
                P.op("dve", lambda e: e.reciprocal(out=rs, in_=rs), r=["res_rs%d" % i], w=["res_rs%d" % i])
                for hf, bk in ((0, b0), (1, b1)):
                    P.op("dve", (lambda e, hf=hf, bk=bk: e.scalar_tensor_tensor(
                        out=self.t[:, hf * 512:(hf + 1) * 512], in0=ps[bk], scalar=rs,
                        in1=GT[:, hf * 512:(hf + 1) * 512], op0=ALU.mult, op1=ALU.mult)),
                        r=["ps%d" % bk, "res_rs%d" % i, "resG"], w=["res_t", "ps%d" % bk])
                P.op("pool", lambda e: e.tensor_tensor(out=xo, in0=self.t, in1=xo, op=ALU.add),
                     r=["res_t", kx], w=[kx])
                dma("pool", dstd, xo, r=[kx], w=[wkey])

        def rope_evict(pbank, n, t0, dst, dkey, ro, is_ctx):
            src = ps[pbank][:, 0:n]
            if is_ctx:
                P.op("act", lambda e: e.copy(out=dst, in_=src), r=["ps%d" % pbank], w=[dkey])
                return
            qs, t1, t2, cosb, sinb, rb = ro
            P.op("act", lambda e: e.copy(out=qs[:, 0:n], in_=src), r=["ps%d" % pbank], w=["ro_qs"])
            mmg(ps[rb][:, 0:n], [(rotb, qs[:, 0:n])], r=["ro_qs"], w=["ps%d" % rb])
            P.op("dve", lambda e: e.tensor_tensor(out=t1[:, 0:n], in0=src, in1=cosb[:, 0:n], op=ALU.mult),
                 r=["ps%d" % pbank, "ro_cs"], w=["ro_t1", "ps%d" % pbank])
            P.op("dve", lambda e: e.tensor_tensor(out=t2[:, 0:n], in0=ps[rb][:, 0:n], in1=sinb[:, 0:n], op=ALU.mult),
                 r=["ps%d" % rb, "ro_cs"], w=["ro_t2", "ps%d" % rb])
            P.op("pool", lambda e: e.tensor_tensor(out=dst, in0=t1[:, 0:n], in1=t2[:, 0:n], op=ALU.add),
                 r=["ro_t1", "ro_t2"], w=[dkey])

        def rope_tiles():
            return (T([128, 512], BF16), T([128, 512]), T([128, 512]), T([128, 512]), T([128, 512]))

        def rope_load(ro, b):
            if b == 0:
                return
            c0 = (b - 1) * 512
            dma("pool", ro[3], din["cos_t"][:, c0:c0 + 512], w=["ro_cs"])
            dma("pool", ro[4], din["sin_t"][:, c0:c0 + 512], w=["ro_cs"])

        def attention(nb_list, heads, scale, post, n_s=3):
            PT = [T([128, 512], BF16) for _ in range(3)]
            cnt = [0]
            for b in nb_list:
                t0, ntl = BLOCKS[b]
                n = ntl * 128
                kts = range(2) if b == 0 else range(NT)
                for hi, hd in enumerate(heads):
                    hd["pre"](b)
                    nk = len(kts)
                    for ki, kt in enumerate(kts):
                        c = cnt[0]
                        cnt[0] += 1
                        sb_ = c % n_s
                        pt = PT[c % 3]
                        kpt = "PT%d" % (c % 3)
                        mmg(ps[sb_][:, 0:n], hd["qk"](b, kt), r=hd["rk"](b), w=["ps%d" % sb_])
                        P.op("act", (lambda e, sb_=sb_, pt=pt, n=n: e.activation(out=pt[:, 0:n], in_=ps[sb_][:, 0:n],
                                                                              func=AF.Exp, scale=scale)),
                             r=["ps%d" % sb_], w=[kpt, "ps%d" % sb_])

                        def pv(e, kt=kt, pt=pt, ki=ki, nk=nk, hd=hd, n=n):
                            e.matmul(ps[3][:, 0:n], lhsT=hd["v"](kt), rhs=pt[:, 0:n], start=(ki == 0), stop=(ki == nk - 1))
                            return e.matmul(ps[4][:, 0:n], lhsT=onesb, rhs=pt[:, 0:n], start=(ki == 0), stop=(ki == nk - 1))
                        P.op("pe", pv, r=[kpt] + hd["rv"], w=["ps3", "ps4"])
                    post(b, hi, n)

        def phase_ffn(l, need_ctx, final):
            W2 = 4355
            Wd = T([128, NJ, D], BF16)
            CW = T([128, 4, NJ])
            for i in range(3):
                dma_nc("sp", CW[:, i, :], din["ffn_conv_w"][l, i].rearrange("(j p) -> p j", p=128), w=["CW"])
            dma_nc("sp", CW[:, 3, :], din["ffn_conv_b"][l].rearrange("(j p) -> p j", p=128), w=["CW"])
            wdsrc = din["ffn_w_down"][l].rearrange("(j p) n -> p j n", p=128)
            for j0 in range(0, NJ, 4):
                j1 = min(NJ, j0 + 4)
                wload(Wd[:, j0:j1, :], wdsrc[:, j0:j1, :])
            top_save = top[0]
            nm = NM(l, 3, 4)
            hTs = [T([128, 8, 512], BF16) for _ in range(2)]
            zt = T([128, 8, 1], BF16)
            P.op("pool", lambda e: e.memset(zt, 0.0), w=["zt"])
            for c in (0, 257, 4354):
                dma_nc("pool", H2D[:, :, c:c + 1], zt, r=["zt"], w=["H2D"])
            blks = list(range(0 if need_ctx else 1, 9))
            for b in blks:
                t0, ntl = BLOCKS[b]
                n = ntl * 128
                hT = hTs[b % 2]
                kh = "hTs%d" % (b % 2)
                for ti in range(ntl):
                    tt = t0 + ti
                    nm.run(x_src(False, tt), tt < 2, hT[:, :, ti * 128:(ti + 1) * 128], kh, tt % 2)
                c0 = 1 if b == 0 else 258 + (b - 1) * 512
                dma("pool", H2D[:, :, c0:c0 + n], hT[:, :, 0:n], r=[kh], w=["H2D"])
            P.barrier()
            top[0] = top_save
            res = RES(l, 5)
            GTt = T([128, NJ, 1024], BF16)
            H2P = [T([128, 8, 1026], BF16) for _ in range(2)]
            wgf = [T([128, 8, 128], BF16) for _ in range(2)]
            wuf = [T([128, 8, 128], BF16) for _ in range(2)]
            acc = [T([128, 512]) for _ in range(2)]
            sil = [T([128, 512]) for _ in range(2)]
            parts = [blks[i:i + 2] for i in range(0, len(blks), 2)]
            wgsrc = din["ffn_w_gate"][l].rearrange("(k p) n -> p k n", p=128)
            wusrc = din["ffn_w_up"][l].rearrange("(k p) n -> p k n", p=128)
            it = [0]
            bc0 = lambda b: 1 if b == 0 else 258 + (b - 1) * 512
            for pi, part in enumerate(parts):
                cstart = bc0(part[0]) - 1
                cend = bc0(part[-1]) + BLOCKS[part[-1]][1] * 128 + 1
                npc = cend - cstart
                H2T = H2P[pi % 2]
                kH = "H2P%d" % (pi % 2)
                dma("sp", H2T[:, :, 0:npc], H2D[:, :, cstart:cend], r=["H2D"], w=[kH])
                goff = {}
                o = 0
                for b in part:
                    goff[b] = o
                    o += BLOCKS[b][1] * 128
                for j in range(NJ):
                    wg, wu = wgf[j % 2], wuf[j % 2]
                    kw = "ffw%d" % (j % 2)
                    for (dst, srcw) in ((wg, wgsrc), (wu, wusrc)):
                        i = stgi[0] % 2
                        stgi[0] += 1
                        s = stg[i][:, 0:1024].rearrange("p (a b) -> p a b", a=8)
                        dma("sp", s, srcw[:, :, j * 128:(j + 1) * 128], w=["stg%d" % i])
                        P.op("pool", (lambda e, dst=dst, s=s: e.tensor_copy(out=dst, in_=s)), r=["stg%d" % i], w=[kw])
                    for b in part:
                        t0, ntl = BLOCKS[b]
                        n = ntl * 128
                        c0 = bc0(b) - cstart
                        q = it[0] % 2
                        it[0] += 1
                        pa, pu, ph = ps[q], ps[2 + q], ps[4 + q]
                        ka, ku, kh = "ps%d" % q, "ps%d" % (2 + q), "ps%d" % (4 + q)
                        mmg(pa[:, 0:n], [(wg[:, k, :], H2T[:, k, c0:c0 + n]) for k in range(8)], r=[kw, kH], w=[ka])
                        hal = H2T[:, :, c0 - 1:c0 + n + 1:n + 1]
                        mmg(ph[:, 0:2], [(wg[:, k, :], hal[:, k, :]) for k in range(8)], r=[kw, kH], w=[kh])
                        mmg(pu[:, 0:n], [(wu[:, k, :], H2T[:, k, c0:c0 + n]) for k in range(8)], r=[kw, kH], w=[ku])
                        ac, sl_ = acc[q], sil[q]
                        kac, ksl = "acc%d" % q, "sil%d" % q
                        w0, w1, w2, bb = (CW[:, i, j:j + 1] for i in range(4))
                        P.op("dve", (lambda e, ac=ac, pa=pa, w1=w1, bb=bb, n=n: e.tensor_scalar(
                            out=ac[:, 0:n], in0=pa[:, 0:n], scalar1=w1, scalar2=bb, op0=ALU.mult, op1=ALU.add)),
                            r=[ka, "CW"], w=[kac])
                        P.op("dve", (lambda e, ac=ac, pa=pa, w0=w0, n=n: e.scalar_tensor_tensor(
                            out=ac[:, 1:n], in0=pa[:, 0:n - 1], scalar=w0, in1=ac[:, 1:n], op0=ALU.mult, op1=ALU.add)),
                            r=[ka, kac], w=[kac])
                        P.op("dve", (lambda e, ac=ac, pa=pa, w2=w2, n=n: e.scalar_tensor_tensor(
                            out=ac[:, 0:n - 1], in0=pa[:, 1:n], scalar=w2, in1=ac[:, 0:n - 1], op0=ALU.mult, op1=ALU.add)),
                            r=[ka, kac], w=[kac, ka])
                        P.op("dve", (lambda e, ac=ac, ph=ph, w0=w0: e.scalar_tensor_tensor(
                            out=ac[:, 0:1], in0=ph[:, 0:1], scalar=w0, in1=ac[:, 0:1], op0=ALU.mult, op1=ALU.add)),
                            r=[kh, kac], w=[kac])
                        P.op("dve", (lambda e, ac=ac, ph=ph, w2=w2, n=n: e.scalar_tensor_tensor(
                            out=ac[:, n - 1:n], in0=ph[:, 1:2], scalar=w2, in1=ac[:, n - 1:n], op0=ALU.mult, op1=ALU.add)),
                            r=[kh, kac], w=[kac, kh])
                        P.op("act", (lambda e, ac=ac, sl_=sl_, n=n: e.activation(out=sl_[:, 0:n], in_=ac[:, 0:n], func=AF.Silu)),
                             r=[kac], w=[ksl])
                        g0 = goff[b]
                        P.op("dve", (lambda e, sl_=sl_, pu=pu, j=j, g0=g0, n=n: e.tensor_tensor(
                            out=GTt[:, j, g0:g0 + n], in0=sl_[:, 0:n], in1=pu[:, 0:n], op=ALU.mult)),
                            r=[ksl, ku], w=["GT", ku])
                for b in part:
                    t0, ntl = BLOCKS[b]
                    for ti in range(ntl):
                        tt = t0 + ti
                        g0 = goff[b] + ti * 128
                        for hf in range(2):
                            mmg(ps[6 + hf], [(GTt[:, j, g0:g0 + 128], Wd[:, j, hf * 512:(hf + 1) * 512]) for j in range(NJ)],
                                r=["GT", "wgt"], w=["ps%d" % (6 + hf)])
                        if final:
                            dstd = out[(tt - 2) * 128:(tt - 1) * 128, :]
                            res.run(6, 7, x_src(False, tt), dstd, tt < 2, "OUT")
                        else:
                            res.run(6, 7, x_src(False, tt), XR[tt * 128:(tt + 1) * 128, :], tt < 2, "XR2")

        def phase_l0():
            l = 0
            LAM_INIT = 0.2
            KT = T([128, 4, NTOK], BF16)
            Vv = T([128, NT, 512], BF16)
            keep = top[0]
            w_in = T([128, 8, 2048], BF16)
            wsrc = din["ev_w_in"][0].rearrange("(k p) n -> p k n", p=128)
            for c in range(4):
                wload(w_in[:, :, c * 512:(c + 1) * 512], wsrc[:, :, c * 512:(c + 1) * 512])
            nm = NM(l, 0, 1)
            hT = T([128, 8, 512], BF16)
            ro = rope_tiles() + (7,)
            ub = [T([128, 512]) for _ in range(2)]
            qb = [T([128, 4, 512], BF16) for _ in range(2)]
            for b, (t0, ntl) in enumerate(BLOCKS):
                n = ntl * 128
                col0 = t0 * 128
                rope_load(ro, b)
                for ti in range(ntl):
                    tt = t0 + ti
                    nm.run(x_src(True, tt), tt < 2, hT[:, :, ti * 128:(ti + 1) * 128], "hT", 6)
                    if dbg:
                        pass
                for ti in range(ntl):
                    tt = t0 + ti
                    bk = 4 + (ti % 2)
                    mmg(ps[bk], [(hT[:, k, ti * 128:(ti + 1) * 128], w_in[:, k, 1536:2048]) for k in range(8)],
                        r=["hT", "wgt"], w=["ps%d" % bk])
                    P.op("act", (lambda e, tt=tt, bk=bk: e.copy(out=Vv[:, tt, :], in_=ps[bk])), r=["ps%d" % bk], w=["Vv"])
                for c in range(4):
                    bk = c % 2
                    mmg(ps[bk][:, 0:n], [(w_in[:, k, c * 128:(c + 1) * 128], hT[:, k, 0:n]) for k in range(8)],
                        r=["hT", "wgt"], w=["ps%d" % bk])
                    u = ub[c % 2]
                    P.op("act", (lambda e, u=u, bk=bk, n=n: e.copy(out=u[:, 0:n], in_=ps[bk][:, 0:n])),
                         r=["ps%d" % bk], w=["ub%d" % (c % 2)])
                    dma("pool", UT[c][:, col0:col0 + n], u[:, 0:n], r=["ub%d" % (c % 2)], w=["UT"])
                qbb = qb[b % 2]
                kq = "qb%d" % (b % 2)
                for c in range(4):
                    bk = 2 + (c % 2)
                    mmg(ps[bk][:, 0:n], [(w_in[:, k, 512 + c * 128:512 + (c + 1) * 128], hT[:, k, 0:n]) for k in range(8)],
                        r=["hT", "wgt"], w=["ps%d" % bk])
                    rope_evict(bk, n, col0, qbb[:, c, 0:n], kq, ro, b == 0)
                dma("pool", QT[b][:, :, 0:n], qbb[:, :, 0:n], r=[kq], w=["QT"])
                for c in range(4):
                    bk = 2 + (c % 2)
                    mmg(ps[bk][:, 0:n], [(w_in[:, k, 1024 + c * 128:1024 + (c + 1) * 128], hT[:, k, 0:n]) for k in range(8)],
                        r=["hT", "wgt"], w=["ps%d" % bk])
                    rope_evict(bk, n, col0, KT[:, c, col0:col0 + n], "KT", ro, b == 0)
            P.barrier()
            top[0] = keep
            pw = T([128, 4, 128], BF16)
            for g in range(4):
                wload(pw[:, g, :], din["pool_w"][0, g])
            psc = T([128, 4])
            dma_nc("sp", psc, din["pool_scale"][0].rearrange("(g p) -> p g", p=128), w=["psc"])
            UP = T([128, PW])
            Aa = T([128, PW])
            Ab = T([128, PW])
            IC = T([128, PW])
            dT = T([128, PW], BF16)
            mpo = [T([128, 512], BF16) for _ in range(2)]
            for g in range(4):
                hw = (1, 2, 4, 8)[g]
                P.op("pool", lambda e: e.memset(UP, 0.0), w=["UP"])
                dma("sp", UP[:, PC0:PC0 + 256], UT[g][:, 0:256], r=["UT"], w=["UP"])
                dma("sp", UP[:, PL0:PL0 + 4096], UT[g][:, 256:NTOK], r=["UT"], w=["UP"])
                dma("pool", IC, din["invcnt"][g].partition_broadcast(128), w=["IC"])
                cur, ck = UP, "UP"
                bufs = [(Aa, "Aa"), (Ab, "Ab")]
                width = PW
                for s in range(g + 1):
                    sh = 1 << s
                    nxt, nk = bufs[s % 2]
                    width -= sh
                    P.op("dve", (lambda e, cur=cur, nxt=nxt, sh=sh, width=width: e.tensor_tensor(
                        out=nxt[:, 0:width], in0=cur[:, 0:width], in1=cur[:, sh:sh + width], op=ALU.add)),
                        r=[ck], w=[nk])
                    cur, ck = nxt, nk
                oth, ok = bufs[(g + 1) % 2]
                P.op("dve", (lambda e, cur=cur, oth=oth, hw=hw: e.tensor_tensor(
                    out=oth[:, 8:PW - 8], in0=cur[:, 8 - hw:PW - 8 - hw], in1=IC[:, 8:PW - 8], op=ALU.mult)),
                    r=[ck, "IC"], w=[ok])
                P.op("pool", (lambda e, oth=oth: e.tensor_tensor(out=dT[:, 8:PW - 8], in0=oth[:, 8:PW - 8],
                                                                 in1=UP[:, 8:PW - 8], op=ALU.subtract)),
                     r=[ok, "UP"], w=["dT"])
                for b, (t0, ntl) in enumerate(BLOCKS):
                    n = ntl * 128
                    col0 = t0 * 128
                    pc = PC0 if b == 0 else PL0 + (b - 1) * 512
                    bk = b % 2
                    mmg(ps[bk][:, 0:n], [(pw[:, g, :], dT[:, pc:pc + n])], r=["dT", "wgt"], w=["ps%d" % bk])
                    mp = mpo[b % 2]
                    P.op("act", (lambda e, bk=bk, n=n, g=g, mp=mp: e.activation(
                        out=mp[:, 0:n], in_=ps[bk][:, 0:n], func=AF.Identity, scale=psc[:, g:g + 1])),
                        r=["ps%d" % bk, "psc"], w=["mpo%d" % (b % 2)])
                    dma("pool", MIXD[g][:, col0:col0 + n], mp[:, 0:n], r=["mpo%d" % (b % 2)], w=["MIXD"])
            P.barrier()
            top[0] = keep
            wo = T([128, 8, D], BF16)
            wosrc = din["mix_w_out"][l].rearrange("(k p) n -> p k n", p=128)
            for c in range(2):
                wload(wo[:, :, c * 512:(c + 1) * 512], wosrc[:, :, c * 512:(c + 1) * 512])
            lamt = T([128, 4, 64])
            lj = T([128, 64])
            lsum = T([128, 2])
            nlam = T([128, 1])
            dma("sp", lamt.rearrange("p a b -> p (a b)"),
                din["diff_lambda"][0].rearrange("a b -> (a b)").partition_broadcast(128), w=["lamt"])
            for i in range(2):
                P.op("dve", (lambda e, i=i: e.scalar_tensor_tensor(out=lj, in0=lamt[:, 2 * i, :], scalar=1.0,
                                                                   in1=lamt[:, 2 * i + 1, :], op0=ALU.mult, op1=ALU.mult,
                                                                   accum_out=lsum[:, i:i + 1])),
                     r=["lamt"], w=["lj", "lsum"])
            P.op("act", lambda e: e.activation(out=lsum, in_=lsum, func=AF.Exp), r=["lsum"], w=["lsum"])
            P.op("dve", lambda e: e.tensor_tensor(out=nlam, in0=lsum[:, 1:2], in1=lsum[:, 0:1], op=ALU.subtract),
                 r=["lsum"], w=["nlam"])
            P.op("dve", lambda e: e.tensor_scalar(out=nlam, in0=nlam, scalar1=-LAM_INIT, scalar2=None, op0=ALU.add),
                 r=["nlam"], w=["nlam"])
            sln = T([128, 1])
            dma_nc("sp", sln, din["diff_subln"][0].rearrange("(p o) -> p o", o=1), w=["sln"])
            P.op("dve", lambda e: e.tensor_scalar(out=sln, in0=sln, scalar1=1.0 - LAM_INIT, scalar2=None, op0=ALU.mult),
                 r=["sln"], w=["sln"])
            res = RES(l, 2)
            qtl = [T([128, 4, 512], BF16) for _ in range(2)]
            MIXA = [T([128, 4, 512], BF16) for _ in range(2)]
            rsum = T([128, 512])
            o2 = [T([128, 512]) for _ in range(2)]
            oc = T([128, 512])
            sq = T([128, 512], BF16)
            rstd = T([128, 512])
            state = {}

            mixp = [T([128, 4, 512], BF16) for _ in range(2)]

            def pre(b):
                if state.get("b") != b:
                    state["b"] = b
                    n = BLOCKS[b][1] * 128
                    col0 = BLOCKS[b][0] * 128
                    dma("sp", qtl[b % 2][:, :, 0:n], QT[b][:, :, 0:n], r=["QT"], w=["qtl%d" % (b % 2)])
                    for g in range(4):
                        dma("sp", mixp[b % 2][:, g, 0:n], MIXD[g][:, col0:col0 + n], r=["MIXD"], w=["mixp%d" % (b % 2)])

            heads = []
            for h in range(4):
                for j in range(2):
                    pr = slice(64 * j, 64 * j + 64)
                    heads.append(dict(
                        pre=pre,
                        qk=(lambda b, kt, h=h, pr=pr: [(KT[pr, h, kt * 128:(kt + 1) * 128],
                                                        qtl[b % 2][pr, h, 0:BLOCKS[b][1] * 128])]),
                        v=(lambda kt, h=h: Vv[:, kt, h * 128:(h + 1) * 128]),
                        rk=(lambda b: ["KT", "qtl%d" % (b % 2)]), rv=["Vv"]))

            def post(b, hi, n):
                h, j = hi // 2, hi % 2
                mixa = MIXA[b % 2]
                km = "MIXA%d" % (b % 2)
                P.op("dve", lambda e: e.reciprocal(out=rsum[:, 0:n], in_=ps[4][:, 0:n]), r=["ps4"], w=["rsum", "ps4"])
                P.op("dve", lambda e: e.tensor_tensor(out=o2[j][:, 0:n], in0=ps[3][:, 0:n], in1=rsum[:, 0:n], op=ALU.mult),
                     r=["ps3", "rsum"], w=["o2_%d" % j, "ps3"])
                if j == 1:
                    P.op("dve", lambda e: e.scalar_tensor_tensor(out=oc[:, 0:n], in0=o2[1][:, 0:n], scalar=nlam,
                                                                 in1=o2[0][:, 0:n], op0=ALU.mult, op1=ALU.add),
                         r=["o2_0", "o2_1", "nlam"], w=["oc"])
                    P.op("act", lambda e: e.activation(out=sq[:, 0:n], in_=oc[:, 0:n], func=AF.Square), r=["oc"], w=["sq"])
                    mmg(ps[5][:, 0:n], [(onesb, sq[:, 0:n])], r=["sq"], w=["ps5"])
                    P.op("act", lambda e: e.activation(out=rstd[:, 0:n], in_=ps[5][:, 0:n], func=AF.Sqrt, scale=1.0 / 128,
                                                       bias=EPS), r=["ps5"], w=["rstd", "ps5"])
                    P.op("dve", lambda e: e.reciprocal(out=rstd[:, 0:n], in_=rstd[:, 0:n]), r=["rstd"], w=["rstd"])
                    P.op("dve", lambda e: e.scalar_tensor_tensor(out=mixa[:, h, 0:n], in0=oc[:, 0:n], scalar=sln,
                                                                 in1=rstd[:, 0:n], op0=ALU.mult, op1=ALU.mult),
                         r=["oc", "rstd", "sln"], w=[km])
                    if dbg:
                        col0 = BLOCKS[b][0] * 128
                        dma("sp", MIXD[4 + h][:, col0:col0 + n], mixa[:, h, 0:n], r=[km], w=["MIXD"])
                    if h == 3:
                        t0, ntl = BLOCKS[b]
                        for ti in range(ntl):
                            tt = t0 + ti
                            cs = slice(tt * 128, (tt + 1) * 128)
                            ls = slice(ti * 128, (ti + 1) * 128)
                            for hf in range(2):
                                prs = [(mixp[b % 2][:, g, ls], wo[:, g, hf * 512:(hf + 1) * 512]) for g in range(4)] + \
                                      [(mixa[:, hh, ls], wo[:, 4 + hh, hf * 512:(hf + 1) * 512]) for hh in range(4)]
                                mmg(ps[6 + hf], prs, r=["mixp%d" % (b % 2), km, "wgt"], w=["ps%d" % (6 + hf)])
                            res.run(6, 7, x_src(True, tt), XR[tt * 128:(tt + 1) * 128, :], tt < 2, "XR1")

            attention(list(range(9)), heads, 0.125, post)

        def rownorm(pt, W, NB, dst, dkey, tg):
            junk, ss, rs = tg
            P.op("act", lambda e: e.activation(out=junk[:, 0:W], in_=pt[0][:, 0:W], func=AF.Square, accum_out=ss),
                 r=[pt[1]], w=["rn_junk", "rn_ss"])
            P.op("act", lambda e: e.activation(out=rs, in_=ss, func=AF.Sqrt, scale=1.0 / W, bias=EPS), r=["rn_ss"], w=["rn_rs"])
            P.op("dve", lambda e: e.reciprocal(out=rs, in_=rs), r=["rn_rs"], w=["rn_rs"])
            P.op("dve", lambda e: e.scalar_tensor_tensor(out=dst, in0=pt[0][:, 0:W], scalar=rs, in1=NB[:, 0:W], op0=ALU.mult,
                                                         op1=ALU.mult), r=[pt[1], "rn_rs", "wgt"], w=[dkey, pt[1]])

        def phase_l1():
            l = 1
            KN = T([128, 4, NTOK], BF16)
            VM = T([128, NT, 512], BF16)
            KR2 = T([128, NTOK], BF16)
            keep = top[0]
            WA = T([128, 8, 512], BF16)
            WB = T([128, 8, 256], BF16)
            WKR = T([128, 8, 128], BF16)
            WQN = T([128, 4, 4, 128], BF16)
            WQR = T([128, 4, 256], BF16)
            WKN = T([128, 2, 4, 128], BF16)
            WVV = T([128, 2, 512], BF16)
            wsrc = din["od_w_in"][0].rearrange("(k p) n -> p k n", p=128)
            wload(WA, wsrc[:, :, 0:512])
            wload(WB, wsrc[:, :, 512:768])
            wload(WKR[:, :, 0:64], wsrc[:, :, 768:832])
            wload(WKR[:, :, 64:128], wsrc[:, :, 768:832])
            uq = din["mla_w_uq"][0].rearrange("(k p) (h c) -> p k h c", p=128, c=192)
            ukv = din["mla_w_ukv"][0].rearrange("(k p) (h c) -> p k h c", p=128, c=256)
            for k in range(4):
                wload(WQN[:, k], uq[:, k, :, 0:128])
                wload(WQR[:, k, :].rearrange("p (h c) -> p h c", h=4), uq[:, k, :, 128:192])
            for k in range(2):
                wload(WKN[:, k], ukv[:, k, :, 0:128])
                wload(WVV[:, k, :].rearrange("p (h c) -> p h c", h=4), ukv[:, k, :, 128:256])
            QNb = T([128, 512])
            KVNb = T([128, 256])
            dma("sp", QNb, din["mla_q_norm"][0].partition_broadcast(128), w=["wgt"])
            dma("sp", KVNb, din["mla_kv_norm"][0].partition_broadcast(128), w=["wgt"])
            nm = NM(l, 0, 1)
            hTs = [T([128, 8, 512], BF16)] * 2
            ro = rope_tiles() + (7,)
            tg = (T([128, 512], BF16), T([128, 1]), T([128, 1]))
            cqn = T([128, 512], BF16)
            ckvn = T([128, 256], BF16)
            cqT = T([128, 4, 512], BF16)
            ckvT = T([128, 2, 512], BF16)
            qnb = [T([128, 4, 512], BF16)] * 2
            qrb = [T([128, 2, 512], BF16)] * 2
            for b, (t0, ntl) in enumerate(BLOCKS):
                n = ntl * 128
                col0 = t0 * 128
                hT = hTs[b % 2]
                khT = "hTs0"
                rope_load(ro, b)
                for ti in range(ntl):
                    tt = t0 + ti
                    nm.run(x_src(False, tt), tt < 2, hT[:, :, ti * 128:(ti + 1) * 128], khT, 6)
                dma("pool", HT1[b][:, :, 0:n], hT[:, :, 0:n], r=[khT], w=["HT1"])
                for ti in range(ntl):
                    ts_ = slice(ti * 128, (ti + 1) * 128)
                    mmg(ps[0], [(hT[:, k, ts_], WA[:, k, :]) for k in range(8)], r=[khT, "wgt"], w=["ps0"])
                    rownorm((ps[0], "ps0"), 512, QNb, cqn, "cqn", tg)

                    def trq(e):
                        for k in range(4):
                            ins = e.transpose(out=psb[2][:, k * 128:(k + 1) * 128], in_=cqn[:, k * 128:(k + 1) * 128], identity=identb)
                        return ins
                    P.op("pe", trq, r=["cqn"], w=["ps2"])
                    P.op("act", (lambda e, ts_=ts_: e.copy(out=cqT[:, :, ts_], in_=psb[2][:, 0:512].rearrange("p (k n) -> p k n", k=4))),
                         r=["ps2"], w=["cqT"])
                    mmg(ps[1][:, 0:256], [(hT[:, k, ts_], WB[:, k, :]) for k in range(8)], r=[khT, "wgt"], w=["ps1"])
                    rownorm((ps[1], "ps1"), 256, KVNb, ckvn, "ckvn", tg)

                    def trk(e):
                        for k in range(2):
                            ins = e.transpose(out=psb[3][:, k * 128:(k + 1) * 128], in_=ckvn[:, k * 128:(k + 1) * 128], identity=identb)
                        return ins
                    P.op("pe", trk, r=["ckvn"], w=["ps3"])
                    P.op("act", (lambda e, ts_=ts_: e.copy(out=ckvT[:, :, ts_], in_=psb[3][:, 0:256].rearrange("p (k n) -> p k n", k=2))),
                         r=["ps3"], w=["ckvT"])
                for ti in range(ntl):
                    tt = t0 + ti
                    ts_ = slice(ti * 128, (ti + 1) * 128)
                    mmg(ps[0], [(ckvT[:, k, ts_], WVV[:, k, :]) for k in range(2)], r=["ckvT", "wgt"], w=["ps0"])
                    P.op("act", (lambda e, tt=tt: e.copy(out=VM[:, tt, :], in_=ps[0])), r=["ps0"], w=["VM", "ps0"])
                for h in range(4):
                    bk = 4 + (h % 2)
                    mmg(ps[bk][:, 0:n], [(WKN[:, k, h, :], ckvT[:, k, 0:n]) for k in range(2)], r=["ckvT", "wgt"], w=["ps%d" % bk])
                    P.op("act", (lambda e, h=h, bk=bk, n=n, col0=col0: e.copy(out=KN[:, h, col0:col0 + n], in_=ps[bk][:, 0:n])),
                         r=["ps%d" % bk], w=["KN", "ps%d" % bk])
                mmg(ps[4][:, 0:n], [(WKR[:, k, :], hT[:, k, 0:n]) for k in range(8)], r=[khT, "wgt"], w=["ps4"])
                rope_evict(4, n, col0, KR2[:, col0:col0 + n], "KR2", ro, b == 0)
                if b >= 1:
                    qn, qr = qnb[b % 2], qrb[b % 2]
                    for h in range(4):
                        bk = 4 + (h % 2)
                        mmg(ps[bk][:, 0:n], [(WQN[:, k, h, :], cqT[:, k, 0:n]) for k in range(4)], r=["cqT", "wgt"], w=["ps%d" % bk])
                        P.op("act", (lambda e, h=h, bk=bk, n=n, qn=qn: e.copy(out=qn[:, h, 0:n], in_=ps[bk][:, 0:n])),
                             r=["ps%d" % bk], w=["qnb0", "ps%d" % bk])
                    dma("pool", QT[b][:, :, 0:n], qn[:, :, 0:n], r=["qnb0"], w=["QT"])
                    for c in range(2):
                        bk = 4 + (c % 2)
                        mmg(ps[bk][:, 0:n], [(WQR[:, k, c * 128:(c + 1) * 128], cqT[:, k, 0:n]) for k in range(4)],
                            r=["cqT", "wgt"], w=["ps%d" % bk])
                        rope_evict(bk, n, col0, qr[:, c, 0:n], "qrb0", ro, False)
                    dma("pool", QR[b][:, :, 0:n], qr[:, :, 0:n], r=["qrb0"], w=["QR"])
            P.barrier()
            top[0] = keep
            qnl = [T([128, 4, 512], BF16) for _ in range(2)]
            qrl = [T([128, 2, 512], BF16) for _ in range(2)]
            rsum = T([128, 512])
            mo = [T([128, 512], BF16) for _ in range(2)]
            state = {}

            def pre(b):
                if state.get("b") != b:
                    state["b"] = b
                    n = BLOCKS[b][1] * 128
                    dma("sp", qnl[b % 2][:, :, 0:n], QT[b][:, :, 0:n], r=["QT"], w=["qnl%d" % (b % 2)])
                    dma("sp", qrl[b % 2][:, :, 0:n], QR[b][:, :, 0:n], r=["QR"], w=["qnl%d" % (b % 2)])

            heads = []
            for h in range(4):
                pr = slice(64 * (h % 2), 64 * (h % 2) + 64)
                heads.append(dict(
                    pre=pre,
                    qk=(lambda b, kt, h=h, pr=pr: [(KN[:, h, kt * 128:(kt + 1) * 128], qnl[b % 2][:, h, 0:BLOCKS[b][1] * 128]),
                                                    (KR2[pr, kt * 128:(kt + 1) * 128], qrl[b % 2][pr, h // 2, 0:BLOCKS[b][1] * 128])]),
                    v=(lambda kt, h=h: VM[:, kt, h * 128:(h + 1) * 128]),
                    rk=(lambda b: ["KN", "KR2", "qnl%d" % (b % 2)]), rv=["VM"]))

            def post(b, hi, n):
                col0 = BLOCKS[b][0] * 128
                m = mo[hi % 2]
                km = "mo%d" % (hi % 2)
                P.op("dve", lambda e: e.reciprocal(out=rsum[:, 0:n], in_=ps[4][:, 0:n]), r=["ps4"], w=["rsum", "ps4"])
                P.op("dve", lambda e: e.tensor_tensor(out=m[:, 0:n], in0=ps[3][:, 0:n], in1=rsum[:, 0:n], op=ALU.mult),
                     r=["ps3", "rsum"], w=[km, "ps3"])
                dma("pool", MIXD[hi][:, col0:col0 + n], m[:, 0:n], r=[km], w=["MIXD"])

            attention(list(range(1, 9)), heads, 192.0 ** -0.5, post)
            phase_reset()
            LB = T([128, 2, 2, 4])
            lbv = T([128, 2, 4])
            oml = T([128, 2, 4])
            hgn = T([128, 1])
            ones1 = T([128, 1])
            for d in range(2):
                for ll in range(2):
                    dma_nc("sp", LB[:, d, ll, :], din["hgrn_lb"][d, ll].rearrange("(h p) -> p h", p=128), w=["LB"])
            dma_nc("sp", hgn, din["hgrn_norm"][0].rearrange("(p o) -> p o", o=1), w=["hgn"])
            P.op("dve", lambda e: e.tensor_tensor(out=lbv, in0=LB[:, :, 1, :], in1=LB[:, :, 0, :], op=ALU.subtract), r=["LB"], w=["lbv"])
            P.op("act", lambda e: e.activation(out=lbv, in_=lbv, func=AF.Sigmoid), r=["lbv"], w=["lbv"])
            P.op("dve", lambda e: e.tensor_scalar(out=oml, in0=lbv, scalar1=-1.0, scalar2=1.0, op0=ALU.mult, op1=ALU.add),
                 r=["lbv"], w=["oml"])
            P.op("pool", lambda e: e.memset(ones1, 1.0), w=["ones1"])
            WH = T([128, 8, 5, 128], BF16)
            hTl = [T([128, 8, 512], BF16)] * 2
            SG = T([128, NTOK], BF16)
            Vt = T([128, NT, 128], BF16)
            QP = [T([128, NTOK], BF16) for _ in range(2)]
            QPP = [T([128, NTOK], BF16) for _ in range(2)]
            KP = [T([128, NTOK], BF16) for _ in range(2)]
            KPt = [T([128, NT, 128], BF16) for _ in range(2)]
            EL = [T([128, 68]) for _ in range(2)]
            OT = [T([128, NTOK]) for _ in range(2)]
            qf = T([128, 512])
            sgm = T([128, 512])
            kk = T([128, 512])
            lf = T([128, 512])
            gb = T([128, 516])
            Da = T([128, 512])
            Db = T([128, 512])
            E1 = T([128, 512])
            E2 = T([128, 512])
            E3 = T([128, 512])
            elt = T([128, 8])
            Sf = [[T([128, 128]) for _ in range(2)] for _ in range(2)]
            Sb = [[T([128, 128], BF16) for _ in range(2)] for _ in range(2)]
            Am = [[T([128, 64], BF16) for _ in range(2)] for _ in range(2)]
            osum, rstd, otmp = Da, Db, E1
            sq = T([128, 512], BF16)
            mh = [T([128, 512], BF16) for _ in range(2)]
            P.op("pool", lambda e: e.memset(gb[:, 0:1], 0.0), w=["gb0"])
            wsrc = din["od_w_in"][0].rearrange("(k p) n -> p k n", p=128)
            for h in range(4):
                for i, c0_ in enumerate((832, 1344, 1856, 2368, 2880)):
                    wload(WH[:, :, i, :], wsrc[:, :, c0_ + h * 128:c0_ + (h + 1) * 128])
                for b, (t0, ntl) in enumerate(BLOCKS):
                    n = ntl * 128
                    nch = n // 64
                    col0 = t0 * 128
                    ch0 = col0 // 64
                    hT = hTl[b % 2]
                    khT = "hTl0"
                    dma("sp", hT[:, :, 0:n], HT1[b][:, :, 0:n], r=["HT1"], w=[khT])
                    mmg(ps[0][:, 0:n], [(WH[:, k, 0, :], hT[:, k, 0:n]) for k in range(8)], r=[khT, "wgt"], w=["ps0"])
                    P.op("act", (lambda e, n=n: e.activation(out=qf[:, 0:n], in_=ps[0][:, 0:n], func=AF.Silu)), r=["ps0"], w=["qf", "ps0"])
                    mmg(ps[1][:, 0:n], [(WH[:, k, 4, :], hT[:, k, 0:n]) for k in range(8)], r=[khT, "wgt"], w=["ps1"])
                    P.op("act", (lambda e, n=n, col0=col0: e.activation(out=SG[:, col0:col0 + n], in_=ps[1][:, 0:n], func=AF.Silu)),
                         r=["ps1"], w=["SG", "ps1"])
                    for ti in range(ntl):
                        tt = t0 + ti
                        ts_ = slice(ti * 128, (ti + 1) * 128)
                        mmg(ps[2][:, 0:128], [(hT[:, k, ts_], WH[:, k, 3, :]) for k in range(8)], r=[khT, "wgt"], w=["ps2"])
                        P.op("act", (lambda e, tt=tt: e.copy(out=Vt[:, tt, :], in_=ps[2][:, 0:128])), r=["ps2"], w=["Vt", "ps2"])
                    for d in range(2):
                        bk = 3 + d
                        kb = "ps%d" % bk
                        mmg(ps[bk][:, 0:n], [(WH[:, k, 1 + d, :], hT[:, k, 0:n]) for k in range(8)], r=[khT, "wgt"], w=[kb])
                        P.op("act", (lambda e, n=n, bk=bk: e.activation(out=sgm[:, 0:n], in_=ps[bk][:, 0:n], func=AF.Sigmoid)),
                             r=[kb], w=["sgm", kb])
                        P.op("dve", (lambda e, n=n, d=d, h=h: e.tensor_scalar(out=sgm[:, 0:n], in0=sgm[:, 0:n], scalar1=oml[:, d, h:h + 1],
                                                                             scalar2=lbv[:, d, h:h + 1], op0=ALU.mult, op1=ALU.add)),
                             r=["sgm", "oml", "lbv"], w=["sgm"])
                        P.op("pool", (lambda e, n=n: e.tensor_scalar(out=kk[:, 0:n], in0=sgm[:, 0:n], scalar1=-1.0, scalar2=1.0,
                                                                    op0=ALU.mult, op1=ALU.add)), r=["sgm"], w=["kk"])
                        P.op("act", (lambda e, n=n: e.activation(out=lf[:, 0:n], in_=sgm[:, 0:n], func=AF.Ln)), r=["sgm"], w=["lf"])
                        P.op("dve", (lambda e, n=n: e.tensor_tensor_scan(out=gb[:, 1:1 + n], data0=ones1[:, 0:1].to_broadcast([128, n]),
                                                                        data1=lf[:, 0:n], initial=0.0, op0=ALU.mult, op1=ALU.add)),
                             r=["lf", "ones1", "gb0"], w=["gb"])
                        Gi3 = gb[:, 1:1 + n].rearrange("p (c j) -> p c j", j=64)
                        Gs3 = gb[:, 0:n].rearrange("p (c j) -> p c j", j=64)
                        S0 = Gs3[:, :, 0:1].to_broadcast([128, nch, 64])
                        I63 = Gi3[:, :, 63:64].to_broadcast([128, nch, 64])
                        Da3 = Da[:, 0:n].rearrange("p (c j) -> p c j", j=64)
                        Db3 = Db[:, 0:n].rearrange("p (c j) -> p c j", j=64)
                        if d == 0:
                            P.op("dve", (lambda e, Gi3=Gi3, S0=S0, Da3=Da3: e.tensor_tensor(out=Da3, in0=Gi3, in1=S0, op=ALU.subtract)),
                                 r=["gb"], w=["Da"])
                            P.op("dve", (lambda e, Gi3=Gi3, I63=I63, Db3=Db3: e.tensor_tensor(out=Db3, in0=Gi3, in1=I63, op=ALU.subtract)),
                                 r=["gb"], w=["Db"])
                            sc1, sc2, sc3 = 1.0, -1.0, 1.0
                        else:
                            P.op("dve", (lambda e, Gs3=Gs3, I63=I63, Da3=Da3: e.tensor_tensor(out=Da3, in0=Gs3, in1=I63, op=ALU.subtract)),
                                 r=["gb"], w=["Da"])
                            P.op("dve", (lambda e, Gs3=Gs3, S0=S0, Db3=Db3: e.tensor_tensor(out=Db3, in0=Gs3, in1=S0, op=ALU.subtract)),
                                 r=["gb"], w=["Db"])
                            sc1, sc2, sc3 = -1.0, 1.0, -1.0
                        P.op("dve", (lambda e, Gi3=Gi3, Gs3=Gs3, nch=nch: e.tensor_tensor(out=elt[:, 0:nch], in0=Gi3[:, :, 63], in1=Gs3[:, :, 0],
                                                                                          op=ALU.subtract)), r=["gb"], w=["elt"])
                        P.op("act", (lambda e, n=n, sc1=sc1: e.activation(out=E1[:, 0:n], in_=Da[:, 0:n], func=AF.Exp, scale=sc1)), r=["Da"], w=["E1"])
                        P.op("act", (lambda e, n=n, sc2=sc2: e.activation(out=E2[:, 0:n], in_=Db[:, 0:n], func=AF.Exp, scale=sc2)), r=["Db"], w=["E2"])
                        P.op("act", (lambda e, n=n, sc3=sc3: e.activation(out=E3[:, 0:n], in_=Db[:, 0:n], func=AF.Exp, scale=sc3)), r=["Db"], w=["E3"])
                        P.op("act", (lambda e, d=d, ch0=ch0, nch=nch: e.activation(out=EL[d][:, ch0:ch0 + nch], in_=elt[:, 0:nch], func=AF.Exp)),
                             r=["elt"], w=["EL%d" % d])
                        P.op("dve", (lambda e, n=n, d=d, col0=col0: e.tensor_tensor(out=QP[d][:, col0:col0 + n], in0=qf[:, 0:n], in1=E1[:, 0:n], op=ALU.mult)),
                             r=["qf", "E1"], w=["QP%d" % d])
                        P.op("dve", (lambda e, n=n, d=d, col0=col0: e.tensor_tensor(out=KP[d][:, col0:col0 + n], in0=kk[:, 0:n], in1=E2[:, 0:n], op=ALU.mult)),
                             r=["kk", "E2"], w=["KP%d" % d])
                        P.op("pool", (lambda e, n=n, d=d, col0=col0: e.tensor_tensor(out=QPP[d][:, col0:col0 + n], in0=qf[:, 0:n], in1=E3[:, 0:n], op=ALU.mult)),
                             r=["qf", "E3"], w=["QPP%d" % d])
                        for ti in range(ntl):
                            tt = t0 + ti

                            def trp(e, d=d, tt=tt):
                                return e.transpose(out=psb[6][:, 0:128], in_=KP[d][:, tt * 128:(tt + 1) * 128], identity=identb)
                            P.op("pe", trp, r=["KP%d" % d], w=["ps6"])
                            P.op("act", (lambda e, d=d, tt=tt: e.copy(out=KPt[d][:, tt, :], in_=psb[6][:, 0:128])), r=["ps6"], w=["KPt%d" % d, "ps6"])
                orders = [list(range(68)), [3, 2, 1, 0] + list(range(67, 3, -1))]
                first = [True, True]
                cur = [0, 0]
                for step in range(68):
                    for d in range(2):
                        c = orders[d][step]
                        tt, hh = c // 2, c % 2
                        rows = slice(64 * hh, 64 * hh + 64)
                        cols = slice(64 * c, 64 * c + 64)
                        if c < 4:
                            pos, blk_n, blk_c0 = c, 256, 0
                        else:
                            pos, blk_n, blk_c0 = (c - 4) % 8, 512, 256 + ((c - 4) // 8) * 512
                        pA, pU, pO = ps[d], ps[2 + d], ps[4 + d]
                        kA, kU, kO = "ps%d" % d, "ps%d" % (2 + d), "ps%d" % (4 + d)
                        am = Am[d][step % 2]
                        kam = "Am%d_%d" % (d, step % 2)
                        mmg(pA[rows, 0:64], [(KP[d][:, cols], QPP[d][:, cols])], r=["KP%d" % d, "QPP%d" % d], w=[kA])
                        mk = maskf if d == 0 else maskb
                        P.op("dve", (lambda e, pA=pA, rows=rows, am=am, mk=mk: e.tensor_tensor(out=am[rows, :], in0=pA[rows, 0:64], in1=mk[rows, :],
                                                                                              op=ALU.mult)), r=[kA], w=[kam, kA])
                        so, sn = cur[d], 1 - cur[d]
                        prs = [(Vt[rows, tt, :], am[rows, :])]
                        rr = ["Vt", kam]
                        if not first[d]:
                            prs.append((Sb[d][so], QP[d][:, cols]))
                            rr += ["Sb%d_%d" % (d, so), "QP%d" % d]
                        mmg(pO[:, pos * 64:(pos + 1) * 64], prs, r=rr, w=[kO])
                        mmg(pU[:, 0:128], [(KPt[d][rows, tt, :], Vt[rows, tt, :])], r=["KPt%d" % d, "Vt"], w=[kU])
                        if first[d]:
                            P.op("dve", (lambda e, d=d, sn=sn, pU=pU: e.tensor_copy(out=Sf[d][sn], in_=pU[:, 0:128])),
                                 r=[kU], w=["Sf%d_%d" % (d, sn), kU])
                        else:
                            P.op("dve", (lambda e, d=d, sn=sn, so=so, pU=pU, c=c: e.scalar_tensor_tensor(
                                out=Sf[d][sn], in0=Sf[d][so], scalar=EL[d][:, c:c + 1], in1=pU[:, 0:128], op0=ALU.mult, op1=ALU.add)),
                                r=[kU, "Sf%d_%d" % (d, so), "EL%d" % d], w=["Sf%d_%d" % (d, sn), kU])
                        P.op("act", (lambda e, d=d, sn=sn: e.copy(out=Sb[d][sn], in_=Sf[d][sn])), r=["Sf%d_%d" % (d, sn)], w=["Sb%d_%d" % (d, sn)])
                        cur[d] = sn
                        first[d] = False
                        last_in_blk = (pos == (blk_n // 64 - 1)) if d == 0 else (pos == 0)
                        if last_in_blk:
                            P.op("act", (lambda e, d=d, pO=pO, blk_n=blk_n, blk_c0=blk_c0: e.copy(out=OT[d][:, blk_c0:blk_c0 + blk_n], in_=pO[:, 0:blk_n])),
                                 r=[kO], w=["OT%d" % d, kO])
                for b in range(1, 9):
                    n = 512
                    col0 = BLOCKS[b][0] * 128
                    cs = slice(col0, col0 + n)
                    m = mh[b % 2]
                    km = "mh%d" % (b % 2)
                    P.op("pool", (lambda e, cs=cs: e.tensor_tensor(out=osum, in0=OT[0][:, cs], in1=OT[1][:, cs], op=ALU.add)),
                         r=["OT0", "OT1"], w=["Da"])
                    P.op("act", lambda e: e.activation(out=sq, in_=osum, func=AF.Square), r=["Da"], w=["sq"])
                    mmg(ps[7], [(onesb, sq)], r=["sq"], w=["ps7"])
                    P.op("act", lambda e: e.activation(out=rstd, in_=ps[7], func=AF.Sqrt, scale=1.0 / 128, bias=EPS), r=["ps7"], w=["Db", "ps7"])
                    P.op("dve", lambda e: e.reciprocal(out=rstd, in_=rstd), r=["Db"], w=["Db"])
                    P.op("dve", lambda e: e.scalar_tensor_tensor(out=otmp, in0=osum, scalar=hgn, in1=rstd, op0=ALU.mult, op1=ALU.mult),
                         r=["Da", "Db", "hgn"], w=["E1"])
                    P.op("dve", (lambda e, cs=cs, m=m: e.tensor_tensor(out=m, in0=otmp, in1=SG[:, cs], op=ALU.mult)), r=["E1", "SG"], w=[km])
                    dma("pool", MIXD[4 + h][:, cs], m, r=[km], w=["MIXD"])
            phase_reset()
            wo = T([128, 8, D], BF16)
            wosrc = din["mix_w_out"][l].rearrange("(k p) n -> p k n", p=128)
            for c in range(2):
                wload(wo[:, :, c * 512:(c + 1) * 512], wosrc[:, :, c * 512:(c + 1) * 512])
            res = RES(l, 2)
            mix = [T([128, 8, 512], BF16) for _ in range(2)]
            for b in range(1, 9):
                t0, ntl = BLOCKS[b]
                col0 = t0 * 128
                mx = mix[b % 2]
                kx = "mix%d" % (b % 2)
                for k in range(8):
                    dma("sp", mx[:, k, :], MIXD[k][:, col0:col0 + 512], r=["MIXD"], w=[kx])
                for ti in range(ntl):
                    tt = t0 + ti
                    ls = slice(ti * 128, (ti + 1) * 128)
                    for hf in range(2):
                        mmg(ps[6 + hf], [(mx[:, k, ls], wo[:, k, hf * 512:(hf + 1) * 512]) for k in range(8)],
                            r=[kx, "wgt"], w=["ps%d" % (6 + hf)])
                    res.run(6, 7, x_src(False, tt), XR[tt * 128:(tt + 1) * 128, :], False, "XR1")

        phase_mod()
        phase_reset()
        if stop != "mod":
            phase_l0()
            phase_reset()
        if stop not in ("mod", "l0mix"):
            phase_ffn(0, True, False)
            phase_reset()
        if stop not in ("mod", "l0mix", "l0"):
            phase_l1()
            phase_reset()
        if stop not in ("mod", "l0mix", "l0", "l1mix"):
            phase_ffn(1, False, True)
            phase_reset()
        P.emit(st)
    return nc


_CACHE = {}


def kernel(**inputs):
    consts = _consts()
    if "nc" not in _CACHE:
        _CACHE["nc"] = build()
    nc = _CACHE["nc"]
    in_maps = []
    for b in range(8):
        m = {"x": np.ascontiguousarray(inputs["x"][b]), "ctx": np.ascontiguousarray(inputs["ctx"][b]),
             "cvec": np.ascontiguousarray(np.stack([inputs["c"][b], inputs["c_ctx"]], 0))}
        for n in W_NAMES:
            m[n] = np.ascontiguousarray(inputs[n])
        m.update(consts)
        in_maps.append(m)
    res = run_bass_kernel_spmd(nc, in_maps, core_ids=list(range(8)))
    return np.stack([r["out"] for r in res.results], 0).astype(np.float32)
```

```python
import numpy as np
from contextlib import ExitStack
import concourse.bass as bass
import concourse.mybir as mybir
from concourse.bass_utils import run_bass_kernel_spmd

F32 = mybir.dt.float32
BF16 = mybir.dt.bfloat16
ALU = mybir.AluOpType
AF = mybir.ActivationFunctionType

NDSEM = 8
D = 1024
NT = 34
NTOK = 4352
DFF = 2816
NJ = 22
EPS = 1e-6


class Prog:
    ENGS = ("pe", "dve", "act", "pool", "sp")

    def __init__(self, nc):
        self.nc = nc
        self.ops = []
        self.lw = {}
        self.rd = {}
        self.cnt = {e: 0 for e in self.ENGS}
        self.dcnt = {e: 0 for e in self.ENGS}
        self.dslot_last = {e: [None] * NDSEM for e in self.ENGS}
        self.last_nd = {e: None for e in self.ENGS}
        self.pending_bar = {e: [] for e in self.ENGS}

    def op(self, eng, fn, r=(), w=(), dma=False):
        oid = len(self.ops)
        deps = []
        for k in r:
            y = self.lw.get(k)
            if y is not None:
                deps.append((y, "RAW"))
        for k in w:
            y = self.lw.get(k)
            if y is not None:
                deps.append((y, "WAW"))
            for y in self.rd.get(k, ()):
                deps.append((y, "WAR"))
        for y in self.pending_bar[eng]:
            deps.append((y, "RAW"))
        self.pending_bar[eng] = []
        o = dict(id=oid, eng=eng, fn=fn, deps=deps, dma=dma)
        if dma:
            i = self.dcnt[eng]
            self.dcnt[eng] += 1
            slot = i % NDSEM
            o["dslot"] = slot
            o["dval"] = 16 * (i // NDSEM + 1)
            prev = self.dslot_last[eng][slot]
            if prev is not None:
                deps.append((prev, "RAW"))
            self.dslot_last[eng][slot] = oid
        else:
            self.cnt[eng] += 1
            o["val"] = self.cnt[eng]
            self.last_nd[eng] = oid
        self.ops.append(o)
        for k in w:
            self.lw[k] = oid
            self.rd[k] = []
        for k in r:
            if k not in w:
                self.rd.setdefault(k, []).append(oid)
        return oid

    def barrier(self):
        snap = []
        for e in self.ENGS:
            if self.last_nd[e] is not None:
                snap.append(self.last_nd[e])
            for y in self.dslot_last[e]:
                if y is not None:
                    snap.append(y)
        for e in self.ENGS:
            self.pending_bar[e] = list(snap)
        self.lw = {}
        self.rd = {}

    def emit(self, st):
        nc = self.nc
        sems = {e: st.enter_context(nc.semaphore("s_" + e)) for e in self.ENGS}
        dsems = {e: [st.enter_context(nc.semaphore("d_%s%d" % (e, i))) for i in range(NDSEM)]
                 for e in ("sp", "pool", "act") if self.dcnt[e] > 0}
        block = st.enter_context(nc.Block())
        ops = self.ops

        def run(ename, eng):
            seen = {}
            for o in ops:
                if o["eng"] != ename:
                    continue
                need = {}
                for (y, kind) in o["deps"]:
                    Y = ops[y]
                    if Y["dma"]:
                        key = ("d", Y["eng"], Y["dslot"])
                        sem = dsems[Y["eng"]][Y["dslot"]]
                        val = Y["dval"]
                    else:
                        if Y["eng"] == ename and not o["dma"]:
                            if ename == "pe" or kind != "RAW":
                                continue
                        key = ("c", Y["eng"])
                        sem = sems[Y["eng"]]
                        val = Y["val"]
                    if seen.get(key, 0) >= val:
                        continue
                    if key not in need or need[key][1] < val:
                        need[key] = (sem, val)
                for key, (sem, val) in need.items():
                    eng.wait_ge(sem, val)
                    seen[key] = val
                ins = o["fn"](eng)
                if o["dma"]:
                    ins.then_inc(dsems[ename][o["dslot"]], 16)
                else:
                    ins.then_inc(sems[ename], 1)
            if ename in dsems:
                for slot in range(NDSEM):
                    y = self.dslot_last[ename][slot]
                    if y is not None:
                        Y = ops[y]
                        if seen.get(("d", ename, slot), 0) < Y["dval"]:
                            eng.wait_ge(dsems[ename][slot], Y["dval"])

        @block.tensor
        def _(e):
            run("pe", e)

        @block.vector
        def _(e):
            run("dve", e)

        @block.scalar
        def _(e):
            run("act", e)

        @block.gpsimd
        def _(e):
            run("pool", e)

        @block.sync
        def _(e):
            run("sp", e)


PW = 8 + 256 + 16 + 4096 + 8
PC0, PL0 = 8, 280


def _consts():
    c = {}
    c["ident"] = np.eye(128, dtype=np.float32)
    rot = np.zeros((128, 128), np.float32)
    for d in range(128):
        rot[d ^ 16, d] = 1.0
    c["rot"] = rot
    n = 4096
    pos_row = np.repeat(np.arange(n // 64), 64)
    pos_col = np.tile(np.arange(64), n // 64)
    inv_freq = (10000.0 ** (-np.arange(0, 32, 2, dtype=np.float32) / 32)).astype(np.float32)
    ang = np.stack([pos_row, pos_col], -1).astype(np.float32)[..., None] * inv_freq
    cs, sn = np.cos(ang).astype(np.float32), np.sin(ang).astype(np.float32)
    cos_t = np.zeros((128, n), np.float32)
    sin_t = np.zeros((128, n), np.float32)
    for d in range(128):
        dd = d % 64
        a, hf, i = dd // 32, (dd // 16) % 2, dd % 16
        cos_t[d] = cs[:, a, i]
        sin_t[d] = sn[:, a, i] * (-1.0 if hf == 0 else 1.0)
    c["cos_t"] = cos_t
    c["sin_t"] = sin_t
    inv = np.zeros((4, PW), np.float32)
    for g, w in enumerate((2, 4, 8, 16)):
        h = w // 2
        for (n_, off) in ((256, PC0), (4096, PL0)):
            t = np.arange(n_)
            lo = np.clip(t - h, 0, n_)
            hi = np.clip(t + h, 0, n_)
            inv[g, off:off + n_] = 1.0 / (hi - lo).astype(np.float32)
    c["invcnt"] = inv
    p = np.arange(128)[:, None] % 64
    t = np.arange(64)[None, :]
    c["mask_f"] = (p <= t).astype(np.float32)
    c["mask_b"] = (p >= t).astype(np.float32)
    return c


W_NAMES = ["ada_w", "ada_b", "norm_g", "mix_w_out", "ffn_w_gate", "ffn_w_up", "ffn_conv_w",
           "ffn_conv_b", "ffn_w_down", "ev_w_in", "pool_w", "pool_scale", "diff_lambda",
           "diff_subln", "od_w_in", "mla_q_norm", "mla_w_uq", "mla_kv_norm", "mla_w_ukv",
           "hgrn_norm", "hgrn_lb"]
W_SHAPES = {"ada_w": (2, 1024, 6144), "ada_b": (2, 6144), "norm_g": (2, 4, 1024),
            "mix_w_out": (2, 1024, 1024), "ffn_w_gate": (2, 1024, 2816), "ffn_w_up": (2, 1024, 2816),
            "ffn_conv_w": (2, 3, 2816), "ffn_conv_b": (2, 2816), "ffn_w_down": (2, 2816, 1024),
            "ev_w_in": (1, 1024, 2048), "pool_w": (1, 4, 128, 128), "pool_scale": (1, 512),
            "diff_lambda": (1, 4, 64), "diff_subln": (1, 128), "od_w_in": (1, 1024, 3392),
            "mla_q_norm": (1, 512), "mla_w_uq": (1, 512, 768), "mla_kv_norm": (1, 256),
            "mla_w_ukv": (1, 256, 1024), "hgrn_norm": (1, 128), "hgrn_lb": (2, 2, 512)}
C_SHAPES = {"ident": (128, 128), "rot": (128, 128), "cos_t": (128, 4096), "sin_t": (128, 4096),
            "invcnt": (4, PW), "mask_f": (128, 64), "mask_b": (128, 64)}

BLOCKS = [(0, 2)] + [(2 + 4 * i, 4) for i in range(8)]


def build(stop=None, dbg=False):
    nc = bass.Bass("TRN2", target_bir_lowering=False)
    din = {}
    din["x"] = nc.dram_tensor("x", [4096, D], F32, kind="ExternalInput").ap()
    din["ctx"] = nc.dram_tensor("ctx", [256, D], F32, kind="ExternalInput").ap()
    din["cvec"] = nc.dram_tensor("cvec", [2, D], F32, kind="ExternalInput").ap()
    for n in W_NAMES:
        din[n] = nc.dram_tensor(n, list(W_SHAPES[n]), F32, kind="ExternalInput").ap()
    for n in C_SHAPES:
        din[n] = nc.dram_tensor(n, list(C_SHAPES[n]), F32, kind="ExternalInput").ap()
    out = nc.dram_tensor("out", [4096, D], F32, kind="ExternalOutput").ap()
    XR = nc.dram_tensor("XR", [NTOK, D], F32, kind="ExternalOutput" if dbg else "Internal").ap()
    MODV = nc.dram_tensor("MODV", [2, 2, 6, D], F32, kind="Internal").ap()
    QT = nc.dram_tensor("QT", [9, 128, 4, 512], BF16, kind="Internal").ap()
    QR = nc.dram_tensor("QR", [9, 128, 2, 512], BF16, kind="Internal").ap()
    UT = nc.dram_tensor("UT", [4, 128, NTOK], F32, kind="Internal").ap()
    HT1 = nc.dram_tensor("HT1", [9, 128, 8, 512], BF16, kind="Internal").ap()
    H2D = nc.dram_tensor("H2D", [128, 8, 4355], BF16, kind="Internal").ap()
    MIXD = nc.dram_tensor("MIXD", [8, 128, NTOK], BF16, kind="ExternalOutput" if dbg else "Internal").ap()

    st = ExitStack()
    with st:
        P = Prog(nc)
        AW = 52000
        arena = st.enter_context(nc.sbuf_tensor("arena", [128, AW], F32))
        ps = [st.enter_context(nc.psum_tensor("ps%d" % i, [128, 512], F32)) for i in range(8)]
        ps = [p[:] for p in ps]
        psb = [p.bitcast(BF16) for p in ps]
        top = [0]

        def T(shape, dt=F32):
            n = int(np.prod(shape[1:]))
            cols = n if dt == F32 else (n + 1) // 2
            off = top[0]
            top[0] += cols
            assert top[0] <= AW, "SBUF arena overflow %d" % top[0]
            a = arena[0:shape[0], off:off + cols]
            if dt != F32:
                a = a.bitcast(dt)
            if len(shape) == 3:
                a = a.rearrange("p (a b) -> p a b", a=shape[1])
            elif len(shape) == 4:
                a = a.rearrange("p (a b c) -> p a b c", a=shape[1], b=shape[2])
            return a

        uid = [0]

        def K(s):
            uid[0] += 1
            return "%s#%d" % (s, uid[0])

        def dma(q, o, i, r=(), w=()):
            P.op(q, lambda e: e.dma_start(out=o, in_=i), r=r, w=w, dma=True)

        def dma_nc(q, o, i, r=(), w=()):
            P.op(q, lambda e: e.dma_start(out=o, in_=i, allow_slow_non_contiguous=True), r=r, w=w, dma=True)

        def mmg(o, pairs, r, w):
            def f(e):
                n = len(pairs)
                for i, (l, rh) in enumerate(pairs):
                    ins = e.matmul(o, lhsT=l, rhs=rh, start=(i == 0), stop=(i == n - 1))
                return ins
            P.op("pe", f, r=r, w=w)

        identb = T([128, 128], BF16)
        rotb = T([128, 128], BF16)
        onesb = T([128, 128], BF16)
        maskf = T([128, 64])
        maskb = T([128, 64])
        stg = [T([128, 2048]) for _ in range(2)]
        stgi = [0]
        PERSIST = None

        def wload(dst, src, q=None, ce="pool"):
            i = stgi[0] % 2
            stgi[0] += 1
            shp = list(dst.shape)
            n = int(np.prod(shp[1:]))
            if n > 2048:
                hh = shp[-1] // 2
                if len(shp) == 2:
                    wload(dst[:, 0:hh], src[:, 0:hh], q, ce)
                    wload(dst[:, hh:], src[:, hh:], q, ce)
                else:
                    wload(dst[:, :, 0:hh], src[:, :, 0:hh], q, ce)
                    wload(dst[:, :, hh:], src[:, :, hh:], q, ce)
                return
            s = stg[i][0:shp[0], 0:n]
            if len(shp) == 3:
                s = s.rearrange("p (a b) -> p a b", a=shp[1])
            qq = q or ("sp" if i == 0 else "pool")
            dma(qq, s, src, w=["stg%d" % i])
            kd = "W" + str(id(dst))
            if ce == "pool":
                P.op("pool", lambda e: e.tensor_copy(out=dst, in_=s), r=["stg%d" % i], w=["wgt"])
            elif ce == "dve":
                P.op("dve", lambda e: e.tensor_copy(out=dst, in_=s), r=["stg%d" % i], w=["wgt"])
            else:
                P.op("act", lambda e: e.copy(out=dst, in_=s), r=["stg%d" % i], w=["wgt"])

        for (dst, nm) in ((identb, "ident"), (rotb, "rot")):
            wload(dst, din[nm])
        P.op("pool", lambda e: e.memset(onesb, 1.0), w=["wgt"])
        dma("sp", maskf, din["mask_f"], w=["wgt"])
        dma("sp", maskb, din["mask_b"], w=["wgt"])
        PERSIST = top[0]

        def phase_reset():
            P.barrier()
            top[0] = PERSIST

        def phase_mod():
            cv = T([128, 2, 8])
            cvs = T([128, 2, 8])
            cvb = T([128, 8, 2], BF16)
            for j in range(2):
                dma_nc("sp", cv[:, j, :], din["cvec"][j].rearrange("(k p) -> p k", p=128), w=["cv"])
            P.op("act", lambda e: e.activation(out=cvs, in_=cv, func=AF.Silu), r=["cv"], w=["cvs"])
            P.op("dve", lambda e: e.tensor_copy(out=cvb, in_=cvs.rearrange("p j k -> p k j")), r=["cvs"], w=["cvb"])
            awb = [T([128, 8, 256], BF16) for _ in range(2)]
            Mt = T([2, 6 * D])
            bt = T([2, 6 * D])
            ng = T([2, 4, D])
            V = T([2, 6, D])
            for l in range(2):
                dma("sp", bt, din["ada_b"][l].partition_broadcast(2), w=["bt"])
                dma("sp", ng.rearrange("p a b -> p (a b)"),
                    din["norm_g"][l].rearrange("a b -> (a b)").partition_broadcast(2), w=["ng"])
                for nb in range(24):
                    ab = awb[nb % 2]
                    kab = "awb%d" % (nb % 2)
                    src = din["ada_w"][l].rearrange("(k p) n -> p k n", p=128)[:, :, nb * 256:(nb + 1) * 256]
                    i = stgi[0] % 2
                    stgi[0] += 1
                    s = stg[i][:, :].rearrange("p (a b) -> p a b", a=8)
                    dma("sp" if i == 0 else "pool", s, src, w=["stg%d" % i])
                    P.op("pool" if nb % 2 == 0 else "dve", (lambda e, ab=ab, s=s: e.tensor_copy(out=ab, in_=s)),
                         r=["stg%d" % i], w=[kab])
                    pb = ps[nb % 2][0:2, 0:256]
                    mmg(pb, [(cvb[:, k, :], ab[:, k, :]) for k in range(8)], r=["cvb", kab], w=["ps%d" % (nb % 2)])
                    P.op("dve", (lambda e, pb=pb, nb=nb: e.tensor_tensor(out=Mt[:, nb * 256:(nb + 1) * 256], in0=pb,
                                                                       in1=bt[:, nb * 256:(nb + 1) * 256], op=ALU.add)),
                         r=["ps%d" % (nb % 2), "bt"], w=["Mt"])
                sl = lambda i: Mt[:, i * D:(i + 1) * D]
                P.op("dve", lambda e: e.scalar_tensor_tensor(out=V[:, 0, :], in0=sl(1), scalar=1.0, in1=ng[:, 0, :],
                                                             op0=ALU.add, op1=ALU.mult), r=["Mt", "ng"], w=["V"])
                P.op("dve", lambda e: e.tensor_copy(out=V[:, 1, :], in_=sl(0)), r=["Mt"], w=["V"])
                P.op("dve", lambda e: e.tensor_tensor(out=V[:, 2, :], in0=sl(2), in1=ng[:, 1, :], op=ALU.mult),
                     r=["Mt", "ng"], w=["V"])
                P.op("dve", lambda e: e.scalar_tensor_tensor(out=V[:, 3, :], in0=sl(4), scalar=1.0, in1=ng[:, 2, :],
                                                             op0=ALU.add, op1=ALU.mult), r=["Mt", "ng"], w=["V"])
                P.op("dve", lambda e: e.tensor_copy(out=V[:, 4, :], in_=sl(3)), r=["Mt"], w=["V"])
                P.op("dve", lambda e: e.tensor_tensor(out=V[:, 5, :], in0=sl(5), in1=ng[:, 3, :], op=ALU.mult),
                     r=["Mt", "ng"], w=["V"])
                dma("sp", MODV[l].rearrange("j a b -> j (a b)"), V.rearrange("p a b -> p (a b)"), r=["V"], w=["MODV"])

        def x_src(layer0_in, tt):
            if layer0_in:
                return din["ctx"][tt * 128:(tt + 1) * 128, :] if tt < 2 else din["x"][(tt - 2) * 128:(tt - 1) * 128, :]
            return XR[tt * 128:(tt + 1) * 128, :]

        class NM:
            def __init__(self, l, gi, si):
                self.xt = [T([128, D]) for _ in range(2)]
                self.junk = T([128, D], BF16)
                self.tmp = [T([128, D]) for _ in range(2)]
                self.pending = None
                self.hb = [T([128, D], BF16) for _ in range(2)]
                self.ss = T([128, 2])
                self.rs = T([128, 2])
                self.G = [T([128, D]) for _ in range(2)]
                self.SH = [T([128, D]) for _ in range(2)]
                for j in range(2):
                    dma("sp", self.G[j], MODV[l, j, gi].partition_broadcast(128), r=["MODV"], w=["nmG"])
                    dma("sp", self.SH[j], MODV[l, j, si].partition_broadcast(128), r=["MODV"], w=["nmG"])
                self.i = 0

            def run(self, src, is_ctx, dst, dkey, bank):
                i = self.i % 2
                self.i += 1
                xt, hb = self.xt[i], self.hb[i]
                kx, kh = "nm_xt%d" % i, "nm_hb%d" % i
                ss, rs = self.ss[:, i:i + 1], self.rs[:, i:i + 1]
                G, SH = self.G[1 if is_ctx else 0], self.SH[1 if is_ctx else 0]
                tmp = self.tmp[i]
                dma("sp", xt, src, r=["XR"], w=[kx])
                P.op("act", lambda e: e.activation(out=self.junk, in_=xt, func=AF.Square, accum_out=ss),
                     r=[kx], w=["nm_junk", "nm_ss%d" % i])
                P.op("act", lambda e: e.activation(out=rs, in_=ss, func=AF.Sqrt, scale=1.0 / D, bias=EPS),
                     r=["nm_ss%d" % i], w=["nm_rs%d" % i])
                P.op("dve", lambda e: e.reciprocal(out=rs, in_=rs), r=["nm_rs%d" % i], w=["nm_rs%d" % i])
                P.op("dve", lambda e: e.scalar_tensor_tensor(out=tmp, in0=xt, scalar=rs, in1=G, op0=ALU.mult,
                                                             op1=ALU.mult), r=[kx, "nm_rs%d" % i, "nmG"], w=["nm_tmp%d" % i])
                P.op("pool", lambda e: e.tensor_tensor(out=hb, in0=tmp, in1=SH, op=ALU.add),
                     r=["nm_tmp%d" % i, "nmG"], w=[kh])
                prev = self.pending
                self.pending = (hb, kh, dst, dkey, bank)
                if prev is not None:
                    self.stage_b(*prev)

            def stage_b(self, hb, kh, dst, dkey, bank):
                pb = psb[bank]

                def tr(e):
                    for k in range(8):
                        ins = e.transpose(out=pb[:, k * 128:(k + 1) * 128], in_=hb[:, k * 128:(k + 1) * 128],
                                          identity=identb)
                    return ins
                P.op("pe", tr, r=[kh], w=["ps%d" % bank])
                P.op("act", lambda e: e.copy(out=dst, in_=pb.rearrange("p (k n) -> p k n", k=8)),
                     r=["ps%d" % bank], w=[dkey])

            def flush(self):
                if self.pending is not None:
                    self.stage_b(*self.pending)
                    self.pending = None

        class RES:
            def __init__(self, l, gidx):
                self.GT = [T([128, D]) for _ in range(2)]
                for j in range(2):
                    dma("sp", self.GT[j], MODV[l, j, gidx].partition_broadcast(128), r=["MODV"], w=["resG"])
                self.xo = [T([128, D]) for _ in range(2)]
                self.t = T([128, D])
                self.junk = T([128, 512], BF16)
                self.ss = T([128, 4])
                self.rs = T([128, 2])
                self.i = 0

            def run(self, b0, b1, src, dstd, is_ctx, wkey):
                i = self.i % 2
                self.i += 1
                xo = self.xo[i]
                kx = "res_x%d" % i
                ss = self.ss[:, 2 * i:2 * i + 2]
                rs = self.rs[:, i:i + 1]
                GT = self.GT[1 if is_ctx else 0]
                dma("pool", xo, src, r=["XR"], w=[kx])
                P.op("act", lambda e: e.activation(out=self.junk, in_=ps[b0], func=AF.Square, accum_out=ss[:, 0:1]),
                     r=["ps%d" % b0], w=["res_junk", "res_ss%d" % i])
                P.op("act", lambda e: e.activation(out=self.junk, in_=ps[b1], func=AF.Square, accum_out=ss[:, 1:2]),
                     r=["ps%d" % b1], w=["res_junk", "res_ss%d" % i])
                P.op("dve", lambda e: e.tensor_tensor(out=rs, in0=ss[:, 0:1], in1=ss[:, 1:2], op=ALU.add),
                     r=["res_ss%d" % i], w=["res_rs%d" % i])
                P.op("act", lambda e: e.activation(out=rs, in_=rs, func=AF.Sqrt, scale=1.0 / D, bias=EPS),
                     r=["res_rs%d" % i], w=["res_rs%d" % i])
                P.op("dve", lambda e: e.reciprocal(out=rs, in_=rs), r=["res_rs%d" % i], w=["res_rs%d" % i])
                for hf, bk in ((0, b0), (1, b1)):
                    P.op("dve", (lambda e, hf=hf, bk=bk: e.scalar_tensor_tensor(
                        out=self.t[:, hf * 512:(hf + 1) * 512], in0=ps[bk], scalar=rs,
                        in1=GT[:, hf * 512:(hf + 1) * 512], op0=ALU.mult, op1=ALU.mult)),
                        r=["ps%d" % bk, "res_rs%d" % i, "resG"], w=["res_t", "ps%d" % bk])
                P.op("pool", lambda e: e.tensor_tensor(out=xo, in0=self.t, in1=xo, op=ALU.add),
                     r=["res_t", kx], w=[kx])
                dma("pool", dstd, xo, r=[kx], w=[wkey])

        def rope_evict(pbank, n, t0, dst, dkey, ro, is_ctx):
            src = ps[pbank][:, 0:n]
            if is_ctx:
                P.op("act", lambda e: e.copy(out=dst, in_=src), r=["ps%d" % pbank], w=[dkey])
                return
            qs, t1, t2, cosb, sinb, rb = ro
            P.op("act", lambda e: e.copy(out=qs[:, 0:n], in_=src), r=["ps%d" % pbank], w=["ro_qs"])
            mmg(ps[rb][:, 0:n], [(rotb, qs[:, 0:n])], r=["ro_qs"], w=["ps%d" % rb])
            P.op("dve", lambda e: e.tensor_tensor(out=t1[:, 0:n], in0=src, in1=cosb[:, 0:n], op=ALU.mult),
                 r=["ps%d" % pbank, "ro_cs"], w=["ro_t1", "ps%d" % pbank])
            P.op("dve", lambda e: e.tensor_tensor(out=t2[:, 0:n], in0=ps[rb][:, 0:n], in1=sinb[:, 0:n], op=ALU.mult),
                 r=["ps%d" % rb, "ro_cs"], w=["ro_t2", "ps%d" % rb])
            P.op("pool", lambda e: e.tensor_tensor(out=dst, in0=t1[:, 0:n], in1=t2[:, 0:n], op=ALU.add),
                 r=["ro_t1", "ro_t2"], w=[dkey])

        def rope_tiles():
            return (T([128, 512], BF16), T([128, 512]), T([128, 512]), T([128, 512]), T([128, 512]))

        def rope_load(ro, b):
            if b == 0:
                return
            c0 = (b - 1) * 512
            dma("pool", ro[3], din["cos_t"][:, c0:c0 + 512], w=["ro_cs"])
            dma("pool", ro[4], din["sin_t"][:, c0:c0 + 512], w=["ro_cs"])

        def attention(nb_list, heads, scale, post, sbanks, pairs):
            NS = len(sbanks)
            L = NS - 1
            PT = [T([128, 512], BF16) for _ in range(NS + 1)]
            its = []
            g = 0
            for bi, b in enumerate(nb_list):
                n = BLOCKS[b][1] * 128
                kts = list(range(2)) if b == 0 else list(range(NT))
                for hi, hd in enumerate(heads):
                    for ki, kt in enumerate(kts):
                        its.append(dict(b=b, bi=bi, hi=hi, hd=hd, ki=ki, kt=kt, nk=len(kts), n=n, g=g))
                    g += 1
            N = len(its)
            seen_b = set()

            def emitS(i):
                it = its[i]
                b, hd, n = it["b"], it["hd"], it["n"]
                if it["hi"] == 0 and it["ki"] == 0:
                    if b not in seen_b:
                        seen_b.add(b)
                        hd["pre"](b)
                    if it["bi"] + 1 < len(nb_list):
                        nb_ = nb_list[it["bi"] + 1]
                        if nb_ not in seen_b:
                            seen_b.add(nb_)
                            hd["pre"](nb_)
                sb_ = sbanks[i % NS]
                mmg(ps[sb_][:, 0:n], hd["qk"](b, it["kt"]), r=hd["rk"](b), w=["ps%d" % sb_])

            def emitE(i):
                it = its[i]
                n = it["n"]
                sb_ = sbanks[i % NS]
                pt = PT[i % (NS + 1)]
                P.op("act", (lambda e: e.activation(out=pt[:, 0:n], in_=ps[sb_][:, 0:n], func=AF.Exp, scale=scale)),
                     r=["ps%d" % sb_], w=["PT%d" % (i % (NS + 1)), "ps%d" % sb_])

            def emitPV(i):
                it = its[i]
                n, hd, ki, nk, kt = it["n"], it["hd"], it["ki"], it["nk"], it["kt"]
                ob, sb2 = pairs[it["g"] % len(pairs)]
                pt = PT[i % (NS + 1)]

                def pv(e):
                    e.matmul(ps[ob][:, 0:n], lhsT=hd["v"](kt), rhs=pt[:, 0:n], start=(ki == 0), stop=(ki == nk - 1))
                    return e.matmul(ps[sb2][:, 0:n], lhsT=onesb, rhs=pt[:, 0:n], start=(ki == 0), stop=(ki == nk - 1))
                P.op("pe", pv, r=["PT%d" % (i % (NS + 1))] + hd["rv"], w=["ps%d" % ob, "ps%d" % sb2])
                if ki == nk - 1:
                    post(it["b"], it["hi"], n, ob, sb2)

            for i in range(min(L, N)):
                emitS(i)
            for i in range(N):
                emitE(i)
                if i + L < N:
                    emitS(i + L)
                emitPV(i)

        def phase_ffn(l, need_ctx, final):
            W2 = 4355
            Wd = T([128, NJ, D], BF16)
            CW = T([128, 4, NJ])
            for i in range(3):
                dma_nc("sp", CW[:, i, :], din["ffn_conv_w"][l, i].rearrange("(j p) -> p j", p=128), w=["CW"])
            dma_nc("sp", CW[:, 3, :], din["ffn_conv_b"][l].rearrange("(j p) -> p j", p=128), w=["CW"])
            wdsrc = din["ffn_w_down"][l].rearrange("(j p) n -> p j n", p=128)
            for j0 in range(0, NJ, 4):
                j1 = min(NJ, j0 + 4)
                wload(Wd[:, j0:j1, :], wdsrc[:, j0:j1, :])
            top_save = top[0]
            nm = NM(l, 3, 4)
            hTs = [T([128, 8, 512], BF16) for _ in range(2)]
            zt = T([128, 8, 1], BF16)
            P.op("pool", lambda e: e.memset(zt, 0.0), w=["zt"])
            for c in (0, 257, 4354):
                dma_nc("pool", H2D[:, :, c:c + 1], zt, r=["zt"], w=["H2D"])
            blks = list(range(0 if need_ctx else 1, 9))
            for b in blks:
                t0, ntl = BLOCKS[b]
                n = ntl * 128
                hT = hTs[b % 2]
                kh = "hTs%d" % (b % 2)
                for ti in range(ntl):
                    tt = t0 + ti
                    nm.run(x_src(False, tt), tt < 2, hT[:, :, ti * 128:(ti + 1) * 128], kh, 6 + tt % 2)
                nm.flush()
                c0 = 1 if b == 0 else 258 + (b - 1) * 512
                dma("pool", H2D[:, :, c0:c0 + n], hT[:, :, 0:n], r=[kh], w=["H2D"])
            P.barrier()
            top[0] = top_save
            res = RES(l, 5)
            GTt = T([128, NJ, 1024], BF16)
            H2P = [T([128, 8, 1026], BF16) for _ in range(2)]
            wgf = [T([128, 8, 128], BF16) for _ in range(2)]
            wuf = [T([128, 8, 128], BF16) for _ in range(2)]
            acc = [T([128, 512]) for _ in range(2)]
            sil = [T([128, 512]) for _ in range(2)]
            parts = [blks[i:i + 2] for i in range(0, len(blks), 2)]
            wgsrc = din["ffn_w_gate"][l].rearrange("(k p) n -> p k n", p=128)
            wusrc = din["ffn_w_up"][l].rearrange("(k p) n -> p k n", p=128)
            it = [0]
            bc0 = lambda b: 1 if b == 0 else 258 + (b - 1) * 512
            for pi, part in enumerate(parts):
                cstart = bc0(part[0]) - 1
                cend = bc0(part[-1]) + BLOCKS[part[-1]][1] * 128 + 1
                npc = cend - cstart
                H2T = H2P[pi % 2]
                kH = "H2P%d" % (pi % 2)
                dma("sp", H2T[:, :, 0:npc], H2D[:, :, cstart:cend], r=["H2D"], w=[kH])
                goff = {}
                o = 0
                for b in part:
                    goff[b] = o
                    o += BLOCKS[b][1] * 128
                for j in range(NJ):
                    wg, wu = wgf[j % 2], wuf[j % 2]
                    kw = "ffw%d" % (j % 2)
                    for (dst, srcw) in ((wg, wgsrc), (wu, wusrc)):
                        i = stgi[0] % 2
                        stgi[0] += 1
                        s = stg[i][:, 0:1024].rearrange("p (a b) -> p a b", a=8)
                        dma("sp", s, srcw[:, :, j * 128:(j + 1) * 128], w=["stg%d" % i])
                        P.op("pool", (lambda e, dst=dst, s=s: e.tensor_copy(out=dst, in_=s)), r=["stg%d" % i], w=[kw])
                    for b in part:
                        t0, ntl = BLOCKS[b]
                        n = ntl * 128
                        c0 = bc0(b) - cstart
                        q = it[0] % 2
                        it[0] += 1
                        pa, pu, ph = ps[q], ps[2 + q], ps[4 + q]
                        ka, ku, kh = "ps%d" % q, "ps%d" % (2 + q), "ps%d" % (4 + q)
                        mmg(pa[:, 0:n], [(wg[:, k, :], H2T[:, k, c0:c0 + n]) for k in range(8)], r=[kw, kH], w=[ka])
                        hal = H2T[:, :, c0 - 1:c0 + n + 1:n + 1]
                        mmg(ph[:, 0:2], [(wg[:, k, :], hal[:, k, :]) for k in range(8)], r=[kw, kH], w=[kh])
                        mmg(pu[:, 0:n], [(wu[:, k, :], H2T[:, k, c0:c0 + n]) for k in range(8)], r=[kw, kH], w=[ku])
                        ac, sl_ = acc[q], sil[q]
                        kac, ksl = "acc%d" % q, "sil%d" % q
                        w0, w1, w2, bb = (CW[:, i, j:j + 1] for i in range(4))
                        P.op("dve", (lambda e, ac=ac, pa=pa, w1=w1, bb=bb, n=n: e.tensor_scalar(
                            out=ac[:, 0:n], in0=pa[:, 0:n], scalar1=w1, scalar2=bb, op0=ALU.mult, op1=ALU.add)),
                            r=[ka, "CW"], w=[kac])
                        P.op("dve", (lambda e, ac=ac, pa=pa, w0=w0, n=n: e.scalar_tensor_tensor(
                            out=ac[:, 1:n], in0=pa[:, 0:n - 1], scalar=w0, in1=ac[:, 1:n], op0=ALU.mult, op1=ALU.add)),
                            r=[ka, kac], w=[kac])
                        P.op("dve", (lambda e, ac=ac, pa=pa, w2=w2, n=n: e.scalar_tensor_tensor(
                            out=ac[:, 0:n - 1], in0=pa[:, 1:n], scalar=w2, in1=ac[:, 0:n - 1], op0=ALU.mult, op1=ALU.add)),
                            r=[ka, kac], w=[kac, ka])
                        P.op("dve", (lambda e, ac=ac, ph=ph, w0=w0: e.scalar_tensor_tensor(
                            out=ac[:, 0:1], in0=ph[:, 0:1], scalar=w0, in1=ac[:, 0:1], op0=ALU.mult, op1=ALU.add)),
                            r=[kh, kac], w=[kac])
                        P.op("dve", (lambda e, ac=ac, ph=ph, w2=w2, n=n: e.scalar_tensor_tensor(
                            out=ac[:, n - 1:n], in0=ph[:, 1:2], scalar=w2, in1=ac[:, n - 1:n], op0=ALU.mult, op1=ALU.add)),
                            r=[kh, kac], w=[kac, kh])
                        P.op("act", (lambda e, ac=ac, sl_=sl_, n=n: e.activation(out=sl_[:, 0:n], in_=ac[:, 0:n], func=AF.Silu)),
                             r=[kac], w=[ksl])
                        g0 = goff[b]
                        P.op("dve", (lambda e, sl_=sl_, pu=pu, j=j, g0=g0, n=n: e.tensor_tensor(
                            out=GTt[:, j, g0:g0 + n], in0=sl_[:, 0:n], in1=pu[:, 0:n], op=ALU.mult)),
                            r=[ksl, ku], w=["GT", ku])
                for b in part:
                    t0, ntl = BLOCKS[b]
                    for ti in range(ntl):
                        tt = t0 + ti
                        g0 = goff[b] + ti * 128
                        for hf in range(2):
                            mmg(ps[6 + hf], [(GTt[:, j, g0:g0 + 128], Wd[:, j, hf * 512:(hf + 1) * 512]) for j in range(NJ)],
                                r=["GT", "wgt"], w=["ps%d" % (6 + hf)])
                        if final:
                            dstd = out[(tt - 2) * 128:(tt - 1) * 128, :]
                            res.run(6, 7, x_src(False, tt), dstd, tt < 2, "OUT")
                        else:
                            res.run(6, 7, x_src(False, tt), XR[tt * 128:(tt + 1) * 128, :], tt < 2, "XR2")

        def phase_l0():
            l = 0
            LAM_INIT = 0.2
            KT = T([128, 4, NTOK], BF16)
            Vv = T([128, NT, 512], BF16)
            keep = top[0]
            w_in = T([128, 8, 2048], BF16)
            wsrc = din["ev_w_in"][0].rearrange("(k p) n -> p k n", p=128)
            for c in range(4):
                wload(w_in[:, :, c * 512:(c + 1) * 512], wsrc[:, :, c * 512:(c + 1) * 512])
            nm = NM(l, 0, 1)
            hT = T([128, 8, 512], BF16)
            ro = rope_tiles() + (7,)
            ub = [T([128, 512]) for _ in range(2)]
            qb = [T([128, 4, 512], BF16) for _ in range(2)]
            for b, (t0, ntl) in enumerate(BLOCKS):
                n = ntl * 128
                col0 = t0 * 128
                rope_load(ro, b)
                for ti in range(ntl):
                    tt = t0 + ti
                    nm.run(x_src(True, tt), tt < 2, hT[:, :, ti * 128:(ti + 1) * 128], "hT", 6)
                nm.flush()
                for ti in range(ntl):
                    tt = t0 + ti
                    bk = 4 + (ti % 2)
                    mmg(ps[bk], [(hT[:, k, ti * 128:(ti + 1) * 128], w_in[:, k, 1536:2048]) for k in range(8)],
                        r=["hT", "wgt"], w=["ps%d" % bk])
                    P.op("act", (lambda e, tt=tt, bk=bk: e.copy(out=Vv[:, tt, :], in_=ps[bk])), r=["ps%d" % bk], w=["Vv"])
                for c in range(4):
                    bk = c % 2
                    mmg(ps[bk][:, 0:n], [(w_in[:, k, c * 128:(c + 1) * 128], hT[:, k, 0:n]) for k in range(8)],
                        r=["hT", "wgt"], w=["ps%d" % bk])
                    u = ub[c % 2]
                    P.op("act", (lambda e, u=u, bk=bk, n=n: e.copy(out=u[:, 0:n], in_=ps[bk][:, 0:n])),
                         r=["ps%d" % bk], w=["ub%d" % (c % 2)])
                    dma("pool", UT[c][:, col0:col0 + n], u[:, 0:n], r=["ub%d" % (c % 2)], w=["UT"])
                qbb = qb[b % 2]
                kq = "qb%d" % (b % 2)
                for c in range(4):
                    bk = 2 + (c % 2)
                    mmg(ps[bk][:, 0:n], [(w_in[:, k, 512 + c * 128:512 + (c + 1) * 128], hT[:, k, 0:n]) for k in range(8)],
                        r=["hT", "wgt"], w=["ps%d" % bk])
                    rope_evict(bk, n, col0, qbb[:, c, 0:n], kq, ro, b == 0)
                dma("pool", QT[b][:, :, 0:n], qbb[:, :, 0:n], r=[kq], w=["QT"])
                for c in range(4):
                    bk = 2 + (c % 2)
                    mmg(ps[bk][:, 0:n], [(w_in[:, k, 1024 + c * 128:1024 + (c + 1) * 128], hT[:, k, 0:n]) for k in range(8)],
                        r=["hT", "wgt"], w=["ps%d" % bk])
                    rope_evict(bk, n, col0, KT[:, c, col0:col0 + n], "KT", ro, b == 0)
            P.barrier()
            top[0] = keep
            if stop == "l0a1":
                return
            pw = T([128, 4, 128], BF16)
            for g in range(4):
                wload(pw[:, g, :], din["pool_w"][0, g])
            psc = T([128, 4])
            dma_nc("sp", psc, din["pool_scale"][0].rearrange("(g p) -> p g", p=128), w=["psc"])
            UP = T([128, PW])
            Aa = T([128, PW])
            Ab = T([128, PW])
            IC = T([128, PW])
            dT = T([128, PW], BF16)
            mpo = [T([128, 512], BF16) for _ in range(2)]
            for g in range(4):
                hw = (1, 2, 4, 8)[g]
                P.op("pool", lambda e: e.memset(UP, 0.0), w=["UP"])
                dma("sp", UP[:, PC0:PC0 + 256], UT[g][:, 0:256], r=["UT"], w=["UP"])
                dma("sp", UP[:, PL0:PL0 + 4096], UT[g][:, 256:NTOK], r=["UT"], w=["UP"])
                dma("pool", IC, din["invcnt"][g].partition_broadcast(128), w=["IC"])
                cur, ck = UP, "UP"
                bufs = [(Aa, "Aa"), (Ab, "Ab")]
                width = PW
                for s in range(g + 1):
                    sh = 1 << s
                    nxt, nk = bufs[s % 2]
                    width -= sh
                    P.op("dve", (lambda e, cur=cur, nxt=nxt, sh=sh, width=width: e.tensor_tensor(
                        out=nxt[:, 0:width], in0=cur[:, 0:width], in1=cur[:, sh:sh + width], op=ALU.add)),
                        r=[ck], w=[nk])
                    cur, ck = nxt, nk
                oth, ok = bufs[(g + 1) % 2]
                P.op("dve", (lambda e, cur=cur, oth=oth, hw=hw: e.tensor_tensor(
                    out=oth[:, 8:PW - 8], in0=cur[:, 8 - hw:PW - 8 - hw], in1=IC[:, 8:PW - 8], op=ALU.mult)),
                    r=[ck, "IC"], w=[ok])
                P.op("pool", (lambda e, oth=oth: e.tensor_tensor(out=dT[:, 8:PW - 8], in0=oth[:, 8:PW - 8],
                                                                 in1=UP[:, 8:PW - 8], op=ALU.subtract)),
                     r=[ok, "UP"], w=["dT"])
                for b, (t0, ntl) in enumerate(BLOCKS):
                    n = ntl * 128
                    col0 = t0 * 128
                    pc = PC0 if b == 0 else PL0 + (b - 1) * 512
                    bk = b % 2
                    mmg(ps[bk][:, 0:n], [(pw[:, g, :], dT[:, pc:pc + n])], r=["dT", "wgt"], w=["ps%d" % bk])
                    mp = mpo[b % 2]
                    P.op("act", (lambda e, bk=bk, n=n, g=g, mp=mp: e.activation(
                        out=mp[:, 0:n], in_=ps[bk][:, 0:n], func=AF.Identity, scale=psc[:, g:g + 1])),
                        r=["ps%d" % bk, "psc"], w=["mpo%d" % (b % 2)])
                    dma("pool", MIXD[g][:, col0:col0 + n], mp[:, 0:n], r=["mpo%d" % (b % 2)], w=["MIXD"])
            P.barrier()
            top[0] = keep
            if stop == "l0a2":
                return
            wo = T([128, 8, D], BF16)
            wosrc = din["mix_w_out"][l].rearrange("(k p) n -> p k n", p=128)
            for c in range(2):
                wload(wo[:, :, c * 512:(c + 1) * 512], wosrc[:, :, c * 512:(c + 1) * 512])
            lamt = T([128, 4, 64])
            lj = T([128, 64])
            lsum = T([128, 2])
            nlam = T([128, 1])
            dma("sp", lamt.rearrange("p a b -> p (a b)"),
                din["diff_lambda"][0].rearrange("a b -> (a b)").partition_broadcast(128), w=["lamt"])
            for i in range(2):
                P.op("dve", (lambda e, i=i: e.scalar_tensor_tensor(out=lj, in0=lamt[:, 2 * i, :], scalar=1.0,
                                                                   in1=lamt[:, 2 * i + 1, :], op0=ALU.mult, op1=ALU.mult,
                                                                   accum_out=lsum[:, i:i + 1])),
                     r=["lamt"], w=["lj", "lsum"])
            P.op("act", lambda e: e.activation(out=lsum, in_=lsum, func=AF.Exp), r=["lsum"], w=["lsum"])
            P.op("dve", lambda e: e.tensor_tensor(out=nlam, in0=lsum[:, 1:2], in1=lsum[:, 0:1], op=ALU.subtract),
                 r=["lsum"], w=["nlam"])
            P.op("dve", lambda e: e.tensor_scalar(out=nlam, in0=nlam, scalar1=-LAM_INIT, scalar2=None, op0=ALU.add),
                 r=["nlam"], w=["nlam"])
            sln = T([128, 1])
            dma_nc("sp", sln, din["diff_subln"][0].rearrange("(p o) -> p o", o=1), w=["sln"])
            P.op("dve", lambda e: e.tensor_scalar(out=sln, in0=sln, scalar1=1.0 - LAM_INIT, scalar2=None, op0=ALU.mult),
                 r=["sln"], w=["sln"])
            res = RES(l, 2)
            qtl = [T([128, 4, 512], BF16) for _ in range(3)]
            MIXA = [T([128, 4, 512], BF16) for _ in range(2)]
            rsum = T([128, 512])
            o2 = [T([128, 512]) for _ in range(2)]
            oc = T([128, 512])
            sq = T([128, 512], BF16)
            rstd = T([128, 512])
            state = {}

            mixp = [T([128, 4, 512], BF16) for _ in range(3)]

            def pre(b):
                if state.get("b") != b:
                    state["b"] = b
                    n = BLOCKS[b][1] * 128
                    col0 = BLOCKS[b][0] * 128
                    dma("sp", qtl[b % 3][:, :, 0:n], QT[b][:, :, 0:n], r=["QT"], w=["qtl%d" % (b % 3)])
                    for g in range(4):
                        dma("sp", mixp[b % 3][:, g, 0:n], MIXD[g][:, col0:col0 + n], r=["MIXD"], w=["mixp%d" % (b % 3)])

            heads = []
            for h in range(4):
                for j in range(2):
                    pr = slice(64 * j, 64 * j + 64)
                    heads.append(dict(
                        pre=pre,
                        qk=(lambda b, kt, h=h, pr=pr: [(KT[pr, h, kt * 128:(kt + 1) * 128],
                                                        qtl[b % 3][pr, h, 0:BLOCKS[b][1] * 128])]),
                        v=(lambda kt, h=h: Vv[:, kt, h * 128:(h + 1) * 128]),
                        rk=(lambda b: ["KT", "qtl%d" % (b % 3)]), rv=["Vv"]))

            def post(b, hi, n, ob, sb2):
                h, j = hi // 2, hi % 2
                mixa = MIXA[b % 2]
                km = "MIXA%d" % (b % 2)
                P.op("dve", lambda e: e.reciprocal(out=rsum[:, 0:n], in_=ps[sb2][:, 0:n]), r=["ps%d" % sb2], w=["rsum", "ps%d" % sb2])
                P.op("dve", lambda e: e.tensor_tensor(out=o2[j][:, 0:n], in0=ps[ob][:, 0:n], in1=rsum[:, 0:n], op=ALU.mult),
                     r=["ps%d" % ob, "rsum"], w=["o2_%d" % j, "ps%d" % ob])
                if j == 1:
                    P.op("dve", lambda e: e.scalar_tensor_tensor(out=oc[:, 0:n], in0=o2[1][:, 0:n], scalar=nlam,
                                                                 in1=o2[0][:, 0:n], op0=ALU.mult, op1=ALU.add),
                         r=["o2_0", "o2_1", "nlam"], w=["oc"])
                    P.op("act", lambda e: e.activation(out=sq[:, 0:n], in_=oc[:, 0:n], func=AF.Square), r=["oc"], w=["sq"])
                    mmg(ps[6][:, 0:n], [(onesb, sq[:, 0:n])], r=["sq"], w=["ps6"])
                    P.op("act", lambda e: e.activation(out=rstd[:, 0:n], in_=ps[6][:, 0:n], func=AF.Sqrt, scale=1.0 / 128,
                                                       bias=EPS), r=["ps6"], w=["rstd", "ps6"])
                    P.op("dve", lambda e: e.reciprocal(out=rstd[:, 0:n], in_=rstd[:, 0:n]), r=["rstd"], w=["rstd"])
                    P.op("dve", lambda e: e.scalar_tensor_tensor(out=mixa[:, h, 0:n], in0=oc[:, 0:n], scalar=sln,
                                                                 in1=rstd[:, 0:n], op0=ALU.mult, op1=ALU.mult),
                         r=["oc", "rstd", "sln"], w=[km])
                    if dbg:
                        col0 = BLOCKS[b][0] * 128
                        dma("sp", MIXD[4 + h][:, col0:col0 + n], mixa[:, h, 0:n], r=[km], w=["MIXD"])
                    if h == 3:
                        t0, ntl = BLOCKS[b]
                        for ti in range(ntl):
                            tt = t0 + ti
                            cs = slice(tt * 128, (tt + 1) * 128)
                            ls = slice(ti * 128, (ti + 1) * 128)
                            for hf in range(2):
                                prs = [(mixp[b % 3][:, g, ls], wo[:, g, hf * 512:(hf + 1) * 512]) for g in range(4)] + \
                                      [(mixa[:, hh, ls], wo[:, 4 + hh, hf * 512:(hf + 1) * 512]) for hh in range(4)]
                                mmg(ps[6 + hf], prs, r=["mixp%d" % (b % 3), km, "wgt"], w=["ps%d" % (6 + hf)])
                            res.run(6, 7, x_src(True, tt), XR[tt * 128:(tt + 1) * 128, :], tt < 2, "XR1")

            attention(list(range(9)), heads, 0.125, post, [0, 1], [(2, 3), (4, 5)])

        def rownorm(pt, W, NB, dst, dkey, tg):
            junk, ss, rs = tg
            P.op("act", lambda e: e.activation(out=junk[:, 0:W], in_=pt[0][:, 0:W], func=AF.Square, accum_out=ss),
                 r=[pt[1]], w=["rn_junk", "rn_ss"])
            P.op("act", lambda e: e.activation(out=rs, in_=ss, func=AF.Sqrt, scale=1.0 / W, bias=EPS), r=["rn_ss"], w=["rn_rs"])
            P.op("dve", lambda e: e.reciprocal(out=rs, in_=rs), r=["rn_rs"], w=["rn_rs"])
            P.op("dve", lambda e: e.scalar_tensor_tensor(out=dst, in0=pt[0][:, 0:W], scalar=rs, in1=NB[:, 0:W], op0=ALU.mult,
                                                         op1=ALU.mult), r=[pt[1], "rn_rs", "wgt"], w=[dkey, pt[1]])

        def phase_l1():
            l = 1
            KN = T([128, 4, NTOK], BF16)
            VM = T([128, NT, 512], BF16)
            KR2 = T([128, NTOK], BF16)
            keep = top[0]
            WA = T([128, 8, 512], BF16)
            WB = T([128, 8, 256], BF16)
            WKR = T([128, 8, 128], BF16)
            WQN = T([128, 4, 4, 128], BF16)
            WQR = T([128, 4, 256], BF16)
            WKN = T([128, 2, 4, 128], BF16)
            WVV = T([128, 2, 512], BF16)
            wsrc = din["od_w_in"][0].rearrange("(k p) n -> p k n", p=128)
            wload(WA, wsrc[:, :, 0:512])
            wload(WB, wsrc[:, :, 512:768])
            wload(WKR[:, :, 0:64], wsrc[:, :, 768:832])
            wload(WKR[:, :, 64:128], wsrc[:, :, 768:832])
            uq = din["mla_w_uq"][0].rearrange("(k p) (h c) -> p k h c", p=128, c=192)
            ukv = din["mla_w_ukv"][0].rearrange("(k p) (h c) -> p k h c", p=128, c=256)
            for k in range(4):
                wload(WQN[:, k], uq[:, k, :, 0:128])
                wload(WQR[:, k, :].rearrange("p (h c) -> p h c", h=4), uq[:, k, :, 128:192])
            for k in range(2):
                wload(WKN[:, k], ukv[:, k, :, 0:128])
                wload(WVV[:, k, :].rearrange("p (h c) -> p h c", h=4), ukv[:, k, :, 128:256])
            QNb = T([128, 512])
            KVNb = T([128, 256])
            dma("sp", QNb, din["mla_q_norm"][0].partition_broadcast(128), w=["wgt"])
            dma("sp", KVNb, din["mla_kv_norm"][0].partition_broadcast(128), w=["wgt"])
            nm = NM(l, 0, 1)
            hTs = [T([128, 8, 512], BF16)] * 2
            ro = rope_tiles() + (7,)
            tg = (T([128, 512], BF16), T([128, 1]), T([128, 1]))
            cqn = T([128, 512], BF16)
            ckvn = T([128, 256], BF16)
            cqT = T([128, 4, 512], BF16)
            ckvT = T([128, 2, 512], BF16)
            qnb = [T([128, 4, 512], BF16)] * 2
            qrb = [T([128, 2, 512], BF16)] * 2
            for b, (t0, ntl) in enumerate(BLOCKS):
                n = ntl * 128
                col0 = t0 * 128
                hT = hTs[b % 2]
                khT = "hTs0"
                rope_load(ro, b)
                for ti in range(ntl):
                    tt = t0 + ti
                    nm.run(x_src(False, tt), tt < 2, hT[:, :, ti * 128:(ti + 1) * 128], khT, 6)
                nm.flush()
                dma("pool", HT1[b][:, :, 0:n], hT[:, :, 0:n], r=[khT], w=["HT1"])
                for ti in range(ntl):
                    ts_ = slice(ti * 128, (ti + 1) * 128)
                    mmg(ps[0], [(hT[:, k, ts_], WA[:, k, :]) for k in range(8)], r=[khT, "wgt"], w=["ps0"])
                    rownorm((ps[0], "ps0"), 512, QNb, cqn, "cqn", tg)

                    def trq(e):
                        for k in range(4):
                            ins = e.transpose(out=psb[2][:, k * 128:(k + 1) * 128], in_=cqn[:, k * 128:(k + 1) * 128], identity=identb)
                        return ins
                    P.op("pe", trq, r=["cqn"], w=["ps2"])
                    P.op("act", (lambda e, ts_=ts_: e.copy(out=cqT[:, :, ts_], in_=psb[2][:, 0:512].rearrange("p (k n) -> p k n", k=4))),
                         r=["ps2"], w=["cqT"])
                    mmg(ps[1][:, 0:256], [(hT[:, k, ts_], WB[:, k, :]) for k in range(8)], r=[khT, "wgt"], w=["ps1"])
                    rownorm((ps[1], "ps1"), 256, KVNb, ckvn, "ckvn", tg)

                    def trk(e):
                        for k in range(2):
                            ins = e.transpose(out=psb[3][:, k * 128:(k + 1) * 128], in_=ckvn[:, k * 128:(k + 1) * 128], identity=identb)
                        return ins
                    P.op("pe", trk, r=["ckvn"], w=["ps3"])
                    P.op("act", (lambda e, ts_=ts_: e.copy(out=ckvT[:, :, ts_], in_=psb[3][:, 0:256].rearrange("p (k n) -> p k n", k=2))),
                         r=["ps3"], w=["ckvT"])
                for ti in range(ntl):
                    tt = t0 + ti
                    ts_ = slice(ti * 128, (ti + 1) * 128)
                    mmg(ps[0], [(ckvT[:, k, ts_], WVV[:, k, :]) for k in range(2)], r=["ckvT", "wgt"], w=["ps0"])
                    P.op("act", (lambda e, tt=tt: e.copy(out=VM[:, tt, :], in_=ps[0])), r=["ps0"], w=["VM", "ps0"])
                for h in range(4):
                    bk = 4 + (h % 2)
                    mmg(ps[bk][:, 0:n], [(WKN[:, k, h, :], ckvT[:, k, 0:n]) for k in range(2)], r=["ckvT", "wgt"], w=["ps%d" % bk])
                    P.op("act", (lambda e, h=h, bk=bk, n=n, col0=col0: e.copy(out=KN[:, h, col0:col0 + n], in_=ps[bk][:, 0:n])),
                         r=["ps%d" % bk], w=["KN", "ps%d" % bk])
                mmg(ps[4][:, 0:n], [(WKR[:, k, :], hT[:, k, 0:n]) for k in range(8)], r=[khT, "wgt"], w=["ps4"])
                rope_evict(4, n, col0, KR2[:, col0:col0 + n], "KR2", ro, b == 0)
                if b >= 1:
                    qn, qr = qnb[b % 2], qrb[b % 2]
                    for h in range(4):
                        bk = 4 + (h % 2)
                        mmg(ps[bk][:, 0:n], [(WQN[:, k, h, :], cqT[:, k, 0:n]) for k in range(4)], r=["cqT", "wgt"], w=["ps%d" % bk])
                        P.op("act", (lambda e, h=h, bk=bk, n=n, qn=qn: e.copy(out=qn[:, h, 0:n], in_=ps[bk][:, 0:n])),
                             r=["ps%d" % bk], w=["qnb0", "ps%d" % bk])
                    dma("pool", QT[b][:, :, 0:n], qn[:, :, 0:n], r=["qnb0"], w=["QT"])
                    for c in range(2):
                        bk = 4 + (c % 2)
                        mmg(ps[bk][:, 0:n], [(WQR[:, k, c * 128:(c + 1) * 128], cqT[:, k, 0:n]) for k in range(4)],
                            r=["cqT", "wgt"], w=["ps%d" % bk])
                        rope_evict(bk, n, col0, qr[:, c, 0:n], "qrb0", ro, False)
                    dma("pool", QR[b][:, :, 0:n], qr[:, :, 0:n], r=["qrb0"], w=["QR"])
            P.barrier()
            top[0] = keep
            if stop == "l1b1":
                return
            qnl = [T([128, 4, 512], BF16) for _ in range(3)]
            qrl = [T([128, 2, 512], BF16) for _ in range(3)]
            rsum = T([128, 512])
            mo = [T([128, 512], BF16) for _ in range(2)]
            state = {}

            def pre(b):
                if state.get("b") != b:
                    state["b"] = b
                    n = BLOCKS[b][1] * 128
                    dma("sp", qnl[b % 3][:, :, 0:n], QT[b][:, :, 0:n], r=["QT"], w=["qnl%d" % (b % 3)])
                    dma("sp", qrl[b % 3][:, :, 0:n], QR[b][:, :, 0:n], r=["QR"], w=["qnl%d" % (b % 3)])

            heads = []
            for h in range(4):
                pr = slice(64 * (h % 2), 64 * (h % 2) + 64)
                heads.append(dict(
                    pre=pre,
                    qk=(lambda b, kt, h=h, pr=pr: [(KN[:, h, kt * 128:(kt + 1) * 128], qnl[b % 3][:, h, 0:BLOCKS[b][1] * 128]),
                                                    (KR2[pr, kt * 128:(kt + 1) * 128], qrl[b % 3][pr, h // 2, 0:BLOCKS[b][1] * 128])]),
                    v=(lambda kt, h=h: VM[:, kt, h * 128:(h + 1) * 128]),
                    rk=(lambda b: ["KN", "KR2", "qnl%d" % (b % 3)]), rv=["VM"]))

            def post(b, hi, n, ob, sb2):
                col0 = BLOCKS[b][0] * 128
                m = mo[hi % 2]
                km = "mo%d" % (hi % 2)
                P.op("dve", lambda e: e.reciprocal(out=rsum[:, 0:n], in_=ps[sb2][:, 0:n]), r=["ps%d" % sb2], w=["rsum", "ps%d" % sb2])
                P.op("dve", lambda e: e.tensor_tensor(out=m[:, 0:n], in0=ps[ob][:, 0:n], in1=rsum[:, 0:n], op=ALU.mult),
                     r=["ps%d" % ob, "rsum"], w=[km, "ps%d" % ob])
                dma("pool", MIXD[hi][:, col0:col0 + n], m[:, 0:n], r=[km], w=["MIXD"])

            attention(list(range(1, 9)), heads, 192.0 ** -0.5, post, [0, 1, 2], [(3, 4), (5, 6)])
            phase_reset()
            if stop == "l1b2":
                return
            LB = T([128, 2, 2, 4])
            lbv = T([128, 2, 4])
            oml = T([128, 2, 4])
            hgn = T([128, 1])
            ones1 = T([128, 1])
            for d in range(2):
                for ll in range(2):
                    dma_nc("sp", LB[:, d, ll, :], din["hgrn_lb"][d, ll].rearrange("(h p) -> p h", p=128), w=["LB"])
            dma_nc("sp", hgn, din["hgrn_norm"][0].rearrange("(p o) -> p o", o=1), w=["hgn"])
            P.op("dve", lambda e: e.tensor_tensor(out=lbv, in0=LB[:, :, 1, :], in1=LB[:, :, 0, :], op=ALU.subtract), r=["LB"], w=["lbv"])
            P.op("act", lambda e: e.activation(out=lbv, in_=lbv, func=AF.Sigmoid), r=["lbv"], w=["lbv"])
            P.op("dve", lambda e: e.tensor_scalar(out=oml, in0=lbv, scalar1=-1.0, scalar2=1.0, op0=ALU.mult, op1=ALU.add),
                 r=["lbv"], w=["oml"])
            P.op("pool", lambda e: e.memset(ones1, 1.0), w=["ones1"])
            WH = T([128, 8, 5, 128], BF16)
            hTl = [T([128, 8, 512], BF16)] * 2
            SG = T([128, NTOK], BF16)
            Vt = T([128, NT, 128], BF16)
            QP = [T([128, NTOK], BF16) for _ in range(2)]
            QPP = [T([128, NTOK], BF16) for _ in range(2)]
            KP = [T([128, NTOK], BF16) for _ in range(2)]
            KPt = [T([128, NT, 128], BF16) for _ in range(2)]
            EL = [T([128, 68]) for _ in range(2)]
            OT = [T([128, NTOK]) for _ in range(2)]
            qf = T([128, 512])
            sgm = T([128, 512])
            kk = T([128, 512])
            lf = T([128, 512])
            gb = T([128, 516])
            Da = T([128, 512])
            Db = T([128, 512])
            E1 = T([128, 512])
            E2 = T([128, 512])
            E3 = T([128, 512])
            elt = T([128, 8])
            Sf = [[T([128, 128]) for _ in range(2)] for _ in range(2)]
            Sb = [[T([128, 128], BF16) for _ in range(2)] for _ in range(2)]
            Am = [[T([128, 64], BF16) for _ in range(2)] for _ in range(2)]
            osum, rstd, otmp = Da, Db, E1
            sq = T([128, 512], BF16)
            mh = [T([128, 512], BF16) for _ in range(2)]
            P.op("pool", lambda e: e.memset(gb[:, 0:1], 0.0), w=["gb0"])
            wsrc = din["od_w_in"][0].rearrange("(k p) n -> p k n", p=128)
            for h in range(4):
                for i, c0_ in enumerate((832, 1344, 1856, 2368, 2880)):
                    wload(WH[:, :, i, :], wsrc[:, :, c0_ + h * 128:c0_ + (h + 1) * 128])
                for b, (t0, ntl) in enumerate(BLOCKS):
                    n = ntl * 128
                    nch = n // 64
                    col0 = t0 * 128
                    ch0 = col0 // 64
                    hT = hTl[b % 2]
                    khT = "hTl0"
                    dma("sp", hT[:, :, 0:n], HT1[b][:, :, 0:n], r=["HT1"], w=[khT])
                    mmg(ps[0][:, 0:n], [(WH[:, k, 0, :], hT[:, k, 0:n]) for k in range(8)], r=[khT, "wgt"], w=["ps0"])
                    P.op("act", (lambda e, n=n: e.activation(out=qf[:, 0:n], in_=ps[0][:, 0:n], func=AF.Silu)), r=["ps0"], w=["qf", "ps0"])
                    mmg(ps[1][:, 0:n], [(WH[:, k, 4, :], hT[:, k, 0:n]) for k in range(8)], r=[khT, "wgt"], w=["ps1"])
                    P.op("act", (lambda e, n=n, col0=col0: e.activation(out=SG[:, col0:col0 + n], in_=ps[1][:, 0:n], func=AF.Silu)),
                         r=["ps1"], w=["SG", "ps1"])
                    for ti in range(ntl):
                        tt = t0 + ti
                        ts_ = slice(ti * 128, (ti + 1) * 128)
                        mmg(ps[2][:, 0:128], [(hT[:, k, ts_], WH[:, k, 3, :]) for k in range(8)], r=[khT, "wgt"], w=["ps2"])
                        P.op("act", (lambda e, tt=tt: e.copy(out=Vt[:, tt, :], in_=ps[2][:, 0:128])), r=["ps2"], w=["Vt", "ps2"])
                    for d in range(2):
                        bk = 3 + d
                        kb = "ps%d" % bk
                        mmg(ps[bk][:, 0:n], [(WH[:, k, 1 + d, :], hT[:, k, 0:n]) for k in range(8)], r=[khT, "wgt"], w=[kb])
                        P.op("act", (lambda e, n=n, bk=bk: e.activation(out=sgm[:, 0:n], in_=ps[bk][:, 0:n], func=AF.Sigmoid)),
                             r=[kb], w=["sgm", kb])
                        P.op("dve", (lambda e, n=n, d=d, h=h: e.tensor_scalar(out=sgm[:, 0:n], in0=sgm[:, 0:n], scalar1=oml[:, d, h:h + 1],
                                                                             scalar2=lbv[:, d, h:h + 1], op0=ALU.mult, op1=ALU.add)),
                             r=["sgm", "oml", "lbv"], w=["sgm"])
                        P.op("pool", (lambda e, n=n: e.tensor_scalar(out=kk[:, 0:n], in0=sgm[:, 0:n], scalar1=-1.0, scalar2=1.0,
                                                                    op0=ALU.mult, op1=ALU.add)), r=["sgm"], w=["kk"])
                        P.op("act", (lambda e, n=n: e.activation(out=lf[:, 0:n], in_=sgm[:, 0:n], func=AF.Ln)), r=["sgm"], w=["lf"])
                        P.op("dve", (lambda e, n=n: e.tensor_tensor_scan(out=gb[:, 1:1 + n], data0=ones1[:, 0:1].to_broadcast([128, n]),
                                                                        data1=lf[:, 0:n], initial=0.0, op0=ALU.mult, op1=ALU.add)),
                             r=["lf", "ones1", "gb0"], w=["gb"])
                        Gi3 = gb[:, 1:1 + n].rearrange("p (c j) -> p c j", j=64)
                        Gs3 = gb[:, 0:n].rearrange("p (c j) -> p c j", j=64)
                        S0 = Gs3[:, :, 0:1].to_broadcast([128, nch, 64])
                        I63 = Gi3[:, :, 63:64].to_broadcast([128, nch, 64])
                        Da3 = Da[:, 0:n].rearrange("p (c j) -> p c j", j=64)
                        Db3 = Db[:, 0:n].rearrange("p (c j) -> p c j", j=64)
                        if d == 0:
                            P.op("dve", (lambda e, Gi3=Gi3, S0=S0, Da3=Da3: e.tensor_tensor(out=Da3, in0=Gi3, in1=S0, op=ALU.subtract)),
                                 r=["gb"], w=["Da"])
                            P.op("dve", (lambda e, Gi3=Gi3, I63=I63, Db3=Db3: e.tensor_tensor(out=Db3, in0=Gi3, in1=I63, op=ALU.subtract)),
                                 r=["gb"], w=["Db"])
                            sc1, sc2, sc3 = 1.0, -1.0, 1.0
                        else:
                            P.op("dve", (lambda e, Gs3=Gs3, I63=I63, Da3=Da3: e.tensor_tensor(out=Da3, in0=Gs3, in1=I63, op=ALU.subtract)),
                                 r=["gb"], w=["Da"])
                            P.op("dve", (lambda e, Gs3=Gs3, S0=S0, Db3=Db3: e.tensor_tensor(out=Db3, in0=Gs3, in1=S0, op=ALU.subtract)),
                                 r=["gb"], w=["Db"])
                            sc1, sc2, sc3 = -1.0, 1.0, -1.0
                        P.op("dve", (lambda e, Gi3=Gi3, Gs3=Gs3, nch=nch: e.tensor_tensor(out=elt[:, 0:nch], in0=Gi3[:, :, 63], in1=Gs3[:, :, 0],
                                                                                          op=ALU.subtract)), r=["gb"], w=["elt"])
                        P.op("act", (lambda e, n=n, sc1=sc1: e.activation(out=E1[:, 0:n], in_=Da[:, 0:n], func=AF.Exp, scale=sc1)), r=["Da"], w=["E1"])
                        P.op("act", (lambda e, n=n, sc2=sc2: e.activation(out=E2[:, 0:n], in_=Db[:, 0:n], func=AF.Exp, scale=sc2)), r=["Db"], w=["E2"])
                        P.op("act", (lambda e, n=n, sc3=sc3: e.activation(out=E3[:, 0:n], in_=Db[:, 0:n], func=AF.Exp, scale=sc3)), r=["Db"], w=["E3"])
                        P.op("act", (lambda e, d=d, ch0=ch0, nch=nch: e.activation(out=EL[d][:, ch0:ch0 + nch], in_=elt[:, 0:nch], func=AF.Exp)),
                             r=["elt"], w=["EL%d" % d])
                        P.op("dve", (lambda e, n=n, d=d, col0=col0: e.tensor_tensor(out=QP[d][:, col0:col0 + n], in0=qf[:, 0:n], in1=E1[:, 0:n], op=ALU.mult)),
                             r=["qf", "E1"], w=["QP%d" % d])
                        P.op("dve", (lambda e, n=n, d=d, col0=col0: e.tensor_tensor(out=KP[d][:, col0:col0 + n], in0=kk[:, 0:n], in1=E2[:, 0:n], op=ALU.mult)),
                             r=["kk", "E2"], w=["KP%d" % d])
                        P.op("pool", (lambda e, n=n, d=d, col0=col0: e.tensor_tensor(out=QPP[d][:, col0:col0 + n], in0=qf[:, 0:n], in1=E3[:, 0:n], op=ALU.mult)),
                             r=["qf", "E3"], w=["QPP%d" % d])
                        for ti in range(ntl):
                            tt = t0 + ti

                            def trp(e, d=d, tt=tt):
                                return e.transpose(out=psb[6][:, 0:128], in_=KP[d][:, tt * 128:(tt + 1) * 128], identity=identb)
                            P.op("pe", trp, r=["KP%d" % d], w=["ps6"])
                            P.op("act", (lambda e, d=d, tt=tt: e.copy(out=KPt[d][:, tt, :], in_=psb[6][:, 0:128])), r=["ps6"], w=["KPt%d" % d, "ps6"])
                orders = [list(range(68)), [3, 2, 1, 0] + list(range(67, 3, -1))]
                first = [True, True]
                cur = [0, 0]
                for step in range(68):
                    for d in range(2):
                        c = orders[d][step]
                        tt, hh = c // 2, c % 2
                        rows = slice(64 * hh, 64 * hh + 64)
                        cols = slice(64 * c, 64 * c + 64)
                        if c < 4:
                            pos, blk_n, blk_c0 = c, 256, 0
                        else:
                            pos, blk_n, blk_c0 = (c - 4) % 8, 512, 256 + ((c - 4) // 8) * 512
                        pA, pU, pO = ps[d], ps[2 + d], ps[4 + d]
                        kA, kU, kO = "ps%d" % d, "ps%d" % (2 + d), "ps%d" % (4 + d)
                        am = Am[d][step % 2]
                        kam = "Am%d_%d" % (d, step % 2)
                        mmg(pA[rows, 0:64], [(KP[d][:, cols], QPP[d][:, cols])], r=["KP%d" % d, "QPP%d" % d], w=[kA])
                        mk = maskf if d == 0 else maskb
                        P.op("dve", (lambda e, pA=pA, rows=rows, am=am, mk=mk: e.tensor_tensor(out=am[rows, :], in0=pA[rows, 0:64], in1=mk[rows, :],
                                                                                              op=ALU.mult)), r=[kA], w=[kam, kA])
                        so, sn = cur[d], 1 - cur[d]
                        prs = [(Vt[rows, tt, :], am[rows, :])]
                        rr = ["Vt", kam]
                        if not first[d]:
                            prs.append((Sb[d][so], QP[d][:, cols]))
                            rr += ["Sb%d_%d" % (d, so), "QP%d" % d]
                        mmg(pO[:, pos * 64:(pos + 1) * 64], prs, r=rr, w=[kO])
                        mmg(pU[:, 0:128], [(KPt[d][rows, tt, :], Vt[rows, tt, :])], r=["KPt%d" % d, "Vt"], w=[kU])
                        if first[d]:
                            P.op("dve", (lambda e, d=d, sn=sn, pU=pU: e.tensor_copy(out=Sf[d][sn], in_=pU[:, 0:128])),
                                 r=[kU], w=["Sf%d_%d" % (d, sn), kU])
                        else:
                            P.op("dve", (lambda e, d=d, sn=sn, so=so, pU=pU, c=c: e.scalar_tensor_tensor(
                                out=Sf[d][sn], in0=Sf[d][so], scalar=EL[d][:, c:c + 1], in1=pU[:, 0:128], op0=ALU.mult, op1=ALU.add)),
                                r=[kU, "Sf%d_%d" % (d, so), "EL%d" % d], w=["Sf%d_%d" % (d, sn), kU])
                        P.op("act", (lambda e, d=d, sn=sn: e.copy(out=Sb[d][sn], in_=Sf[d][sn])), r=["Sf%d_%d" % (d, sn)], w=["Sb%d_%d" % (d, sn)])
                        cur[d] = sn
                        first[d] = False
                        last_in_blk = (pos == (blk_n // 64 - 1)) if d == 0 else (pos == 0)
                        if last_in_blk:
                            P.op("act", (lambda e, d=d, pO=pO, blk_n=blk_n, blk_c0=blk_c0: e.copy(out=OT[d][:, blk_c0:blk_c0 + blk_n], in_=pO[:, 0:blk_n])),
                                 r=[kO], w=["OT%d" % d, kO])
                for b in range(1, 9):
                    n = 512
                    col0 = BLOCKS[b][0] * 128
                    cs = slice(col0, col0 + n)
                    m = mh[b % 2]
                    km = "mh%d" % (b % 2)
                    P.op("pool", (lambda e, cs=cs: e.tensor_tensor(out=osum, in0=OT[0][:, cs], in1=OT[1][:, cs], op=ALU.add)),
                         r=["OT0", "OT1"], w=["Da"])
                    P.op("act", lambda e: e.activation(out=sq, in_=osum, func=AF.Square), r=["Da"], w=["sq"])
                    mmg(ps[7], [(onesb, sq)], r=["sq"], w=["ps7"])
                    P.op("act", lambda e: e.activation(out=rstd, in_=ps[7], func=AF.Sqrt, scale=1.0 / 128, bias=EPS), r=["ps7"], w=["Db", "ps7"])
                    P.op("dve", lambda e: e.reciprocal(out=rstd, in_=rstd), r=["Db"], w=["Db"])
                    P.op("dve", lambda e: e.scalar_tensor_tensor(out=otmp, in0=osum, scalar=hgn, in1=rstd, op0=ALU.mult, op1=ALU.mult),
                         r=["Da", "Db", "hgn"], w=["E1"])
                    P.op("dve", (lambda e, cs=cs, m=m: e.tensor_tensor(out=m, in0=otmp, in1=SG[:, cs], op=ALU.mult)), r=["E1", "SG"], w=[km])
                    dma("pool", MIXD[4 + h][:, cs], m, r=[km], w=["MIXD"])
            phase_reset()
            if stop == "l1b3":
                return
            wo = T([128, 8, D], BF16)
            wosrc = din["mix_w_out"][l].rearrange("(k p) n -> p k n", p=128)
            for c in range(2):
                wload(wo[:, :, c * 512:(c + 1) * 512], wosrc[:, :, c * 512:(c + 1) * 512])
            res = RES(l, 2)
            mix = [T([128, 8, 512], BF16) for _ in range(2)]
            for b in range(1, 9):
                t0, ntl = BLOCKS[b]
                col0 = t0 * 128
                mx = mix[b % 2]
                kx = "mix%d" % (b % 2)
                for k in range(8):
                    dma("sp", mx[:, k, :], MIXD[k][:, col0:col0 + 512], r=["MIXD"], w=[kx])
                for ti in range(ntl):
                    tt = t0 + ti
                    ls = slice(ti * 128, (ti + 1) * 128)
                    for hf in range(2):
                        mmg(ps[6 + hf], [(mx[:, k, ls], wo[:, k, hf * 512:(hf + 1) * 512]) for k in range(8)],
                            r=[kx, "wgt"], w=["ps%d" % (6 + hf)])
                    res.run(6, 7, x_src(False, tt), XR[tt * 128:(tt + 1) * 128, :], False, "XR1")

        phase_mod()
        phase_reset()
        S0 = ("mod", "l0a1", "l0a2", "l0mix")
        S1 = S0 + ("l0", "l1b1", "l1b2", "l1b3", "l1mix")
        if stop != "mod":
            phase_l0()
            phase_reset()
        if stop not in S0:
            phase_ffn(0, True, False)
            phase_reset()
        if stop not in S0 + ("l0",):
            phase_l1()
            phase_reset()
        if stop not in S1:
            phase_ffn(1, False, True)
            phase_reset()
        P.emit(st)
    return nc


_CACHE = {}


def kernel(**inputs):
    consts = _consts()
    if "nc" not in _CACHE:
        _CACHE["nc"] = build()
    nc = _CACHE["nc"]
    in_maps = []
    for b in range(8):
        m = {"x": np.ascontiguousarray(inputs["x"][b]), "ctx": np.ascontiguousarray(inputs["ctx"][b]),
             "cvec": np.ascontiguousarray(np.stack([inputs["c"][b], inputs["c_ctx"]], 0))}
        for n in W_NAMES:
            m[n] = np.ascontiguousarray(inputs[n])
        m.update(consts)
        in_maps.append(m)
    res = run_bass_kernel_spmd(nc, in_maps, core_ids=list(range(8)))
    return np.stack([r["out"] for r in res.results], 0).astype(np.float32)
```

```python
import numpy as np
from contextlib import ExitStack
import concourse.bass as bass
import concourse.mybir as mybir
from concourse.bass_utils import run_bass_kernel_spmd

F32 = mybir.dt.float32
BF16 = mybir.dt.bfloat16
ALU = mybir.AluOpType
AF = mybir.ActivationFunctionType

NDSEM = 8
D = 1024
NT = 34
NTOK = 4352
DFF = 2816
NJ = 22
EPS = 1e-6


class Prog:
    ENGS = ("pe", "dve", "act", "pool", "sp")

    def __init__(self, nc):
        self.nc = nc
        self.ops = []
        self.lw = {}
        self.rd = {}
        self.cnt = {e: 0 for e in self.ENGS}
        self.dcnt = {e: 0 for e in self.ENGS}
        self.dslot_last = {e: [None] * NDSEM for e in self.ENGS}
        self.last_nd = {e: None for e in self.ENGS}
        self.pending_bar = {e: [] for e in self.ENGS}

    def op(self, eng, fn, r=(), w=(), dma=False):
        oid = len(self.ops)
        deps = []
        for k in r:
            y = self.lw.get(k)
            if y is not None:
                deps.append((y, "RAW"))
        for k in w:
            y = self.lw.get(k)
            if y is not None:
                deps.append((y, "WAW"))
            for y in self.rd.get(k, ()):
                deps.append((y, "WAR"))
        for y in self.pending_bar[eng]:
            deps.append((y, "RAW"))
        self.pending_bar[eng] = []
        o = dict(id=oid, eng=eng, fn=fn, deps=deps, dma=dma)
        if dma:
            i = self.dcnt[eng]
            self.dcnt[eng] += 1
            slot = i % NDSEM
            o["dslot"] = slot
            o["dval"] = 16 * (i // NDSEM + 1)
            prev = self.dslot_last[eng][slot]
            if prev is not None:
                deps.append((prev, "RAW"))
            self.dslot_last[eng][slot] = oid
        else:
            self.cnt[eng] += 1
            o["val"] = self.cnt[eng]
            self.last_nd[eng] = oid
        self.ops.append(o)
        for k in w:
            self.lw[k] = oid
            self.rd[k] = []
        for k in r:
            if k not in w:
                self.rd.setdefault(k, []).append(oid)
        return oid

    def barrier(self):
        snap = []
        for e in self.ENGS:
            if self.last_nd[e] is not None:
                snap.append(self.last_nd[e])
            for y in self.dslot_last[e]:
                if y is not None:
                    snap.append(y)
        for e in self.ENGS:
            self.pending_bar[e] = list(snap)
        self.lw = {}
        self.rd = {}

    def emit(self, st):
        nc = self.nc
        sems = {e: st.enter_context(nc.semaphore("s_" + e)) for e in self.ENGS}
        dsems = {e: [st.enter_context(nc.semaphore("d_%s%d" % (e, i))) for i in range(NDSEM)]
                 for e in ("sp", "pool", "act") if self.dcnt[e] > 0}
        block = st.enter_context(nc.Block())
        ops = self.ops

        def run(ename, eng):
            seen = {}
            for o in ops:
                if o["eng"] != ename:
                    continue
                need = {}
                for (y, kind) in o["deps"]:
                    Y = ops[y]
                    if Y["dma"]:
                        key = ("d", Y["eng"], Y["dslot"])
                        sem = dsems[Y["eng"]][Y["dslot"]]
                        val = Y["dval"]
                    else:
                        if Y["eng"] == ename and not o["dma"]:
                            if ename == "pe" or kind != "RAW":
                                continue
                        key = ("c", Y["eng"])
                        sem = sems[Y["eng"]]
                        val = Y["val"]
                    if seen.get(key, 0) >= val:
                        continue
                    if key not in need or need[key][1] < val:
                        need[key] = (sem, val)
                for key, (sem, val) in need.items():
                    eng.wait_ge(sem, val)
                    seen[key] = val
                ins = o["fn"](eng)
                if o["dma"]:
                    ins.then_inc(dsems[ename][o["dslot"]], 16)
                else:
                    ins.then_inc(sems[ename], 1)
            if ename in dsems:
                for slot in range(NDSEM):
                    y = self.dslot_last[ename][slot]
                    if y is not None:
                        Y = ops[y]
                        if seen.get(("d", ename, slot), 0) < Y["dval"]:
                            eng.wait_ge(dsems[ename][slot], Y["dval"])

        @block.tensor
        def _(e):
            run("pe", e)

        @block.vector
        def _(e):
            run("dve", e)

        @block.scalar
        def _(e):
            run("act", e)

        @block.gpsimd
        def _(e):
            run("pool", e)

        @block.sync
        def _(e):
            run("sp", e)


PW = 8 + 256 + 16 + 4096 + 8
PC0, PL0 = 8, 280


def _consts():
    c = {}
    c["ident"] = np.eye(128, dtype=np.float32)
    rot = np.zeros((128, 128), np.float32)
    for d in range(128):
        rot[d ^ 16, d] = 1.0
    c["rot"] = rot
    n = 4096
    pos_row = np.repeat(np.arange(n // 64), 64)
    pos_col = np.tile(np.arange(64), n // 64)
    inv_freq = (10000.0 ** (-np.arange(0, 32, 2, dtype=np.float32) / 32)).astype(np.float32)
    ang = np.stack([pos_row, pos_col], -1).astype(np.float32)[..., None] * inv_freq
    cs, sn = np.cos(ang).astype(np.float32), np.sin(ang).astype(np.float32)
    cos_t = np.zeros((128, n), np.float32)
    sin_t = np.zeros((128, n), np.float32)
    for d in range(128):
        dd = d % 64
        a, hf, i = dd // 32, (dd // 16) % 2, dd % 16
        cos_t[d] = cs[:, a, i]
        sin_t[d] = sn[:, a, i] * (-1.0 if hf == 0 else 1.0)
    c["cos_t"] = cos_t
    c["sin_t"] = sin_t
    inv = np.zeros((4, PW), np.float32)
    for g, w in enumerate((2, 4, 8, 16)):
        h = w // 2
        for (n_, off) in ((256, PC0), (4096, PL0)):
            t = np.arange(n_)
            lo = np.clip(t - h, 0, n_)
            hi = np.clip(t + h, 0, n_)
            inv[g, off:off + n_] = 1.0 / (hi - lo).astype(np.float32)
    c["invcnt"] = inv
    p = np.arange(128)[:, None] % 64
    t = np.arange(64)[None, :]
    c["mask_f"] = (p <= t).astype(np.float32)
    c["mask_b"] = (p >= t).astype(np.float32)
    return c


W_NAMES = ["ada_w", "ada_b", "norm_g", "mix_w_out", "ffn_w_gate", "ffn_w_up", "ffn_conv_w",
           "ffn_conv_b", "ffn_w_down", "ev_w_in", "pool_w", "pool_scale", "diff_lambda",
           "diff_subln", "od_w_in", "mla_q_norm", "mla_w_uq", "mla_kv_norm", "mla_w_ukv",
           "hgrn_norm", "hgrn_lb"]
W_SHAPES = {"ada_w": (2, 1024, 6144), "ada_b": (2, 6144), "norm_g": (2, 4, 1024),
            "mix_w_out": (2, 1024, 1024), "ffn_w_gate": (2, 1024, 2816), "ffn_w_up": (2, 1024, 2816),
            "ffn_conv_w": (2, 3, 2816), "ffn_conv_b": (2, 2816), "ffn_w_down": (2, 2816, 1024),
            "ev_w_in": (1, 1024, 2048), "pool_w": (1, 4, 128, 128), "pool_scale": (1, 512),
            "diff_lambda": (1, 4, 64), "diff_subln": (1, 128), "od_w_in": (1, 1024, 3392),
            "mla_q_norm": (1, 512), "mla_w_uq": (1, 512, 768), "mla_kv_norm": (1, 256),
            "mla_w_ukv": (1, 256, 1024), "hgrn_norm": (1, 128), "hgrn_lb": (2, 2, 512)}
C_SHAPES = {"ident": (128, 128), "rot": (128, 128), "cos_t": (128, 4096), "sin_t": (128, 4096),
            "invcnt": (4, PW), "mask_f": (128, 64), "mask_b": (128, 64)}

BLOCKS = [(0, 2)] + [(2 + 4 * i, 4) for i in range(8)]


def build(stop=None, dbg=False):
    nc = bass.Bass("TRN2", target_bir_lowering=False)
    din = {}
    din["x"] = nc.dram_tensor("x", [4096, D], F32, kind="ExternalInput").ap()
    din["ctx"] = nc.dram_tensor("ctx", [256, D], F32, kind="ExternalInput").ap()
    din["cvec"] = nc.dram_tensor("cvec", [2, D], F32, kind="ExternalInput").ap()
    for n in W_NAMES:
        din[n] = nc.dram_tensor(n, list(W_SHAPES[n]), F32, kind="ExternalInput").ap()
    for n in C_SHAPES:
        din[n] = nc.dram_tensor(n, list(C_SHAPES[n]), F32, kind="ExternalInput").ap()
    out = nc.dram_tensor("out", [4096, D], F32, kind="ExternalOutput").ap()
    XR = nc.dram_tensor("XR", [NTOK, D], F32, kind="ExternalOutput" if dbg else "Internal").ap()
    MODV = nc.dram_tensor("MODV", [2, 2, 6, D], F32, kind="Internal").ap()
    QT = nc.dram_tensor("QT", [9, 128, 4, 512], BF16, kind="Internal").ap()
    QR = nc.dram_tensor("QR", [9, 128, 2, 512], BF16, kind="Internal").ap()
    UT = nc.dram_tensor("UT", [4, 128, NTOK], F32, kind="Internal").ap()
    HT1 = nc.dram_tensor("HT1", [9, 128, 8, 512], BF16, kind="Internal").ap()
    H2D = nc.dram_tensor("H2D", [128, 8, 4355], BF16, kind="Internal").ap()
    MIXD = nc.dram_tensor("MIXD", [8, 128, NTOK], BF16, kind="ExternalOutput" if dbg else "Internal").ap()

    st = ExitStack()
    with st:
        P = Prog(nc)
        AW = 52000
        arena = st.enter_context(nc.sbuf_tensor("arena", [128, AW], F32))
        psall = st.enter_context(nc.psum_tensor("psall", [128, 4096], F32))[:]
        ps = [psall[:, i * 512:(i + 1) * 512] for i in range(8)]
        psb = [p.bitcast(BF16) for p in ps]
        top = [0]

        def T(shape, dt=F32):
            n = int(np.prod(shape[1:]))
            cols = n if dt == F32 else (n + 1) // 2
            off = top[0]
            top[0] += cols
            assert top[0] <= AW, "SBUF arena overflow %d" % top[0]
            a = arena[0:shape[0], off:off + cols]
            if dt != F32:
                a = a.bitcast(dt)
            if len(shape) == 3:
                a = a.rearrange("p (a b) -> p a b", a=shape[1])
            elif len(shape) == 4:
                a = a.rearrange("p (a b c) -> p a b c", a=shape[1], b=shape[2])
            return a

        uid = [0]

        def K(s):
            uid[0] += 1
            return "%s#%d" % (s, uid[0])

        def dma(q, o, i, r=(), w=()):
            P.op(q, lambda e: e.dma_start(out=o, in_=i), r=r, w=w, dma=True)

        def dma_nc(q, o, i, r=(), w=()):
            P.op(q, lambda e: e.dma_start(out=o, in_=i, allow_slow_non_contiguous=True), r=r, w=w, dma=True)

        def mmg(o, pairs, r, w):
            def f(e):
                n = len(pairs)
                for i, (l, rh) in enumerate(pairs):
                    ins = e.matmul(o, lhsT=l, rhs=rh, start=(i == 0), stop=(i == n - 1))
                return ins
            P.op("pe", f, r=r, w=w)

        identb = T([128, 128], BF16)
        rotb = T([128, 128], BF16)
        onesb = T([128, 128], BF16)
        maskf = T([128, 64])
        maskb = T([128, 64])
        stg = [T([128, 2048]) for _ in range(2)]
        stgi = [0]
        PERSIST = None

        def wload(dst, src, q=None, ce="pool"):
            i = stgi[0] % 2
            stgi[0] += 1
            shp = list(dst.shape)
            n = int(np.prod(shp[1:]))
            if n > 2048:
                hh = shp[-1] // 2
                if len(shp) == 2:
                    wload(dst[:, 0:hh], src[:, 0:hh], q, ce)
                    wload(dst[:, hh:], src[:, hh:], q, ce)
                else:
                    wload(dst[:, :, 0:hh], src[:, :, 0:hh], q, ce)
                    wload(dst[:, :, hh:], src[:, :, hh:], q, ce)
                return
            s = stg[i][0:shp[0], 0:n]
            if len(shp) == 3:
                s = s.rearrange("p (a b) -> p a b", a=shp[1])
            qq = q or ("sp" if i == 0 else "pool")
            dma(qq, s, src, w=["stg%d" % i])
            kd = "W" + str(id(dst))
            if ce == "pool":
                P.op("pool", lambda e: e.tensor_copy(out=dst, in_=s), r=["stg%d" % i], w=["wgt"])
            elif ce == "dve":
                P.op("dve", lambda e: e.tensor_copy(out=dst, in_=s), r=["stg%d" % i], w=["wgt"])
            else:
                P.op("act", lambda e: e.copy(out=dst, in_=s), r=["stg%d" % i], w=["wgt"])

        for (dst, nm) in ((identb, "ident"), (rotb, "rot")):
            wload(dst, din[nm])
        P.op("pool", lambda e: e.memset(onesb, 1.0), w=["wgt"])
        dma("sp", maskf, din["mask_f"], w=["wgt"])
        dma("sp", maskb, din["mask_b"], w=["wgt"])
        PERSIST = top[0]

        def phase_reset():
            P.barrier()
            top[0] = PERSIST

        def phase_mod():
            cv = T([128, 2, 8])
            cvs = T([128, 2, 8])
            cvb = T([128, 8, 2], BF16)
            for j in range(2):
                dma_nc("sp", cv[:, j, :], din["cvec"][j].rearrange("(k p) -> p k", p=128), w=["cv"])
            P.op("act", lambda e: e.activation(out=cvs, in_=cv, func=AF.Silu), r=["cv"], w=["cvs"])
            P.op("dve", lambda e: e.tensor_copy(out=cvb, in_=cvs.rearrange("p j k -> p k j")), r=["cvs"], w=["cvb"])
            awb = [T([128, 8, 256], BF16) for _ in range(2)]
            Mt = T([2, 6 * D])
            bt = T([2, 6 * D])
            ng = T([2, 4, D])
            V = T([2, 6, D])
            for l in range(2):
                dma("sp", bt, din["ada_b"][l].partition_broadcast(2), w=["bt"])
                dma("sp", ng.rearrange("p a b -> p (a b)"),
                    din["norm_g"][l].rearrange("a b -> (a b)").partition_broadcast(2), w=["ng"])
                for nb in range(24):
                    ab = awb[nb % 2]
                    kab = "awb%d" % (nb % 2)
                    src = din["ada_w"][l].rearrange("(k p) n -> p k n", p=128)[:, :, nb * 256:(nb + 1) * 256]
                    i = stgi[0] % 2
                    stgi[0] += 1
                    s = stg[i][:, :].rearrange("p (a b) -> p a b", a=8)
                    dma("sp" if i == 0 else "pool", s, src, w=["stg%d" % i])
                    P.op("pool" if nb % 2 == 0 else "dve", (lambda e, ab=ab, s=s: e.tensor_copy(out=ab, in_=s)),
                         r=["stg%d" % i], w=[kab])
                    pb = ps[nb % 2][0:2, 0:256]
                    mmg(pb, [(cvb[:, k, :], ab[:, k, :]) for k in range(8)], r=["cvb", kab], w=["ps%d" % (nb % 2)])
                    P.op("dve", (lambda e, pb=pb, nb=nb: e.tensor_tensor(out=Mt[:, nb * 256:(nb + 1) * 256], in0=pb,
                                                                       in1=bt[:, nb * 256:(nb + 1) * 256], op=ALU.add)),
                         r=["ps%d" % (nb % 2), "bt"], w=["Mt"])
                sl = lambda i: Mt[:, i * D:(i + 1) * D]
                P.op("dve", lambda e: e.scalar_tensor_tensor(out=V[:, 0, :], in0=sl(1), scalar=1.0, in1=ng[:, 0, :],
                                                             op0=ALU.add, op1=ALU.mult), r=["Mt", "ng"], w=["V"])
                P.op("dve", lambda e: e.tensor_copy(out=V[:, 1, :], in_=sl(0)), r=["Mt"], w=["V"])
                P.op("dve", lambda e: e.tensor_tensor(out=V[:, 2, :], in0=sl(2), in1=ng[:, 1, :], op=ALU.mult),
                     r=["Mt", "ng"], w=["V"])
                P.op("dve", lambda e: e.scalar_tensor_tensor(out=V[:, 3, :], in0=sl(4), scalar=1.0, in1=ng[:, 2, :],
                                                             op0=ALU.add, op1=ALU.mult), r=["Mt", "ng"], w=["V"])
                P.op("dve", lambda e: e.tensor_copy(out=V[:, 4, :], in_=sl(3)), r=["Mt"], w=["V"])
                P.op("dve", lambda e: e.tensor_tensor(out=V[:, 5, :], in0=sl(5), in1=ng[:, 3, :], op=ALU.mult),
                     r=["Mt", "ng"], w=["V"])
                dma("sp", MODV[l].rearrange("j a b -> j (a b)"), V.rearrange("p a b -> p (a b)"), r=["V"], w=["MODV"])

        def x_src(layer0_in, tt):
            if layer0_in:
                return din["ctx"][tt * 128:(tt + 1) * 128, :] if tt < 2 else din["x"][(tt - 2) * 128:(tt - 1) * 128, :]
            return XR[tt * 128:(tt + 1) * 128, :]

        class NM:
            def __init__(self, l, gi, si):
                self.xt = [T([128, D]) for _ in range(2)]
                self.junk = T([128, D], BF16)
                self.tmp = [T([128, D]) for _ in range(2)]
                self.pending = None
                self.hb = [T([128, D], BF16) for _ in range(2)]
                self.ss = T([128, 2])
                self.rs = T([128, 2])
                self.G = [T([128, D]) for _ in range(2)]
                self.SH = [T([128, D]) for _ in range(2)]
                for j in range(2):
                    dma("sp", self.G[j], MODV[l, j, gi].partition_broadcast(128), r=["MODV"], w=["nmG"])
                    dma("sp", self.SH[j], MODV[l, j, si].partition_broadcast(128), r=["MODV"], w=["nmG"])
                self.i = 0

            def run(self, src, is_ctx, dst, dkey, bank):
                i = self.i % 2
                self.i += 1
                xt, hb = self.xt[i], self.hb[i]
                kx, kh = "nm_xt%d" % i, "nm_hb%d" % i
                ss, rs = self.ss[:, i:i + 1], self.rs[:, i:i + 1]
                G, SH = self.G[1 if is_ctx else 0], self.SH[1 if is_ctx else 0]
                tmp = self.tmp[i]
                dma("sp", xt, src, r=["XR"], w=[kx])
                P.op("act", lambda e: e.activation(out=self.junk, in_=xt, func=AF.Square, accum_out=ss),
                     r=[kx], w=["nm_junk", "nm_ss%d" % i])
                P.op("act", lambda e: e.activation(out=rs, in_=ss, func=AF.Sqrt, scale=1.0 / D, bias=EPS),
                     r=["nm_ss%d" % i], w=["nm_rs%d" % i])
                P.op("dve", lambda e: e.reciprocal(out=rs, in_=rs), r=["nm_rs%d" % i], w=["nm_rs%d" % i])
                P.op("dve", lambda e: e.scalar_tensor_tensor(out=tmp, in0=xt, scalar=rs, in1=G, op0=ALU.mult,
                                                             op1=ALU.mult), r=[kx, "nm_rs%d" % i, "nmG"], w=["nm_tmp%d" % i])
                P.op("pool", lambda e: e.tensor_tensor(out=hb, in0=tmp, in1=SH, op=ALU.add),
                     r=["nm_tmp%d" % i, "nmG"], w=[kh])
                prev = self.pending
                self.pending = (hb, kh, dst, dkey, bank)
                if prev is not None:
                    self.stage_b(*prev)

            def stage_b(self, hb, kh, dst, dkey, bank):
                pb = psb[bank]

                def tr(e):
                    for k in range(8):
                        ins = e.transpose(out=pb[:, k * 128:(k + 1) * 128], in_=hb[:, k * 128:(k + 1) * 128],
                                          identity=identb)
                    return ins
                P.op("pe", tr, r=[kh], w=["ps%d" % bank])
                P.op("act", lambda e: e.copy(out=dst, in_=pb.rearrange("p (k n) -> p k n", k=8)),
                     r=["ps%d" % bank], w=[dkey])

            def flush(self):
                if self.pending is not None:
                    self.stage_b(*self.pending)
                    self.pending = None

        class RES:
            def __init__(self, l, gidx):
                self.GT = [T([128, D]) for _ in range(2)]
                for j in range(2):
                    dma("sp", self.GT[j], MODV[l, j, gidx].partition_broadcast(128), r=["MODV"], w=["resG"])
                self.xo = [T([128, D]) for _ in range(2)]
                self.t = T([128, D])
                self.junk = T([128, 512], BF16)
                self.ss = T([128, 4])
                self.rs = T([128, 2])
                self.i = 0

            def run(self, b0, b1, src, dstd, is_ctx, wkey):
                i = self.i % 2
                self.i += 1
                xo = self.xo[i]
                kx = "res_x%d" % i
                ss = self.ss[:, 2 * i:2 * i + 2]
                rs = self.rs[:, i:i + 1]
                GT = self.GT[1 if is_ctx else 0]
                dma("pool", xo, src, r=["XR"], w=[kx])
                P.op("act", lambda e: e.activation(out=self.junk, in_=ps[b0], func=AF.Square, accum_out=ss[:, 0:1]),
                     r=["ps%d" % b0], w=["res_junk", "res_ss%d" % i])
                P.op("act", lambda e: e.activation(out=self.junk, in_=ps[b1], func=AF.Square, accum_out=ss[:, 1:2]),
                     r=["ps%d" % b1], w=["res_junk", "res_ss%d" % i])
                P.op("dve", lambda e: e.tensor_tensor(out=rs, in0=ss[:, 0:1], in1=ss[:, 1:2], op=ALU.add),
                     r=["res_ss%d" % i], w=["res_rs%d" % i])
                P.op("act", lambda e: e.activation(out=rs, in_=rs, func=AF.Sqrt, scale=1.0 / D, bias=EPS),
                     r=["res_rs%d" % i], w=["res_rs%d" % i])
                P.op("dve", lambda e: e.reciprocal(out=rs, in_=rs), r=["res_rs%d" % i], w=["res_rs%d" % i])
                for hf, bk in ((0, b0), (1, b1)):
                    P.op("dve", (lambda e, hf=hf, bk=bk: e.scalar_tensor_tensor(
                        out=self.t[:, hf * 512:(hf + 1) * 512], in0=ps[bk], scalar=rs,
                        in1=GT[:, hf * 512:(hf + 1) * 512], op0=ALU.mult, op1=ALU.mult)),
                        r=["ps%d" % bk, "res_rs%d" % i, "resG"], w=["res_t", "ps%d" % bk])
                P.op("pool", lambda e: e.tensor_tensor(out=xo, in0=self.t, in1=xo, op=ALU.add),
                     r=["res_t", kx], w=[kx])
                dma("pool", dstd, xo, r=[kx], w=[wkey])

        def rope_evict(pbank, n, t0, dst, dkey, ro, is_ctx):
            src = ps[pbank][:, 0:n]
            if is_ctx:
                P.op("act", lambda e: e.copy(out=dst, in_=src), r=["ps%d" % pbank], w=[dkey])
                return
            qs, t1, t2, cosb, sinb, rb = ro
            P.op("act", lambda e: e.copy(out=qs[:, 0:n], in_=src), r=["ps%d" % pbank], w=["ro_qs"])
            mmg(ps[rb][:, 0:n], [(rotb, qs[:, 0:n])], r=["ro_qs"], w=["ps%d" % rb])
            P.op("dve", lambda e: e.tensor_tensor(out=t1[:, 0:n], in0=src, in1=cosb[:, 0:n], op=ALU.mult),
                 r=["ps%d" % pbank, "ro_cs"], w=["ro_t1", "ps%d" % pbank])
            P.op("dve", lambda e: e.tensor_tensor(out=t2[:, 0:n], in0=ps[rb][:, 0:n], in1=sinb[:, 0:n], op=ALU.mult),
                 r=["ps%d" % rb, "ro_cs"], w=["ro_t2", "ps%d" % rb])
            P.op("pool", lambda e: e.tensor_tensor(out=dst, in0=t1[:, 0:n], in1=t2[:, 0:n], op=ALU.add),
                 r=["ro_t1", "ro_t2"], w=[dkey])

        def rope_tiles():
            return (T([128, 512], BF16), T([128, 512]), T([128, 512]), T([128, 512]), T([128, 512]))

        def rope_load(ro, b):
            if b == 0:
                return
            c0 = (b - 1) * 512
            dma("pool", ro[3], din["cos_t"][:, c0:c0 + 512], w=["ro_cs"])
            dma("pool", ro[4], din["sin_t"][:, c0:c0 + 512], w=["ro_cs"])

        def attention(groups, scale, accsets):
            PT = [T([128, 2, 512], BF16) for _ in range(3)]
            N = len(groups)

            def emitS(i):
                g = groups[i]
                if g.get("pre"):
                    g["pre"]()
                A = 2 * (i % 2)
                n = g["n"]
                for mi, m in enumerate(g["members"]):
                    mmg(ps[A + mi][:, 0:n], m["qk"], r=g["rk"], w=["ps%d" % (A + mi)])

            def emitE(i):
                g = groups[i]
                A = 2 * (i % 2)
                n = g["n"]
                nm_ = len(g["members"])
                pt = PT[i % 3]
                src = psall[:, A * 512:(A + 2) * 512].rearrange("p (j n) -> p j n", j=2)[:, 0:nm_, 0:n]
                P.op("act", (lambda e: e.activation(out=pt[:, 0:nm_, 0:n], in_=src, func=AF.Exp, scale=scale)),
                     r=["ps%d" % A, "ps%d" % (A + 1)], w=["PT%d" % (i % 3), "ps%d" % A, "ps%d" % (A + 1)])

            def emitPV(i):
                g = groups[i]
                n = g["n"]
                pt = PT[i % 3]
                acc = accsets[g["accset"]]
                wk = []
                for m in g["members"]:
                    ob, sb2 = acc[m["acc"]]
                    wk += ["ps%d" % ob, "ps%d" % sb2]

                def pv(e):
                    for mi, m in enumerate(g["members"]):
                        ob, sb2 = acc[m["acc"]]
                        e.matmul(ps[ob][:, 0:n], lhsT=m["v"], rhs=pt[:, mi, 0:n], start=m["start"], stop=m["stop"])
                        ins = e.matmul(ps[sb2][:, 0:n], lhsT=onesb, rhs=pt[:, mi, 0:n], start=m["start"], stop=m["stop"])
                    return ins
                P.op("pe", pv, r=["PT%d" % (i % 3)] + g["rv"], w=list(dict.fromkeys(wk)))
                if g.get("post"):
                    A = 2 * (i % 2)
                    g["post"](acc, (A, A + 1))

            if N:
                emitS(0)
            for i in range(N):
                emitE(i)
                if i + 1 < N:
                    emitS(i + 1)
                emitPV(i)

        def phase_ffn(l, need_ctx, final):
            W2 = 4355
            Wd = T([128, NJ, D], BF16)
            CW = T([128, 4, NJ])
            for i in range(3):
                dma_nc("sp", CW[:, i, :], din["ffn_conv_w"][l, i].rearrange("(j p) -> p j", p=128), w=["CW"])
            dma_nc("sp", CW[:, 3, :], din["ffn_conv_b"][l].rearrange("(j p) -> p j", p=128), w=["CW"])
            wdsrc = din["ffn_w_down"][l].rearrange("(j p) n -> p j n", p=128)
            for j0 in range(0, NJ, 4):
                j1 = min(NJ, j0 + 4)
                wload(Wd[:, j0:j1, :], wdsrc[:, j0:j1, :])
            top_save = top[0]
            nm = NM(l, 3, 4)
            hTs = [T([128, 8, 512], BF16) for _ in range(2)]
            zt = T([128, 8, 1], BF16)
            P.op("pool", lambda e: e.memset(zt, 0.0), w=["zt"])
            for c in (0, 257, 4354):
                dma_nc("pool", H2D[:, :, c:c + 1], zt, r=["zt"], w=["H2D"])
            blks = list(range(0 if need_ctx else 1, 9))
            for b in blks:
                t0, ntl = BLOCKS[b]
                n = ntl * 128
                hT = hTs[b % 2]
                kh = "hTs%d" % (b % 2)
                for ti in range(ntl):
                    tt = t0 + ti
                    nm.run(x_src(False, tt), tt < 2, hT[:, :, ti * 128:(ti + 1) * 128], kh, 6 + tt % 2)
                nm.flush()
                c0 = 1 if b == 0 else 258 + (b - 1) * 512
                dma("pool", H2D[:, :, c0:c0 + n], hT[:, :, 0:n], r=[kh], w=["H2D"])
            P.barrier()
            top[0] = top_save
            res = RES(l, 5)
            GTt = T([128, NJ, 1024], BF16)
            H2P = [T([128, 8, 1026], BF16) for _ in range(2)]
            wgf = [T([128, 8, 128], BF16) for _ in range(2)]
            wuf = [T([128, 8, 128], BF16) for _ in range(2)]
            acc = [T([128, 512]) for _ in range(2)]
            sil = [T([128, 512]) for _ in range(2)]
            parts = [blks[i:i + 2] for i in range(0, len(blks), 2)]
            wgsrc = din["ffn_w_gate"][l].rearrange("(k p) n -> p k n", p=128)
            wusrc = din["ffn_w_up"][l].rearrange("(k p) n -> p k n", p=128)
            it = [0]
            bc0 = lambda b: 1 if b == 0 else 258 + (b - 1) * 512
            for pi, part in enumerate(parts):
                cstart = bc0(part[0]) - 1
                cend = bc0(part[-1]) + BLOCKS[part[-1]][1] * 128 + 1
                npc = cend - cstart
                H2T = H2P[pi % 2]
                kH = "H2P%d" % (pi % 2)
                dma("sp", H2T[:, :, 0:npc], H2D[:, :, cstart:cend], r=["H2D"], w=[kH])
                goff = {}
                o = 0
                for b in part:
                    goff[b] = o
                    o += BLOCKS[b][1] * 128
                for j in range(NJ):
                    wg, wu = wgf[j % 2], wuf[j % 2]
                    kw = "ffw%d" % (j % 2)
                    for (dst, srcw) in ((wg, wgsrc), (wu, wusrc)):
                        i = stgi[0] % 2
                        stgi[0] += 1
                        s = stg[i][:, 0:1024].rearrange("p (a b) -> p a b", a=8)
                        dma("sp", s, srcw[:, :, j * 128:(j + 1) * 128], w=["stg%d" % i])
                        P.op("pool", (lambda e, dst=dst, s=s: e.tensor_copy(out=dst, in_=s)), r=["stg%d" % i], w=[kw])
                    for b in part:
                        t0, ntl = BLOCKS[b]
                        n = ntl * 128
                        c0 = bc0(b) - cstart
                        q = it[0] % 2
                        it[0] += 1
                        pa, pu, ph = ps[q], ps[2 + q], ps[4 + q]
                        ka, ku, kh = "ps%d" % q, "ps%d" % (2 + q), "ps%d" % (4 + q)
                        mmg(pa[:, 0:n], [(wg[:, k, :], H2T[:, k, c0:c0 + n]) for k in range(8)], r=[kw, kH], w=[ka])
                        hal = H2T[:, :, c0 - 1:c0 + n + 1:n + 1]
                        mmg(ph[:, 0:2], [(wg[:, k, :], hal[:, k, :]) for k in range(8)], r=[kw, kH], w=[kh])
                        mmg(pu[:, 0:n], [(wu[:, k, :], H2T[:, k, c0:c0 + n]) for k in range(8)], r=[kw, kH], w=[ku])
                        ac, sl_ = acc[q], sil[q]
                        kac, ksl = "acc%d" % q, "sil%d" % q
                        w0, w1, w2, bb = (CW[:, i, j:j + 1] for i in range(4))
                        P.op("dve", (lambda e, ac=ac, pa=pa, w1=w1, bb=bb, n=n: e.tensor_scalar(
                            out=ac[:, 0:n], in0=pa[:, 0:n], scalar1=w1, scalar2=bb, op0=ALU.mult, op1=ALU.add)),
                            r=[ka, "CW"], w=[kac])
                        P.op("dve", (lambda e, ac=ac, pa=pa, w0=w0, n=n: e.scalar_tensor_tensor(
                            out=ac[:, 1:n], in0=pa[:, 0:n - 1], scalar=w0, in1=ac[:, 1:n], op0=ALU.mult, op1=ALU.add)),
                            r=[ka, kac], w=[kac])
                        P.op("dve", (lambda e, ac=ac, pa=pa, w2=w2, n=n: e.scalar_tensor_tensor(
                            out=ac[:, 0:n - 1], in0=pa[:, 1:n], scalar=w2, in1=ac[:, 0:n - 1], op0=ALU.mult, op1=ALU.add)),
                            r=[ka, kac], w=[kac, ka])
                        P.op("dve", (lambda e, ac=ac, ph=ph, w0=w0: e.scalar_tensor_tensor(
                            out=ac[:, 0:1], in0=ph[:, 0:1], scalar=w0, in1=ac[:, 0:1], op0=ALU.mult, op1=ALU.add)),
                            r=[kh, kac], w=[kac])
                        P.op("dve", (lambda e, ac=ac, ph=ph, w2=w2, n=n: e.scalar_tensor_tensor(
                            out=ac[:, n - 1:n], in0=ph[:, 1:2], scalar=w2, in1=ac[:, n - 1:n], op0=ALU.mult, op1=ALU.add)),
                            r=[kh, kac], w=[kac, kh])
                        P.op("act", (lambda e, ac=ac, sl_=sl_, n=n: e.activation(out=sl_[:, 0:n], in_=ac[:, 0:n], func=AF.Silu)),
                             r=[kac], w=[ksl])
                        g0 = goff[b]
                        P.op("dve", (lambda e, sl_=sl_, pu=pu, j=j, g0=g0, n=n: e.tensor_tensor(
                            out=GTt[:, j, g0:g0 + n], in0=sl_[:, 0:n], in1=pu[:, 0:n], op=ALU.mult)),
                            r=[ksl, ku], w=["GT", ku])
                for b in part:
                    t0, ntl = BLOCKS[b]
                    for ti in range(ntl):
                        tt = t0 + ti
                        g0 = goff[b] + ti * 128
                        for hf in range(2):
                            mmg(ps[6 + hf], [(GTt[:, j, g0:g0 + 128], Wd[:, j, hf * 512:(hf + 1) * 512]) for j in range(NJ)],
                                r=["GT", "wgt"], w=["ps%d" % (6 + hf)])
                        if final:
                            dstd = out[(tt - 2) * 128:(tt - 1) * 128, :]
                            res.run(6, 7, x_src(False, tt), dstd, tt < 2, "OUT")
                        else:
                            res.run(6, 7, x_src(False, tt), XR[tt * 128:(tt + 1) * 128, :], tt < 2, "XR2")

        def phase_l0():
            l = 0
            LAM_INIT = 0.2
            KT = T([128, 4, NTOK], BF16)
            Vv = T([128, NT, 512], BF16)
            keep = top[0]
            w_in = T([128, 8, 2048], BF16)
            wsrc = din["ev_w_in"][0].rearrange("(k p) n -> p k n", p=128)
            for c in range(4):
                wload(w_in[:, :, c * 512:(c + 1) * 512], wsrc[:, :, c * 512:(c + 1) * 512])
            nm = NM(l, 0, 1)
            hT = T([128, 8, 512], BF16)
            ro = rope_tiles() + (7,)
            ub = [T([128, 512]) for _ in range(2)]
            qb = [T([128, 4, 512], BF16) for _ in range(2)]
            for b, (t0, ntl) in enumerate(BLOCKS):
                n = ntl * 128
                col0 = t0 * 128
                rope_load(ro, b)
                for ti in range(ntl):
                    tt = t0 + ti
                    nm.run(x_src(True, tt), tt < 2, hT[:, :, ti * 128:(ti + 1) * 128], "hT", 6)
                nm.flush()
                for ti in range(ntl):
                    tt = t0 + ti
                    bk = 4 + (ti % 2)
                    mmg(ps[bk], [(hT[:, k, ti * 128:(ti + 1) * 128], w_in[:, k, 1536:2048]) for k in range(8)],
                        r=["hT", "wgt"], w=["ps%d" % bk])
                    P.op("act", (lambda e, tt=tt, bk=bk: e.copy(out=Vv[:, tt, :], in_=ps[bk])), r=["ps%d" % bk], w=["Vv"])
                for c in range(4):
                    bk = c % 2
                    mmg(ps[bk][:, 0:n], [(w_in[:, k, c * 128:(c + 1) * 128], hT[:, k, 0:n]) for k in range(8)],
                        r=["hT", "wgt"], w=["ps%d" % bk])
                    u = ub[c % 2]
                    P.op("act", (lambda e, u=u, bk=bk, n=n: e.copy(out=u[:, 0:n], in_=ps[bk][:, 0:n])),
                         r=["ps%d" % bk], w=["ub%d" % (c % 2)])
                    dma("pool", UT[c][:, col0:col0 + n], u[:, 0:n], r=["ub%d" % (c % 2)], w=["UT"])
                qbb = qb[b % 2]
                kq = "qb%d" % (b % 2)
                for c in range(4):
                    bk = 2 + (c % 2)
                    mmg(ps[bk][:, 0:n], [(w_in[:, k, 512 + c * 128:512 + (c + 1) * 128], hT[:, k, 0:n]) for k in range(8)],
                        r=["hT", "wgt"], w=["ps%d" % bk])
                    rope_evict(bk, n, col0, qbb[:, c, 0:n], kq, ro, b == 0)
                dma("pool", QT[b][:, :, 0:n], qbb[:, :, 0:n], r=[kq], w=["QT"])
                for c in range(4):
                    bk = 2 + (c % 2)
                    mmg(ps[bk][:, 0:n], [(w_in[:, k, 1024 + c * 128:1024 + (c + 1) * 128], hT[:, k, 0:n]) for k in range(8)],
                        r=["hT", "wgt"], w=["ps%d" % bk])
                    rope_evict(bk, n, col0, KT[:, c, col0:col0 + n], "KT", ro, b == 0)
            P.barrier()
            top[0] = keep
            if stop == "l0a1":
                return
            pw = T([128, 4, 128], BF16)
            for g in range(4):
                wload(pw[:, g, :], din["pool_w"][0, g])
            psc = T([128, 4])
            dma_nc("sp", psc, din["pool_scale"][0].rearrange("(g p) -> p g", p=128), w=["psc"])
            UP = T([128, PW])
            Aa = T([128, PW])
            Ab = T([128, PW])
            IC = T([128, PW])
            dT = T([128, PW], BF16)
            mpo = [T([128, 512], BF16) for _ in range(2)]
            for g in range(4):
                hw = (1, 2, 4, 8)[g]
                P.op("pool", lambda e: e.memset(UP, 0.0), w=["UP"])
                dma("sp", UP[:, PC0:PC0 + 256], UT[g][:, 0:256], r=["UT"], w=["UP"])
                dma("sp", UP[:, PL0:PL0 + 4096], UT[g][:, 256:NTOK], r=["UT"], w=["UP"])
                dma("pool", IC, din["invcnt"][g].partition_broadcast(128), w=["IC"])
                cur, ck = UP, "UP"
                bufs = [(Aa, "Aa"), (Ab, "Ab")]
                width = PW
                for s in range(g + 1):
                    sh = 1 << s
                    nxt, nk = bufs[s % 2]
                    width -= sh
                    P.op("dve", (lambda e, cur=cur, nxt=nxt, sh=sh, width=width: e.tensor_tensor(
                        out=nxt[:, 0:width], in0=cur[:, 0:width], in1=cur[:, sh:sh + width], op=ALU.add)),
                        r=[ck], w=[nk])
                    cur, ck = nxt, nk
                oth, ok = bufs[(g + 1) % 2]
                P.op("dve", (lambda e, cur=cur, oth=oth, hw=hw: e.tensor_tensor(
                    out=oth[:, 8:PW - 8], in0=cur[:, 8 - hw:PW - 8 - hw], in1=IC[:, 8:PW - 8], op=ALU.mult)),
                    r=[ck, "IC"], w=[ok])
                P.op("pool", (lambda e, oth=oth: e.tensor_tensor(out=dT[:, 8:PW - 8], in0=oth[:, 8:PW - 8],
                                                                 in1=UP[:, 8:PW - 8], op=ALU.subtract)),
                     r=[ok, "UP"], w=["dT"])
                for b, (t0, ntl) in enumerate(BLOCKS):
                    n = ntl * 128
                    col0 = t0 * 128
                    pc = PC0 if b == 0 else PL0 + (b - 1) * 512
                    bk = b % 2
                    mmg(ps[bk][:, 0:n], [(pw[:, g, :], dT[:, pc:pc + n])], r=["dT", "wgt"], w=["ps%d" % bk])
                    mp = mpo[b % 2]
                    P.op("act", (lambda e, bk=bk, n=n, g=g, mp=mp: e.activation(
                        out=mp[:, 0:n], in_=ps[bk][:, 0:n], func=AF.Identity, scale=psc[:, g:g + 1])),
                        r=["ps%d" % bk, "psc"], w=["mpo%d" % (b % 2)])
                    dma("pool", MIXD[g][:, col0:col0 + n], mp[:, 0:n], r=["mpo%d" % (b % 2)], w=["MIXD"])
            P.barrier()
            top[0] = keep
            if stop == "l0a2":
                return
            wo = T([128, 8, D], BF16)
            wosrc = din["mix_w_out"][l].rearrange("(k p) n -> p k n", p=128)
            for c in range(2):
                wload(wo[:, :, c * 512:(c + 1) * 512], wosrc[:, :, c * 512:(c + 1) * 512])
            lamt = T([128, 4, 64])
            lj = T([128, 64])
            lsum = T([128, 2])
            nlam = T([128, 1])
            dma("sp", lamt.rearrange("p a b -> p (a b)"),
                din["diff_lambda"][0].rearrange("a b -> (a b)").partition_broadcast(128), w=["lamt"])
            for i in range(2):
                P.op("dve", (lambda e, i=i: e.scalar_tensor_tensor(out=lj, in0=lamt[:, 2 * i, :], scalar=1.0,
                                                                   in1=lamt[:, 2 * i + 1, :], op0=ALU.mult, op1=ALU.mult,
                                                                   accum_out=lsum[:, i:i + 1])),
                     r=["lamt"], w=["lj", "lsum"])
            P.op("act", lambda e: e.activation(out=lsum, in_=lsum, func=AF.Exp), r=["lsum"], w=["lsum"])
            P.op("dve", lambda e: e.tensor_tensor(out=nlam, in0=lsum[:, 1:2], in1=lsum[:, 0:1], op=ALU.subtract),
                 r=["lsum"], w=["nlam"])
            P.op("dve", lambda e: e.tensor_scalar(out=nlam, in0=nlam, scalar1=-LAM_INIT, scalar2=None, op0=ALU.add),
                 r=["nlam"], w=["nlam"])
            sln = T([128, 1])
            dma_nc("sp", sln, din["diff_subln"][0].rearrange("(p o) -> p o", o=1), w=["sln"])
            P.op("dve", lambda e: e.tensor_scalar(out=sln, in0=sln, scalar1=1.0 - LAM_INIT, scalar2=None, op0=ALU.mult),
                 r=["sln"], w=["sln"])
            res = RES(l, 2)
            qtl = [T([128, 4, 512], BF16) for _ in range(3)]
            MIXA = [T([128, 4, 512], BF16) for _ in range(2)]
            rsum = T([128, 512])
            o2 = [T([128, 512]) for _ in range(2)]
            oc = T([128, 512])
            sq = T([128, 512], BF16)
            rstd = T([128, 512])
            state = {}

            mixp = [T([128, 4, 512], BF16) for _ in range(3)]

            def pre(b):
                if state.get("b") != b:
                    state["b"] = b
                    n = BLOCKS[b][1] * 128
                    col0 = BLOCKS[b][0] * 128
                    dma("sp", qtl[b % 3][:, :, 0:n], QT[b][:, :, 0:n], r=["QT"], w=["qtl%d" % (b % 3)])
                    for g in range(4):
                        dma("sp", mixp[b % 3][:, g, 0:n], MIXD[g][:, col0:col0 + n], r=["MIXD"], w=["mixp%d" % (b % 3)])

            def post(b, h, n, acc, scr):
                mixa = MIXA[b % 2]
                km = "MIXA%d" % (b % 2)
                for j in range(2):
                    ob, sb2 = acc[j]
                    P.op("dve", (lambda e, sb2=sb2: e.reciprocal(out=rsum[:, 0:n], in_=ps[sb2][:, 0:n])),
                         r=["ps%d" % sb2], w=["rsum", "ps%d" % sb2])
                    P.op("dve", (lambda e, ob=ob, j=j: e.tensor_tensor(out=o2[j][:, 0:n], in0=ps[ob][:, 0:n], in1=rsum[:, 0:n], op=ALU.mult)),
                         r=["ps%d" % ob, "rsum"], w=["o2_%d" % j, "ps%d" % ob])
                P.op("dve", lambda e: e.scalar_tensor_tensor(out=oc[:, 0:n], in0=o2[1][:, 0:n], scalar=nlam,
                                                             in1=o2[0][:, 0:n], op0=ALU.mult, op1=ALU.add),
                     r=["o2_0", "o2_1", "nlam"], w=["oc"])
                P.op("act", lambda e: e.activation(out=sq[:, 0:n], in_=oc[:, 0:n], func=AF.Square), r=["oc"], w=["sq"])
                sbk = scr[0]
                mmg(ps[sbk][:, 0:n], [(onesb, sq[:, 0:n])], r=["sq"], w=["ps%d" % sbk])
                P.op("act", lambda e: e.activation(out=rstd[:, 0:n], in_=ps[sbk][:, 0:n], func=AF.Sqrt, scale=1.0 / 128,
                                                   bias=EPS), r=["ps%d" % sbk], w=["rstd", "ps%d" % sbk])
                P.op("dve", lambda e: e.reciprocal(out=rstd[:, 0:n], in_=rstd[:, 0:n]), r=["rstd"], w=["rstd"])
                P.op("dve", lambda e: e.scalar_tensor_tensor(out=mixa[:, h, 0:n], in0=oc[:, 0:n], scalar=sln,
                                                             in1=rstd[:, 0:n], op0=ALU.mult, op1=ALU.mult),
                     r=["oc", "rstd", "sln"], w=[km])
                if dbg:
                    col0 = BLOCKS[b][0] * 128
                    dma("sp", MIXD[4 + h][:, col0:col0 + n], mixa[:, h, 0:n], r=[km], w=["MIXD"])
                if h == 3:
                    t0, ntl = BLOCKS[b]
                    y0, y1 = scr
                    for ti in range(ntl):
                        tt = t0 + ti
                        ls = slice(ti * 128, (ti + 1) * 128)
                        for hf, yb in ((0, y0), (1, y1)):
                            prs = [(mixp[b % 3][:, g, ls], wo[:, g, hf * 512:(hf + 1) * 512]) for g in range(4)] + \
                                  [(mixa[:, hh, ls], wo[:, 4 + hh, hf * 512:(hf + 1) * 512]) for hh in range(4)]
                            mmg(ps[yb], prs, r=["mixp%d" % (b % 3), km, "wgt"], w=["ps%d" % yb])
                        res.run(y0, y1, x_src(True, tt), XR[tt * 128:(tt + 1) * 128, :], tt < 2, "XR1")

            groups = []
            for b in range(9):
                n = BLOCKS[b][1] * 128
                kts = list(range(2)) if b == 0 else list(range(NT))
                for h in range(4):
                    for ki, kt in enumerate(kts):
                        mem = []
                        for j in range(2):
                            pr = slice(64 * j, 64 * j + 64)
                            mem.append(dict(qk=[(KT[pr, h, kt * 128:(kt + 1) * 128], qtl[b % 3][pr, h, 0:n])],
                                            v=Vv[:, kt, h * 128:(h + 1) * 128], acc=j, start=(ki == 0), stop=(ki == len(kts) - 1)))
                        g = dict(n=n, members=mem, rk=["KT", "qtl%d" % (b % 3)], rv=["Vv"], accset=0)
                        if h == 0 and ki == 0:
                            g["pre"] = (lambda b=b: (pre(b), pre(b + 1) if b + 1 < 9 else None))
                        if ki == len(kts) - 1:
                            g["post"] = (lambda acc, scr, b=b, h=h, n=n: post(b, h, n, acc, scr))
                        groups.append(g)
            attention(groups, 0.125, [[(4, 5), (6, 7)]])

        def rownorm(pt, W, NB, dst, dkey, tg):
            junk, ss, rs = tg
            P.op("act", lambda e: e.activation(out=junk[:, 0:W], in_=pt[0][:, 0:W], func=AF.Square, accum_out=ss),
                 r=[pt[1]], w=["rn_junk", "rn_ss"])
            P.op("act", lambda e: e.activation(out=rs, in_=ss, func=AF.Sqrt, scale=1.0 / W, bias=EPS), r=["rn_ss"], w=["rn_rs"])
            P.op("dve", lambda e: e.reciprocal(out=rs, in_=rs), r=["rn_rs"], w=["rn_rs"])
            P.op("dve", lambda e: e.scalar_tensor_tensor(out=dst, in0=pt[0][:, 0:W], scalar=rs, in1=NB[:, 0:W], op0=ALU.mult,
                                                         op1=ALU.mult), r=[pt[1], "rn_rs", "wgt"], w=[dkey, pt[1]])

        def phase_l1():
            l = 1
            KN = T([128, 4, NTOK], BF16)
            VM = T([128, NT, 512], BF16)
            KR2 = T([128, NTOK], BF16)
            keep = top[0]
            WA = T([128, 8, 512], BF16)
            WB = T([128, 8, 256], BF16)
            WKR = T([128, 8, 128], BF16)
            WQN = T([128, 4, 4, 128], BF16)
            WQR = T([128, 4, 256], BF16)
            WKN = T([128, 2, 4, 128], BF16)
            WVV = T([128, 2, 512], BF16)
            wsrc = din["od_w_in"][0].rearrange("(k p) n -> p k n", p=128)
            wload(WA, wsrc[:, :, 0:512])
            wload(WB, wsrc[:, :, 512:768])
            wload(WKR[:, :, 0:64], wsrc[:, :, 768:832])
            wload(WKR[:, :, 64:128], wsrc[:, :, 768:832])
            uq = din["mla_w_uq"][0].rearrange("(k p) (h c) -> p k h c", p=128, c=192)
            ukv = din["mla_w_ukv"][0].rearrange("(k p) (h c) -> p k h c", p=128, c=256)
            for k in range(4):
                wload(WQN[:, k], uq[:, k, :, 0:128])
                wload(WQR[:, k, :].rearrange("p (h c) -> p h c", h=4), uq[:, k, :, 128:192])
            for k in range(2):
                wload(WKN[:, k], ukv[:, k, :, 0:128])
                wload(WVV[:, k, :].rearrange("p (h c) -> p h c", h=4), ukv[:, k, :, 128:256])
            QNb = T([128, 512])
            KVNb = T([128, 256])
            dma("sp", QNb, din["mla_q_norm"][0].partition_broadcast(128), w=["wgt"])
            dma("sp", KVNb, din["mla_kv_norm"][0].partition_broadcast(128), w=["wgt"])
            nm = NM(l, 0, 1)
            hTs = [T([128, 8, 512], BF16)] * 2
            ro = rope_tiles() + (7,)
            tg = (T([128, 512], BF16), T([128, 1]), T([128, 1]))
            cqn = T([128, 512], BF16)
            ckvn = T([128, 256], BF16)
            cqT = T([128, 4, 512], BF16)
            ckvT = T([128, 2, 512], BF16)
            qnb = [T([128, 4, 512], BF16)] * 2
            qrb = [T([128, 2, 512], BF16)] * 2
            for b, (t0, ntl) in enumerate(BLOCKS):
                n = ntl * 128
                col0 = t0 * 128
                hT = hTs[b % 2]
                khT = "hTs0"
                rope_load(ro, b)
                for ti in range(ntl):
                    tt = t0 + ti
                    nm.run(x_src(False, tt), tt < 2, hT[:, :, ti * 128:(ti + 1) * 128], khT, 6)
                nm.flush()
                dma("pool", HT1[b][:, :, 0:n], hT[:, :, 0:n], r=[khT], w=["HT1"])
                for ti in range(ntl):
                    ts_ = slice(ti * 128, (ti + 1) * 128)
                    mmg(ps[0], [(hT[:, k, ts_], WA[:, k, :]) for k in range(8)], r=[khT, "wgt"], w=["ps0"])
                    rownorm((ps[0], "ps0"), 512, QNb, cqn, "cqn", tg)

                    def trq(e):
                        for k in range(4):
                            ins = e.transpose(out=psb[2][:, k * 128:(k + 1) * 128], in_=cqn[:, k * 128:(k + 1) * 128], identity=identb)
                        return ins
                    P.op("pe", trq, r=["cqn"], w=["ps2"])
                    P.op("act", (lambda e, ts_=ts_: e.copy(out=cqT[:, :, ts_], in_=psb[2][:, 0:512].rearrange("p (k n) -> p k n", k=4))),
                         r=["ps2"], w=["cqT"])
                    mmg(ps[1][:, 0:256], [(hT[:, k, ts_], WB[:, k, :]) for k in range(8)], r=[khT, "wgt"], w=["ps1"])
                    rownorm((ps[1], "ps1"), 256, KVNb, ckvn, "ckvn", tg)

                    def trk(e):
                        for k in range(2):
                            ins = e.transpose(out=psb[3][:, k * 128:(k + 1) * 128], in_=ckvn[:, k * 128:(k + 1) * 128], identity=identb)
                        return ins
                    P.op("pe", trk, r=["ckvn"], w=["ps3"])
                    P.op("act", (lambda e, ts_=ts_: e.copy(out=ckvT[:, :, ts_], in_=psb[3][:, 0:256].rearrange("p (k n) -> p k n", k=2))),
                         r=["ps3"], w=["ckvT"])
                for ti in range(ntl):
                    tt = t0 + ti
                    ts_ = slice(ti * 128, (ti + 1) * 128)
                    mmg(ps[0], [(ckvT[:, k, ts_], WVV[:, k, :]) for k in range(2)], r=["ckvT", "wgt"], w=["ps0"])
                    P.op("act", (lambda e, tt=tt: e.copy(out=VM[:, tt, :], in_=ps[0])), r=["ps0"], w=["VM", "ps0"])
                for h in range(4):
                    bk = 4 + (h % 2)
                    mmg(ps[bk][:, 0:n], [(WKN[:, k, h, :], ckvT[:, k, 0:n]) for k in range(2)], r=["ckvT", "wgt"], w=["ps%d" % bk])
                    P.op("act", (lambda e, h=h, bk=bk, n=n, col0=col0: e.copy(out=KN[:, h, col0:col0 + n], in_=ps[bk][:, 0:n])),
                         r=["ps%d" % bk], w=["KN", "ps%d" % bk])
                mmg(ps[4][:, 0:n], [(WKR[:, k, :], hT[:, k, 0:n]) for k in range(8)], r=[khT, "wgt"], w=["ps4"])
                rope_evict(4, n, col0, KR2[:, col0:col0 + n], "KR2", ro, b == 0)
                if b >= 1:
                    qn, qr = qnb[b % 2], qrb[b % 2]
                    for h in range(4):
                        bk = 4 + (h % 2)
                        mmg(ps[bk][:, 0:n], [(WQN[:, k, h, :], cqT[:, k, 0:n]) for k in range(4)], r=["cqT", "wgt"], w=["ps%d" % bk])
                        P.op("act", (lambda e, h=h, bk=bk, n=n, qn=qn: e.copy(out=qn[:, h, 0:n], in_=ps[bk][:, 0:n])),
                             r=["ps%d" % bk], w=["qnb0", "ps%d" % bk])
                    dma("pool", QT[b][:, :, 0:n], qn[:, :, 0:n], r=["qnb0"], w=["QT"])
                    for c in range(2):
                        bk = 4 + (c % 2)
                        mmg(ps[bk][:, 0:n], [(WQR[:, k, c * 128:(c + 1) * 128], cqT[:, k, 0:n]) for k in range(4)],
                            r=["cqT", "wgt"], w=["ps%d" % bk])
                        rope_evict(bk, n, col0, qr[:, c, 0:n], "qrb0", ro, False)
                    dma("pool", QR[b][:, :, 0:n], qr[:, :, 0:n], r=["qrb0"], w=["QR"])
            P.barrier()
            top[0] = keep
            if stop == "l1b1":
                return
            qnl = [T([128, 4, 512], BF16) for _ in range(3)]
            qrl = [T([128, 2, 512], BF16) for _ in range(3)]
            rsum = T([128, 512])
            mo = [T([128, 512], BF16) for _ in range(2)]
            state = {}

            def pre(b):
                if state.get("b") != b:
                    state["b"] = b
                    n = BLOCKS[b][1] * 128
                    dma("sp", qnl[b % 3][:, :, 0:n], QT[b][:, :, 0:n], r=["QT"], w=["qnl%d" % (b % 3)])
                    dma("sp", qrl[b % 3][:, :, 0:n], QR[b][:, :, 0:n], r=["QR"], w=["qnl%d" % (b % 3)])

            def post(b, h, n, acc, scr):
                col0 = BLOCKS[b][0] * 128
                m = mo[h % 2]
                km = "mo%d" % (h % 2)
                ob, sb2 = acc[0]
                P.op("dve", lambda e: e.reciprocal(out=rsum[:, 0:n], in_=ps[sb2][:, 0:n]), r=["ps%d" % sb2], w=["rsum", "ps%d" % sb2])
                P.op("dve", lambda e: e.tensor_tensor(out=m[:, 0:n], in0=ps[ob][:, 0:n], in1=rsum[:, 0:n], op=ALU.mult),
                     r=["ps%d" % ob, "rsum"], w=[km, "ps%d" % ob])
                dma("pool", MIXD[h][:, col0:col0 + n], m[:, 0:n], r=[km], w=["MIXD"])

            groups = []
            gi = 0
            for b in range(1, 9):
                n = 512
                for h in range(4):
                    pr = slice(64 * (h % 2), 64 * (h % 2) + 64)
                    for kp in range(NT // 2):
                        mem = []
                        for mi in range(2):
                            kt = 2 * kp + mi
                            ks = slice(kt * 128, (kt + 1) * 128)
                            mem.append(dict(qk=[(KN[:, h, ks], qnl[b % 3][:, h, 0:n]), (KR2[pr, ks], qrl[b % 3][pr, h // 2, 0:n])],
                                            v=VM[:, kt, h * 128:(h + 1) * 128], acc=0,
                                            start=(kp == 0 and mi == 0), stop=(kp == NT // 2 - 1 and mi == 1)))
                        g = dict(n=n, members=mem, rk=["KN", "KR2", "qnl%d" % (b % 3)], rv=["VM"], accset=gi % 2)
                        if h == 0 and kp == 0:
                            g["pre"] = (lambda b=b: (pre(b), pre(b + 1) if b + 1 < 9 else None))
                        if kp == NT // 2 - 1:
                            g["post"] = (lambda acc, scr, b=b, h=h, n=n: post(b, h, n, acc, scr))
                        groups.append(g)
                    gi += 1
            attention(groups, 192.0 ** -0.5, [[(4, 5)], [(6, 7)]])
            phase_reset()
            if stop == "l1b2":
                return
            LB = T([128, 2, 2, 4])
            lbv = T([128, 2, 4])
            oml = T([128, 2, 4])
            hgn = T([128, 1])
            ones1 = T([128, 1])
            for d in range(2):
                for ll in range(2):
                    dma_nc("sp", LB[:, d, ll, :], din["hgrn_lb"][d, ll].rearrange("(h p) -> p h", p=128), w=["LB"])
            dma_nc("sp", hgn, din["hgrn_norm"][0].rearrange("(p o) -> p o", o=1), w=["hgn"])
            P.op("dve", lambda e: e.tensor_tensor(out=lbv, in0=LB[:, :, 1, :], in1=LB[:, :, 0, :], op=ALU.subtract), r=["LB"], w=["lbv"])
            P.op("act", lambda e: e.activation(out=lbv, in_=lbv, func=AF.Sigmoid), r=["lbv"], w=["lbv"])
            P.op("dve", lambda e: e.tensor_scalar(out=oml, in0=lbv, scalar1=-1.0, scalar2=1.0, op0=ALU.mult, op1=ALU.add),
                 r=["lbv"], w=["oml"])
            P.op("pool", lambda e: e.memset(ones1, 1.0), w=["ones1"])
            WH = T([128, 8, 5, 128], BF16)
            hTl = [T([128, 8, 512], BF16)] * 2
            SG = T([128, NTOK], BF16)
            Vt = T([128, NT, 128], BF16)
            QP = [T([128, NTOK], BF16) for _ in range(2)]
            QPP = [T([128, NTOK], BF16) for _ in range(2)]
            KP = [T([128, NTOK], BF16) for _ in range(2)]
            KPt = [T([128, NT, 128], BF16) for _ in range(2)]
            EL = [T([128, 68]) for _ in range(2)]
            OT = [T([128, NTOK]) for _ in range(2)]
            qf = T([128, 512])
            sgm = T([128, 512])
            kk = T([128, 512])
            lf = T([128, 512])
            gb = T([128, 516])
            Da = T([128, 512])
            Db = T([128, 512])
            E1 = T([128, 512])
            E2 = T([128, 512])
            E3 = T([128, 512])
            elt = T([128, 8])
            Sf = [[T([128, 128]) for _ in range(2)] for _ in range(2)]
            Sb = [[T([128, 128], BF16) for _ in range(2)] for _ in range(2)]
            Am = [[T([128, 64], BF16) for _ in range(2)] for _ in range(2)]
            osum, rstd, otmp = Da, Db, E1
            sq = T([128, 512], BF16)
            mh = [T([128, 512], BF16) for _ in range(2)]
            P.op("pool", lambda e: e.memset(gb[:, 0:1], 0.0), w=["gb0"])
            wsrc = din["od_w_in"][0].rearrange("(k p) n -> p k n", p=128)
            for h in range(4):
                for i, c0_ in enumerate((832, 1344, 1856, 2368, 2880)):
                    wload(WH[:, :, i, :], wsrc[:, :, c0_ + h * 128:c0_ + (h + 1) * 128])
                for b, (t0, ntl) in enumerate(BLOCKS):
                    n = ntl * 128
                    nch = n // 64
                    col0 = t0 * 128
                    ch0 = col0 // 64
                    hT = hTl[b % 2]
                    khT = "hTl0"
                    dma("sp", hT[:, :, 0:n], HT1[b][:, :, 0:n], r=["HT1"], w=[khT])
                    mmg(ps[0][:, 0:n], [(WH[:, k, 0, :], hT[:, k, 0:n]) for k in range(8)], r=[khT, "wgt"], w=["ps0"])
                    P.op("act", (lambda e, n=n: e.activation(out=qf[:, 0:n], in_=ps[0][:, 0:n], func=AF.Silu)), r=["ps0"], w=["qf", "ps0"])
                    mmg(ps[1][:, 0:n], [(WH[:, k, 4, :], hT[:, k, 0:n]) for k in range(8)], r=[khT, "wgt"], w=["ps1"])
                    P.op("act", (lambda e, n=n, col0=col0: e.activation(out=SG[:, col0:col0 + n], in_=ps[1][:, 0:n], func=AF.Silu)),
                         r=["ps1"], w=["SG", "ps1"])
                    for ti in range(ntl):
                        tt = t0 + ti
                        ts_ = slice(ti * 128, (ti + 1) * 128)
                        mmg(ps[2][:, 0:128], [(hT[:, k, ts_], WH[:, k, 3, :]) for k in range(8)], r=[khT, "wgt"], w=["ps2"])
                        P.op("act", (lambda e, tt=tt: e.copy(out=Vt[:, tt, :], in_=ps[2][:, 0:128])), r=["ps2"], w=["Vt", "ps2"])
                    for d in range(2):
                        bk = 3 + d
                        kb = "ps%d" % bk
                        mmg(ps[bk][:, 0:n], [(WH[:, k, 1 + d, :], hT[:, k, 0:n]) for k in range(8)], r=[khT, "wgt"], w=[kb])
                        P.op("act", (lambda e, n=n, bk=bk: e.activation(out=sgm[:, 0:n], in_=ps[bk][:, 0:n], func=AF.Sigmoid)),
                             r=[kb], w=["sgm", kb])
                        P.op("dve", (lambda e, n=n, d=d, h=h: e.tensor_scalar(out=sgm[:, 0:n], in0=sgm[:, 0:n], scalar1=oml[:, d, h:h + 1],
                                                                             scalar2=lbv[:, d, h:h + 1], op0=ALU.mult, op1=ALU.add)),
                             r=["sgm", "oml", "lbv"], w=["sgm"])
                        P.op("pool", (lambda e, n=n: e.tensor_scalar(out=kk[:, 0:n], in0=sgm[:, 0:n], scalar1=-1.0, scalar2=1.0,
                                                                    op0=ALU.mult, op1=ALU.add)), r=["sgm"], w=["kk"])
                        P.op("act", (lambda e, n=n: e.activation(out=lf[:, 0:n], in_=sgm[:, 0:n], func=AF.Ln)), r=["sgm"], w=["lf"])
                        P.op("dve", (lambda e, n=n: e.tensor_tensor_scan(out=gb[:, 1:1 + n], data0=ones1[:, 0:1].to_broadcast([128, n]),
                                                                        data1=lf[:, 0:n], initial=0.0, op0=ALU.mult, op1=ALU.add)),
                             r=["lf", "ones1", "gb0"], w=["gb"])
                        Gi3 = gb[:, 1:1 + n].rearrange("p (c j) -> p c j", j=64)
                        Gs3 = gb[:, 0:n].rearrange("p (c j) -> p c j", j=64)
                        S0 = Gs3[:, :, 0:1].to_broadcast([128, nch, 64])
                        I63 = Gi3[:, :, 63:64].to_broadcast([128, nch, 64])
                        Da3 = Da[:, 0:n].rearrange("p (c j) -> p c j", j=64)
                        Db3 = Db[:, 0:n].rearrange("p (c j) -> p c j", j=64)
                        if d == 0:
                            P.op("dve", (lambda e, Gi3=Gi3, S0=S0, Da3=Da3: e.tensor_tensor(out=Da3, in0=Gi3, in1=S0, op=ALU.subtract)),
                                 r=["gb"], w=["Da"])
                            P.op("dve", (lambda e, Gi3=Gi3, I63=I63, Db3=Db3: e.tensor_tensor(out=Db3, in0=Gi3, in1=I63, op=ALU.subtract)),
                                 r=["gb"], w=["Db"])
                            sc1, sc2, sc3 = 1.0, -1.0, 1.0
                        else:
                            P.op("dve", (lambda e, Gs3=Gs3, I63=I63, Da3=Da3: e.tensor_tensor(out=Da3, in0=Gs3, in1=I63, op=ALU.subtract)),
                                 r=["gb"], w=["Da"])
                            P.op("dve", (lambda e, Gs3=Gs3, S0=S0, Db3=Db3: e.tensor_tensor(out=Db3, in0=Gs3, in1=S0, op=ALU.subtract)),
                                 r=["gb"], w=["Db"])
                            sc1, sc2, sc3 = -1.0, 1.0, -1.0
                        P.op("dve", (lambda e, Gi3=Gi3, Gs3=Gs3, nch=nch: e.tensor_tensor(out=elt[:, 0:nch], in0=Gi3[:, :, 63], in1=Gs3[:, :, 0],
                                                                                          op=ALU.subtract)), r=["gb"], w=["elt"])
                        P.op("act", (lambda e, n=n, sc1=sc1: e.activation(out=E1[:, 0:n], in_=Da[:, 0:n], func=AF.Exp, scale=sc1)), r=["Da"], w=["E1"])
                        P.op("act", (lambda e, n=n, sc2=sc2: e.activation(out=E2[:, 0:n], in_=Db[:, 0:n], func=AF.Exp, scale=sc2)), r=["Db"], w=["E2"])
                        P.op("act", (lambda e, n=n, sc3=sc3: e.activation(out=E3[:, 0:n], in_=Db[:, 0:n], func=AF.Exp, scale=sc3)), r=["Db"], w=["E3"])
                        P.op("act", (lambda e, d=d, ch0=ch0, nch=nch: e.activation(out=EL[d][:, ch0:ch0 + nch], in_=elt[:, 0:nch], func=AF.Exp)),
                             r=["elt"], w=["EL%d" % d])
                        P.op("dve", (lambda e, n=n, d=d, col0=col0: e.tensor_tensor(out=QP[d][:, col0:col0 + n], in0=qf[:, 0:n], in1=E1[:, 0:n], op=ALU.mult)),
                             r=["qf", "E1"], w=["QP%d" % d])
                        P.op("dve", (lambda e, n=n, d=d, col0=col0: e.tensor_tensor(out=KP[d][:, col0:col0 + n], in0=kk[:, 0:n], in1=E2[:, 0:n], op=ALU.mult)),
                             r=["kk", "E2"], w=["KP%d" % d])
                        P.op("pool", (lambda e, n=n, d=d, col0=col0: e.tensor_tensor(out=QPP[d][:, col0:col0 + n], in0=qf[:, 0:n], in1=E3[:, 0:n], op=ALU.mult)),
                             r=["qf", "E3"], w=["QPP%d" % d])
                        for ti in range(ntl):
                            tt = t0 + ti

                            def trp(e, d=d, tt=tt):
                                return e.transpose(out=psb[6][:, 0:128], in_=KP[d][:, tt * 128:(tt + 1) * 128], identity=identb)
                            P.op("pe", trp, r=["KP%d" % d], w=["ps6"])
                            P.op("act", (lambda e, d=d, tt=tt: e.copy(out=KPt[d][:, tt, :], in_=psb[6][:, 0:128])), r=["ps6"], w=["KPt%d" % d, "ps6"])
                orders = [list(range(68)), [3, 2, 1, 0] + list(range(67, 3, -1))]
                first = [True, True]
                cur = [0, 0]
                for step in range(68):
                    for d in range(2):
                        c = orders[d][step]
                        tt, hh = c // 2, c % 2
                        rows = slice(64 * hh, 64 * hh + 64)
                        cols = slice(64 * c, 64 * c + 64)
                        if c < 4:
                            pos, blk_n, blk_c0 = c, 256, 0
                        else:
                            pos, blk_n, blk_c0 = (c - 4) % 8, 512, 256 + ((c - 4) // 8) * 512
                        pA, pU, pO = ps[d], ps[2 + d], ps[4 + d]
                        kA, kU, kO = "ps%d" % d, "ps%d" % (2 + d), "ps%d" % (4 + d)
                        am = Am[d][step % 2]
                        kam = "Am%d_%d" % (d, step % 2)
                        mmg(pA[rows, 0:64], [(KP[d][:, cols], QPP[d][:, cols])], r=["KP%d" % d, "QPP%d" % d], w=[kA])
                        mk = maskf if d == 0 else maskb
                        P.op("dve", (lambda e, pA=pA, rows=rows, am=am, mk=mk: e.tensor_tensor(out=am[rows, :], in0=pA[rows, 0:64], in1=mk[rows, :],
                                                                                              op=ALU.mult)), r=[kA], w=[kam, kA])
                        so, sn = cur[d], 1 - cur[d]
                        prs = [(Vt[rows, tt, :], am[rows, :])]
                        rr = ["Vt", kam]
                        if not first[d]:
                            prs.append((Sb[d][so], QP[d][:, cols]))
                            rr += ["Sb%d_%d" % (d, so), "QP%d" % d]
                        mmg(pO[:, pos * 64:(pos + 1) * 64], prs, r=rr, w=[kO])
                        mmg(pU[:, 0:128], [(KPt[d][rows, tt, :], Vt[rows, tt, :])], r=["KPt%d" % d, "Vt"], w=[kU])
                        if first[d]:
                            P.op("dve", (lambda e, d=d, sn=sn, pU=pU: e.tensor_copy(out=Sf[d][sn], in_=pU[:, 0:128])),
                                 r=[kU], w=["Sf%d_%d" % (d, sn), kU])
                        else:
                            P.op("dve", (lambda e, d=d, sn=sn, so=so, pU=pU, c=c: e.scalar_tensor_tensor(
                                out=Sf[d][sn], in0=Sf[d][so], scalar=EL[d][:, c:c + 1], in1=pU[:, 0:128], op0=ALU.mult, op1=ALU.add)),
                                r=[kU, "Sf%d_%d" % (d, so), "EL%d" % d], w=["Sf%d_%d" % (d, sn), kU])
                        P.op("act", (lambda e, d=d, sn=sn: e.copy(out=Sb[d][sn], in_=Sf[d][sn])), r=["Sf%d_%d" % (d, sn)], w=["Sb%d_%d" % (d, sn)])
                        cur[d] = sn
                        first[d] = False
                        last_in_blk = (pos == (blk_n // 64 - 1)) if d == 0 else (pos == 0)
                        if last_in_blk:
                            P.op("act", (lambda e, d=d, pO=pO, blk_n=blk_n, blk_c0=blk_c0: e.copy(out=OT[d][:, blk_c0:blk_c0 + blk_n], in_=pO[:, 0:blk_n])),
                                 r=[kO], w=["OT%d" % d, kO])
                for b in range(1, 9):
                    n = 512
                    col0 = BLOCKS[b][0] * 128
                    cs = slice(col0, col0 + n)
                    m = mh[b % 2]
                    km = "mh%d" % (b % 2)
                    P.op("pool", (lambda e, cs=cs: e.tensor_tensor(out=osum, in0=OT[0][:, cs], in1=OT[1][:, cs], op=ALU.add)),
                         r=["OT0", "OT1"], w=["Da"])
                    P.op("act", lambda e: e.activation(out=sq, in_=osum, func=AF.Square), r=["Da"], w=["sq"])
                    mmg(ps[7], [(onesb, sq)], r=["sq"], w=["ps7"])
                    P.op("act", lambda e: e.activation(out=rstd, in_=ps[7], func=AF.Sqrt, scale=1.0 / 128, bias=EPS), r=["ps7"], w=["Db", "ps7"])
                    P.op("dve", lambda e: e.reciprocal(out=rstd, in_=rstd), r=["Db"], w=["Db"])
                    P.op("dve", lambda e: e.scalar_tensor_tensor(out=otmp, in0=osum, scalar=hgn, in1=rstd, op0=ALU.mult, op1=ALU.mult),
                         r=["Da", "Db", "hgn"], w=["E1"])
                    P.op("dve", (lambda e, cs=cs, m=m: e.tensor_tensor(out=m, in0=otmp, in1=SG[:, cs], op=ALU.mult)), r=["E1", "SG"], w=[km])
                    dma("pool", MIXD[4 + h][:, cs], m, r=[km], w=["MIXD"])
            phase_reset()
            if stop == "l1b3":
                return
            wo = T([128, 8, D], BF16)
            wosrc = din["mix_w_out"][l].rearrange("(k p) n -> p k n", p=128)
            for c in range(2):
                wload(wo[:, :, c * 512:(c + 1) * 512], wosrc[:, :, c * 512:(c + 1) * 512])
            res = RES(l, 2)
            mix = [T([128, 8, 512], BF16) for _ in range(2)]
            for b in range(1, 9):
                t0, ntl = BLOCKS[b]
                col0 = t0 * 128
                mx = mix[b % 2]
                kx = "mix%d" % (b % 2)
                for k in range(8):
                    dma("sp", mx[:, k, :], MIXD[k][:, col0:col0 + 512], r=["MIXD"], w=[kx])
                for ti in range(ntl):
                    tt = t0 + ti
                    ls = slice(ti * 128, (ti + 1) * 128)
                    for hf in range(2):
                        mmg(ps[6 + hf], [(mx[:, k, ls], wo[:, k, hf * 512:(hf + 1) * 512]) for k in range(8)],
                            r=[kx, "wgt"], w=["ps%d" % (6 + hf)])
                    res.run(6, 7, x_src(False, tt), XR[tt * 128:(tt + 1) * 128, :], False, "XR1")

        phase_mod()
        phase_reset()
        S0 = ("mod", "l0a1", "l0a2", "l0mix")
        S1 = S0 + ("l0", "l1b1", "l1b2", "l1b3", "l1mix")
        if stop != "mod":
            phase_l0()
            phase_reset()
        if stop not in S0:
            phase_ffn(0, True, False)
            phase_reset()
        if stop not in S0 + ("l0",):
            phase_l1()
            phase_reset()
        if stop not in S1:
            phase_ffn(1, False, True)
            phase_reset()
        P.emit(st)
    return nc


_CACHE = {}


def kernel(**inputs):
    consts = _consts()
    if "nc" not in _CACHE:
        _CACHE["nc"] = build()
    nc = _CACHE["nc"]
    in_maps = []
    for b in range(8):
        m = {"x": np.ascontiguousarray(inputs["x"][b]), "ctx": np.ascontiguousarray(inputs["ctx"][b]),
             "cvec": np.ascontiguousarray(np.stack([inputs["c"][b], inputs["c_ctx"]], 0))}
        for n in W_NAMES:
            m[n] = np.ascontiguousarray(inputs[n])
        m.update(consts)
        in_maps.append(m)
    res = run_bass_kernel_spmd(nc, in_maps, core_ids=list(range(8)))
    return np.stack([r["out"] for r in res.results], 0).astype(np.float32)
```

```python
import numpy as np
from contextlib import ExitStack
import concourse.bass as bass
import concourse.mybir as mybir
from concourse.bass_utils import run_bass_kernel_spmd

F32 = mybir.dt.float32
BF16 = mybir.dt.bfloat16
ALU = mybir.AluOpType
AF = mybir.ActivationFunctionType

NDSEM = 8
D = 1024
NT = 34
NTOK = 4352
DFF = 2816
NJ = 22
EPS = 1e-6


class Prog:
    ENGS = ("pe", "dve", "act", "pool", "sp")

    def __init__(self, nc):
        self.nc = nc
        self.ops = []
        self.lw = {}
        self.rd = {}
        self.cnt = {e: 0 for e in self.ENGS}
        self.dcnt = {e: 0 for e in self.ENGS}
        self.dslot_last = {e: [None] * NDSEM for e in self.ENGS}
        self.last_nd = {e: None for e in self.ENGS}
        self.pending_bar = {e: [] for e in self.ENGS}

    def op(self, eng, fn, r=(), w=(), dma=False):
        oid = len(self.ops)
        deps = []
        for k in r:
            y = self.lw.get(k)
            if y is not None:
                deps.append((y, "RAW"))
        for k in w:
            y = self.lw.get(k)
            if y is not None:
                deps.append((y, "WAW"))
            for y in self.rd.get(k, ()):
                deps.append((y, "WAR"))
        for y in self.pending_bar[eng]:
            deps.append((y, "RAW"))
        self.pending_bar[eng] = []
        o = dict(id=oid, eng=eng, fn=fn, deps=deps, dma=dma)
        if dma:
            i = self.dcnt[eng]
            self.dcnt[eng] += 1
            slot = i % NDSEM
            o["dslot"] = slot
            o["dval"] = 16 * (i // NDSEM + 1)
            prev = self.dslot_last[eng][slot]
            if prev is not None:
                deps.append((prev, "RAW"))
            self.dslot_last[eng][slot] = oid
        else:
            self.cnt[eng] += 1
            o["val"] = self.cnt[eng]
            self.last_nd[eng] = oid
        self.ops.append(o)
        for k in w:
            self.lw[k] = oid
            self.rd[k] = []
        for k in r:
            if k not in w:
                self.rd.setdefault(k, []).append(oid)
        return oid

    def barrier(self):
        snap = []
        for e in self.ENGS:
            if self.last_nd[e] is not None:
                snap.append(self.last_nd[e])
            for y in self.dslot_last[e]:
                if y is not None:
                    snap.append(y)
        for e in self.ENGS:
            self.pending_bar[e] = list(snap)
        self.lw = {}
        self.rd = {}

    def emit(self, st):
        nc = self.nc
        sems = {e: st.enter_context(nc.semaphore("s_" + e)) for e in self.ENGS}
        dsems = {e: [st.enter_context(nc.semaphore("d_%s%d" % (e, i))) for i in range(NDSEM)]
                 for e in ("sp", "pool", "act") if self.dcnt[e] > 0}
        block = st.enter_context(nc.Block())
        ops = self.ops

        def run(ename, eng):
            seen = {}
            for o in ops:
                if o["eng"] != ename:
                    continue
                need = {}
                for (y, kind) in o["deps"]:
                    Y = ops[y]
                    if Y["dma"]:
                        key = ("d", Y["eng"], Y["dslot"])
                        sem = dsems[Y["eng"]][Y["dslot"]]
                        val = Y["dval"]
                    else:
                        if Y["eng"] == ename and not o["dma"]:
                            if ename == "pe" or kind != "RAW":
                                continue
                        key = ("c", Y["eng"])
                        sem = sems[Y["eng"]]
                        val = Y["val"]
                    if seen.get(key, 0) >= val:
                        continue
                    if key not in need or need[key][1] < val:
                        need[key] = (sem, val)
                for key, (sem, val) in need.items():
                    eng.wait_ge(sem, val)
                    seen[key] = val
                ins = o["fn"](eng)
                if o["dma"]:
                    ins.then_inc(dsems[ename][o["dslot"]], 16)
                else:
                    ins.then_inc(sems[ename], 1)
            if ename in dsems:
                for slot in range(NDSEM):
                    y = self.dslot_last[ename][slot]
                    if y is not None:
                        Y = ops[y]
                        if seen.get(("d", ename, slot), 0) < Y["dval"]:
                            eng.wait_ge(dsems[ename][slot], Y["dval"])

        @block.tensor
        def _(e):
            run("pe", e)

        @block.vector
        def _(e):
            run("dve", e)

        @block.scalar
        def _(e):
            run("act", e)

        @block.gpsimd
        def _(e):
            run("pool", e)

        @block.sync
        def _(e):
            run("sp", e)


PW = 8 + 256 + 16 + 4096 + 8
PC0, PL0 = 8, 280


def _consts():
    c = {}
    c["ident"] = np.eye(128, dtype=np.float32)
    rot = np.zeros((128, 128), np.float32)
    for d in range(128):
        rot[d ^ 16, d] = 1.0
    c["rot"] = rot
    n = 4096
    pos_row = np.repeat(np.arange(n // 64), 64)
    pos_col = np.tile(np.arange(64), n // 64)
    inv_freq = (10000.0 ** (-np.arange(0, 32, 2, dtype=np.float32) / 32)).astype(np.float32)
    ang = np.stack([pos_row, pos_col], -1).astype(np.float32)[..., None] * inv_freq
    cs, sn = np.cos(ang).astype(np.float32), np.sin(ang).astype(np.float32)
    cos_t = np.zeros((128, n), np.float32)
    sin_t = np.zeros((128, n), np.float32)
    for d in range(128):
        dd = d % 64
        a, hf, i = dd // 32, (dd // 16) % 2, dd % 16
        cos_t[d] = cs[:, a, i]
        sin_t[d] = sn[:, a, i] * (-1.0 if hf == 0 else 1.0)
    c["cos_t"] = cos_t
    c["sin_t"] = sin_t
    inv = np.zeros((4, PW), np.float32)
    for g, w in enumerate((2, 4, 8, 16)):
        h = w // 2
        for (n_, off) in ((256, PC0), (4096, PL0)):
            t = np.arange(n_)
            lo = np.clip(t - h, 0, n_)
            hi = np.clip(t + h, 0, n_)
            inv[g, off:off + n_] = 1.0 / (hi - lo).astype(np.float32)
    c["invcnt"] = inv
    p = np.arange(128)[:, None] % 64
    t = np.arange(64)[None, :]
    c["mask_f"] = (p <= t).astype(np.float32)
    c["mask_b"] = (p >= t).astype(np.float32)
    return c


W_NAMES = ["ada_w", "ada_b", "norm_g", "mix_w_out", "ffn_w_gate", "ffn_w_up", "ffn_conv_w",
           "ffn_conv_b", "ffn_w_down", "ev_w_in", "pool_w", "pool_scale", "diff_lambda",
           "diff_subln", "od_w_in", "mla_q_norm", "mla_w_uq", "mla_kv_norm", "mla_w_ukv",
           "hgrn_norm", "hgrn_lb"]
W_SHAPES = {"ada_w": (2, 1024, 6144), "ada_b": (2, 6144), "norm_g": (2, 4, 1024),
            "mix_w_out": (2, 1024, 1024), "ffn_w_gate": (2, 1024, 2816), "ffn_w_up": (2, 1024, 2816),
            "ffn_conv_w": (2, 3, 2816), "ffn_conv_b": (2, 2816), "ffn_w_down": (2, 2816, 1024),
            "ev_w_in": (1, 1024, 2048), "pool_w": (1, 4, 128, 128), "pool_scale": (1, 512),
            "diff_lambda": (1, 4, 64), "diff_subln": (1, 128), "od_w_in": (1, 1024, 3392),
            "mla_q_norm": (1, 512), "mla_w_uq": (1, 512, 768), "mla_kv_norm": (1, 256),
            "mla_w_ukv": (1, 256, 1024), "hgrn_norm": (1, 128), "hgrn_lb": (2, 2, 512)}
C_SHAPES = {"ident": (128, 128), "rot": (128, 128), "cos_t": (128, 4096), "sin_t": (128, 4096),
            "invcnt": (4, PW), "mask_f": (128, 64), "mask_b": (128, 64)}

BLOCKS = [(0, 2)] + [(2 + 4 * i, 4) for i in range(8)]


def build(stop=None, dbg=False):
    nc = bass.Bass("TRN2", target_bir_lowering=False)
    din = {}
    din["x"] = nc.dram_tensor("x", [4096, D], F32, kind="ExternalInput").ap()
    din["ctx"] = nc.dram_tensor("ctx", [256, D], F32, kind="ExternalInput").ap()
    din["cvec"] = nc.dram_tensor("cvec", [2, D], F32, kind="ExternalInput").ap()
    for n in W_NAMES:
        din[n] = nc.dram_tensor(n, list(W_SHAPES[n]), F32, kind="ExternalInput").ap()
    for n in C_SHAPES:
        din[n] = nc.dram_tensor(n, list(C_SHAPES[n]), F32, kind="ExternalInput").ap()
    out = nc.dram_tensor("out", [4096, D], F32, kind="ExternalOutput").ap()
    XR = nc.dram_tensor("XR", [NTOK, D], F32, kind="ExternalOutput" if dbg else "Internal").ap()
    MODV = nc.dram_tensor("MODV", [2, 2, 6, D], F32, kind="Internal").ap()
    QT = nc.dram_tensor("QT", [9, 128, 4, 512], BF16, kind="Internal").ap()
    QR = nc.dram_tensor("QR", [9, 128, 2, 512], BF16, kind="Internal").ap()
    UT = nc.dram_tensor("UT", [4, 128, NTOK], F32, kind="Internal").ap()
    HT1 = nc.dram_tensor("HT1", [9, 128, 8, 512], BF16, kind="Internal").ap()
    H2D = nc.dram_tensor("H2D", [128, 8, 4355], BF16, kind="Internal").ap()
    MIXD = nc.dram_tensor("MIXD", [8, 128, NTOK], BF16, kind="ExternalOutput" if dbg else "Internal").ap()

    st = ExitStack()
    with st:
        P = Prog(nc)
        AW = 52000
        arena = st.enter_context(nc.sbuf_tensor("arena", [128, AW], F32))
        psall = st.enter_context(nc.psum_tensor("psall", [128, 4096], F32))[:]
        ps = [psall[:, i * 512:(i + 1) * 512] for i in range(8)]
        psb = [p.bitcast(BF16) for p in ps]
        top = [0]

        def T(shape, dt=F32):
            n = int(np.prod(shape[1:]))
            cols = n if dt == F32 else (n + 1) // 2
            off = top[0]
            top[0] += cols
            assert top[0] <= AW, "SBUF arena overflow %d" % top[0]
            a = arena[0:shape[0], off:off + cols]
            if dt != F32:
                a = a.bitcast(dt)
            if len(shape) == 3:
                a = a.rearrange("p (a b) -> p a b", a=shape[1])
            elif len(shape) == 4:
                a = a.rearrange("p (a b c) -> p a b c", a=shape[1], b=shape[2])
            return a

        uid = [0]

        def K(s):
            uid[0] += 1
            return "%s#%d" % (s, uid[0])

        def dma(q, o, i, r=(), w=()):
            P.op(q, lambda e: e.dma_start(out=o, in_=i), r=r, w=w, dma=True)

        def dma_nc(q, o, i, r=(), w=()):
            P.op(q, lambda e: e.dma_start(out=o, in_=i, allow_slow_non_contiguous=True), r=r, w=w, dma=True)

        def mmg(o, pairs, r, w):
            def f(e):
                n = len(pairs)
                for i, (l, rh) in enumerate(pairs):
                    ins = e.matmul(o, lhsT=l, rhs=rh, start=(i == 0), stop=(i == n - 1))
                return ins
            P.op("pe", f, r=r, w=w)

        identb = T([128, 128], BF16)
        rotb = T([128, 128], BF16)
        onesb = T([128, 128], BF16)
        maskf = T([128, 64])
        maskb = T([128, 64])
        stg = [T([128, 2048]) for _ in range(2)]
        stgi = [0]
        PERSIST = None

        def wload(dst, src, q=None, ce="pool"):
            i = stgi[0] % 2
            stgi[0] += 1
            shp = list(dst.shape)
            n = int(np.prod(shp[1:]))
            if n > 2048:
                hh = shp[-1] // 2
                if len(shp) == 2:
                    wload(dst[:, 0:hh], src[:, 0:hh], q, ce)
                    wload(dst[:, hh:], src[:, hh:], q, ce)
                else:
                    wload(dst[:, :, 0:hh], src[:, :, 0:hh], q, ce)
                    wload(dst[:, :, hh:], src[:, :, hh:], q, ce)
                return
            s = stg[i][0:shp[0], 0:n]
            if len(shp) == 3:
                s = s.rearrange("p (a b) -> p a b", a=shp[1])
            qq = q or ("sp" if i == 0 else "pool")
            dma(qq, s, src, w=["stg%d" % i])
            kd = "W" + str(id(dst))
            if ce == "pool":
                P.op("pool", lambda e: e.tensor_copy(out=dst, in_=s), r=["stg%d" % i], w=["wgt"])
            elif ce == "dve":
                P.op("dve", lambda e: e.tensor_copy(out=dst, in_=s), r=["stg%d" % i], w=["wgt"])
            else:
                P.op("act", lambda e: e.copy(out=dst, in_=s), r=["stg%d" % i], w=["wgt"])

        for (dst, nm) in ((identb, "ident"), (rotb, "rot")):
            wload(dst, din[nm])
        P.op("pool", lambda e: e.memset(onesb, 1.0), w=["wgt"])
        dma("sp", maskf, din["mask_f"], w=["wgt"])
        dma("sp", maskb, din["mask_b"], w=["wgt"])
        PERSIST = top[0]

        def phase_reset():
            P.barrier()
            top[0] = PERSIST

        def phase_mod():
            cv = T([128, 2, 8])
            cvs = T([128, 2, 8])
            cvb = T([128, 8, 2], BF16)
            for j in range(2):
                dma_nc("sp", cv[:, j, :], din["cvec"][j].rearrange("(k p) -> p k", p=128), w=["cv"])
            P.op("act", lambda e: e.activation(out=cvs, in_=cv, func=AF.Silu), r=["cv"], w=["cvs"])
            P.op("dve", lambda e: e.tensor_copy(out=cvb, in_=cvs.rearrange("p j k -> p k j")), r=["cvs"], w=["cvb"])
            awb = [T([128, 8, 256], BF16) for _ in range(2)]
            Mt = T([2, 6 * D])
            bt = T([2, 6 * D])
            ng = T([2, 4, D])
            V = T([2, 6, D])
            for l in range(2):
                dma("sp", bt, din["ada_b"][l].partition_broadcast(2), w=["bt"])
                dma("sp", ng.rearrange("p a b -> p (a b)"),
                    din["norm_g"][l].rearrange("a b -> (a b)").partition_broadcast(2), w=["ng"])
                for nb in range(24):
                    ab = awb[nb % 2]
                    kab = "awb%d" % (nb % 2)
                    src = din["ada_w"][l].rearrange("(k p) n -> p k n", p=128)[:, :, nb * 256:(nb + 1) * 256]
                    i = stgi[0] % 2
                    stgi[0] += 1
                    s = stg[i][:, :].rearrange("p (a b) -> p a b", a=8)
                    dma("sp" if i == 0 else "pool", s, src, w=["stg%d" % i])
                    P.op("pool" if nb % 2 == 0 else "dve", (lambda e, ab=ab, s=s: e.tensor_copy(out=ab, in_=s)),
                         r=["stg%d" % i], w=[kab])
                    pb = ps[nb % 2][0:2, 0:256]
                    mmg(pb, [(cvb[:, k, :], ab[:, k, :]) for k in range(8)], r=["cvb", kab], w=["ps%d" % (nb % 2)])
                    P.op("dve", (lambda e, pb=pb, nb=nb: e.tensor_tensor(out=Mt[:, nb * 256:(nb + 1) * 256], in0=pb,
                                                                       in1=bt[:, nb * 256:(nb + 1) * 256], op=ALU.add)),
                         r=["ps%d" % (nb % 2), "bt"], w=["Mt"])
                sl = lambda i: Mt[:, i * D:(i + 1) * D]
                P.op("dve", lambda e: e.scalar_tensor_tensor(out=V[:, 0, :], in0=sl(1), scalar=1.0, in1=ng[:, 0, :],
                                                             op0=ALU.add, op1=ALU.mult), r=["Mt", "ng"], w=["V"])
                P.op("dve", lambda e: e.tensor_copy(out=V[:, 1, :], in_=sl(0)), r=["Mt"], w=["V"])
                P.op("dve", lambda e: e.tensor_tensor(out=V[:, 2, :], in0=sl(2), in1=ng[:, 1, :], op=ALU.mult),
                     r=["Mt", "ng"], w=["V"])
                P.op("dve", lambda e: e.scalar_tensor_tensor(out=V[:, 3, :], in0=sl(4), scalar=1.0, in1=ng[:, 2, :],
                                                             op0=ALU.add, op1=ALU.mult), r=["Mt", "ng"], w=["V"])
                P.op("dve", lambda e: e.tensor_copy(out=V[:, 4, :], in_=sl(3)), r=["Mt"], w=["V"])
                P.op("dve", lambda e: e.tensor_tensor(out=V[:, 5, :], in0=sl(5), in1=ng[:, 3, :], op=ALU.mult),
                     r=["Mt", "ng"], w=["V"])
                dma("sp", MODV[l].rearrange("j a b -> j (a b)"), V.rearrange("p a b -> p (a b)"), r=["V"], w=["MODV"])

        def x_src(layer0_in, tt):
            if layer0_in:
                return din["ctx"][tt * 128:(tt + 1) * 128, :] if tt < 2 else din["x"][(tt - 2) * 128:(tt - 1) * 128, :]
            return XR[tt * 128:(tt + 1) * 128, :]

        class NM:
            def __init__(self, l, gi, si):
                self.xt = [T([128, D]) for _ in range(2)]
                self.junk = T([128, D], BF16)
                self.tmp = [T([128, D]) for _ in range(2)]
                self.pending = None
                self.hb = [T([128, D], BF16) for _ in range(2)]
                self.ss = T([128, 2])
                self.rs = T([128, 2])
                self.G = [T([128, D]) for _ in range(2)]
                self.SH = [T([128, D]) for _ in range(2)]
                for j in range(2):
                    dma("sp", self.G[j], MODV[l, j, gi].partition_broadcast(128), r=["MODV"], w=["nmG"])
                    dma("sp", self.SH[j], MODV[l, j, si].partition_broadcast(128), r=["MODV"], w=["nmG"])
                self.i = 0

            def run(self, src, is_ctx, dst, dkey, bank):
                i = self.i % 2
                self.i += 1
                xt, hb = self.xt[i], self.hb[i]
                kx, kh = "nm_xt%d" % i, "nm_hb%d" % i
                ss, rs = self.ss[:, i:i + 1], self.rs[:, i:i + 1]
                G, SH = self.G[1 if is_ctx else 0], self.SH[1 if is_ctx else 0]
                tmp = self.tmp[i]
                dma("sp", xt, src, r=["XR"], w=[kx])
                P.op("act", lambda e: e.activation(out=self.junk, in_=xt, func=AF.Square, accum_out=ss),
                     r=[kx], w=["nm_junk", "nm_ss%d" % i])
                P.op("act", lambda e: e.activation(out=rs, in_=ss, func=AF.Sqrt, scale=1.0 / D, bias=EPS),
                     r=["nm_ss%d" % i], w=["nm_rs%d" % i])
                P.op("dve", lambda e: e.reciprocal(out=rs, in_=rs), r=["nm_rs%d" % i], w=["nm_rs%d" % i])
                P.op("dve", lambda e: e.scalar_tensor_tensor(out=tmp, in0=xt, scalar=rs, in1=G, op0=ALU.mult,
                                                             op1=ALU.mult), r=[kx, "nm_rs%d" % i, "nmG"], w=["nm_tmp%d" % i])
                P.op("pool", lambda e: e.tensor_tensor(out=hb, in0=tmp, in1=SH, op=ALU.add),
                     r=["nm_tmp%d" % i, "nmG"], w=[kh])
                prev = self.pending
                self.pending = (hb, kh, dst, dkey, bank)
                if prev is not None:
                    self.stage_b(*prev)

            def stage_b(self, hb, kh, dst, dkey, bank):
                pb = psb[bank]

                def tr(e):
                    for k in range(8):
                        ins = e.transpose(out=pb[:, k * 128:(k + 1) * 128], in_=hb[:, k * 128:(k + 1) * 128],
                                          identity=identb)
                    return ins
                P.op("pe", tr, r=[kh], w=["ps%d" % bank])
                P.op("act", lambda e: e.copy(out=dst, in_=pb.rearrange("p (k n) -> p k n", k=8)),
                     r=["ps%d" % bank], w=[dkey])

            def flush(self):
                if self.pending is not None:
                    self.stage_b(*self.pending)
                    self.pending = None

        class RES:
            def __init__(self, l, gidx):
                self.GT = [T([128, D]) for _ in range(2)]
                for j in range(2):
                    dma("sp", self.GT[j], MODV[l, j, gidx].partition_broadcast(128), r=["MODV"], w=["resG"])
                self.xo = [T([128, D]) for _ in range(2)]
                self.tt_ = [T([128, D]) for _ in range(2)]
                self.junk = T([128, 512], BF16)
                self.ss = T([128, 4])
                self.rs = T([128, 2])
                self.i = 0

            def run(self, b0, b1, src, dstd, is_ctx, wkey):
                i = self.i % 2
                self.i += 1
                xo = self.xo[i]
                tbuf = self.tt_[i]
                kt_ = "res_t%d" % i
                kx = "res_x%d" % i
                ss = self.ss[:, 2 * i:2 * i + 2]
                rs = self.rs[:, i:i + 1]
                GT = self.GT[1 if is_ctx else 0]
                dma("pool", xo, src, r=["XR"], w=[kx])
                P.op("act", lambda e: e.activation(out=self.junk, in_=ps[b0], func=AF.Square, accum_out=ss[:, 0:1]),
                     r=["ps%d" % b0], w=["res_junk", "res_ss%d" % i])
                P.op("act", lambda e: e.activation(out=self.junk, in_=ps[b1], func=AF.Square, accum_out=ss[:, 1:2]),
                     r=["ps%d" % b1], w=["res_junk", "res_ss%d" % i])
                P.op("dve", lambda e: e.tensor_tensor(out=rs, in0=ss[:, 0:1], in1=ss[:, 1:2], op=ALU.add),
                     r=["res_ss%d" % i], w=["res_rs%d" % i])
                P.op("act", lambda e: e.activation(out=rs, in_=rs, func=AF.Sqrt, scale=1.0 / D, bias=EPS),
                     r=["res_rs%d" % i], w=["res_rs%d" % i])
                P.op("dve", lambda e: e.reciprocal(out=rs, in_=rs), r=["res_rs%d" % i], w=["res_rs%d" % i])
                for hf, bk in ((0, b0), (1, b1)):
                    P.op("dve", (lambda e, hf=hf, bk=bk: e.scalar_tensor_tensor(
                        out=tbuf[:, hf * 512:(hf + 1) * 512], in0=ps[bk], scalar=rs,
                        in1=GT[:, hf * 512:(hf + 1) * 512], op0=ALU.mult, op1=ALU.mult)),
                        r=["ps%d" % bk, "res_rs%d" % i, "resG"], w=[kt_, "ps%d" % bk])
                P.op("pool", lambda e: e.tensor_tensor(out=xo, in0=tbuf, in1=xo, op=ALU.add),
                     r=[kt_, kx], w=[kx])
                dma("pool", dstd, xo, r=[kx], w=[wkey])

        def rope_evict(pbank, n, t0, dst, dkey, ro, is_ctx):
            src = ps[pbank][:, 0:n]
            if is_ctx:
                P.op("act", lambda e: e.copy(out=dst, in_=src), r=["ps%d" % pbank], w=[dkey])
                return
            qs, t1, t2, cosb, sinb, rb = ro
            P.op("act", lambda e: e.copy(out=qs[:, 0:n], in_=src), r=["ps%d" % pbank], w=["ro_qs"])
            mmg(ps[rb][:, 0:n], [(rotb, qs[:, 0:n])], r=["ro_qs"], w=["ps%d" % rb])
            P.op("dve", lambda e: e.tensor_tensor(out=t1[:, 0:n], in0=src, in1=cosb[:, 0:n], op=ALU.mult),
                 r=["ps%d" % pbank, "ro_cs"], w=["ro_t1", "ps%d" % pbank])
            P.op("dve", lambda e: e.tensor_tensor(out=t2[:, 0:n], in0=ps[rb][:, 0:n], in1=sinb[:, 0:n], op=ALU.mult),
                 r=["ps%d" % rb, "ro_cs"], w=["ro_t2", "ps%d" % rb])
            P.op("pool", lambda e: e.tensor_tensor(out=dst, in0=t1[:, 0:n], in1=t2[:, 0:n], op=ALU.add),
                 r=["ro_t1", "ro_t2"], w=[dkey])

        def rope_tiles():
            return (T([128, 512], BF16), T([128, 512]), T([128, 512]), T([128, 512]), T([128, 512]))

        def rope_load(ro, b):
            if b == 0:
                return
            c0 = (b - 1) * 512
            dma("pool", ro[3], din["cos_t"][:, c0:c0 + 512], w=["ro_cs"])
            dma("pool", ro[4], din["sin_t"][:, c0:c0 + 512], w=["ro_cs"])

        def attention(groups, scale, accsets):
            PT = [T([128, 2, 512], BF16) for _ in range(3)]
            N = len(groups)

            def emitS(i):
                g = groups[i]
                if g.get("pre"):
                    g["pre"]()
                A = 2 * (i % 2)
                n = g["n"]
                for mi, m in enumerate(g["members"]):
                    mmg(ps[A + mi][:, 0:n], m["qk"], r=g["rk"], w=["ps%d" % (A + mi)])

            def emitE(i):
                g = groups[i]
                A = 2 * (i % 2)
                n = g["n"]
                nm_ = len(g["members"])
                pt = PT[i % 3]
                src = psall[:, A * 512:(A + 2) * 512].rearrange("p (j n) -> p j n", j=2)[:, 0:nm_, 0:n]
                P.op("act", (lambda e: e.activation(out=pt[:, 0:nm_, 0:n], in_=src, func=AF.Exp, scale=scale)),
                     r=["ps%d" % A, "ps%d" % (A + 1)], w=["PT%d" % (i % 3), "ps%d" % A, "ps%d" % (A + 1)])

            def emitPV(i):
                g = groups[i]
                n = g["n"]
                pt = PT[i % 3]
                acc = accsets[g["accset"]]
                wk = []
                for m in g["members"]:
                    ob, sb2 = acc[m["acc"]]
                    wk += ["ps%d" % ob, "ps%d" % sb2]

                def pv(e):
                    for mi, m in enumerate(g["members"]):
                        ob, sb2 = acc[m["acc"]]
                        e.matmul(ps[ob][:, 0:n], lhsT=m["v"], rhs=pt[:, mi, 0:n], start=m["start"], stop=m["stop"])
                        ins = e.matmul(ps[sb2][:, 0:n], lhsT=onesb, rhs=pt[:, mi, 0:n], start=m["start"], stop=m["stop"])
                    return ins
                P.op("pe", pv, r=["PT%d" % (i % 3)] + g["rv"], w=list(dict.fromkeys(wk)))
                if g.get("post"):
                    A = 2 * (i % 2)
                    g["post"](acc, (A, A + 1))

            if N:
                emitS(0)
            for i in range(N):
                emitE(i)
                if i + 1 < N:
                    emitS(i + 1)
                emitPV(i)

        def phase_ffn(l, need_ctx, final):
            W2 = 4355
            Wd = T([128, NJ, D], BF16)
            CW = T([128, 4, NJ])
            for i in range(3):
                dma_nc("sp", CW[:, i, :], din["ffn_conv_w"][l, i].rearrange("(j p) -> p j", p=128), w=["CW"])
            dma_nc("sp", CW[:, 3, :], din["ffn_conv_b"][l].rearrange("(j p) -> p j", p=128), w=["CW"])
            wdsrc = din["ffn_w_down"][l].rearrange("(j p) n -> p j n", p=128)
            for j0 in range(0, NJ, 4):
                j1 = min(NJ, j0 + 4)
                wload(Wd[:, j0:j1, :], wdsrc[:, j0:j1, :])
            top_save = top[0]
            nm = NM(l, 3, 4)
            hTs = [T([128, 8, 512], BF16) for _ in range(2)]
            zt = T([128, 8, 1], BF16)
            P.op("pool", lambda e: e.memset(zt, 0.0), w=["zt"])
            for c in (0, 257, 4354):
                dma_nc("pool", H2D[:, :, c:c + 1], zt, r=["zt"], w=["H2D"])
            blks = list(range(0 if need_ctx else 1, 9))
            for b in blks:
                t0, ntl = BLOCKS[b]
                n = ntl * 128
                hT = hTs[b % 2]
                kh = "hTs%d" % (b % 2)
                for ti in range(ntl):
                    tt = t0 + ti
                    nm.run(x_src(False, tt), tt < 2, hT[:, :, ti * 128:(ti + 1) * 128], kh, 6 + tt % 2)
                nm.flush()
                c0 = 1 if b == 0 else 258 + (b - 1) * 512
                dma("pool", H2D[:, :, c0:c0 + n], hT[:, :, 0:n], r=[kh], w=["H2D"])
            P.barrier()
            top[0] = top_save
            res = RES(l, 5)
            GTt = T([128, NJ, 1024], BF16)
            H2P = [T([128, 8, 1026], BF16) for _ in range(2)]
            wgf = [T([128, 8, 128], BF16) for _ in range(2)]
            wuf = [T([128, 8, 128], BF16) for _ in range(2)]
            acc = [T([128, 512]) for _ in range(2)]
            sil = [T([128, 512]) for _ in range(2)]
            parts = [blks[i:i + 2] for i in range(0, len(blks), 2)]
            wgsrc = din["ffn_w_gate"][l].rearrange("(k p) n -> p k n", p=128)
            wusrc = din["ffn_w_up"][l].rearrange("(k p) n -> p k n", p=128)
            it = [0]
            yi = [0]
            bc0 = lambda b: 1 if b == 0 else 258 + (b - 1) * 512
            for pi, part in enumerate(parts):
                cstart = bc0(part[0]) - 1
                cend = bc0(part[-1]) + BLOCKS[part[-1]][1] * 128 + 1
                npc = cend - cstart
                H2T = H2P[pi % 2]
                kH = "H2P%d" % (pi % 2)
                dma("sp", H2T[:, :, 0:npc], H2D[:, :, cstart:cend], r=["H2D"], w=[kH])
                goff = {}
                o = 0
                for b in part:
                    goff[b] = o
                    o += BLOCKS[b][1] * 128
                for j in range(NJ):
                    wg, wu = wgf[j % 2], wuf[j % 2]
                    kw = "ffw%d" % (j % 2)
                    for (dst, srcw) in ((wg, wgsrc), (wu, wusrc)):
                        i = stgi[0] % 2
                        stgi[0] += 1
                        s = stg[i][:, 0:1024].rearrange("p (a b) -> p a b", a=8)
                        dma("sp", s, srcw[:, :, j * 128:(j + 1) * 128], w=["stg%d" % i])
                        P.op("pool", (lambda e, dst=dst, s=s: e.tensor_copy(out=dst, in_=s)), r=["stg%d" % i], w=[kw])
                    for b in part:
                        t0, ntl = BLOCKS[b]
                        n = ntl * 128
                        c0 = bc0(b) - cstart
                        q = it[0] % 2
                        it[0] += 1
                        pa, pu, ph = ps[q], ps[2 + q], ps[4 + q]
                        ka, ku, kh = "ps%d" % q, "ps%d" % (2 + q), "ps%d" % (4 + q)
                        mmg(pa[:, 0:n], [(wg[:, k, :], H2T[:, k, c0:c0 + n]) for k in range(8)], r=[kw, kH], w=[ka])
                        hal = H2T[:, :, c0 - 1:c0 + n + 1:n + 1]
                        mmg(ph[:, 0:2], [(wg[:, k, :], hal[:, k, :]) for k in range(8)], r=[kw, kH], w=[kh])
                        mmg(pu[:, 0:n], [(wu[:, k, :], H2T[:, k, c0:c0 + n]) for k in range(8)], r=[kw, kH], w=[ku])
                        ac, sl_ = acc[q], sil[q]
                        kac, ksl = "acc%d" % q, "sil%d" % q
                        w0, w1, w2, bb = (CW[:, i, j:j + 1] for i in range(4))
                        P.op("dve", (lambda e, ac=ac, pa=pa, w1=w1, bb=bb, n=n: e.tensor_scalar(
                            out=ac[:, 0:n], in0=pa[:, 0:n], scalar1=w1, scalar2=bb, op0=ALU.mult, op1=ALU.add)),
                            r=[ka, "CW"], w=[kac])
                        P.op("dve", (lambda e, ac=ac, pa=pa, w0=w0, n=n: e.scalar_tensor_tensor(
                            out=ac[:, 1:n], in0=pa[:, 0:n - 1], scalar=w0, in1=ac[:, 1:n], op0=ALU.mult, op1=ALU.add)),
                            r=[ka, kac], w=[kac])
                        P.op("dve", (lambda e, ac=ac, pa=pa, w2=w2, n=n: e.scalar_tensor_tensor(
                            out=ac[:, 0:n - 1], in0=pa[:, 1:n], scalar=w2, in1=ac[:, 0:n - 1], op0=ALU.mult, op1=ALU.add)),
                            r=[ka, kac], w=[kac, ka])
                        P.op("dve", (lambda e, ac=ac, ph=ph, w0=w0: e.scalar_tensor_tensor(
                            out=ac[:, 0:1], in0=ph[:, 0:1], scalar=w0, in1=ac[:, 0:1], op0=ALU.mult, op1=ALU.add)),
                            r=[kh, kac], w=[kac])
                        P.op("dve", (lambda e, ac=ac, ph=ph, w2=w2, n=n: e.scalar_tensor_tensor(
                            out=ac[:, n - 1:n], in0=ph[:, 1:2], scalar=w2, in1=ac[:, n - 1:n], op0=ALU.mult, op1=ALU.add)),
                            r=[kh, kac], w=[kac, kh])
                        P.op("act", (lambda e, ac=ac, sl_=sl_, n=n: e.activation(out=sl_[:, 0:n], in_=ac[:, 0:n], func=AF.Silu)),
                             r=[kac], w=[ksl])
                        g0 = goff[b]
                        P.op("dve", (lambda e, sl_=sl_, pu=pu, j=j, g0=g0, n=n: e.tensor_tensor(
                            out=GTt[:, j, g0:g0 + n], in0=sl_[:, 0:n], in1=pu[:, 0:n], op=ALU.mult)),
                            r=[ksl, ku], w=["GT", ku])
                for b in part:
                    t0, ntl = BLOCKS[b]
                    for ti in range(ntl):
                        tt = t0 + ti
                        g0 = goff[b] + ti * 128
                        y0, y1 = ((6, 7), (0, 1), (2, 3))[yi[0] % 3]
                        yi[0] += 1
                        for hf, yb in ((0, y0), (1, y1)):
                            mmg(ps[yb], [(GTt[:, j, g0:g0 + 128], Wd[:, j, hf * 512:(hf + 1) * 512]) for j in range(NJ)],
                                r=["GT", "wgt"], w=["ps%d" % yb])
                        if final:
                            dstd = out[(tt - 2) * 128:(tt - 1) * 128, :]
                            res.run(y0, y1, x_src(False, tt), dstd, tt < 2, "OUT")
                        else:
                            res.run(y0, y1, x_src(False, tt), XR[tt * 128:(tt + 1) * 128, :], tt < 2, "XR2")

        def phase_l0():
            l = 0
            LAM_INIT = 0.2
            KT = T([128, 4, NTOK], BF16)
            Vv = T([128, NT, 512], BF16)
            keep = top[0]
            w_in = T([128, 8, 2048], BF16)
            wsrc = din["ev_w_in"][0].rearrange("(k p) n -> p k n", p=128)
            for c in range(4):
                wload(w_in[:, :, c * 512:(c + 1) * 512], wsrc[:, :, c * 512:(c + 1) * 512])
            nm = NM(l, 0, 1)
            hT = T([128, 8, 512], BF16)
            ro = rope_tiles() + (7,)
            ub = [T([128, 512]) for _ in range(2)]
            qb = [T([128, 4, 512], BF16) for _ in range(2)]
            for b, (t0, ntl) in enumerate(BLOCKS):
                n = ntl * 128
                col0 = t0 * 128
                rope_load(ro, b)
                for ti in range(ntl):
                    tt = t0 + ti
                    nm.run(x_src(True, tt), tt < 2, hT[:, :, ti * 128:(ti + 1) * 128], "hT", 6)
                nm.flush()
                for ti in range(ntl):
                    tt = t0 + ti
                    bk = 4 + (ti % 2)
                    mmg(ps[bk], [(hT[:, k, ti * 128:(ti + 1) * 128], w_in[:, k, 1536:2048]) for k in range(8)],
                        r=["hT", "wgt"], w=["ps%d" % bk])
                    P.op("act", (lambda e, tt=tt, bk=bk: e.copy(out=Vv[:, tt, :], in_=ps[bk])), r=["ps%d" % bk], w=["Vv"])
                for c in range(4):
                    bk = c % 2
                    mmg(ps[bk][:, 0:n], [(w_in[:, k, c * 128:(c + 1) * 128], hT[:, k, 0:n]) for k in range(8)],
                        r=["hT", "wgt"], w=["ps%d" % bk])
                    u = ub[c % 2]
                    P.op("act", (lambda e, u=u, bk=bk, n=n: e.copy(out=u[:, 0:n], in_=ps[bk][:, 0:n])),
                         r=["ps%d" % bk], w=["ub%d" % (c % 2)])
                    dma("pool", UT[c][:, col0:col0 + n], u[:, 0:n], r=["ub%d" % (c % 2)], w=["UT"])
                qbb = qb[b % 2]
                kq = "qb%d" % (b % 2)
                for c in range(4):
                    bk = 2 + (c % 2)
                    mmg(ps[bk][:, 0:n], [(w_in[:, k, 512 + c * 128:512 + (c + 1) * 128], hT[:, k, 0:n]) for k in range(8)],
                        r=["hT", "wgt"], w=["ps%d" % bk])
                    rope_evict(bk, n, col0, qbb[:, c, 0:n], kq, ro, b == 0)
                dma("pool", QT[b][:, :, 0:n], qbb[:, :, 0:n], r=[kq], w=["QT"])
                for c in range(4):
                    bk = 2 + (c % 2)
                    mmg(ps[bk][:, 0:n], [(w_in[:, k, 1024 + c * 128:1024 + (c + 1) * 128], hT[:, k, 0:n]) for k in range(8)],
                        r=["hT", "wgt"], w=["ps%d" % bk])
                    rope_evict(bk, n, col0, KT[:, c, col0:col0 + n], "KT", ro, b == 0)
            P.barrier()
            top[0] = keep
            if stop == "l0a1":
                return
            pw = T([128, 4, 128], BF16)
            for g in range(4):
                wload(pw[:, g, :], din["pool_w"][0, g])
            psc = T([128, 4])
            dma_nc("sp", psc, din["pool_scale"][0].rearrange("(g p) -> p g", p=128), w=["psc"])
            UP = T([128, PW])
            Aa = T([128, PW])
            Ab = T([128, PW])
            IC = T([128, PW])
            dT = T([128, PW], BF16)
            mpo = [T([128, 512], BF16) for _ in range(2)]
            for g in range(4):
                hw = (1, 2, 4, 8)[g]
                P.op("pool", lambda e: e.memset(UP, 0.0), w=["UP"])
                dma("sp", UP[:, PC0:PC0 + 256], UT[g][:, 0:256], r=["UT"], w=["UP"])
                dma("sp", UP[:, PL0:PL0 + 4096], UT[g][:, 256:NTOK], r=["UT"], w=["UP"])
                dma("pool", IC, din["invcnt"][g].partition_broadcast(128), w=["IC"])
                cur, ck = UP, "UP"
                bufs = [(Aa, "Aa"), (Ab, "Ab")]
                width = PW
                for s in range(g + 1):
                    sh = 1 << s
                    nxt, nk = bufs[s % 2]
                    width -= sh
                    P.op("dve", (lambda e, cur=cur, nxt=nxt, sh=sh, width=width: e.tensor_tensor(
                        out=nxt[:, 0:width], in0=cur[:, 0:width], in1=cur[:, sh:sh + width], op=ALU.add)),
                        r=[ck], w=[nk])
                    cur, ck = nxt, nk
                oth, ok = bufs[(g + 1) % 2]
                P.op("dve", (lambda e, cur=cur, oth=oth, hw=hw: e.tensor_tensor(
                    out=oth[:, 8:PW - 8], in0=cur[:, 8 - hw:PW - 8 - hw], in1=IC[:, 8:PW - 8], op=ALU.mult)),
                    r=[ck, "IC"], w=[ok])
                P.op("pool", (lambda e, oth=oth: e.tensor_tensor(out=dT[:, 8:PW - 8], in0=oth[:, 8:PW - 8],
                                                                 in1=UP[:, 8:PW - 8], op=ALU.subtract)),
                     r=[ok, "UP"], w=["dT"])
                for b, (t0, ntl) in enumerate(BLOCKS):
                    n = ntl * 128
                    col0 = t0 * 128
                    pc = PC0 if b == 0 else PL0 + (b - 1) * 512
                    bk = b % 2
                    mmg(ps[bk][:, 0:n], [(pw[:, g, :], dT[:, pc:pc + n])], r=["dT", "wgt"], w=["ps%d" % bk])
                    mp = mpo[b % 2]
                    P.op("act", (lambda e, bk=bk, n=n, g=g, mp=mp: e.activation(
                        out=mp[:, 0:n], in_=ps[bk][:, 0:n], func=AF.Identity, scale=psc[:, g:g + 1])),
                        r=["ps%d" % bk, "psc"], w=["mpo%d" % (b % 2)])
                    dma("pool", MIXD[g][:, col0:col0 + n], mp[:, 0:n], r=["mpo%d" % (b % 2)], w=["MIXD"])
            P.barrier()
            top[0] = keep
            if stop == "l0a2":
                return
            lamt = T([128, 4, 64])
            lj = T([128, 64])
            lsum = T([128, 2])
            nlam = T([128, 1])
            dma("sp", lamt.rearrange("p a b -> p (a b)"),
                din["diff_lambda"][0].rearrange("a b -> (a b)").partition_broadcast(128), w=["lamt"])
            for i in range(2):
                P.op("dve", (lambda e, i=i: e.scalar_tensor_tensor(out=lj, in0=lamt[:, 2 * i, :], scalar=1.0,
                                                                   in1=lamt[:, 2 * i + 1, :], op0=ALU.mult, op1=ALU.mult,
                                                                   accum_out=lsum[:, i:i + 1])),
                     r=["lamt"], w=["lj", "lsum"])
            P.op("act", lambda e: e.activation(out=lsum, in_=lsum, func=AF.Exp), r=["lsum"], w=["lsum"])
            P.op("dve", lambda e: e.tensor_tensor(out=nlam, in0=lsum[:, 1:2], in1=lsum[:, 0:1], op=ALU.subtract),
                 r=["lsum"], w=["nlam"])
            P.op("dve", lambda e: e.tensor_scalar(out=nlam, in0=nlam, scalar1=-LAM_INIT, scalar2=None, op0=ALU.add),
                 r=["nlam"], w=["nlam"])
            sln = T([128, 1])
            dma_nc("sp", sln, din["diff_subln"][0].rearrange("(p o) -> p o", o=1), w=["sln"])
            P.op("dve", lambda e: e.tensor_scalar(out=sln, in0=sln, scalar1=1.0 - LAM_INIT, scalar2=None, op0=ALU.mult),
                 r=["sln"], w=["sln"])
            qtl = [T([128, 4, 512], BF16) for _ in range(3)]
            MIXA = [T([128, 4, 512], BF16) for _ in range(2)]
            rsum = T([128, 512])
            o2 = [T([128, 512]) for _ in range(2)]
            oc = T([128, 512])
            sq = T([128, 512], BF16)
            rstd = T([128, 512])
            state = {}

            def pre(b):
                if state.get("b") != b:
                    state["b"] = b
                    n = BLOCKS[b][1] * 128
                    col0 = BLOCKS[b][0] * 128
                    dma("sp", qtl[b % 3][:, :, 0:n], QT[b][:, :, 0:n], r=["QT"], w=["qtl%d" % (b % 3)])

            def post(b, h, n, acc, scr):
                mixa = MIXA[b % 2]
                km = "MIXA%d" % (b % 2)
                for j in range(2):
                    ob, sb2 = acc[j]
                    P.op("dve", (lambda e, sb2=sb2: e.reciprocal(out=rsum[:, 0:n], in_=ps[sb2][:, 0:n])),
                         r=["ps%d" % sb2], w=["rsum", "ps%d" % sb2])
                    P.op("dve", (lambda e, ob=ob, j=j: e.tensor_tensor(out=o2[j][:, 0:n], in0=ps[ob][:, 0:n], in1=rsum[:, 0:n], op=ALU.mult)),
                         r=["ps%d" % ob, "rsum"], w=["o2_%d" % j, "ps%d" % ob])
                P.op("dve", lambda e: e.scalar_tensor_tensor(out=oc[:, 0:n], in0=o2[1][:, 0:n], scalar=nlam,
                                                             in1=o2[0][:, 0:n], op0=ALU.mult, op1=ALU.add),
                     r=["o2_0", "o2_1", "nlam"], w=["oc"])
                P.op("act", lambda e: e.activation(out=sq[:, 0:n], in_=oc[:, 0:n], func=AF.Square), r=["oc"], w=["sq"])
                sbk = scr[0]
                mmg(ps[sbk][:, 0:n], [(onesb, sq[:, 0:n])], r=["sq"], w=["ps%d" % sbk])
                P.op("act", lambda e: e.activation(out=rstd[:, 0:n], in_=ps[sbk][:, 0:n], func=AF.Sqrt, scale=1.0 / 128,
                                                   bias=EPS), r=["ps%d" % sbk], w=["rstd", "ps%d" % sbk])
                P.op("dve", lambda e: e.reciprocal(out=rstd[:, 0:n], in_=rstd[:, 0:n]), r=["rstd"], w=["rstd"])
                P.op("dve", lambda e: e.scalar_tensor_tensor(out=mixa[:, h, 0:n], in0=oc[:, 0:n], scalar=sln,
                                                             in1=rstd[:, 0:n], op0=ALU.mult, op1=ALU.mult),
                     r=["oc", "rstd", "sln"], w=[km])
                col0 = BLOCKS[b][0] * 128
                dma("pool", MIXD[4 + h][:, col0:col0 + n], mixa[:, h, 0:n], r=[km], w=["MIXD"])

            groups = []
            for b in range(9):
                n = BLOCKS[b][1] * 128
                kts = list(range(2)) if b == 0 else list(range(NT))
                for h in range(4):
                    for ki, kt in enumerate(kts):
                        mem = []
                        for j in range(2):
                            pr = slice(64 * j, 64 * j + 64)
                            mem.append(dict(qk=[(KT[pr, h, kt * 128:(kt + 1) * 128], qtl[b % 3][pr, h, 0:n])],
                                            v=Vv[:, kt, h * 128:(h + 1) * 128], acc=j, start=(ki == 0), stop=(ki == len(kts) - 1)))
                        g = dict(n=n, members=mem, rk=["KT", "qtl%d" % (b % 3)], rv=["Vv"], accset=0)
                        if h == 0 and ki == 0:
                            g["pre"] = (lambda b=b: (pre(b), pre(b + 1) if b + 1 < 9 else None))
                        if ki == len(kts) - 1:
                            g["post"] = (lambda acc, scr, b=b, h=h, n=n: post(b, h, n, acc, scr))
                        groups.append(g)
            attention(groups, 0.125, [[(4, 5), (6, 7)]])
            phase_reset()
            phase_outproj(0, list(range(9)), True)

        def rownorm(pt, W, NB, dst, dkey, tg):
            junk, ss, rs = tg
            P.op("act", lambda e: e.activation(out=junk[:, 0:W], in_=pt[0][:, 0:W], func=AF.Square, accum_out=ss),
                 r=[pt[1]], w=["rn_junk", "rn_ss"])
            P.op("act", lambda e: e.activation(out=rs, in_=ss, func=AF.Sqrt, scale=1.0 / W, bias=EPS), r=["rn_ss"], w=["rn_rs"])
            P.op("dve", lambda e: e.reciprocal(out=rs, in_=rs), r=["rn_rs"], w=["rn_rs"])
            P.op("dve", lambda e: e.scalar_tensor_tensor(out=dst, in0=pt[0][:, 0:W], scalar=rs, in1=NB[:, 0:W], op0=ALU.mult,
                                                         op1=ALU.mult), r=[pt[1], "rn_rs", "wgt"], w=[dkey, pt[1]])

        def phase_l1():
            l = 1
            KN = T([128, 4, NTOK], BF16)
            VM = T([128, NT, 512], BF16)
            KR2 = T([128, NTOK], BF16)
            keep = top[0]
            WA = T([128, 8, 512], BF16)
            WB = T([128, 8, 256], BF16)
            WKR = T([128, 8, 128], BF16)
            WQN = T([128, 4, 4, 128], BF16)
            WQR = T([128, 4, 256], BF16)
            WKN = T([128, 2, 4, 128], BF16)
            WVV = T([128, 2, 512], BF16)
            wsrc = din["od_w_in"][0].rearrange("(k p) n -> p k n", p=128)
            wload(WA, wsrc[:, :, 0:512])
            wload(WB, wsrc[:, :, 512:768])
            wload(WKR[:, :, 0:64], wsrc[:, :, 768:832])
            wload(WKR[:, :, 64:128], wsrc[:, :, 768:832])
            uq = din["mla_w_uq"][0].rearrange("(k p) (h c) -> p k h c", p=128, c=192)
            ukv = din["mla_w_ukv"][0].rearrange("(k p) (h c) -> p k h c", p=128, c=256)
            for k in range(4):
                wload(WQN[:, k], uq[:, k, :, 0:128])
                wload(WQR[:, k, :].rearrange("p (h c) -> p h c", h=4), uq[:, k, :, 128:192])
            for k in range(2):
                wload(WKN[:, k], ukv[:, k, :, 0:128])
                wload(WVV[:, k, :].rearrange("p (h c) -> p h c", h=4), ukv[:, k, :, 128:256])
            QNb = T([128, 512])
            KVNb = T([128, 256])
            dma("sp", QNb, din["mla_q_norm"][0].partition_broadcast(128), w=["wgt"])
            dma("sp", KVNb, din["mla_kv_norm"][0].partition_broadcast(128), w=["wgt"])
            nm = NM(l, 0, 1)
            hTs = [T([128, 8, 512], BF16)] * 2
            ro = rope_tiles() + (7,)
            tg = (T([128, 512], BF16), T([128, 1]), T([128, 1]))
            cqn = T([128, 512], BF16)
            ckvn = T([128, 256], BF16)
            cqT = T([128, 4, 512], BF16)
            ckvT = T([128, 2, 512], BF16)
            qnb = [T([128, 4, 512], BF16)] * 2
            qrb = [T([128, 2, 512], BF16)] * 2
            for b, (t0, ntl) in enumerate(BLOCKS):
                n = ntl * 128
                col0 = t0 * 128
                hT = hTs[b % 2]
                khT = "hTs0"
                rope_load(ro, b)
                for ti in range(ntl):
                    tt = t0 + ti
                    nm.run(x_src(False, tt), tt < 2, hT[:, :, ti * 128:(ti + 1) * 128], khT, 6)
                nm.flush()
                dma("pool", HT1[b][:, :, 0:n], hT[:, :, 0:n], r=[khT], w=["HT1"])
                for ti in range(ntl):
                    ts_ = slice(ti * 128, (ti + 1) * 128)
                    mmg(ps[0], [(hT[:, k, ts_], WA[:, k, :]) for k in range(8)], r=[khT, "wgt"], w=["ps0"])
                    rownorm((ps[0], "ps0"), 512, QNb, cqn, "cqn", tg)

                    def trq(e):
                        for k in range(4):
                            ins = e.transpose(out=psb[2][:, k * 128:(k + 1) * 128], in_=cqn[:, k * 128:(k + 1) * 128], identity=identb)
                        return ins
                    P.op("pe", trq, r=["cqn"], w=["ps2"])
                    P.op("act", (lambda e, ts_=ts_: e.copy(out=cqT[:, :, ts_], in_=psb[2][:, 0:512].rearrange("p (k n) -> p k n", k=4))),
                         r=["ps2"], w=["cqT"])
                    mmg(ps[1][:, 0:256], [(hT[:, k, ts_], WB[:, k, :]) for k in range(8)], r=[khT, "wgt"], w=["ps1"])
                    rownorm((ps[1], "ps1"), 256, KVNb, ckvn, "ckvn", tg)

                    def trk(e):
                        for k in range(2):
                            ins = e.transpose(out=psb[3][:, k * 128:(k + 1) * 128], in_=ckvn[:, k * 128:(k + 1) * 128], identity=identb)
                        return ins
                    P.op("pe", trk, r=["ckvn"], w=["ps3"])
                    P.op("act", (lambda e, ts_=ts_: e.copy(out=ckvT[:, :, ts_], in_=psb[3][:, 0:256].rearrange("p (k n) -> p k n", k=2))),
                         r=["ps3"], w=["ckvT"])
                for ti in range(ntl):
                    tt = t0 + ti
                    ts_ = slice(ti * 128, (ti + 1) * 128)
                    mmg(ps[0], [(ckvT[:, k, ts_], WVV[:, k, :]) for k in range(2)], r=["ckvT", "wgt"], w=["ps0"])
                    P.op("act", (lambda e, tt=tt: e.copy(out=VM[:, tt, :], in_=ps[0])), r=["ps0"], w=["VM", "ps0"])
                for h in range(4):
                    bk = 4 + (h % 2)
                    mmg(ps[bk][:, 0:n], [(WKN[:, k, h, :], ckvT[:, k, 0:n]) for k in range(2)], r=["ckvT", "wgt"], w=["ps%d" % bk])
                    P.op("act", (lambda e, h=h, bk=bk, n=n, col0=col0: e.copy(out=KN[:, h, col0:col0 + n], in_=ps[bk][:, 0:n])),
                         r=["ps%d" % bk], w=["KN", "ps%d" % bk])
                mmg(ps[4][:, 0:n], [(WKR[:, k, :], hT[:, k, 0:n]) for k in range(8)], r=[khT, "wgt"], w=["ps4"])
                rope_evict(4, n, col0, KR2[:, col0:col0 + n], "KR2", ro, b == 0)
                if b >= 1:
                    qn, qr = qnb[b % 2], qrb[b % 2]
                    for h in range(4):
                        bk = 4 + (h % 2)
                        mmg(ps[bk][:, 0:n], [(WQN[:, k, h, :], cqT[:, k, 0:n]) for k in range(4)], r=["cqT", "wgt"], w=["ps%d" % bk])
                        P.op("act", (lambda e, h=h, bk=bk, n=n, qn=qn: e.copy(out=qn[:, h, 0:n], in_=ps[bk][:, 0:n])),
                             r=["ps%d" % bk], w=["qnb0", "ps%d" % bk])
                    dma("pool", QT[b][:, :, 0:n], qn[:, :, 0:n], r=["qnb0"], w=["QT"])
                    for c in range(2):
                        bk = 4 + (c % 2)
                        mmg(ps[bk][:, 0:n], [(WQR[:, k, c * 128:(c + 1) * 128], cqT[:, k, 0:n]) for k in range(4)],
                            r=["cqT", "wgt"], w=["ps%d" % bk])
                        rope_evict(bk, n, col0, qr[:, c, 0:n], "qrb0", ro, False)
                    dma("pool", QR[b][:, :, 0:n], qr[:, :, 0:n], r=["qrb0"], w=["QR"])
            P.barrier()
            top[0] = keep
            if stop == "l1b1":
                return
            qnl = [T([128, 4, 512], BF16) for _ in range(3)]
            qrl = [T([128, 2, 512], BF16) for _ in range(3)]
            rsum = T([128, 512])
            mo = [T([128, 512], BF16) for _ in range(2)]
            state = {}

            def pre(b):
                if state.get("b") != b:
                    state["b"] = b
                    n = BLOCKS[b][1] * 128
                    dma("sp", qnl[b % 3][:, :, 0:n], QT[b][:, :, 0:n], r=["QT"], w=["qnl%d" % (b % 3)])
                    dma("sp", qrl[b % 3][:, :, 0:n], QR[b][:, :, 0:n], r=["QR"], w=["qnl%d" % (b % 3)])

            def post(b, h, n, acc, scr):
                col0 = BLOCKS[b][0] * 128
                m = mo[h % 2]
                km = "mo%d" % (h % 2)
                ob, sb2 = acc[0]
                P.op("dve", lambda e: e.reciprocal(out=rsum[:, 0:n], in_=ps[sb2][:, 0:n]), r=["ps%d" % sb2], w=["rsum", "ps%d" % sb2])
                P.op("dve", lambda e: e.tensor_tensor(out=m[:, 0:n], in0=ps[ob][:, 0:n], in1=rsum[:, 0:n], op=ALU.mult),
                     r=["ps%d" % ob, "rsum"], w=[km, "ps%d" % ob])
                dma("pool", MIXD[h][:, col0:col0 + n], m[:, 0:n], r=[km], w=["MIXD"])

            groups = []
            gi = 0
            for b in range(1, 9):
                n = 512
                for h in range(4):
                    pr = slice(64 * (h % 2), 64 * (h % 2) + 64)
                    for kp in range(NT // 2):
                        mem = []
                        for mi in range(2):
                            kt = 2 * kp + mi
                            ks = slice(kt * 128, (kt + 1) * 128)
                            mem.append(dict(qk=[(KN[:, h, ks], qnl[b % 3][:, h, 0:n]), (KR2[pr, ks], qrl[b % 3][pr, h // 2, 0:n])],
                                            v=VM[:, kt, h * 128:(h + 1) * 128], acc=0,
                                            start=(kp == 0 and mi == 0), stop=(kp == NT // 2 - 1 and mi == 1)))
                        g = dict(n=n, members=mem, rk=["KN", "KR2", "qnl%d" % (b % 3)], rv=["VM"], accset=gi % 2)
                        if h == 0 and kp == 0:
                            g["pre"] = (lambda b=b: (pre(b), pre(b + 1) if b + 1 < 9 else None))
                        if kp == NT // 2 - 1:
                            g["post"] = (lambda acc, scr, b=b, h=h, n=n: post(b, h, n, acc, scr))
                        groups.append(g)
                    gi += 1
            attention(groups, 192.0 ** -0.5, [[(4, 5)], [(6, 7)]])
            phase_reset()
            if stop == "l1b2":
                return
            LB = T([128, 2, 2, 4])
            lbv = T([128, 2, 4])
            oml = T([128, 2, 4])
            hgn = T([128, 1])
            ones1 = T([128, 1])
            for d in range(2):
                for ll in range(2):
                    dma_nc("sp", LB[:, d, ll, :], din["hgrn_lb"][d, ll].rearrange("(h p) -> p h", p=128), w=["LB"])
            dma_nc("sp", hgn, din["hgrn_norm"][0].rearrange("(p o) -> p o", o=1), w=["hgn"])
            P.op("dve", lambda e: e.tensor_tensor(out=lbv, in0=LB[:, :, 1, :], in1=LB[:, :, 0, :], op=ALU.subtract), r=["LB"], w=["lbv"])
            P.op("act", lambda e: e.activation(out=lbv, in_=lbv, func=AF.Sigmoid), r=["lbv"], w=["lbv"])
            P.op("dve", lambda e: e.tensor_scalar(out=oml, in0=lbv, scalar1=-1.0, scalar2=1.0, op0=ALU.mult, op1=ALU.add),
                 r=["lbv"], w=["oml"])
            P.op("pool", lambda e: e.memset(ones1, 1.0), w=["ones1"])
            WH = T([128, 8, 5, 128], BF16)
            hTl = [T([128, 8, 512], BF16)] * 2
            SG = T([128, NTOK], BF16)
            Vt = T([128, NT, 128], BF16)
            QP = [T([128, NTOK], BF16) for _ in range(2)]
            QPP = [T([128, NTOK], BF16) for _ in range(2)]
            KP = [T([128, NTOK], BF16) for _ in range(2)]
            KPt = [T([128, NT, 128], BF16) for _ in range(2)]
            EL = [T([128, 68]) for _ in range(2)]
            OT = [T([128, NTOK]) for _ in range(2)]
            qf = T([128, 512])
            sgm = T([128, 512])
            kk = T([128, 512])
            lf = T([128, 512])
            gb = T([128, 516])
            Da = T([128, 512])
            Db = T([128, 512])
            E1 = T([128, 512])
            E2 = T([128, 512])
            E3 = T([128, 512])
            elt = T([128, 8])
            Sf = [[T([128, 128]) for _ in range(2)] for _ in range(2)]
            Sb = [[T([128, 128], BF16) for _ in range(2)] for _ in range(2)]
            Am = [[T([128, 64], BF16) for _ in range(2)] for _ in range(2)]
            osum, rstd, otmp = Da, Db, E1
            sq = T([128, 512], BF16)
            mh = [T([128, 512], BF16) for _ in range(2)]
            P.op("pool", lambda e: e.memset(gb[:, 0:1], 0.0), w=["gb0"])
            wsrc = din["od_w_in"][0].rearrange("(k p) n -> p k n", p=128)
            for h in range(4):
                for i, c0_ in enumerate((832, 1344, 1856, 2368, 2880)):
                    wload(WH[:, :, i, :], wsrc[:, :, c0_ + h * 128:c0_ + (h + 1) * 128])
                for b, (t0, ntl) in enumerate(BLOCKS):
                    n = ntl * 128
                    nch = n // 64
                    col0 = t0 * 128
                    ch0 = col0 // 64
                    hT = hTl[b % 2]
                    khT = "hTl0"
                    dma("sp", hT[:, :, 0:n], HT1[b][:, :, 0:n], r=["HT1"], w=[khT])
                    mmg(ps[0][:, 0:n], [(WH[:, k, 0, :], hT[:, k, 0:n]) for k in range(8)], r=[khT, "wgt"], w=["ps0"])
                    P.op("act", (lambda e, n=n: e.activation(out=qf[:, 0:n], in_=ps[0][:, 0:n], func=AF.Silu)), r=["ps0"], w=["qf", "ps0"])
                    mmg(ps[1][:, 0:n], [(WH[:, k, 4, :], hT[:, k, 0:n]) for k in range(8)], r=[khT, "wgt"], w=["ps1"])
                    P.op("act", (lambda e, n=n, col0=col0: e.activation(out=SG[:, col0:col0 + n], in_=ps[1][:, 0:n], func=AF.Silu)),
                         r=["ps1"], w=["SG", "ps1"])
                    for ti in range(ntl):
                        tt = t0 + ti
                        ts_ = slice(ti * 128, (ti + 1) * 128)
                        mmg(ps[2][:, 0:128], [(hT[:, k, ts_], WH[:, k, 3, :]) for k in range(8)], r=[khT, "wgt"], w=["ps2"])
                        P.op("act", (lambda e, tt=tt: e.copy(out=Vt[:, tt, :], in_=ps[2][:, 0:128])), r=["ps2"], w=["Vt", "ps2"])
                    for d in range(2):
                        bk = 3 + d
                        kb = "ps%d" % bk
                        mmg(ps[bk][:, 0:n], [(WH[:, k, 1 + d, :], hT[:, k, 0:n]) for k in range(8)], r=[khT, "wgt"], w=[kb])
                        P.op("act", (lambda e, n=n, bk=bk: e.activation(out=sgm[:, 0:n], in_=ps[bk][:, 0:n], func=AF.Sigmoid)),
                             r=[kb], w=["sgm", kb])
                        P.op("dve", (lambda e, n=n, d=d, h=h: e.tensor_scalar(out=sgm[:, 0:n], in0=sgm[:, 0:n], scalar1=oml[:, d, h:h + 1],
                                                                             scalar2=lbv[:, d, h:h + 1], op0=ALU.mult, op1=ALU.add)),
                             r=["sgm", "oml", "lbv"], w=["sgm"])
                        P.op("pool", (lambda e, n=n: e.tensor_scalar(out=kk[:, 0:n], in0=sgm[:, 0:n], scalar1=-1.0, scalar2=1.0,
                                                                    op0=ALU.mult, op1=ALU.add)), r=["sgm"], w=["kk"])
                        P.op("act", (lambda e, n=n: e.activation(out=lf[:, 0:n], in_=sgm[:, 0:n], func=AF.Ln)), r=["sgm"], w=["lf"])
                        P.op("dve", (lambda e, n=n: e.tensor_tensor_scan(out=gb[:, 1:1 + n], data0=ones1[:, 0:1].to_broadcast([128, n]),
                                                                        data1=lf[:, 0:n], initial=0.0, op0=ALU.mult, op1=ALU.add)),
                             r=["lf", "ones1", "gb0"], w=["gb"])
                        Gi3 = gb[:, 1:1 + n].rearrange("p (c j) -> p c j", j=64)
                        Gs3 = gb[:, 0:n].rearrange("p (c j) -> p c j", j=64)
                        S0 = Gs3[:, :, 0:1].to_broadcast([128, nch, 64])
                        I63 = Gi3[:, :, 63:64].to_broadcast([128, nch, 64])
                        Da3 = Da[:, 0:n].rearrange("p (c j) -> p c j", j=64)
                        Db3 = Db[:, 0:n].rearrange("p (c j) -> p c j", j=64)
                        if d == 0:
                            P.op("dve", (lambda e, Gi3=Gi3, S0=S0, Da3=Da3: e.tensor_tensor(out=Da3, in0=Gi3, in1=S0, op=ALU.subtract)),
                                 r=["gb"], w=["Da"])
                            P.op("dve", (lambda e, Gi3=Gi3, I63=I63, Db3=Db3: e.tensor_tensor(out=Db3, in0=Gi3, in1=I63, op=ALU.subtract)),
                                 r=["gb"], w=["Db"])
                            sc1, sc2, sc3 = 1.0, -1.0, 1.0
                        else:
                            P.op("dve", (lambda e, Gs3=Gs3, I63=I63, Da3=Da3: e.tensor_tensor(out=Da3, in0=Gs3, in1=I63, op=ALU.subtract)),
                                 r=["gb"], w=["Da"])
                            P.op("dve", (lambda e, Gs3=Gs3, S0=S0, Db3=Db3: e.tensor_tensor(out=Db3, in0=Gs3, in1=S0, op=ALU.subtract)),
                                 r=["gb"], w=["Db"])
                            sc1, sc2, sc3 = -1.0, 1.0, -1.0
                        P.op("dve", (lambda e, Gi3=Gi3, Gs3=Gs3, nch=nch: e.tensor_tensor(out=elt[:, 0:nch], in0=Gi3[:, :, 63], in1=Gs3[:, :, 0],
                                                                                          op=ALU.subtract)), r=["gb"], w=["elt"])
                        P.op("act", (lambda e, n=n, sc1=sc1: e.activation(out=E1[:, 0:n], in_=Da[:, 0:n], func=AF.Exp, scale=sc1)), r=["Da"], w=["E1"])
                        P.op("act", (lambda e, n=n, sc2=sc2: e.activation(out=E2[:, 0:n], in_=Db[:, 0:n], func=AF.Exp, scale=sc2)), r=["Db"], w=["E2"])
                        P.op("act", (lambda e, n=n, sc3=sc3: e.activation(out=E3[:, 0:n], in_=Db[:, 0:n], func=AF.Exp, scale=sc3)), r=["Db"], w=["E3"])
                        P.op("act", (lambda e, d=d, ch0=ch0, nch=nch: e.activation(out=EL[d][:, ch0:ch0 + nch], in_=elt[:, 0:nch], func=AF.Exp)),
                             r=["elt"], w=["EL%d" % d])
                        P.op("dve", (lambda e, n=n, d=d, col0=col0: e.tensor_tensor(out=QP[d][:, col0:col0 + n], in0=qf[:, 0:n], in1=E1[:, 0:n], op=ALU.mult)),
                             r=["qf", "E1"], w=["QP%d" % d])
                        P.op("dve", (lambda e, n=n, d=d, col0=col0: e.tensor_tensor(out=KP[d][:, col0:col0 + n], in0=kk[:, 0:n], in1=E2[:, 0:n], op=ALU.mult)),
                             r=["kk", "E2"], w=["KP%d" % d])
                        P.op("pool", (lambda e, n=n, d=d, col0=col0: e.tensor_tensor(out=QPP[d][:, col0:col0 + n], in0=qf[:, 0:n], in1=E3[:, 0:n], op=ALU.mult)),
                             r=["qf", "E3"], w=["QPP%d" % d])
                        for ti in range(ntl):
                            tt = t0 + ti

                            def trp(e, d=d, tt=tt):
                                return e.transpose(out=psb[6][:, 0:128], in_=KP[d][:, tt * 128:(tt + 1) * 128], identity=identb)
                            P.op("pe", trp, r=["KP%d" % d], w=["ps6"])
                            P.op("act", (lambda e, d=d, tt=tt: e.copy(out=KPt[d][:, tt, :], in_=psb[6][:, 0:128])), r=["ps6"], w=["KPt%d" % d, "ps6"])
                orders = [list(range(68)), [3, 2, 1, 0] + list(range(67, 3, -1))]
                first = [True, True]
                cur = [0, 0]
                for step in range(68):
                    for d in range(2):
                        c = orders[d][step]
                        tt, hh = c // 2, c % 2
                        rows = slice(64 * hh, 64 * hh + 64)
                        cols = slice(64 * c, 64 * c + 64)
                        if c < 4:
                            pos, blk_n, blk_c0 = c, 256, 0
                        else:
                            pos, blk_n, blk_c0 = (c - 4) % 8, 512, 256 + ((c - 4) // 8) * 512
                        pA, pU, pO = ps[d], ps[2 + d], ps[4 + d]
                        kA, kU, kO = "ps%d" % d, "ps%d" % (2 + d), "ps%d" % (4 + d)
                        am = Am[d][step % 2]
                        kam = "Am%d_%d" % (d, step % 2)
                        mmg(pA[rows, 0:64], [(KP[d][:, cols], QPP[d][:, cols])], r=["KP%d" % d, "QPP%d" % d], w=[kA])
                        mk = maskf if d == 0 else maskb
                        P.op("dve", (lambda e, pA=pA, rows=rows, am=am, mk=mk: e.tensor_tensor(out=am[rows, :], in0=pA[rows, 0:64], in1=mk[rows, :],
                                                                                              op=ALU.mult)), r=[kA], w=[kam, kA])
                        so, sn = cur[d], 1 - cur[d]
                        prs = [(Vt[rows, tt, :], am[rows, :])]
                        rr = ["Vt", kam]
                        if not first[d]:
                            prs.append((Sb[d][so], QP[d][:, cols]))
                            rr += ["Sb%d_%d" % (d, so), "QP%d" % d]
                        mmg(pO[:, pos * 64:(pos + 1) * 64], prs, r=rr, w=[kO])
                        mmg(pU[:, 0:128], [(KPt[d][rows, tt, :], Vt[rows, tt, :])], r=["KPt%d" % d, "Vt"], w=[kU])
                        if first[d]:
                            P.op("dve", (lambda e, d=d, sn=sn, pU=pU: e.tensor_copy(out=Sf[d][sn], in_=pU[:, 0:128])),
                                 r=[kU], w=["Sf%d_%d" % (d, sn), kU])
                        else:
                            P.op("dve", (lambda e, d=d, sn=sn, so=so, pU=pU, c=c: e.scalar_tensor_tensor(
                                out=Sf[d][sn], in0=Sf[d][so], scalar=EL[d][:, c:c + 1], in1=pU[:, 0:128], op0=ALU.mult, op1=ALU.add)),
                                r=[kU, "Sf%d_%d" % (d, so), "EL%d" % d], w=["Sf%d_%d" % (d, sn), kU])
                        P.op("act", (lambda e, d=d, sn=sn: e.copy(out=Sb[d][sn], in_=Sf[d][sn])), r=["Sf%d_%d" % (d, sn)], w=["Sb%d_%d" % (d, sn)])
                        cur[d] = sn
                        first[d] = False
                        last_in_blk = (pos == (blk_n // 64 - 1)) if d == 0 else (pos == 0)
                        if last_in_blk:
                            P.op("act", (lambda e, d=d, pO=pO, blk_n=blk_n, blk_c0=blk_c0: e.copy(out=OT[d][:, blk_c0:blk_c0 + blk_n], in_=pO[:, 0:blk_n])),
                                 r=[kO], w=["OT%d" % d, kO])
                for b in range(1, 9):
                    n = 512
                    col0 = BLOCKS[b][0] * 128
                    cs = slice(col0, col0 + n)
                    m = mh[b % 2]
                    km = "mh%d" % (b % 2)
                    P.op("pool", (lambda e, cs=cs: e.tensor_tensor(out=osum, in0=OT[0][:, cs], in1=OT[1][:, cs], op=ALU.add)),
                         r=["OT0", "OT1"], w=["Da"])
                    P.op("act", lambda e: e.activation(out=sq, in_=osum, func=AF.Square), r=["Da"], w=["sq"])
                    mmg(ps[7], [(onesb, sq)], r=["sq"], w=["ps7"])
                    P.op("act", lambda e: e.activation(out=rstd, in_=ps[7], func=AF.Sqrt, scale=1.0 / 128, bias=EPS), r=["ps7"], w=["Db", "ps7"])
                    P.op("dve", lambda e: e.reciprocal(out=rstd, in_=rstd), r=["Db"], w=["Db"])
                    P.op("dve", lambda e: e.scalar_tensor_tensor(out=otmp, in0=osum, scalar=hgn, in1=rstd, op0=ALU.mult, op1=ALU.mult),
                         r=["Da", "Db", "hgn"], w=["E1"])
                    P.op("dve", (lambda e, cs=cs, m=m: e.tensor_tensor(out=m, in0=otmp, in1=SG[:, cs], op=ALU.mult)), r=["E1", "SG"], w=[km])
                    dma("pool", MIXD[4 + h][:, cs], m, r=[km], w=["MIXD"])
            phase_reset()
            if stop == "l1b3":
                return
            phase_outproj(1, list(range(1, 9)), False)

        def phase_outproj(l, blks, from_inputs):
            wo = T([128, 8, D], BF16)
            wosrc = din["mix_w_out"][l].rearrange("(k p) n -> p k n", p=128)
            for c in range(2):
                wload(wo[:, :, c * 512:(c + 1) * 512], wosrc[:, :, c * 512:(c + 1) * 512])
            res = RES(l, 2)
            mix = [T([128, 8, 512], BF16) for _ in range(2)]
            ypairs = [(0, 1), (2, 3), (4, 5), (6, 7)]
            yi = 0
            for b in blks:
                t0, ntl = BLOCKS[b]
                n = ntl * 128
                col0 = t0 * 128
                mx = mix[b % 2]
                kx = "mix%d" % (b % 2)
                for k in range(8):
                    dma("sp", mx[:, k, 0:n], MIXD[k][:, col0:col0 + n], r=["MIXD"], w=[kx])
                for ti in range(ntl):
                    tt = t0 + ti
                    ls = slice(ti * 128, (ti + 1) * 128)
                    y0, y1 = ypairs[yi % 4]
                    yi += 1
                    for hf, yb in ((0, y0), (1, y1)):
                        mmg(ps[yb], [(mx[:, k, ls], wo[:, k, hf * 512:(hf + 1) * 512]) for k in range(8)],
                            r=[kx, "wgt"], w=["ps%d" % yb])
                    res.run(y0, y1, x_src(from_inputs, tt), XR[tt * 128:(tt + 1) * 128, :], tt < 2, "XR1")

        phase_mod()
        phase_reset()
        S0 = ("mod", "l0a1", "l0a2", "l0mix")
        S1 = S0 + ("l0", "l1b1", "l1b2", "l1b3", "l1mix")
        if stop != "mod":
            phase_l0()
            phase_reset()
        if stop not in S0:
            phase_ffn(0, True, False)
            phase_reset()
        if stop not in S0 + ("l0",):
            phase_l1()
            phase_reset()
        if stop not in S1:
            phase_ffn(1, False, True)
            phase_reset()
        P.emit(st)
    return nc


_CACHE = {}


def kernel(**inputs):
    consts = _consts()
    if "nc" not in _CACHE:
        _CACHE["nc"] = build()
    nc = _CACHE["nc"]
    in_maps = []
    for b in range(8):
        m = {"x": np.ascontiguousarray(inputs["x"][b]), "ctx": np.ascontiguousarray(inputs["ctx"][b]),
             "cvec": np.ascontiguousarray(np.stack([inputs["c"][b], inputs["c_ctx"]], 0))}
        for n in W_NAMES:
            m[n] = np.ascontiguousarray(inputs[n])
        m.update(consts)
        in_maps.append(m)
    res = run_bass_kernel_spmd(nc, in_maps, core_ids=list(range(8)))
    return np.stack([r["out"] for r in res.results], 0).astype(np.float32)
```

```python
import numpy as np
from contextlib import ExitStack
import concourse.bass as bass
import concourse.mybir as mybir
from concourse.bass_utils import run_bass_kernel_spmd

F32 = mybir.dt.float32
BF16 = mybir.dt.bfloat16
ALU = mybir.AluOpType
AF = mybir.ActivationFunctionType

NDSEM = 8
D = 1024
NT = 34
NTOK = 4352
DFF = 2816
NJ = 22
EPS = 1e-6


class Prog:
    ENGS = ("pe", "dve", "act", "pool", "sp")

    def __init__(self, nc):
        self.nc = nc
        self.ops = []
        self.lw = {}
        self.rd = {}
        self.cnt = {e: 0 for e in self.ENGS}
        self.dcnt = {e: 0 for e in self.ENGS}
        self.dslot_last = {e: [None] * NDSEM for e in self.ENGS}
        self.last_nd = {e: None for e in self.ENGS}
        self.pending_bar = {e: [] for e in self.ENGS}

    def op(self, eng, fn, r=(), w=(), dma=False):
        oid = len(self.ops)
        deps = []
        for k in r:
            y = self.lw.get(k)
            if y is not None:
                deps.append((y, "RAW"))
        for k in w:
            y = self.lw.get(k)
            if y is not None:
                deps.append((y, "WAW"))
            for y in self.rd.get(k, ()):
                deps.append((y, "WAR"))
        for y in self.pending_bar[eng]:
            deps.append((y, "RAW"))
        self.pending_bar[eng] = []
        o = dict(id=oid, eng=eng, fn=fn, deps=deps, dma=dma)
        if dma:
            i = self.dcnt[eng]
            self.dcnt[eng] += 1
            slot = i % NDSEM
            o["dslot"] = slot
            o["dval"] = 16 * (i // NDSEM + 1)
            prev = self.dslot_last[eng][slot]
            if prev is not None:
                deps.append((prev, "RAW"))
            self.dslot_last[eng][slot] = oid
        else:
            self.cnt[eng] += 1
            o["val"] = self.cnt[eng]
            self.last_nd[eng] = oid
        self.ops.append(o)
        for k in w:
            self.lw[k] = oid
            self.rd[k] = []
        for k in r:
            if k not in w:
                self.rd.setdefault(k, []).append(oid)
        return oid

    def barrier(self):
        snap = []
        for e in self.ENGS:
            if self.last_nd[e] is not None:
                snap.append(self.last_nd[e])
            for y in self.dslot_last[e]:
                if y is not None:
                    snap.append(y)
        for e in self.ENGS:
            self.pending_bar[e] = list(snap)
        self.lw = {}
        self.rd = {}

    def emit(self, st):
        nc = self.nc
        sems = {e: st.enter_context(nc.semaphore("s_" + e)) for e in self.ENGS}
        dsems = {e: [st.enter_context(nc.semaphore("d_%s%d" % (e, i))) for i in range(NDSEM)]
                 for e in ("sp", "pool", "act") if self.dcnt[e] > 0}
        block = st.enter_context(nc.Block())
        ops = self.ops

        def run(ename, eng):
            seen = {}
            for o in ops:
                if o["eng"] != ename:
                    continue
                need = {}
                for (y, kind) in o["deps"]:
                    Y = ops[y]
                    if Y["dma"]:
                        key = ("d", Y["eng"], Y["dslot"])
                        sem = dsems[Y["eng"]][Y["dslot"]]
                        val = Y["dval"]
                    else:
                        if Y["eng"] == ename and not o["dma"]:
                            if ename == "pe" or kind != "RAW":
                                continue
                        key = ("c", Y["eng"])
                        sem = sems[Y["eng"]]
                        val = Y["val"]
                    if seen.get(key, 0) >= val:
                        continue
                    if key not in need or need[key][1] < val:
                        need[key] = (sem, val)
                for key, (sem, val) in need.items():
                    eng.wait_ge(sem, val)
                    seen[key] = val
                ins = o["fn"](eng)
                if o["dma"]:
                    ins.then_inc(dsems[ename][o["dslot"]], 16)
                else:
                    ins.then_inc(sems[ename], 1)
            if ename in dsems:
                for slot in range(NDSEM):
                    y = self.dslot_last[ename][slot]
                    if y is not None:
                        Y = ops[y]
                        if seen.get(("d", ename, slot), 0) < Y["dval"]:
                            eng.wait_ge(dsems[ename][slot], Y["dval"])

        @block.tensor
        def _(e):
            run("pe", e)

        @block.vector
        def _(e):
            run("dve", e)

        @block.scalar
        def _(e):
            run("act", e)

        @block.gpsimd
        def _(e):
            run("pool", e)

        @block.sync
        def _(e):
            run("sp", e)


PW = 8 + 256 + 16 + 4096 + 8
PC0, PL0 = 8, 280


def _consts():
    c = {}
    c["ident"] = np.eye(128, dtype=np.float32)
    rot = np.zeros((128, 128), np.float32)
    for d in range(128):
        rot[d ^ 16, d] = 1.0
    c["rot"] = rot
    n = 4096
    pos_row = np.repeat(np.arange(n // 64), 64)
    pos_col = np.tile(np.arange(64), n // 64)
    inv_freq = (10000.0 ** (-np.arange(0, 32, 2, dtype=np.float32) / 32)).astype(np.float32)
    ang = np.stack([pos_row, pos_col], -1).astype(np.float32)[..., None] * inv_freq
    cs, sn = np.cos(ang).astype(np.float32), np.sin(ang).astype(np.float32)
    cos_t = np.zeros((128, n), np.float32)
    sin_t = np.zeros((128, n), np.float32)
    for d in range(128):
        dd = d % 64
        a, hf, i = dd // 32, (dd // 16) % 2, dd % 16
        cos_t[d] = cs[:, a, i]
        sin_t[d] = sn[:, a, i] * (-1.0 if hf == 0 else 1.0)
    c["cos_t"] = cos_t
    c["sin_t"] = sin_t
    inv = np.zeros((4, PW), np.float32)
    for g, w in enumerate((2, 4, 8, 16)):
        h = w // 2
        for (n_, off) in ((256, PC0), (4096, PL0)):
            t = np.arange(n_)
            lo = np.clip(t - h, 0, n_)
            hi = np.clip(t + h, 0, n_)
            inv[g, off:off + n_] = 1.0 / (hi - lo).astype(np.float32)
    c["invcnt"] = inv
    p = np.arange(128)[:, None] % 64
    t = np.arange(64)[None, :]
    c["mask_f"] = (p <= t).astype(np.float32)
    c["mask_b"] = (p >= t).astype(np.float32)
    return c


W_NAMES = ["ada_w", "ada_b", "norm_g", "mix_w_out", "ffn_w_gate", "ffn_w_up", "ffn_conv_w",
           "ffn_conv_b", "ffn_w_down", "ev_w_in", "pool_w", "pool_scale", "diff_lambda",
           "diff_subln", "od_w_in", "mla_q_norm", "mla_w_uq", "mla_kv_norm", "mla_w_ukv",
           "hgrn_norm", "hgrn_lb"]
W_SHAPES = {"ada_w": (2, 1024, 6144), "ada_b": (2, 6144), "norm_g": (2, 4, 1024),
            "mix_w_out": (2, 1024, 1024), "ffn_w_gate": (2, 1024, 2816), "ffn_w_up": (2, 1024, 2816),
            "ffn_conv_w": (2, 3, 2816), "ffn_conv_b": (2, 2816), "ffn_w_down": (2, 2816, 1024),
            "ev_w_in": (1, 1024, 2048), "pool_w": (1, 4, 128, 128), "pool_scale": (1, 512),
            "diff_lambda": (1, 4, 64), "diff_subln": (1, 128), "od_w_in": (1, 1024, 3392),
            "mla_q_norm": (1, 512), "mla_w_uq": (1, 512, 768), "mla_kv_norm": (1, 256),
            "mla_w_ukv": (1, 256, 1024), "hgrn_norm": (1, 128), "hgrn_lb": (2, 2, 512)}
C_SHAPES = {"ident": (128, 128), "rot": (128, 128), "cos_t": (128, 4096), "sin_t": (128, 4096),
            "invcnt": (4, PW), "mask_f": (128, 64), "mask_b": (128, 64)}

BLOCKS = [(0, 2)] + [(2 + 4 * i, 4) for i in range(8)]


def build(stop=None, dbg=False):
    nc = bass.Bass("TRN2", target_bir_lowering=False)
    din = {}
    din["x"] = nc.dram_tensor("x", [4096, D], F32, kind="ExternalInput").ap()
    din["ctx"] = nc.dram_tensor("ctx", [256, D], F32, kind="ExternalInput").ap()
    din["cvec"] = nc.dram_tensor("cvec", [2, D], F32, kind="ExternalInput").ap()
    for n in W_NAMES:
        din[n] = nc.dram_tensor(n, list(W_SHAPES[n]), F32, kind="ExternalInput").ap()
    for n in C_SHAPES:
        din[n] = nc.dram_tensor(n, list(C_SHAPES[n]), F32, kind="ExternalInput").ap()
    out = nc.dram_tensor("out", [4096, D], F32, kind="ExternalOutput").ap()
    XR = nc.dram_tensor("XR", [NTOK, D], F32, kind="ExternalOutput" if dbg else "Internal").ap()
    MODV = nc.dram_tensor("MODV", [2, 2, 6, D], F32, kind="Internal").ap()
    QT = nc.dram_tensor("QT", [9, 128, 4, 512], BF16, kind="Internal").ap()
    QR = nc.dram_tensor("QR", [9, 128, 2, 512], BF16, kind="Internal").ap()
    UT = nc.dram_tensor("UT", [4, 128, NTOK], F32, kind="Internal").ap()
    HT1 = nc.dram_tensor("HT1", [9, 128, 8, 512], BF16, kind="Internal").ap()
    H2D = nc.dram_tensor("H2D", [128, 8, 4355], BF16, kind="Internal").ap()
    MIXD = nc.dram_tensor("MIXD", [8, 128, NTOK], BF16, kind="ExternalOutput" if dbg else "Internal").ap()

    st = ExitStack()
    with st:
        P = Prog(nc)
        AW = 52000
        arena = st.enter_context(nc.sbuf_tensor("arena", [128, AW], F32))
        psall = st.enter_context(nc.psum_tensor("psall", [128, 4096], F32))[:]
        ps = [psall[:, i * 512:(i + 1) * 512] for i in range(8)]
        psb = [p.bitcast(BF16) for p in ps]
        top = [0]

        def T(shape, dt=F32):
            n = int(np.prod(shape[1:]))
            cols = n if dt == F32 else (n + 1) // 2
            off = top[0]
            top[0] += cols
            assert top[0] <= AW, "SBUF arena overflow %d" % top[0]
            a = arena[0:shape[0], off:off + cols]
            if dt != F32:
                a = a.bitcast(dt)
            if len(shape) == 3:
                a = a.rearrange("p (a b) -> p a b", a=shape[1])
            elif len(shape) == 4:
                a = a.rearrange("p (a b c) -> p a b c", a=shape[1], b=shape[2])
            return a

        uid = [0]

        def K(s):
            uid[0] += 1
            return "%s#%d" % (s, uid[0])

        def dma(q, o, i, r=(), w=()):
            P.op(q, lambda e: e.dma_start(out=o, in_=i), r=r, w=w, dma=True)

        def dma_nc(q, o, i, r=(), w=()):
            P.op(q, lambda e: e.dma_start(out=o, in_=i, allow_slow_non_contiguous=True), r=r, w=w, dma=True)

        def mmg(o, pairs, r, w):
            def f(e):
                n = len(pairs)
                for i, (l, rh) in enumerate(pairs):
                    ins = e.matmul(o, lhsT=l, rhs=rh, start=(i == 0), stop=(i == n - 1))
                return ins
            P.op("pe", f, r=r, w=w)

        identb = T([128, 128], BF16)
        rotb = T([128, 128], BF16)
        onesb = T([128, 128], BF16)
        maskf = T([128, 64])
        maskb = T([128, 64])
        stg = [T([128, 2048]) for _ in range(2)]
        stgi = [0]
        PERSIST = None

        def wload(dst, src, q=None, ce="pool"):
            i = stgi[0] % 2
            stgi[0] += 1
            shp = list(dst.shape)
            n = int(np.prod(shp[1:]))
            if n > 2048:
                hh = shp[-1] // 2
                if len(shp) == 2:
                    wload(dst[:, 0:hh], src[:, 0:hh], q, ce)
                    wload(dst[:, hh:], src[:, hh:], q, ce)
                else:
                    wload(dst[:, :, 0:hh], src[:, :, 0:hh], q, ce)
                    wload(dst[:, :, hh:], src[:, :, hh:], q, ce)
                return
            s = stg[i][0:shp[0], 0:n]
            if len(shp) == 3:
                s = s.rearrange("p (a b) -> p a b", a=shp[1])
            qq = q or ("sp" if i == 0 else "pool")
            dma(qq, s, src, w=["stg%d" % i])
            kd = "W" + str(id(dst))
            if ce == "pool":
                P.op("pool", lambda e: e.tensor_copy(out=dst, in_=s), r=["stg%d" % i], w=["wgt"])
            elif ce == "dve":
                P.op("dve", lambda e: e.tensor_copy(out=dst, in_=s), r=["stg%d" % i], w=["wgt"])
            else:
                P.op("act", lambda e: e.copy(out=dst, in_=s), r=["stg%d" % i], w=["wgt"])

        for (dst, nm) in ((identb, "ident"), (rotb, "rot")):
            wload(dst, din[nm])
        P.op("pool", lambda e: e.memset(onesb, 1.0), w=["wgt"])
        dma("sp", maskf, din["mask_f"], w=["wgt"])
        dma("sp", maskb, din["mask_b"], w=["wgt"])
        PERSIST = top[0]

        def phase_reset():
            P.barrier()
            top[0] = PERSIST

        def phase_mod():
            cv = T([128, 2, 8])
            cvs = T([128, 2, 8])
            cvb = T([128, 8, 2], BF16)
            for j in range(2):
                dma_nc("sp", cv[:, j, :], din["cvec"][j].rearrange("(k p) -> p k", p=128), w=["cv"])
            P.op("act", lambda e: e.activation(out=cvs, in_=cv, func=AF.Silu), r=["cv"], w=["cvs"])
            P.op("dve", lambda e: e.tensor_copy(out=cvb, in_=cvs.rearrange("p j k -> p k j")), r=["cvs"], w=["cvb"])
            awb = [T([128, 8, 256], BF16) for _ in range(2)]
            Mt = T([2, 6 * D])
            bt = T([2, 6 * D])
            ng = T([2, 4, D])
            V = T([2, 6, D])
            for l in range(2):
                dma("sp", bt, din["ada_b"][l].partition_broadcast(2), w=["bt"])
                dma("sp", ng.rearrange("p a b -> p (a b)"),
                    din["norm_g"][l].rearrange("a b -> (a b)").partition_broadcast(2), w=["ng"])
                for nb in range(24):
                    ab = awb[nb % 2]
                    kab = "awb%d" % (nb % 2)
                    src = din["ada_w"][l].rearrange("(k p) n -> p k n", p=128)[:, :, nb * 256:(nb + 1) * 256]
                    i = stgi[0] % 2
                    stgi[0] += 1
                    s = stg[i][:, :].rearrange("p (a b) -> p a b", a=8)
                    dma("sp" if i == 0 else "pool", s, src, w=["stg%d" % i])
                    P.op("pool" if nb % 2 == 0 else "dve", (lambda e, ab=ab, s=s: e.tensor_copy(out=ab, in_=s)),
                         r=["stg%d" % i], w=[kab])
                    pb = ps[nb % 2][0:2, 0:256]
                    mmg(pb, [(cvb[:, k, :], ab[:, k, :]) for k in range(8)], r=["cvb", kab], w=["ps%d" % (nb % 2)])
                    P.op("dve", (lambda e, pb=pb, nb=nb: e.tensor_tensor(out=Mt[:, nb * 256:(nb + 1) * 256], in0=pb,
                                                                       in1=bt[:, nb * 256:(nb + 1) * 256], op=ALU.add)),
                         r=["ps%d" % (nb % 2), "bt"], w=["Mt"])
                sl = lambda i: Mt[:, i * D:(i + 1) * D]
                P.op("dve", lambda e: e.scalar_tensor_tensor(out=V[:, 0, :], in0=sl(1), scalar=1.0, in1=ng[:, 0, :],
                                                             op0=ALU.add, op1=ALU.mult), r=["Mt", "ng"], w=["V"])
                P.op("dve", lambda e: e.tensor_copy(out=V[:, 1, :], in_=sl(0)), r=["Mt"], w=["V"])
                P.op("dve", lambda e: e.tensor_tensor(out=V[:, 2, :], in0=sl(2), in1=ng[:, 1, :], op=ALU.mult),
                     r=["Mt", "ng"], w=["V"])
                P.op("dve", lambda e: e.scalar_tensor_tensor(out=V[:, 3, :], in0=sl(4), scalar=1.0, in1=ng[:, 2, :],
                                                             op0=ALU.add, op1=ALU.mult), r=["Mt", "ng"], w=["V"])
                P.op("dve", lambda e: e.tensor_copy(out=V[:, 4, :], in_=sl(3)), r=["Mt"], w=["V"])
                P.op("dve", lambda e: e.tensor_tensor(out=V[:, 5, :], in0=sl(5), in1=ng[:, 3, :], op=ALU.mult),
                     r=["Mt", "ng"], w=["V"])
                dma("sp", MODV[l].rearrange("j a b -> j (a b)"), V.rearrange("p a b -> p (a b)"), r=["V"], w=["MODV"])

        def x_src(layer0_in, tt):
            if layer0_in:
                return din["ctx"][tt * 128:(tt + 1) * 128, :] if tt < 2 else din["x"][(tt - 2) * 128:(tt - 1) * 128, :]
            return XR[tt * 128:(tt + 1) * 128, :]

        class NM:
            def __init__(self, l, gi, si):
                self.xt = [T([128, D]) for _ in range(2)]
                self.junk = T([128, D], BF16)
                self.tmp = [T([128, D]) for _ in range(2)]
                self.pending = None
                self.hb = [T([128, D], BF16) for _ in range(2)]
                self.ss = T([128, 2])
                self.rs = T([128, 2])
                self.G = [T([128, D]) for _ in range(2)]
                self.SH = [T([128, D]) for _ in range(2)]
                for j in range(2):
                    dma("sp", self.G[j], MODV[l, j, gi].partition_broadcast(128), r=["MODV"], w=["nmG"])
                    dma("sp", self.SH[j], MODV[l, j, si].partition_broadcast(128), r=["MODV"], w=["nmG"])
                self.i = 0

            def run(self, src, is_ctx, dst, dkey, bank):
                i = self.i % 2
                self.i += 1
                xt, hb = self.xt[i], self.hb[i]
                kx, kh = "nm_xt%d" % i, "nm_hb%d" % i
                ss, rs = self.ss[:, i:i + 1], self.rs[:, i:i + 1]
                G, SH = self.G[1 if is_ctx else 0], self.SH[1 if is_ctx else 0]
                tmp = self.tmp[i]
                dma("sp", xt, src, r=["XR"], w=[kx])
                P.op("act", lambda e: e.activation(out=self.junk, in_=xt, func=AF.Square, accum_out=ss),
                     r=[kx], w=["nm_junk", "nm_ss%d" % i])
                P.op("act", lambda e: e.activation(out=rs, in_=ss, func=AF.Sqrt, scale=1.0 / D, bias=EPS),
                     r=["nm_ss%d" % i], w=["nm_rs%d" % i])
                P.op("dve", lambda e: e.reciprocal(out=rs, in_=rs), r=["nm_rs%d" % i], w=["nm_rs%d" % i])
                P.op("dve", lambda e: e.scalar_tensor_tensor(out=tmp, in0=xt, scalar=rs, in1=G, op0=ALU.mult,
                                                             op1=ALU.mult), r=[kx, "nm_rs%d" % i, "nmG"], w=["nm_tmp%d" % i])
                P.op("pool", lambda e: e.tensor_tensor(out=hb, in0=tmp, in1=SH, op=ALU.add),
                     r=["nm_tmp%d" % i, "nmG"], w=[kh])
                prev = self.pending
                self.pending = (hb, kh, dst, dkey, bank)
                if prev is not None:
                    self.stage_b(*prev)

            def stage_b(self, hb, kh, dst, dkey, bank):
                pb = psb[bank]

                def tr(e):
                    for k in range(8):
                        ins = e.transpose(out=pb[:, k * 128:(k + 1) * 128], in_=hb[:, k * 128:(k + 1) * 128],
                                          identity=identb)
                    return ins
                P.op("pe", tr, r=[kh], w=["ps%d" % bank])
                P.op("act", lambda e: e.copy(out=dst, in_=pb.rearrange("p (k n) -> p k n", k=8)),
                     r=["ps%d" % bank], w=[dkey])

            def flush(self):
                if self.pending is not None:
                    self.stage_b(*self.pending)
                    self.pending = None

        class RES:
            def __init__(self, l, gidx):
                self.GT = [T([128, D]) for _ in range(2)]
                for j in range(2):
                    dma("sp", self.GT[j], MODV[l, j, gidx].partition_broadcast(128), r=["MODV"], w=["resG"])
                self.xo = [T([128, D]) for _ in range(2)]
                self.tt_ = [T([128, D]) for _ in range(2)]
                self.junk = T([128, 512], BF16)
                self.ss = T([128, 4])
                self.rs = T([128, 2])
                self.i = 0

            def run(self, b0, b1, src, dstd, is_ctx, wkey):
                i = self.i % 2
                self.i += 1
                xo = self.xo[i]
                tbuf = self.tt_[i]
                kt_ = "res_t%d" % i
                kx = "res_x%d" % i
                ss = self.ss[:, 2 * i:2 * i + 2]
                rs = self.rs[:, i:i + 1]
                GT = self.GT[1 if is_ctx else 0]
                dma("pool", xo, src, r=["XR"], w=[kx])
                P.op("act", lambda e: e.activation(out=self.junk, in_=ps[b0], func=AF.Square, accum_out=ss[:, 0:1]),
                     r=["ps%d" % b0], w=["res_junk", "res_ss%d" % i])
                P.op("act", lambda e: e.activation(out=self.junk, in_=ps[b1], func=AF.Square, accum_out=ss[:, 1:2]),
                     r=["ps%d" % b1], w=["res_junk", "res_ss%d" % i])
                P.op("dve", lambda e: e.tensor_tensor(out=rs, in0=ss[:, 0:1], in1=ss[:, 1:2], op=ALU.add),
                     r=["res_ss%d" % i], w=["res_rs%d" % i])
                P.op("act", lambda e: e.activation(out=rs, in_=rs, func=AF.Sqrt, scale=1.0 / D, bias=EPS),
                     r=["res_rs%d" % i], w=["res_rs%d" % i])
                P.op("dve", lambda e: e.reciprocal(out=rs, in_=rs), r=["res_rs%d" % i], w=["res_rs%d" % i])
                for hf, bk in ((0, b0), (1, b1)):
                    P.op("dve", (lambda e, hf=hf, bk=bk: e.scalar_tensor_tensor(
                        out=tbuf[:, hf * 512:(hf + 1) * 512], in0=ps[bk], scalar=rs,
                        in1=GT[:, hf * 512:(hf + 1) * 512], op0=ALU.mult, op1=ALU.mult)),
                        r=["ps%d" % bk, "res_rs%d" % i, "resG"], w=[kt_, "ps%d" % bk])
                P.op("pool", lambda e: e.tensor_tensor(out=xo, in0=tbuf, in1=xo, op=ALU.add),
                     r=[kt_, kx], w=[kx])
                dma("pool", dstd, xo, r=[kx], w=[wkey])

        def rope_evict(pbank, n, t0, dst, dkey, ro, is_ctx):
            src = ps[pbank][:, 0:n]
            if is_ctx:
                P.op("act", lambda e: e.copy(out=dst, in_=src), r=["ps%d" % pbank], w=[dkey])
                return
            qs, t1, t2, cosb, sinb, rb = ro
            P.op("act", lambda e: e.copy(out=qs[:, 0:n], in_=src), r=["ps%d" % pbank], w=["ro_qs"])
            mmg(ps[rb][:, 0:n], [(rotb, qs[:, 0:n])], r=["ro_qs"], w=["ps%d" % rb])
            P.op("dve", lambda e: e.tensor_tensor(out=t1[:, 0:n], in0=src, in1=cosb[:, 0:n], op=ALU.mult),
                 r=["ps%d" % pbank, "ro_cs"], w=["ro_t1", "ps%d" % pbank])
            P.op("dve", lambda e: e.tensor_tensor(out=t2[:, 0:n], in0=ps[rb][:, 0:n], in1=sinb[:, 0:n], op=ALU.mult),
                 r=["ps%d" % rb, "ro_cs"], w=["ro_t2", "ps%d" % rb])
            P.op("pool", lambda e: e.tensor_tensor(out=dst, in0=t1[:, 0:n], in1=t2[:, 0:n], op=ALU.add),
                 r=["ro_t1", "ro_t2"], w=[dkey])

        def rope_tiles():
            return (T([128, 512], BF16), T([128, 512]), T([128, 512]), T([128, 512]), T([128, 512]))

        def rope_load(ro, b):
            if b == 0:
                return
            c0 = (b - 1) * 512
            dma("pool", ro[3], din["cos_t"][:, c0:c0 + 512], w=["ro_cs"])
            dma("pool", ro[4], din["sin_t"][:, c0:c0 + 512], w=["ro_cs"])

        def attention(groups, scale, accsets):
            PT = [T([128, 2, 512], BF16) for _ in range(3)]
            N = len(groups)

            def emitS(i):
                g = groups[i]
                if g.get("pre"):
                    g["pre"]()
                A = 2 * (i % 2)
                n = g["n"]
                for mi, m in enumerate(g["members"]):
                    mmg(ps[A + mi][:, 0:n], m["qk"], r=g["rk"], w=["ps%d" % (A + mi)])

            def emitE(i):
                g = groups[i]
                A = 2 * (i % 2)
                n = g["n"]
                nm_ = len(g["members"])
                pt = PT[i % 3]
                src = psall[:, A * 512:(A + 2) * 512].rearrange("p (j n) -> p j n", j=2)[:, 0:nm_, 0:n]
                P.op("act", (lambda e: e.activation(out=pt[:, 0:nm_, 0:n], in_=src, func=AF.Exp, scale=scale)),
                     r=["ps%d" % A, "ps%d" % (A + 1)], w=["PT%d" % (i % 3), "ps%d" % A, "ps%d" % (A + 1)])

            def emitPV(i):
                g = groups[i]
                n = g["n"]
                pt = PT[i % 3]
                acc = accsets[g["accset"]]
                wk = []
                for m in g["members"]:
                    ob, sb2 = acc[m["acc"]]
                    wk += ["ps%d" % ob, "ps%d" % sb2]

                def pv(e):
                    for mi, m in enumerate(g["members"]):
                        ob, sb2 = acc[m["acc"]]
                        e.matmul(ps[ob][:, 0:n], lhsT=m["v"], rhs=pt[:, mi, 0:n], start=m["start"], stop=m["stop"])
                        ins = e.matmul(ps[sb2][:, 0:n], lhsT=onesb, rhs=pt[:, mi, 0:n], start=m["start"], stop=m["stop"])
                    return ins
                P.op("pe", pv, r=["PT%d" % (i % 3)] + g["rv"], w=list(dict.fromkeys(wk)))
                if g.get("post"):
                    A = 2 * (i % 2)
                    g["post"](acc, (A, A + 1))

            if N:
                emitS(0)
            for i in range(N):
                emitE(i)
                if i + 1 < N:
                    emitS(i + 1)
                emitPV(i)

        def phase_ffn(l, need_ctx, final):
            W2 = 4355
            Wd = T([128, NJ, D], BF16)
            CW = T([128, 4, NJ])
            for i in range(3):
                dma_nc("sp", CW[:, i, :], din["ffn_conv_w"][l, i].rearrange("(j p) -> p j", p=128), w=["CW"])
            dma_nc("sp", CW[:, 3, :], din["ffn_conv_b"][l].rearrange("(j p) -> p j", p=128), w=["CW"])
            wdsrc = din["ffn_w_down"][l].rearrange("(j p) n -> p j n", p=128)
            for j0 in range(0, NJ, 4):
                j1 = min(NJ, j0 + 4)
                wload(Wd[:, j0:j1, :], wdsrc[:, j0:j1, :])
            top_save = top[0]
            nm = NM(l, 3, 4)
            hTs = [T([128, 8, 512], BF16) for _ in range(2)]
            zt = T([128, 8, 1], BF16)
            P.op("pool", lambda e: e.memset(zt, 0.0), w=["zt"])
            for c in (0, 257, 4354):
                dma_nc("pool", H2D[:, :, c:c + 1], zt, r=["zt"], w=["H2D"])
            blks = list(range(0 if need_ctx else 1, 9))
            for b in blks:
                t0, ntl = BLOCKS[b]
                n = ntl * 128
                hT = hTs[b % 2]
                kh = "hTs%d" % (b % 2)
                for ti in range(ntl):
                    tt = t0 + ti
                    nm.run(x_src(False, tt), tt < 2, hT[:, :, ti * 128:(ti + 1) * 128], kh, 6 + tt % 2)
                nm.flush()
                c0 = 1 if b == 0 else 258 + (b - 1) * 512
                dma("pool", H2D[:, :, c0:c0 + n], hT[:, :, 0:n], r=[kh], w=["H2D"])
            P.barrier()
            top[0] = top_save
            res = RES(l, 5)
            GTt = T([128, NJ, 1024], BF16)
            H2P = [T([128, 8, 1026], BF16) for _ in range(2)]
            wgf = [T([128, 8, 128], BF16) for _ in range(2)]
            wuf = [T([128, 8, 128], BF16) for _ in range(2)]
            acc = [T([128, 512]) for _ in range(2)]
            sil = [T([128, 512]) for _ in range(2)]
            parts = [blks[i:i + 2] for i in range(0, len(blks), 2)]
            wgsrc = din["ffn_w_gate"][l].rearrange("(k p) n -> p k n", p=128)
            wusrc = din["ffn_w_up"][l].rearrange("(k p) n -> p k n", p=128)
            it = [0]
            yi = [0]
            bc0 = lambda b: 1 if b == 0 else 258 + (b - 1) * 512
            for pi, part in enumerate(parts):
                cstart = bc0(part[0]) - 1
                cend = bc0(part[-1]) + BLOCKS[part[-1]][1] * 128 + 1
                npc = cend - cstart
                H2T = H2P[pi % 2]
                kH = "H2P%d" % (pi % 2)
                dma("sp", H2T[:, :, 0:npc], H2D[:, :, cstart:cend], r=["H2D"], w=[kH])
                goff = {}
                o = 0
                for b in part:
                    goff[b] = o
                    o += BLOCKS[b][1] * 128
                for j in range(NJ):
                    wg, wu = wgf[j % 2], wuf[j % 2]
                    kw = "ffw%d" % (j % 2)
                    for (dst, srcw) in ((wg, wgsrc), (wu, wusrc)):
                        i = stgi[0] % 2
                        stgi[0] += 1
                        s = stg[i][:, 0:1024].rearrange("p (a b) -> p a b", a=8)
                        dma("sp", s, srcw[:, :, j * 128:(j + 1) * 128], w=["stg%d" % i])
                        P.op("pool", (lambda e, dst=dst, s=s: e.tensor_copy(out=dst, in_=s)), r=["stg%d" % i], w=[kw])
                    for b in part:
                        t0, ntl = BLOCKS[b]
                        n = ntl * 128
                        c0 = bc0(b) - cstart
                        q = it[0] % 2
                        it[0] += 1
                        pa, pu, ph = ps[q], ps[2 + q], ps[4 + q]
                        ka, ku, kh = "ps%d" % q, "ps%d" % (2 + q), "ps%d" % (4 + q)
                        mmg(pa[:, 0:n], [(wg[:, k, :], H2T[:, k, c0:c0 + n]) for k in range(8)], r=[kw, kH], w=[ka])
                        hal = H2T[:, :, c0 - 1:c0 + n + 1:n + 1]
                        mmg(ph[:, 0:2], [(wg[:, k, :], hal[:, k, :]) for k in range(8)], r=[kw, kH], w=[kh])
                        mmg(pu[:, 0:n], [(wu[:, k, :], H2T[:, k, c0:c0 + n]) for k in range(8)], r=[kw, kH], w=[ku])
                        ac, sl_ = acc[q], sil[q]
                        kac, ksl = "acc%d" % q, "sil%d" % q
                        w0, w1, w2, bb = (CW[:, i, j:j + 1] for i in range(4))
                        P.op("dve", (lambda e, ac=ac, pa=pa, w1=w1, bb=bb, n=n: e.tensor_scalar(
                            out=ac[:, 0:n], in0=pa[:, 0:n], scalar1=w1, scalar2=bb, op0=ALU.mult, op1=ALU.add)),
                            r=[ka, "CW"], w=[kac])
                        P.op("dve", (lambda e, ac=ac, pa=pa, w0=w0, n=n: e.scalar_tensor_tensor(
                            out=ac[:, 1:n], in0=pa[:, 0:n - 1], scalar=w0, in1=ac[:, 1:n], op0=ALU.mult, op1=ALU.add)),
                            r=[ka, kac], w=[kac])
                        P.op("dve", (lambda e, ac=ac, pa=pa, w2=w2, n=n: e.scalar_tensor_tensor(
                            out=ac[:, 0:n - 1], in0=pa[:, 1:n], scalar=w2, in1=ac[:, 0:n - 1], op0=ALU.mult, op1=ALU.add)),
                            r=[ka, kac], w=[kac, ka])
                        P.op("dve", (lambda e, ac=ac, ph=ph, w0=w0: e.scalar_tensor_tensor(
                            out=ac[:, 0:1], in0=ph[:, 0:1], scalar=w0, in1=ac[:, 0:1], op0=ALU.mult, op1=ALU.add)),
                            r=[kh, kac], w=[kac])
                        P.op("dve", (lambda e, ac=ac, ph=ph, w2=w2, n=n: e.scalar_tensor_tensor(
                            out=ac[:, n - 1:n], in0=ph[:, 1:2], scalar=w2, in1=ac[:, n - 1:n], op0=ALU.mult, op1=ALU.add)),
                            r=[kh, kac], w=[kac, kh])
                        P.op("act", (lambda e, ac=ac, sl_=sl_, n=n: e.activation(out=sl_[:, 0:n], in_=ac[:, 0:n], func=AF.Silu)),
                             r=[kac], w=[ksl])
                        g0 = goff[b]
                        P.op("dve", (lambda e, sl_=sl_, pu=pu, j=j, g0=g0, n=n: e.tensor_tensor(
                            out=GTt[:, j, g0:g0 + n], in0=sl_[:, 0:n], in1=pu[:, 0:n], op=ALU.mult)),
                            r=[ksl, ku], w=["GT", ku])
                for b in part:
                    t0, ntl = BLOCKS[b]
                    for ti in range(ntl):
                        tt = t0 + ti
                        g0 = goff[b] + ti * 128
                        y0, y1 = ((6, 7), (0, 1), (2, 3))[yi[0] % 3]
                        yi[0] += 1
                        for hf, yb in ((0, y0), (1, y1)):
                            mmg(ps[yb], [(GTt[:, j, g0:g0 + 128], Wd[:, j, hf * 512:(hf + 1) * 512]) for j in range(NJ)],
                                r=["GT", "wgt"], w=["ps%d" % yb])
                        if final:
                            dstd = out[(tt - 2) * 128:(tt - 1) * 128, :]
                            res.run(y0, y1, x_src(False, tt), dstd, tt < 2, "OUT")
                        else:
                            res.run(y0, y1, x_src(False, tt), XR[tt * 128:(tt + 1) * 128, :], tt < 2, "XR2")

        def phase_l0():
            l = 0
            LAM_INIT = 0.2
            KT = T([128, 4, NTOK], BF16)
            Vv = T([128, NT, 512], BF16)
            keep = top[0]
            w_in = T([128, 8, 2048], BF16)
            wsrc = din["ev_w_in"][0].rearrange("(k p) n -> p k n", p=128)
            for c in range(4):
                wload(w_in[:, :, c * 512:(c + 1) * 512], wsrc[:, :, c * 512:(c + 1) * 512])
            nm = NM(l, 0, 1)
            hT = T([128, 8, 512], BF16)
            ro = rope_tiles() + (7,)
            ub = [T([128, 512]) for _ in range(2)]
            qb = [T([128, 4, 512], BF16) for _ in range(2)]
            for b, (t0, ntl) in enumerate(BLOCKS):
                n = ntl * 128
                col0 = t0 * 128
                rope_load(ro, b)
                for ti in range(ntl):
                    tt = t0 + ti
                    nm.run(x_src(True, tt), tt < 2, hT[:, :, ti * 128:(ti + 1) * 128], "hT", 6)
                nm.flush()
                for ti in range(ntl):
                    tt = t0 + ti
                    bk = 4 + (ti % 2)
                    mmg(ps[bk], [(hT[:, k, ti * 128:(ti + 1) * 128], w_in[:, k, 1536:2048]) for k in range(8)],
                        r=["hT", "wgt"], w=["ps%d" % bk])
                    P.op("act", (lambda e, tt=tt, bk=bk: e.copy(out=Vv[:, tt, :], in_=ps[bk])), r=["ps%d" % bk], w=["Vv"])
                for c in range(4):
                    bk = c % 2
                    mmg(ps[bk][:, 0:n], [(w_in[:, k, c * 128:(c + 1) * 128], hT[:, k, 0:n]) for k in range(8)],
                        r=["hT", "wgt"], w=["ps%d" % bk])
                    u = ub[c % 2]
                    P.op("act", (lambda e, u=u, bk=bk, n=n: e.copy(out=u[:, 0:n], in_=ps[bk][:, 0:n])),
                         r=["ps%d" % bk], w=["ub%d" % (c % 2)])
                    dma("pool", UT[c][:, col0:col0 + n], u[:, 0:n], r=["ub%d" % (c % 2)], w=["UT"])
                qbb = qb[b % 2]
                kq = "qb%d" % (b % 2)
                for c in range(4):
                    bk = 2 + (c % 2)
                    mmg(ps[bk][:, 0:n], [(w_in[:, k, 512 + c * 128:512 + (c + 1) * 128], hT[:, k, 0:n]) for k in range(8)],
                        r=["hT", "wgt"], w=["ps%d" % bk])
                    rope_evict(bk, n, col0, qbb[:, c, 0:n], kq, ro, b == 0)
                dma("pool", QT[b][:, :, 0:n], qbb[:, :, 0:n], r=[kq], w=["QT"])
                for c in range(4):
                    bk = 2 + (c % 2)
                    mmg(ps[bk][:, 0:n], [(w_in[:, k, 1024 + c * 128:1024 + (c + 1) * 128], hT[:, k, 0:n]) for k in range(8)],
                        r=["hT", "wgt"], w=["ps%d" % bk])
                    rope_evict(bk, n, col0, KT[:, c, col0:col0 + n], "KT", ro, b == 0)
            P.barrier()
            top[0] = keep
            if stop == "l0a1":
                return
            pw = T([128, 4, 128], BF16)
            for g in range(4):
                wload(pw[:, g, :], din["pool_w"][0, g])
            psc = T([128, 4])
            dma_nc("sp", psc, din["pool_scale"][0].rearrange("(g p) -> p g", p=128), w=["psc"])
            UP = T([128, PW])
            Aa = T([128, PW])
            Ab = T([128, PW])
            IC = T([128, PW])
            dT = T([128, PW], BF16)
            mpo = [T([128, 512], BF16) for _ in range(2)]
            for g in range(4):
                hw = (1, 2, 4, 8)[g]
                P.op("pool", lambda e: e.memset(UP, 0.0), w=["UP"])
                dma("sp", UP[:, PC0:PC0 + 256], UT[g][:, 0:256], r=["UT"], w=["UP"])
                dma("sp", UP[:, PL0:PL0 + 4096], UT[g][:, 256:NTOK], r=["UT"], w=["UP"])
                dma("pool", IC, din["invcnt"][g].partition_broadcast(128), w=["IC"])
                cur, ck = UP, "UP"
                bufs = [(Aa, "Aa"), (Ab, "Ab")]
                width = PW
                for s in range(g + 1):
                    sh = 1 << s
                    nxt, nk = bufs[s % 2]
                    width -= sh
                    P.op("dve", (lambda e, cur=cur, nxt=nxt, sh=sh, width=width: e.tensor_tensor(
                        out=nxt[:, 0:width], in0=cur[:, 0:width], in1=cur[:, sh:sh + width], op=ALU.add)),
                        r=[ck], w=[nk])
                    cur, ck = nxt, nk
                oth, ok = bufs[(g + 1) % 2]
                P.op("dve", (lambda e, cur=cur, oth=oth, hw=hw: e.tensor_tensor(
                    out=oth[:, 8:PW - 8], in0=cur[:, 8 - hw:PW - 8 - hw], in1=IC[:, 8:PW - 8], op=ALU.mult)),
                    r=[ck, "IC"], w=[ok])
                P.op("pool", (lambda e, oth=oth: e.tensor_tensor(out=dT[:, 8:PW - 8], in0=oth[:, 8:PW - 8],
                                                                 in1=UP[:, 8:PW - 8], op=ALU.subtract)),
                     r=[ok, "UP"], w=["dT"])
                for b, (t0, ntl) in enumerate(BLOCKS):
                    n = ntl * 128
                    col0 = t0 * 128
                    pc = PC0 if b == 0 else PL0 + (b - 1) * 512
                    bk = b % 2
                    mmg(ps[bk][:, 0:n], [(pw[:, g, :], dT[:, pc:pc + n])], r=["dT", "wgt"], w=["ps%d" % bk])
                    mp = mpo[b % 2]
                    P.op("act", (lambda e, bk=bk, n=n, g=g, mp=mp: e.activation(
                        out=mp[:, 0:n], in_=ps[bk][:, 0:n], func=AF.Identity, scale=psc[:, g:g + 1])),
                        r=["ps%d" % bk, "psc"], w=["mpo%d" % (b % 2)])
                    dma("pool", MIXD[g][:, col0:col0 + n], mp[:, 0:n], r=["mpo%d" % (b % 2)], w=["MIXD"])
            P.barrier()
            top[0] = keep
            if stop == "l0a2":
                return
            lamt = T([128, 4, 64])
            lj = T([128, 64])
            lsum = T([128, 2])
            nlam = T([128, 1])
            dma("sp", lamt.rearrange("p a b -> p (a b)"),
                din["diff_lambda"][0].rearrange("a b -> (a b)").partition_broadcast(128), w=["lamt"])
            for i in range(2):
                P.op("dve", (lambda e, i=i: e.scalar_tensor_tensor(out=lj, in0=lamt[:, 2 * i, :], scalar=1.0,
                                                                   in1=lamt[:, 2 * i + 1, :], op0=ALU.mult, op1=ALU.mult,
                                                                   accum_out=lsum[:, i:i + 1])),
                     r=["lamt"], w=["lj", "lsum"])
            P.op("act", lambda e: e.activation(out=lsum, in_=lsum, func=AF.Exp), r=["lsum"], w=["lsum"])
            P.op("dve", lambda e: e.tensor_tensor(out=nlam, in0=lsum[:, 1:2], in1=lsum[:, 0:1], op=ALU.subtract),
                 r=["lsum"], w=["nlam"])
            P.op("dve", lambda e: e.tensor_scalar(out=nlam, in0=nlam, scalar1=-LAM_INIT, scalar2=None, op0=ALU.add),
                 r=["nlam"], w=["nlam"])
            sln = T([128, 1])
            dma_nc("sp", sln, din["diff_subln"][0].rearrange("(p o) -> p o", o=1), w=["sln"])
            P.op("dve", lambda e: e.tensor_scalar(out=sln, in0=sln, scalar1=1.0 - LAM_INIT, scalar2=None, op0=ALU.mult),
                 r=["sln"], w=["sln"])
            qtl = [T([128, 4, 512], BF16) for _ in range(3)]
            MIXA = [T([128, 4, 512], BF16) for _ in range(2)]
            rsum = T([128, 512])
            o2 = [T([128, 512]) for _ in range(2)]
            oc = T([128, 512])
            sq = T([128, 512], BF16)
            rstd = T([128, 512])
            state = {}

            def pre(b):
                if state.get("b") != b:
                    state["b"] = b
                    n = BLOCKS[b][1] * 128
                    col0 = BLOCKS[b][0] * 128
                    dma("sp", qtl[b % 3][:, :, 0:n], QT[b][:, :, 0:n], r=["QT"], w=["qtl%d" % (b % 3)])

            def post(b, h, n, acc, scr):
                mixa = MIXA[b % 2]
                km = "MIXA%d" % (b % 2)
                for j in range(2):
                    ob, sb2 = acc[j]
                    P.op("dve", (lambda e, sb2=sb2: e.reciprocal(out=rsum[:, 0:n], in_=ps[sb2][:, 0:n])),
                         r=["ps%d" % sb2], w=["rsum", "ps%d" % sb2])
                    P.op("dve", (lambda e, ob=ob, j=j: e.tensor_tensor(out=o2[j][:, 0:n], in0=ps[ob][:, 0:n], in1=rsum[:, 0:n], op=ALU.mult)),
                         r=["ps%d" % ob, "rsum"], w=["o2_%d" % j, "ps%d" % ob])
                P.op("dve", lambda e: e.scalar_tensor_tensor(out=oc[:, 0:n], in0=o2[1][:, 0:n], scalar=nlam,
                                                             in1=o2[0][:, 0:n], op0=ALU.mult, op1=ALU.add),
                     r=["o2_0", "o2_1", "nlam"], w=["oc"])
                P.op("act", lambda e: e.activation(out=sq[:, 0:n], in_=oc[:, 0:n], func=AF.Square), r=["oc"], w=["sq"])
                sbk = scr[0]
                mmg(ps[sbk][:, 0:n], [(onesb, sq[:, 0:n])], r=["sq"], w=["ps%d" % sbk])
                P.op("act", lambda e: e.activation(out=rstd[:, 0:n], in_=ps[sbk][:, 0:n], func=AF.Sqrt, scale=1.0 / 128,
                                                   bias=EPS), r=["ps%d" % sbk], w=["rstd", "ps%d" % sbk])
                P.op("dve", lambda e: e.reciprocal(out=rstd[:, 0:n], in_=rstd[:, 0:n]), r=["rstd"], w=["rstd"])
                P.op("dve", lambda e: e.scalar_tensor_tensor(out=mixa[:, h, 0:n], in0=oc[:, 0:n], scalar=sln,
                                                             in1=rstd[:, 0:n], op0=ALU.mult, op1=ALU.mult),
                     r=["oc", "rstd", "sln"], w=[km])
                col0 = BLOCKS[b][0] * 128
                dma("pool", MIXD[4 + h][:, col0:col0 + n], mixa[:, h, 0:n], r=[km], w=["MIXD"])

            groups = []
            for b in range(9):
                n = BLOCKS[b][1] * 128
                kts = list(range(2)) if b == 0 else list(range(NT))
                for h in range(4):
                    for ki, kt in enumerate(kts):
                        mem = []
                        for j in range(2):
                            pr = slice(64 * j, 64 * j + 64)
                            mem.append(dict(qk=[(KT[pr, h, kt * 128:(kt + 1) * 128], qtl[b % 3][pr, h, 0:n])],
                                            v=Vv[:, kt, h * 128:(h + 1) * 128], acc=j, start=(ki == 0), stop=(ki == len(kts) - 1)))
                        g = dict(n=n, members=mem, rk=["KT", "qtl%d" % (b % 3)], rv=["Vv"], accset=0)
                        if h == 0 and ki == 0:
                            g["pre"] = (lambda b=b: (pre(b), pre(b + 1) if b + 1 < 9 else None))
                        if ki == len(kts) - 1:
                            g["post"] = (lambda acc, scr, b=b, h=h, n=n: post(b, h, n, acc, scr))
                        groups.append(g)
            attention(groups, 0.125, [[(4, 5), (6, 7)]])
            phase_reset()
            phase_outproj(0, list(range(9)), True)

        def rownorm(pt, W, NB, dst, dkey, tg):
            junk, ss, rs = tg
            P.op("act", lambda e: e.activation(out=junk[:, 0:W], in_=pt[0][:, 0:W], func=AF.Square, accum_out=ss),
                 r=[pt[1]], w=["rn_junk", "rn_ss"])
            P.op("act", lambda e: e.activation(out=rs, in_=ss, func=AF.Sqrt, scale=1.0 / W, bias=EPS), r=["rn_ss"], w=["rn_rs"])
            P.op("dve", lambda e: e.reciprocal(out=rs, in_=rs), r=["rn_rs"], w=["rn_rs"])
            P.op("dve", lambda e: e.scalar_tensor_tensor(out=dst, in0=pt[0][:, 0:W], scalar=rs, in1=NB[:, 0:W], op0=ALU.mult,
                                                         op1=ALU.mult), r=[pt[1], "rn_rs", "wgt"], w=[dkey, pt[1]])

        def phase_l1():
            l = 1
            KN = T([128, 4, NTOK], BF16)
            VM = T([128, NT, 512], BF16)
            KR2 = T([128, NTOK], BF16)
            keep = top[0]
            WA = T([128, 8, 512], BF16)
            WB = T([128, 8, 256], BF16)
            WKR = T([128, 8, 128], BF16)
            WQN = T([128, 4, 4, 128], BF16)
            WQR = T([128, 4, 256], BF16)
            WKN = T([128, 2, 4, 128], BF16)
            WVV = T([128, 2, 512], BF16)
            wsrc = din["od_w_in"][0].rearrange("(k p) n -> p k n", p=128)
            wload(WA, wsrc[:, :, 0:512])
            wload(WB, wsrc[:, :, 512:768])
            wload(WKR[:, :, 0:64], wsrc[:, :, 768:832])
            wload(WKR[:, :, 64:128], wsrc[:, :, 768:832])
            uq = din["mla_w_uq"][0].rearrange("(k p) (h c) -> p k h c", p=128, c=192)
            ukv = din["mla_w_ukv"][0].rearrange("(k p) (h c) -> p k h c", p=128, c=256)
            for k in range(4):
                wload(WQN[:, k], uq[:, k, :, 0:128])
                wload(WQR[:, k, :].rearrange("p (h c) -> p h c", h=4), uq[:, k, :, 128:192])
            for k in range(2):
                wload(WKN[:, k], ukv[:, k, :, 0:128])
                wload(WVV[:, k, :].rearrange("p (h c) -> p h c", h=4), ukv[:, k, :, 128:256])
            QNb = T([128, 512])
            KVNb = T([128, 256])
            dma("sp", QNb, din["mla_q_norm"][0].partition_broadcast(128), w=["wgt"])
            dma("sp", KVNb, din["mla_kv_norm"][0].partition_broadcast(128), w=["wgt"])
            nm = NM(l, 0, 1)
            hTs = [T([128, 8, 512], BF16)] * 2
            ro = rope_tiles() + (7,)
            tg = (T([128, 512], BF16), T([128, 1]), T([128, 1]))
            cqn = T([128, 512], BF16)
            ckvn = T([128, 256], BF16)
            cqT = T([128, 4, 512], BF16)
            ckvT = T([128, 2, 512], BF16)
            qnb = [T([128, 4, 512], BF16)] * 2
            qrb = [T([128, 2, 512], BF16)] * 2
            for b, (t0, ntl) in enumerate(BLOCKS):
                n = ntl * 128
                col0 = t0 * 128
                hT = hTs[b % 2]
                khT = "hTs0"
                rope_load(ro, b)
                for ti in range(ntl):
                    tt = t0 + ti
                    nm.run(x_src(False, tt), tt < 2, hT[:, :, ti * 128:(ti + 1) * 128], khT, 6)
                nm.flush()
                dma("pool", HT1[b][:, :, 0:n], hT[:, :, 0:n], r=[khT], w=["HT1"])
                for ti in range(ntl):
                    ts_ = slice(ti * 128, (ti + 1) * 128)
                    mmg(ps[0], [(hT[:, k, ts_], WA[:, k, :]) for k in range(8)], r=[khT, "wgt"], w=["ps0"])
                    rownorm((ps[0], "ps0"), 512, QNb, cqn, "cqn", tg)

                    def trq(e):
                        for k in range(4):
                            ins = e.transpose(out=psb[2][:, k * 128:(k + 1) * 128], in_=cqn[:, k * 128:(k + 1) * 128], identity=identb)
                        return ins
                    P.op("pe", trq, r=["cqn"], w=["ps2"])
                    P.op("act", (lambda e, ts_=ts_: e.copy(out=cqT[:, :, ts_], in_=psb[2][:, 0:512].rearrange("p (k n) -> p k n", k=4))),
                         r=["ps2"], w=["cqT"])
                    mmg(ps[1][:, 0:256], [(hT[:, k, ts_], WB[:, k, :]) for k in range(8)], r=[khT, "wgt"], w=["ps1"])
                    rownorm((ps[1], "ps1"), 256, KVNb, ckvn, "ckvn", tg)

                    def trk(e):
                        for k in range(2):
                            ins = e.transpose(out=psb[3][:, k * 128:(k + 1) * 128], in_=ckvn[:, k * 128:(k + 1) * 128], identity=identb)
                        return ins
                    P.op("pe", trk, r=["ckvn"], w=["ps3"])
                    P.op("act", (lambda e, ts_=ts_: e.copy(out=ckvT[:, :, ts_], in_=psb[3][:, 0:256].rearrange("p (k n) -> p k n", k=2))),
                         r=["ps3"], w=["ckvT"])
                for ti in range(ntl):
                    tt = t0 + ti
                    ts_ = slice(ti * 128, (ti + 1) * 128)
                    mmg(ps[0], [(ckvT[:, k, ts_], WVV[:, k, :]) for k in range(2)], r=["ckvT", "wgt"], w=["ps0"])
                    P.op("act", (lambda e, tt=tt: e.copy(out=VM[:, tt, :], in_=ps[0])), r=["ps0"], w=["VM", "ps0"])
                for h in range(4):
                    bk = 4 + (h % 2)
                    mmg(ps[bk][:, 0:n], [(WKN[:, k, h, :], ckvT[:, k, 0:n]) for k in range(2)], r=["ckvT", "wgt"], w=["ps%d" % bk])
                    P.op("act", (lambda e, h=h, bk=bk, n=n, col0=col0: e.copy(out=KN[:, h, col0:col0 + n], in_=ps[bk][:, 0:n])),
                         r=["ps%d" % bk], w=["KN", "ps%d" % bk])
                mmg(ps[4][:, 0:n], [(WKR[:, k, :], hT[:, k, 0:n]) for k in range(8)], r=[khT, "wgt"], w=["ps4"])
                rope_evict(4, n, col0, KR2[:, col0:col0 + n], "KR2", ro, b == 0)
                if b >= 1:
                    qn, qr = qnb[b % 2], qrb[b % 2]
                    for h in range(4):
                        bk = 4 + (h % 2)
                        mmg(ps[bk][:, 0:n], [(WQN[:, k, h, :], cqT[:, k, 0:n]) for k in range(4)], r=["cqT", "wgt"], w=["ps%d" % bk])
                        P.op("act", (lambda e, h=h, bk=bk, n=n, qn=qn: e.copy(out=qn[:, h, 0:n], in_=ps[bk][:, 0:n])),
                             r=["ps%d" % bk], w=["qnb0", "ps%d" % bk])
                    dma("pool", QT[b][:, :, 0:n], qn[:, :, 0:n], r=["qnb0"], w=["QT"])
                    for c in range(2):
                        bk = 4 + (c % 2)
                        mmg(ps[bk][:, 0:n], [(WQR[:, k, c * 128:(c + 1) * 128], cqT[:, k, 0:n]) for k in range(4)],
                            r=["cqT", "wgt"], w=["ps%d" % bk])
                        rope_evict(bk, n, col0, qr[:, c, 0:n], "qrb0", ro, False)
                    dma("pool", QR[b][:, :, 0:n], qr[:, :, 0:n], r=["qrb0"], w=["QR"])
            P.barrier()
            top[0] = keep
            if stop == "l1b1":
                return
            qnl = [T([128, 4, 512], BF16) for _ in range(3)]
            qrl = [T([128, 2, 512], BF16) for _ in range(3)]
            rsum = T([128, 512])
            mo = [T([128, 512], BF16) for _ in range(2)]
            state = {}

            def pre(b):
                if state.get("b") != b:
                    state["b"] = b
                    n = BLOCKS[b][1] * 128
                    dma("sp", qnl[b % 3][:, :, 0:n], QT[b][:, :, 0:n], r=["QT"], w=["qnl%d" % (b % 3)])
                    dma("sp", qrl[b % 3][:, :, 0:n], QR[b][:, :, 0:n], r=["QR"], w=["qnl%d" % (b % 3)])

            def post(b, h, n, acc, scr):
                col0 = BLOCKS[b][0] * 128
                m = mo[h % 2]
                km = "mo%d" % (h % 2)
                ob, sb2 = acc[0]
                P.op("dve", lambda e: e.reciprocal(out=rsum[:, 0:n], in_=ps[sb2][:, 0:n]), r=["ps%d" % sb2], w=["rsum", "ps%d" % sb2])
                P.op("dve", lambda e: e.tensor_tensor(out=m[:, 0:n], in0=ps[ob][:, 0:n], in1=rsum[:, 0:n], op=ALU.mult),
                     r=["ps%d" % ob, "rsum"], w=[km, "ps%d" % ob])
                dma("pool", MIXD[h][:, col0:col0 + n], m[:, 0:n], r=[km], w=["MIXD"])

            groups = []
            gi = 0
            for b in range(1, 9):
                n = 512
                for h in range(4):
                    pr = slice(64 * (h % 2), 64 * (h % 2) + 64)
                    for kp in range(NT // 2):
                        mem = []
                        for mi in range(2):
                            kt = 2 * kp + mi
                            ks = slice(kt * 128, (kt + 1) * 128)
                            mem.append(dict(qk=[(KN[:, h, ks], qnl[b % 3][:, h, 0:n]), (KR2[pr, ks], qrl[b % 3][pr, h // 2, 0:n])],
                                            v=VM[:, kt, h * 128:(h + 1) * 128], acc=0,
                                            start=(kp == 0 and mi == 0), stop=(kp == NT // 2 - 1 and mi == 1)))
                        g = dict(n=n, members=mem, rk=["KN", "KR2", "qnl%d" % (b % 3)], rv=["VM"], accset=gi % 2)
                        if h == 0 and kp == 0:
                            g["pre"] = (lambda b=b: (pre(b), pre(b + 1) if b + 1 < 9 else None))
                        if kp == NT // 2 - 1:
                            g["post"] = (lambda acc, scr, b=b, h=h, n=n: post(b, h, n, acc, scr))
                        groups.append(g)
                    gi += 1
            attention(groups, 192.0 ** -0.5, [[(4, 5)], [(6, 7)]])
            phase_reset()
            if stop == "l1b2":
                return
            LB = T([128, 2, 2, 4])
            lbv = T([128, 2, 4])
            oml = T([128, 2, 4])
            hgn = T([128, 1])
            ones1 = T([128, 1])
            for d in range(2):
                for ll in range(2):
                    dma_nc("sp", LB[:, d, ll, :], din["hgrn_lb"][d, ll].rearrange("(h p) -> p h", p=128), w=["LB"])
            dma_nc("sp", hgn, din["hgrn_norm"][0].rearrange("(p o) -> p o", o=1), w=["hgn"])
            P.op("dve", lambda e: e.tensor_tensor(out=lbv, in0=LB[:, :, 1, :], in1=LB[:, :, 0, :], op=ALU.subtract), r=["LB"], w=["lbv"])
            P.op("act", lambda e: e.activation(out=lbv, in_=lbv, func=AF.Sigmoid), r=["lbv"], w=["lbv"])
            P.op("dve", lambda e: e.tensor_scalar(out=oml, in0=lbv, scalar1=-1.0, scalar2=1.0, op0=ALU.mult, op1=ALU.add),
                 r=["lbv"], w=["oml"])
            P.op("pool", lambda e: e.memset(ones1, 1.0), w=["ones1"])
            WH = T([128, 8, 5, 128], BF16)
            hTl = [T([128, 8, 512], BF16)] * 2
            SG = T([128, NTOK], BF16)
            Vt = T([128, NT, 128], BF16)
            QP = [T([128, NTOK], BF16) for _ in range(2)]
            QPP = [T([128, NTOK], BF16) for _ in range(2)]
            KP = [T([128, NTOK], BF16) for _ in range(2)]
            KPt = [T([128, NT, 128], BF16) for _ in range(2)]
            EL = [T([128, 68]) for _ in range(2)]
            OT = [T([128, NTOK]) for _ in range(2)]
            qf = T([128, 512])
            sgm = T([128, 512])
            kk = T([128, 512])
            lf = T([128, 512])
            gb = T([128, 516])
            Da = T([128, 512])
            Db = T([128, 512])
            E1 = T([128, 512])
            E2 = T([128, 512])
            E3 = T([128, 512])
            elt = T([128, 8])
            Sf = [[T([128, 128]) for _ in range(2)] for _ in range(2)]
            Sb = [[T([128, 128], BF16) for _ in range(2)] for _ in range(2)]
            Am = [[T([128, 64], BF16) for _ in range(2)] for _ in range(2)]
            osum, rstd, otmp = Da, Db, E1
            sq = T([128, 512], BF16)
            mh = [T([128, 512], BF16) for _ in range(2)]
            P.op("pool", lambda e: e.memset(gb[:, 0:1], 0.0), w=["gb0"])
            wsrc = din["od_w_in"][0].rearrange("(k p) n -> p k n", p=128)
            for h in range(4):
                for i, c0_ in enumerate((832, 1344, 1856, 2368, 2880)):
                    wload(WH[:, :, i, :], wsrc[:, :, c0_ + h * 128:c0_ + (h + 1) * 128])
                for b, (t0, ntl) in enumerate(BLOCKS):
                    n = ntl * 128
                    nch = n // 64
                    col0 = t0 * 128
                    ch0 = col0 // 64
                    hT = hTl[b % 2]
                    khT = "hTl0"
                    dma("sp", hT[:, :, 0:n], HT1[b][:, :, 0:n], r=["HT1"], w=[khT])
                    mmg(ps[0][:, 0:n], [(WH[:, k, 0, :], hT[:, k, 0:n]) for k in range(8)], r=[khT, "wgt"], w=["ps0"])
                    P.op("act", (lambda e, n=n: e.activation(out=qf[:, 0:n], in_=ps[0][:, 0:n], func=AF.Silu)), r=["ps0"], w=["qf", "ps0"])
                    mmg(ps[1][:, 0:n], [(WH[:, k, 4, :], hT[:, k, 0:n]) for k in range(8)], r=[khT, "wgt"], w=["ps1"])
                    P.op("act", (lambda e, n=n, col0=col0: e.activation(out=SG[:, col0:col0 + n], in_=ps[1][:, 0:n], func=AF.Silu)),
                         r=["ps1"], w=["SG", "ps1"])
                    for ti in range(ntl):
                        tt = t0 + ti
                        ts_ = slice(ti * 128, (ti + 1) * 128)
                        mmg(ps[2][:, 0:128], [(hT[:, k, ts_], WH[:, k, 3, :]) for k in range(8)], r=[khT, "wgt"], w=["ps2"])
                        P.op("act", (lambda e, tt=tt: e.copy(out=Vt[:, tt, :], in_=ps[2][:, 0:128])), r=["ps2"], w=["Vt", "ps2"])
                    for d in range(2):
                        bk = 3 + d
                        kb = "ps%d" % bk
                        mmg(ps[bk][:, 0:n], [(WH[:, k, 1 + d, :], hT[:, k, 0:n]) for k in range(8)], r=[khT, "wgt"], w=[kb])
                        P.op("act", (lambda e, n=n, bk=bk: e.activation(out=sgm[:, 0:n], in_=ps[bk][:, 0:n], func=AF.Sigmoid)),
                             r=[kb], w=["sgm", kb])
                        P.op("dve", (lambda e, n=n, d=d, h=h: e.tensor_scalar(out=sgm[:, 0:n], in0=sgm[:, 0:n], scalar1=oml[:, d, h:h + 1],
                                                                             scalar2=lbv[:, d, h:h + 1], op0=ALU.mult, op1=ALU.add)),
                             r=["sgm", "oml", "lbv"], w=["sgm"])
                        P.op("pool", (lambda e, n=n: e.tensor_scalar(out=kk[:, 0:n], in0=sgm[:, 0:n], scalar1=-1.0, scalar2=1.0,
                                                                    op0=ALU.mult, op1=ALU.add)), r=["sgm"], w=["kk"])
                        P.op("act", (lambda e, n=n: e.activation(out=lf[:, 0:n], in_=sgm[:, 0:n], func=AF.Ln)), r=["sgm"], w=["lf"])
                        P.op("dve", (lambda e, n=n: e.tensor_tensor_scan(out=gb[:, 1:1 + n], data0=ones1[:, 0:1].to_broadcast([128, n]),
                                                                        data1=lf[:, 0:n], initial=0.0, op0=ALU.mult, op1=ALU.add)),
                             r=["lf", "ones1", "gb0"], w=["gb"])
                        Gi3 = gb[:, 1:1 + n].rearrange("p (c j) -> p c j", j=64)
                        Gs3 = gb[:, 0:n].rearrange("p (c j) -> p c j", j=64)
                        S0 = Gs3[:, :, 0:1].to_broadcast([128, nch, 64])
                        I63 = Gi3[:, :, 63:64].to_broadcast([128, nch, 64])
                        Da3 = Da[:, 0:n].rearrange("p (c j) -> p c j", j=64)
                        Db3 = Db[:, 0:n].rearrange("p (c j) -> p c j", j=64)
                        if d == 0:
                            P.op("dve", (lambda e, Gi3=Gi3, S0=S0, Da3=Da3: e.tensor_tensor(out=Da3, in0=Gi3, in1=S0, op=ALU.subtract)),
                                 r=["gb"], w=["Da"])
                            P.op("dve", (lambda e, Gi3=Gi3, I63=I63, Db3=Db3: e.tensor_tensor(out=Db3, in0=Gi3, in1=I63, op=ALU.subtract)),
                                 r=["gb"], w=["Db"])
                            sc1, sc2, sc3 = 1.0, -1.0, 1.0
                        else:
                            P.op("dve", (lambda e, Gs3=Gs3, I63=I63, Da3=Da3: e.tensor_tensor(out=Da3, in0=Gs3, in1=I63, op=ALU.subtract)),
                                 r=["gb"], w=["Da"])
                            P.op("dve", (lambda e, Gs3=Gs3, S0=S0, Db3=Db3: e.tensor_tensor(out=Db3, in0=Gs3, in1=S0, op=ALU.subtract)),
                                 r=["gb"], w=["Db"])
                            sc1, sc2, sc3 = -1.0, 1.0, -1.0
                        P.op("dve", (lambda e, Gi3=Gi3, Gs3=Gs3, nch=nch: e.tensor_tensor(out=elt[:, 0:nch], in0=Gi3[:, :, 63], in1=Gs3[:, :, 0],
                                                                                          op=ALU.subtract)), r=["gb"], w=["elt"])
                        P.op("act", (lambda e, n=n, sc1=sc1: e.activation(out=E1[:, 0:n], in_=Da[:, 0:n], func=AF.Exp, scale=sc1)), r=["Da"], w=["E1"])
                        P.op("act", (lambda e, n=n, sc2=sc2: e.activation(out=E2[:, 0:n], in_=Db[:, 0:n], func=AF.Exp, scale=sc2)), r=["Db"], w=["E2"])
                        P.op("act", (lambda e, n=n, sc3=sc3: e.activation(out=E3[:, 0:n], in_=Db[:, 0:n], func=AF.Exp, scale=sc3)), r=["Db"], w=["E3"])
                        P.op("act", (lambda e, d=d, ch0=ch0, nch=nch: e.activation(out=EL[d][:, ch0:ch0 + nch], in_=elt[:, 0:nch], func=AF.Exp)),
                             r=["elt"], w=["EL%d" % d])
                        P.op("dve", (lambda e, n=n, d=d, col0=col0: e.tensor_tensor(out=QP[d][:, col0:col0 + n], in0=qf[:, 0:n], in1=E1[:, 0:n], op=ALU.mult)),
                             r=["qf", "E1"], w=["QP%d" % d])
                        P.op("dve", (lambda e, n=n, d=d, col0=col0: e.tensor_tensor(out=KP[d][:, col0:col0 + n], in0=kk[:, 0:n], in1=E2[:, 0:n], op=ALU.mult)),
                             r=["kk", "E2"], w=["KP%d" % d])
                        P.op("pool", (lambda e, n=n, d=d, col0=col0: e.tensor_tensor(out=QPP[d][:, col0:col0 + n], in0=qf[:, 0:n], in1=E3[:, 0:n], op=ALU.mult)),
                             r=["qf", "E3"], w=["QPP%d" % d])
                        for ti in range(ntl):
                            tt = t0 + ti

                            def trp(e, d=d, tt=tt):
                                return e.transpose(out=psb[6][:, 0:128], in_=KP[d][:, tt * 128:(tt + 1) * 128], identity=identb)
                            P.op("pe", trp, r=["KP%d" % d], w=["ps6"])
                            P.op("act", (lambda e, d=d, tt=tt: e.copy(out=KPt[d][:, tt, :], in_=psb[6][:, 0:128])), r=["ps6"], w=["KPt%d" % d, "ps6"])
                orders = [list(range(68)), [3, 2, 1, 0] + list(range(67, 3, -1))]
                UB = (2, 3)

                def cinfo(c):
                    tt, hh = c // 2, c % 2
                    rows = slice(64 * hh, 64 * hh + 64)
                    cols = slice(64 * c, 64 * c + 64)
                    if c < 4:
                        pos, blk_n, blk_c0 = c, 256, 0
                    else:
                        pos, blk_n, blk_c0 = (c - 4) % 8, 512, 256 + ((c - 4) // 8) * 512
                    return tt, rows, cols, pos, blk_n, blk_c0

                def emitAU(st_):
                    ub = UB[st_ % 2]
                    for d in range(2):
                        c = orders[d][st_]
                        tt, rows, cols, pos, blk_n, blk_c0 = cinfo(c)
                        ab = d
                        mmg(ps[ab][rows, 0:64], [(KP[d][:, cols], QPP[d][:, cols])], r=["KP%d" % d, "QPP%d" % d], w=["ps%d" % ab])
                    for d in range(2):
                        c = orders[d][st_]
                        tt, rows, cols, pos, blk_n, blk_c0 = cinfo(c)
                        ubk = ((2, 6), (3, 7))[d][st_ % 2]
                        mmg(ps[ubk][:, 0:128], [(KPt[d][rows, tt, :], Vt[rows, tt, :])], r=["KPt%d" % d, "Vt"], w=["ps%d" % ubk])

                def emitMask(st_):
                    for d in range(2):
                        c = orders[d][st_]
                        tt, rows, cols, pos, blk_n, blk_c0 = cinfo(c)
                        am = Am[d][st_ % 2]
                        mk = maskf if d == 0 else maskb
                        ab = d
                        pA, kA = ps[ab], "ps%d" % ab
                        P.op("dve", (lambda e, pA=pA, rows=rows, am=am, mk=mk, d=d: e.tensor_tensor(
                            out=am[rows, :], in0=pA[rows, 0:64], in1=mk[rows, :], op=ALU.mult)),
                            r=[kA], w=["Am%d_%d" % (d, st_ % 2), kA])

                def emitO(st_):
                    for d in range(2):
                        c = orders[d][st_]
                        tt, rows, cols, pos, blk_n, blk_c0 = cinfo(c)
                        am = Am[d][st_ % 2]
                        prs = [(Vt[rows, tt, :], am[rows, :])]
                        rr = ["Vt", "Am%d_%d" % (d, st_ % 2)]
                        if st_ > 0:
                            so = (st_ - 1) % 2
                            prs.append((Sb[d][so], QP[d][:, cols]))
                            rr += ["Sb%d_%d" % (d, so), "QP%d" % d]
                        mmg(ps[4 + d][:, pos * 64:(pos + 1) * 64], prs, r=rr, w=["ps%d" % (4 + d)])

                def emitUpd(st_):
                    ub = UB[st_ % 2]
                    kU = "ps%d" % ub
                    sn, so = st_ % 2, (st_ - 1) % 2
                    for d in range(2):
                        c = orders[d][st_]
                        tt, rows, cols, pos, blk_n, blk_c0 = cinfo(c)
                        ubk = ((2, 6), (3, 7))[d][st_ % 2]
                        pU = ps[ubk][:, 0:128]
                        kU = "ps%d" % ubk
                        if st_ == 0:
                            P.op("dve", (lambda e, d=d, pU=pU: e.tensor_copy(out=Sf[d][sn], in_=pU)),
                                 r=[kU], w=["Sf%d_%d" % (d, sn), kU])
                        else:
                            P.op("dve", (lambda e, d=d, pU=pU, c=c: e.scalar_tensor_tensor(
                                out=Sf[d][sn], in0=Sf[d][so], scalar=EL[d][:, c:c + 1], in1=pU, op0=ALU.mult, op1=ALU.add)),
                                r=[kU, "Sf%d_%d" % (d, so), "EL%d" % d], w=["Sf%d_%d" % (d, sn), kU])
                        P.op("act", (lambda e, d=d: e.copy(out=Sb[d][sn], in_=Sf[d][sn])), r=["Sf%d_%d" % (d, sn)], w=["Sb%d_%d" % (d, sn)])
                        last_in_blk = (pos == (blk_n // 64 - 1)) if d == 0 else (pos == 0)
                        if last_in_blk:
                            P.op("act", (lambda e, d=d, blk_n=blk_n, blk_c0=blk_c0: e.copy(out=OT[d][:, blk_c0:blk_c0 + blk_n], in_=ps[4 + d][:, 0:blk_n])),
                                 r=["ps%d" % (4 + d)], w=["OT%d" % d, "ps%d" % (4 + d)])

                LOOK = 1
                if LOOK:
                    emitAU(0)
                    emitMask(0)
                for st_ in range(68):
                    if LOOK:
                        if st_ + 1 < 68:
                            emitAU(st_ + 1)
                            emitMask(st_ + 1)
                    else:
                        emitAU(st_)
                        emitMask(st_)
                    emitO(st_)
                    emitUpd(st_)
                for b in range(1, 9):
                    n = 512
                    col0 = BLOCKS[b][0] * 128
                    cs = slice(col0, col0 + n)
                    m = mh[b % 2]
                    km = "mh%d" % (b % 2)
                    P.op("pool", (lambda e, cs=cs: e.tensor_tensor(out=osum, in0=OT[0][:, cs], in1=OT[1][:, cs], op=ALU.add)),
                         r=["OT0", "OT1"], w=["Da"])
                    P.op("act", lambda e: e.activation(out=sq, in_=osum, func=AF.Square), r=["Da"], w=["sq"])
                    mmg(ps[7], [(onesb, sq)], r=["sq"], w=["ps7"])
                    P.op("act", lambda e: e.activation(out=rstd, in_=ps[7], func=AF.Sqrt, scale=1.0 / 128, bias=EPS), r=["ps7"], w=["Db", "ps7"])
                    P.op("dve", lambda e: e.reciprocal(out=rstd, in_=rstd), r=["Db"], w=["Db"])
                    P.op("dve", lambda e: e.scalar_tensor_tensor(out=otmp, in0=osum, scalar=hgn, in1=rstd, op0=ALU.mult, op1=ALU.mult),
                         r=["Da", "Db", "hgn"], w=["E1"])
                    P.op("dve", (lambda e, cs=cs, m=m: e.tensor_tensor(out=m, in0=otmp, in1=SG[:, cs], op=ALU.mult)), r=["E1", "SG"], w=[km])
                    dma("pool", MIXD[4 + h][:, cs], m, r=[km], w=["MIXD"])
            phase_reset()
            if stop == "l1b3":
                return
            phase_outproj(1, list(range(1, 9)), False)

        def phase_outproj(l, blks, from_inputs):
            wo = T([128, 8, D], BF16)
            wosrc = din["mix_w_out"][l].rearrange("(k p) n -> p k n", p=128)
            for c in range(2):
                wload(wo[:, :, c * 512:(c + 1) * 512], wosrc[:, :, c * 512:(c + 1) * 512])
            res = RES(l, 2)
            mix = [T([128, 8, 512], BF16) for _ in range(2)]
            ypairs = [(0, 1), (2, 3), (4, 5), (6, 7)]
            yi = 0
            for b in blks:
                t0, ntl = BLOCKS[b]
                n = ntl * 128
                col0 = t0 * 128
                mx = mix[b % 2]
                kx = "mix%d" % (b % 2)
                for k in range(8):
                    dma("sp", mx[:, k, 0:n], MIXD[k][:, col0:col0 + n], r=["MIXD"], w=[kx])
                for ti in range(ntl):
                    tt = t0 + ti
                    ls = slice(ti * 128, (ti + 1) * 128)
                    y0, y1 = ypairs[yi % 4]
                    yi += 1
                    for hf, yb in ((0, y0), (1, y1)):
                        mmg(ps[yb], [(mx[:, k, ls], wo[:, k, hf * 512:(hf + 1) * 512]) for k in range(8)],
                            r=[kx, "wgt"], w=["ps%d" % yb])
                    res.run(y0, y1, x_src(from_inputs, tt), XR[tt * 128:(tt + 1) * 128, :], tt < 2, "XR1")

        phase_mod()
        phase_reset()
        S0 = ("mod", "l0a1", "l0a2", "l0mix")
        S1 = S0 + ("l0", "l1b1", "l1b2", "l1b3", "l1mix")
        if stop != "mod":
            phase_l0()
            phase_reset()
        if stop not in S0:
            phase_ffn(0, True, False)
            phase_reset()
        if stop not in S0 + ("l0",):
            phase_l1()
            phase_reset()
        if stop not in S1:
            phase_ffn(1, False, True)
            phase_reset()
        P.emit(st)
    return nc


_CACHE = {}


def kernel(**inputs):
    consts = _consts()
    if "nc" not in _CACHE:
        _CACHE["nc"] = build()
    nc = _CACHE["nc"]
    in_maps = []
    for b in range(8):
        m = {"x": np.ascontiguousarray(inputs["x"][b]), "ctx": np.ascontiguousarray(inputs["ctx"][b]),
             "cvec": np.ascontiguousarray(np.stack([inputs["c"][b], inputs["c_ctx"]], 0))}
        for n in W_NAMES:
            m[n] = np.ascontiguousarray(inputs[n])
        m.update(consts)
        in_maps.append(m)
    res = run_bass_kernel_spmd(nc, in_maps, core_ids=list(range(8)))
    return np.stack([r["out"] for r in res.results], 0).astype(np.float32)
```

```python
import numpy as np
from contextlib import ExitStack
import concourse.bass as bass
import concourse.mybir as mybir
from concourse.bass_utils import run_bass_kernel_spmd

F32 = mybir.dt.float32
BF16 = mybir.dt.bfloat16
ALU = mybir.AluOpType
AF = mybir.ActivationFunctionType

NDSEM = 8
D = 1024
NT = 34
NTOK = 4352
DFF = 2816
NJ = 22
EPS = 1e-6


class Prog:
    ENGS = ("pe", "dve", "act", "pool", "sp")

    def __init__(self, nc):
        self.nc = nc
        self.ops = []
        self.lw = {}
        self.rd = {}
        self.cnt = {e: 0 for e in self.ENGS}
        self.dcnt = {e: 0 for e in self.ENGS}
        self.dslot_last = {e: [None] * NDSEM for e in self.ENGS}
        self.last_nd = {e: None for e in self.ENGS}
        self.pending_bar = {e: [] for e in self.ENGS}

    def op(self, eng, fn, r=(), w=(), dma=False):
        oid = len(self.ops)
        deps = []
        for k in r:
            y = self.lw.get(k)
            if y is not None:
                deps.append((y, "RAW"))
        for k in w:
            y = self.lw.get(k)
            if y is not None:
                deps.append((y, "WAW"))
            for y in self.rd.get(k, ()):
                deps.append((y, "WAR"))
        for y in self.pending_bar[eng]:
            deps.append((y, "RAW"))
        self.pending_bar[eng] = []
        o = dict(id=oid, eng=eng, fn=fn, deps=deps, dma=dma)
        if dma:
            i = self.dcnt[eng]
            self.dcnt[eng] += 1
            slot = i % NDSEM
            o["dslot"] = slot
            o["dval"] = 16 * (i // NDSEM + 1)
            prev = self.dslot_last[eng][slot]
            if prev is not None:
                deps.append((prev, "RAW"))
            self.dslot_last[eng][slot] = oid
        else:
            self.cnt[eng] += 1
            o["val"] = self.cnt[eng]
            self.last_nd[eng] = oid
        self.ops.append(o)
        for k in w:
            self.lw[k] = oid
            self.rd[k] = []
        for k in r:
            if k not in w:
                self.rd.setdefault(k, []).append(oid)
        return oid

    def barrier(self):
        snap = []
        for e in self.ENGS:
            if self.last_nd[e] is not None:
                snap.append(self.last_nd[e])
            for y in self.dslot_last[e]:
                if y is not None:
                    snap.append(y)
        for e in self.ENGS:
            self.pending_bar[e] = list(snap)
        self.lw = {}
        self.rd = {}

    def emit(self, st):
        nc = self.nc
        sems = {e: st.enter_context(nc.semaphore("s_" + e)) for e in self.ENGS}
        dsems = {e: [st.enter_context(nc.semaphore("d_%s%d" % (e, i))) for i in range(NDSEM)]
                 for e in ("sp", "pool", "act") if self.dcnt[e] > 0}
        block = st.enter_context(nc.Block())
        ops = self.ops

        def run(ename, eng):
            seen = {}
            for o in ops:
                if o["eng"] != ename:
                    continue
                need = {}
                for (y, kind) in o["deps"]:
                    Y = ops[y]
                    if Y["dma"]:
                        key = ("d", Y["eng"], Y["dslot"])
                        sem = dsems[Y["eng"]][Y["dslot"]]
                        val = Y["dval"]
                    else:
                        if Y["eng"] == ename and not o["dma"]:
                            if ename == "pe" or kind != "RAW":
                                continue
                        key = ("c", Y["eng"])
                        sem = sems[Y["eng"]]
                        val = Y["val"]
                    if seen.get(key, 0) >= val:
                        continue
                    if key not in need or need[key][1] < val:
                        need[key] = (sem, val)
                for key, (sem, val) in need.items():
                    eng.wait_ge(sem, val)
                    seen[key] = val
                ins = o["fn"](eng)
                if o["dma"]:
                    ins.then_inc(dsems[ename][o["dslot"]], 16)
                else:
                    ins.then_inc(sems[ename], 1)
            if ename in dsems:
                for slot in range(NDSEM):
                    y = self.dslot_last[ename][slot]
                    if y is not None:
                        Y = ops[y]
                        if seen.get(("d", ename, slot), 0) < Y["dval"]:
                            eng.wait_ge(dsems[ename][slot], Y["dval"])

        @block.tensor
        def _(e):
            run("pe", e)

        @block.vector
        def _(e):
            run("dve", e)

        @block.scalar
        def _(e):
            run("act", e)

        @block.gpsimd
        def _(e):
            run("pool", e)

        @block.sync
        def _(e):
            run("sp", e)


PW = 8 + 256 + 16 + 4096 + 8
PC0, PL0 = 8, 280


def _consts():
    c = {}
    c["ident"] = np.eye(128, dtype=np.float32)
    rot = np.zeros((128, 128), np.float32)
    for d in range(128):
        rot[d ^ 16, d] = 1.0
    c["rot"] = rot
    n = 4096
    pos_row = np.repeat(np.arange(n // 64), 64)
    pos_col = np.tile(np.arange(64), n // 64)
    inv_freq = (10000.0 ** (-np.arange(0, 32, 2, dtype=np.float32) / 32)).astype(np.float32)
    ang = np.stack([pos_row, pos_col], -1).astype(np.float32)[..., None] * inv_freq
    cs, sn = np.cos(ang).astype(np.float32), np.sin(ang).astype(np.float32)
    cos_t = np.zeros((128, n), np.float32)
    sin_t = np.zeros((128, n), np.float32)
    for d in range(128):
        dd = d % 64
        a, hf, i = dd // 32, (dd // 16) % 2, dd % 16
        cos_t[d] = cs[:, a, i]
        sin_t[d] = sn[:, a, i] * (-1.0 if hf == 0 else 1.0)
    c["cos_t"] = cos_t
    c["sin_t"] = sin_t
    inv = np.zeros((4, PW), np.float32)
    for g, w in enumerate((2, 4, 8, 16)):
        h = w // 2
        for (n_, off) in ((256, PC0), (4096, PL0)):
            t = np.arange(n_)
            lo = np.clip(t - h, 0, n_)
            hi = np.clip(t + h, 0, n_)
            inv[g, off:off + n_] = 1.0 / (hi - lo).astype(np.float32)
    c["invcnt"] = inv
    p = np.arange(128)[:, None] % 64
    t = np.arange(64)[None, :]
    c["mask_f"] = (p <= t).astype(np.float32)
    c["mask_b"] = (p >= t).astype(np.float32)
    return c


W_NAMES = ["ada_w", "ada_b", "norm_g", "mix_w_out", "ffn_w_gate", "ffn_w_up", "ffn_conv_w",
           "ffn_conv_b", "ffn_w_down", "ev_w_in", "pool_w", "pool_scale", "diff_lambda",
           "diff_subln", "od_w_in", "mla_q_norm", "mla_w_uq", "mla_kv_norm", "mla_w_ukv",
           "hgrn_norm", "hgrn_lb"]
W_SHAPES = {"ada_w": (2, 1024, 6144), "ada_b": (2, 6144), "norm_g": (2, 4, 1024),
            "mix_w_out": (2, 1024, 1024), "ffn_w_gate": (2, 1024, 2816), "ffn_w_up": (2, 1024, 2816),
            "ffn_conv_w": (2, 3, 2816), "ffn_conv_b": (2, 2816), "ffn_w_down": (2, 2816, 1024),
            "ev_w_in": (1, 1024, 2048), "pool_w": (1, 4, 128, 128), "pool_scale": (1, 512),
            "diff_lambda": (1, 4, 64), "diff_subln": (1, 128), "od_w_in": (1, 1024, 3392),
            "mla_q_norm": (1, 512), "mla_w_uq": (1, 512, 768), "mla_kv_norm": (1, 256),
            "mla_w_ukv": (1, 256, 1024), "hgrn_norm": (1, 128), "hgrn_lb": (2, 2, 512)}
C_SHAPES = {"ident": (128, 128), "rot": (128, 128), "cos_t": (128, 4096), "sin_t": (128, 4096),
            "invcnt": (4, PW), "mask_f": (128, 64), "mask_b": (128, 64)}

BLOCKS = [(0, 2)] + [(2 + 4 * i, 4) for i in range(8)]


def build(stop=None, dbg=False):
    nc = bass.Bass("TRN2", target_bir_lowering=False)
    din = {}
    din["x"] = nc.dram_tensor("x", [4096, D], F32, kind="ExternalInput").ap()
    din["ctx"] = nc.dram_tensor("ctx", [256, D], F32, kind="ExternalInput").ap()
    din["cvec"] = nc.dram_tensor("cvec", [2, D], F32, kind="ExternalInput").ap()
    for n in W_NAMES:
        din[n] = nc.dram_tensor(n, list(W_SHAPES[n]), F32, kind="ExternalInput").ap()
    for n in C_SHAPES:
        din[n] = nc.dram_tensor(n, list(C_SHAPES[n]), F32, kind="ExternalInput").ap()
    out = nc.dram_tensor("out", [4096, D], F32, kind="ExternalOutput").ap()
    XR = nc.dram_tensor("XR", [NTOK, D], F32, kind="ExternalOutput" if dbg else "Internal").ap()
    MODV = nc.dram_tensor("MODV", [2, 2, 6, D], F32, kind="Internal").ap()
    QT = nc.dram_tensor("QT", [9, 128, 4, 512], BF16, kind="Internal").ap()
    QR = nc.dram_tensor("QR", [9, 128, 2, 512], BF16, kind="Internal").ap()
    UT = nc.dram_tensor("UT", [4, 128, NTOK], F32, kind="Internal").ap()
    HT1 = nc.dram_tensor("HT1", [9, 128, 8, 512], BF16, kind="Internal").ap()
    H2D = nc.dram_tensor("H2D", [128, 8, 4355], BF16, kind="Internal").ap()
    MIXD = nc.dram_tensor("MIXD", [8, 128, NTOK], BF16, kind="ExternalOutput" if dbg else "Internal").ap()

    st = ExitStack()
    with st:
        P = Prog(nc)
        AW = 52000
        arena = st.enter_context(nc.sbuf_tensor("arena", [128, AW], F32))
        psall = st.enter_context(nc.psum_tensor("psall", [128, 4096], F32))[:]
        ps = [psall[:, i * 512:(i + 1) * 512] for i in range(8)]
        psb = [p.bitcast(BF16) for p in ps]
        top = [0]

        def T(shape, dt=F32):
            n = int(np.prod(shape[1:]))
            cols = n if dt == F32 else (n + 1) // 2
            off = top[0]
            top[0] += cols
            assert top[0] <= AW, "SBUF arena overflow %d" % top[0]
            a = arena[0:shape[0], off:off + cols]
            if dt != F32:
                a = a.bitcast(dt)
            if len(shape) == 3:
                a = a.rearrange("p (a b) -> p a b", a=shape[1])
            elif len(shape) == 4:
                a = a.rearrange("p (a b c) -> p a b c", a=shape[1], b=shape[2])
            return a

        uid = [0]

        def K(s):
            uid[0] += 1
            return "%s#%d" % (s, uid[0])

        def dma(q, o, i, r=(), w=()):
            P.op(q, lambda e: e.dma_start(out=o, in_=i), r=r, w=w, dma=True)

        def dma_nc(q, o, i, r=(), w=()):
            P.op(q, lambda e: e.dma_start(out=o, in_=i, allow_slow_non_contiguous=True), r=r, w=w, dma=True)

        def mmg(o, pairs, r, w):
            def f(e):
                n = len(pairs)
                for i, (l, rh) in enumerate(pairs):
                    ins = e.matmul(o, lhsT=l, rhs=rh, start=(i == 0), stop=(i == n - 1))
                return ins
            P.op("pe", f, r=r, w=w)

        identb = T([128, 128], BF16)
        rotb = T([128, 128], BF16)
        onesb = T([128, 128], BF16)
        maskf = T([128, 64])
        maskb = T([128, 64])
        stg = [T([128, 2048]) for _ in range(2)]
        stgi = [0]
        PERSIST = None

        def wload(dst, src, q=None, ce="pool"):
            i = stgi[0] % 2
            stgi[0] += 1
            shp = list(dst.shape)
            n = int(np.prod(shp[1:]))
            if n > 2048:
                hh = shp[-1] // 2
                if len(shp) == 2:
                    wload(dst[:, 0:hh], src[:, 0:hh], q, ce)
                    wload(dst[:, hh:], src[:, hh:], q, ce)
                else:
                    wload(dst[:, :, 0:hh], src[:, :, 0:hh], q, ce)
                    wload(dst[:, :, hh:], src[:, :, hh:], q, ce)
                return
            s = stg[i][0:shp[0], 0:n]
            if len(shp) == 3:
                s = s.rearrange("p (a b) -> p a b", a=shp[1])
            qq = q or ("sp" if i == 0 else "pool")
            dma(qq, s, src, w=["stg%d" % i])
            kd = "W" + str(id(dst))
            if ce == "pool":
                P.op("pool", lambda e: e.tensor_copy(out=dst, in_=s), r=["stg%d" % i], w=["wgt"])
            elif ce == "dve":
                P.op("dve", lambda e: e.tensor_copy(out=dst, in_=s), r=["stg%d" % i], w=["wgt"])
            else:
                P.op("act", lambda e: e.copy(out=dst, in_=s), r=["stg%d" % i], w=["wgt"])

        for (dst, nm) in ((identb, "ident"), (rotb, "rot")):
            wload(dst, din[nm])
        P.op("pool", lambda e: e.memset(onesb, 1.0), w=["wgt"])
        dma("sp", maskf, din["mask_f"], w=["wgt"])
        dma("sp", maskb, din["mask_b"], w=["wgt"])
        PERSIST = top[0]

        def phase_reset():
            P.barrier()
            top[0] = PERSIST

        def phase_mod():
            cv = T([128, 2, 8])
            cvs = T([128, 2, 8])
            cvb = T([128, 8, 2], BF16)
            for j in range(2):
                dma_nc("sp", cv[:, j, :], din["cvec"][j].rearrange("(k p) -> p k", p=128), w=["cv"])
            P.op("act", lambda e: e.activation(out=cvs, in_=cv, func=AF.Silu), r=["cv"], w=["cvs"])
            P.op("dve", lambda e: e.tensor_copy(out=cvb, in_=cvs.rearrange("p j k -> p k j")), r=["cvs"], w=["cvb"])
            awb = [T([128, 8, 256], BF16) for _ in range(2)]
            Mt = T([2, 6 * D])
            bt = T([2, 6 * D])
            ng = T([2, 4, D])
            V = T([2, 6, D])
            for l in range(2):
                dma("sp", bt, din["ada_b"][l].partition_broadcast(2), w=["bt"])
                dma("sp", ng.rearrange("p a b -> p (a b)"),
                    din["norm_g"][l].rearrange("a b -> (a b)").partition_broadcast(2), w=["ng"])
                for nb in range(24):
                    ab = awb[nb % 2]
                    kab = "awb%d" % (nb % 2)
                    src = din["ada_w"][l].rearrange("(k p) n -> p k n", p=128)[:, :, nb * 256:(nb + 1) * 256]
                    i = stgi[0] % 2
                    stgi[0] += 1
                    s = stg[i][:, :].rearrange("p (a b) -> p a b", a=8)
                    dma("sp" if i == 0 else "pool", s, src, w=["stg%d" % i])
                    P.op("pool" if nb % 2 == 0 else "dve", (lambda e, ab=ab, s=s: e.tensor_copy(out=ab, in_=s)),
                         r=["stg%d" % i], w=[kab])
                    pb = ps[nb % 2][0:2, 0:256]
                    mmg(pb, [(cvb[:, k, :], ab[:, k, :]) for k in range(8)], r=["cvb", kab], w=["ps%d" % (nb % 2)])
                    P.op("dve", (lambda e, pb=pb, nb=nb: e.tensor_tensor(out=Mt[:, nb * 256:(nb + 1) * 256], in0=pb,
                                                                       in1=bt[:, nb * 256:(nb + 1) * 256], op=ALU.add)),
                         r=["ps%d" % (nb % 2), "bt"], w=["Mt"])
                sl = lambda i: Mt[:, i * D:(i + 1) * D]
                P.op("dve", lambda e: e.scalar_tensor_tensor(out=V[:, 0, :], in0=sl(1), scalar=1.0, in1=ng[:, 0, :],
                                                             op0=ALU.add, op1=ALU.mult), r=["Mt", "ng"], w=["V"])
                P.op("dve", lambda e: e.tensor_copy(out=V[:, 1, :], in_=sl(0)), r=["Mt"], w=["V"])
                P.op("dve", lambda e: e.tensor_tensor(out=V[:, 2, :], in0=sl(2), in1=ng[:, 1, :], op=ALU.mult),
                     r=["Mt", "ng"], w=["V"])
                P.op("dve", lambda e: e.scalar_tensor_tensor(out=V[:, 3, :], in0=sl(4), scalar=1.0, in1=ng[:, 2, :],
                                                             op0=ALU.add, op1=ALU.mult), r=["Mt", "ng"], w=["V"])
                P.op("dve", lambda e: e.tensor_copy(out=V[:, 4, :], in_=sl(3)), r=["Mt"], w=["V"])
                P.op("dve", lambda e: e.tensor_tensor(out=V[:, 5, :], in0=sl(5), in1=ng[:, 3, :], op=ALU.mult),
                     r=["Mt", "ng"], w=["V"])
                dma("sp", MODV[l].rearrange("j a b -> j (a b)"), V.rearrange("p a b -> p (a b)"), r=["V"], w=["MODV"])

        def x_src(layer0_in, tt):
            if layer0_in:
                return din["ctx"][tt * 128:(tt + 1) * 128, :] if tt < 2 else din["x"][(tt - 2) * 128:(tt - 1) * 128, :]
            return XR[tt * 128:(tt + 1) * 128, :]

        class NM:
            def __init__(self, l, gi, si):
                self.xt = [T([128, D]) for _ in range(2)]
                self.junk = T([128, D], BF16)
                self.tmp = [T([128, D]) for _ in range(2)]
                self.pending = None
                self.hb = [T([128, D], BF16) for _ in range(2)]
                self.ss = T([128, 2])
                self.rs = T([128, 2])
                self.G = [T([128, D]) for _ in range(2)]
                self.SH = [T([128, D]) for _ in range(2)]
                for j in range(2):
                    dma("sp", self.G[j], MODV[l, j, gi].partition_broadcast(128), r=["MODV"], w=["nmG"])
                    dma("sp", self.SH[j], MODV[l, j, si].partition_broadcast(128), r=["MODV"], w=["nmG"])
                self.i = 0

            def run(self, src, is_ctx, dst, dkey, bank):
                i = self.i % 2
                self.i += 1
                xt, hb = self.xt[i], self.hb[i]
                kx, kh = "nm_xt%d" % i, "nm_hb%d" % i
                ss, rs = self.ss[:, i:i + 1], self.rs[:, i:i + 1]
                G, SH = self.G[1 if is_ctx else 0], self.SH[1 if is_ctx else 0]
                tmp = self.tmp[i]
                dma("sp", xt, src, r=["XR"], w=[kx])
                P.op("act", lambda e: e.activation(out=self.junk, in_=xt, func=AF.Square, accum_out=ss),
                     r=[kx], w=["nm_junk", "nm_ss%d" % i])
                P.op("act", lambda e: e.activation(out=rs, in_=ss, func=AF.Sqrt, scale=1.0 / D, bias=EPS),
                     r=["nm_ss%d" % i], w=["nm_rs%d" % i])
                P.op("dve", lambda e: e.reciprocal(out=rs, in_=rs), r=["nm_rs%d" % i], w=["nm_rs%d" % i])
                P.op("dve", lambda e: e.scalar_tensor_tensor(out=tmp, in0=xt, scalar=rs, in1=G, op0=ALU.mult,
                                                             op1=ALU.mult), r=[kx, "nm_rs%d" % i, "nmG"], w=["nm_tmp%d" % i])
                P.op("pool", lambda e: e.tensor_tensor(out=hb, in0=tmp, in1=SH, op=ALU.add),
                     r=["nm_tmp%d" % i, "nmG"], w=[kh])
                prev = self.pending
                self.pending = (hb, kh, dst, dkey, bank)
                if prev is not None:
                    self.stage_b(*prev)

            def stage_b(self, hb, kh, dst, dkey, bank):
                pb = psb[bank]

                def tr(e):
                    for k in range(8):
                        ins = e.transpose(out=pb[:, k * 128:(k + 1) * 128], in_=hb[:, k * 128:(k + 1) * 128],
                                          identity=identb)
                    return ins
                P.op("pe", tr, r=[kh], w=["ps%d" % bank])
                P.op("act", lambda e: e.copy(out=dst, in_=pb.rearrange("p (k n) -> p k n", k=8)),
                     r=["ps%d" % bank], w=[dkey])

            def flush(self):
                if self.pending is not None:
                    self.stage_b(*self.pending)
                    self.pending = None

        class RES:
            def __init__(self, l, gidx):
                self.GT = [T([128, D]) for _ in range(2)]
                for j in range(2):
                    dma("sp", self.GT[j], MODV[l, j, gidx].partition_broadcast(128), r=["MODV"], w=["resG"])
                self.xo = [T([128, D]) for _ in range(2)]
                self.tt_ = [T([128, D]) for _ in range(2)]
                self.junk = T([128, 512], BF16)
                self.ss = T([128, 4])
                self.rs = T([128, 2])
                self.i = 0

            def run(self, b0, b1, src, dstd, is_ctx, wkey):
                i = self.i % 2
                self.i += 1
                xo = self.xo[i]
                tbuf = self.tt_[i]
                kt_ = "res_t%d" % i
                kx = "res_x%d" % i
                ss = self.ss[:, 2 * i:2 * i + 2]
                rs = self.rs[:, i:i + 1]
                GT = self.GT[1 if is_ctx else 0]
                dma("pool", xo, src, r=["XR"], w=[kx])
                P.op("act", lambda e: e.activation(out=self.junk, in_=ps[b0], func=AF.Square, accum_out=ss[:, 0:1]),
                     r=["ps%d" % b0], w=["res_junk", "res_ss%d" % i])
                P.op("act", lambda e: e.activation(out=self.junk, in_=ps[b1], func=AF.Square, accum_out=ss[:, 1:2]),
                     r=["ps%d" % b1], w=["res_junk", "res_ss%d" % i])
                P.op("dve", lambda e: e.tensor_tensor(out=rs, in0=ss[:, 0:1], in1=ss[:, 1:2], op=ALU.add),
                     r=["res_ss%d" % i], w=["res_rs%d" % i])
                P.op("act", lambda e: e.activation(out=rs, in_=rs, func=AF.Sqrt, scale=1.0 / D, bias=EPS),
                     r=["res_rs%d" % i], w=["res_rs%d" % i])
                P.op("dve", lambda e: e.reciprocal(out=rs, in_=rs), r=["res_rs%d" % i], w=["res_rs%d" % i])
                for hf, bk in ((0, b0), (1, b1)):
                    P.op("dve", (lambda e, hf=hf, bk=bk: e.scalar_tensor_tensor(
                        out=tbuf[:, hf * 512:(hf + 1) * 512], in0=ps[bk], scalar=rs,
                        in1=GT[:, hf * 512:(hf + 1) * 512], op0=ALU.mult, op1=ALU.mult)),
                        r=["ps%d" % bk, "res_rs%d" % i, "resG"], w=[kt_, "ps%d" % bk])
                P.op("pool", lambda e: e.tensor_tensor(out=xo, in0=tbuf, in1=xo, op=ALU.add),
                     r=[kt_, kx], w=[kx])
                dma("pool", dstd, xo, r=[kx], w=[wkey])

        def rope_evict(pbank, n, t0, dst, dkey, ro, is_ctx):
            src = ps[pbank][:, 0:n]
            if is_ctx:
                P.op("act", lambda e: e.copy(out=dst, in_=src), r=["ps%d" % pbank], w=[dkey])
                return
            qs, t1, t2, cosb, sinb, rb = ro
            P.op("act", lambda e: e.copy(out=qs[:, 0:n], in_=src), r=["ps%d" % pbank], w=["ro_qs"])
            mmg(ps[rb][:, 0:n], [(rotb, qs[:, 0:n])], r=["ro_qs"], w=["ps%d" % rb])
            P.op("dve", lambda e: e.tensor_tensor(out=t1[:, 0:n], in0=src, in1=cosb[:, 0:n], op=ALU.mult),
                 r=["ps%d" % pbank, "ro_cs"], w=["ro_t1", "ps%d" % pbank])
            P.op("dve", lambda e: e.tensor_tensor(out=t2[:, 0:n], in0=ps[rb][:, 0:n], in1=sinb[:, 0:n], op=ALU.mult),
                 r=["ps%d" % rb, "ro_cs"], w=["ro_t2", "ps%d" % rb])
            P.op("pool", lambda e: e.tensor_tensor(out=dst, in0=t1[:, 0:n], in1=t2[:, 0:n], op=ALU.add),
                 r=["ro_t1", "ro_t2"], w=[dkey])

        def rope_tiles():
            return (T([128, 512], BF16), T([128, 512]), T([128, 512]), T([128, 512]), T([128, 512]))

        def rope_load(ro, b):
            if b == 0:
                return
            c0 = (b - 1) * 512
            dma("pool", ro[3], din["cos_t"][:, c0:c0 + 512], w=["ro_cs"])
            dma("pool", ro[4], din["sin_t"][:, c0:c0 + 512], w=["ro_cs"])

        def attention(groups, scale, accsets):
            PT = [T([128, 2, 512], BF16) for _ in range(3)]
            N = len(groups)

            def emitS(i):
                g = groups[i]
                if g.get("pre"):
                    g["pre"]()
                A = 2 * (i % 2)
                n = g["n"]
                for mi, m in enumerate(g["members"]):
                    mmg(ps[A + mi][:, 0:n], m["qk"], r=g["rk"], w=["ps%d" % (A + mi)])

            def emitE(i):
                g = groups[i]
                A = 2 * (i % 2)
                n = g["n"]
                nm_ = len(g["members"])
                pt = PT[i % 3]
                src = psall[:, A * 512:(A + 2) * 512].rearrange("p (j n) -> p j n", j=2)[:, 0:nm_, 0:n]
                P.op("act", (lambda e: e.activation(out=pt[:, 0:nm_, 0:n], in_=src, func=AF.Exp, scale=scale)),
                     r=["ps%d" % A, "ps%d" % (A + 1)], w=["PT%d" % (i % 3), "ps%d" % A, "ps%d" % (A + 1)])

            def emitPV(i):
                g = groups[i]
                n = g["n"]
                pt = PT[i % 3]
                acc = accsets[g["accset"]]
                wk = []
                for m in g["members"]:
                    ob, sb2 = acc[m["acc"]]
                    wk += ["ps%d" % ob, "ps%d" % sb2]

                def pv(e):
                    for mi, m in enumerate(g["members"]):
                        ob, sb2 = acc[m["acc"]]
                        e.matmul(ps[ob][:, 0:n], lhsT=m["v"], rhs=pt[:, mi, 0:n], start=m["start"], stop=m["stop"])
                        ins = e.matmul(ps[sb2][:, 0:n], lhsT=onesb, rhs=pt[:, mi, 0:n], start=m["start"], stop=m["stop"])
                    return ins
                P.op("pe", pv, r=["PT%d" % (i % 3)] + g["rv"], w=list(dict.fromkeys(wk)))
                if g.get("post"):
                    A = 2 * (i % 2)
                    g["post"](acc, (A, A + 1))

            if N:
                emitS(0)
            for i in range(N):
                emitE(i)
                if i + 1 < N:
                    emitS(i + 1)
                emitPV(i)

        def phase_ffn(l, need_ctx, final):
            W2 = 4355
            Wd = T([128, NJ, D], BF16)
            CW = T([128, 4, NJ])
            for i in range(3):
                dma_nc("sp", CW[:, i, :], din["ffn_conv_w"][l, i].rearrange("(j p) -> p j", p=128), w=["CW"])
            dma_nc("sp", CW[:, 3, :], din["ffn_conv_b"][l].rearrange("(j p) -> p j", p=128), w=["CW"])
            wdsrc = din["ffn_w_down"][l].rearrange("(j p) n -> p j n", p=128)
            for j0 in range(0, NJ, 4):
                j1 = min(NJ, j0 + 4)
                wload(Wd[:, j0:j1, :], wdsrc[:, j0:j1, :])
            top_save = top[0]
            nm = NM(l, 3, 4)
            hTs = [T([128, 8, 512], BF16) for _ in range(2)]
            zt = T([128, 8, 1], BF16)
            P.op("pool", lambda e: e.memset(zt, 0.0), w=["zt"])
            for c in (0, 257, 4354):
                dma_nc("pool", H2D[:, :, c:c + 1], zt, r=["zt"], w=["H2D"])
            blks = list(range(0 if need_ctx else 1, 9))
            for b in blks:
                t0, ntl = BLOCKS[b]
                n = ntl * 128
                hT = hTs[b % 2]
                kh = "hTs%d" % (b % 2)
                for ti in range(ntl):
                    tt = t0 + ti
                    nm.run(x_src(False, tt), tt < 2, hT[:, :, ti * 128:(ti + 1) * 128], kh, 6 + tt % 2)
                nm.flush()
                c0 = 1 if b == 0 else 258 + (b - 1) * 512
                dma("pool", H2D[:, :, c0:c0 + n], hT[:, :, 0:n], r=[kh], w=["H2D"])
            P.barrier()
            top[0] = top_save
            res = RES(l, 5)
            GTt = T([128, NJ, 1024], BF16)
            H2P = [T([128, 8, 1026], BF16) for _ in range(2)]
            wgf = [T([128, 8, 128], BF16) for _ in range(2)]
            wuf = [T([128, 8, 128], BF16) for _ in range(2)]
            acc = [T([128, 512]) for _ in range(2)]
            sil = [T([128, 512]) for _ in range(2)]
            parts = [blks[i:i + 2] for i in range(0, len(blks), 2)]
            wgsrc = din["ffn_w_gate"][l].rearrange("(k p) n -> p k n", p=128)
            wusrc = din["ffn_w_up"][l].rearrange("(k p) n -> p k n", p=128)
            it = [0]
            yi = [0]
            bc0 = lambda b: 1 if b == 0 else 258 + (b - 1) * 512
            for pi, part in enumerate(parts):
                cstart = bc0(part[0]) - 1
                cend = bc0(part[-1]) + BLOCKS[part[-1]][1] * 128 + 1
                npc = cend - cstart
                H2T = H2P[pi % 2]
                kH = "H2P%d" % (pi % 2)
                dma("sp", H2T[:, :, 0:npc], H2D[:, :, cstart:cend], r=["H2D"], w=[kH])
                goff = {}
                o = 0
                for b in part:
                    goff[b] = o
                    o += BLOCKS[b][1] * 128
                for j in range(NJ):
                    wg, wu = wgf[j % 2], wuf[j % 2]
                    kw = "ffw%d" % (j % 2)
                    for (dst, srcw) in ((wg, wgsrc), (wu, wusrc)):
                        i = stgi[0] % 2
                        stgi[0] += 1
                        s = stg[i][:, 0:1024].rearrange("p (a b) -> p a b", a=8)
                        dma("sp", s, srcw[:, :, j * 128:(j + 1) * 128], w=["stg%d" % i])
                        P.op("pool", (lambda e, dst=dst, s=s: e.tensor_copy(out=dst, in_=s)), r=["stg%d" % i], w=[kw])
                    for b in part:
                        t0, ntl = BLOCKS[b]
                        n = ntl * 128
                        c0 = bc0(b) - cstart
                        q = it[0] % 2
                        it[0] += 1
                        pa, pu, ph = ps[q], ps[2 + q], ps[4 + q]
                        ka, ku, kh = "ps%d" % q, "ps%d" % (2 + q), "ps%d" % (4 + q)
                        mmg(pa[:, 0:n], [(wg[:, k, :], H2T[:, k, c0:c0 + n]) for k in range(8)], r=[kw, kH], w=[ka])
                        hal = H2T[:, :, c0 - 1:c0 + n + 1:n + 1]
                        mmg(ph[:, 0:2], [(wg[:, k, :], hal[:, k, :]) for k in range(8)], r=[kw, kH], w=[kh])
                        mmg(pu[:, 0:n], [(wu[:, k, :], H2T[:, k, c0:c0 + n]) for k in range(8)], r=[kw, kH], w=[ku])
                        ac, sl_ = acc[q], sil[q]
                        kac, ksl = "acc%d" % q, "sil%d" % q
                        w0, w1, w2, bb = (CW[:, i, j:j + 1] for i in range(4))
                        P.op("dve", (lambda e, ac=ac, pa=pa, w1=w1, bb=bb, n=n: e.tensor_scalar(
                            out=ac[:, 0:n], in0=pa[:, 0:n], scalar1=w1, scalar2=bb, op0=ALU.mult, op1=ALU.add)),
                            r=[ka, "CW"], w=[kac])
                        P.op("dve", (lambda e, ac=ac, pa=pa, w0=w0, n=n: e.scalar_tensor_tensor(
                            out=ac[:, 1:n], in0=pa[:, 0:n - 1], scalar=w0, in1=ac[:, 1:n], op0=ALU.mult, op1=ALU.add)),
                            r=[ka, kac], w=[kac])
                        P.op("dve", (lambda e, ac=ac, pa=pa, w2=w2, n=n: e.scalar_tensor_tensor(
                            out=ac[:, 0:n - 1], in0=pa[:, 1:n], scalar=w2, in1=ac[:, 0:n - 1], op0=ALU.mult, op1=ALU.add)),
                            r=[ka, kac], w=[kac, ka])
                        P.op("dve", (lambda e, ac=ac, ph=ph, w0=w0: e.scalar_tensor_tensor(
                            out=ac[:, 0:1], in0=ph[:, 0:1], scalar=w0, in1=ac[:, 0:1], op0=ALU.mult, op1=ALU.add)),
                            r=[kh, kac], w=[kac])
                        P.op("dve", (lambda e, ac=ac, ph=ph, w2=w2, n=n: e.scalar_tensor_tensor(
                            out=ac[:, n - 1:n], in0=ph[:, 1:2], scalar=w2, in1=ac[:, n - 1:n], op0=ALU.mult, op1=ALU.add)),
                            r=[kh, kac], w=[kac, kh])
                        P.op("act", (lambda e, ac=ac, sl_=sl_, n=n: e.activation(out=sl_[:, 0:n], in_=ac[:, 0:n], func=AF.Silu)),
                             r=[kac], w=[ksl])
                        g0 = goff[b]
                        P.op("dve", (lambda e, sl_=sl_, pu=pu, j=j, g0=g0, n=n: e.tensor_tensor(
                            out=GTt[:, j, g0:g0 + n], in0=sl_[:, 0:n], in1=pu[:, 0:n], op=ALU.mult)),
                            r=[ksl, ku], w=["GT", ku])
                for b in part:
                    t0, ntl = BLOCKS[b]
                    for ti in range(ntl):
                        tt = t0 + ti
                        g0 = goff[b] + ti * 128
                        y0, y1 = ((6, 7), (0, 1), (2, 3))[yi[0] % 3]
                        yi[0] += 1
                        for hf, yb in ((0, y0), (1, y1)):
                            mmg(ps[yb], [(GTt[:, j, g0:g0 + 128], Wd[:, j, hf * 512:(hf + 1) * 512]) for j in range(NJ)],
                                r=["GT", "wgt"], w=["ps%d" % yb])
                        if final:
                            dstd = out[(tt - 2) * 128:(tt - 1) * 128, :]
                            res.run(y0, y1, x_src(False, tt), dstd, tt < 2, "OUT")
                        else:
                            res.run(y0, y1, x_src(False, tt), XR[tt * 128:(tt + 1) * 128, :], tt < 2, "XR2")

        def phase_l0():
            l = 0
            LAM_INIT = 0.2
            KT = T([128, 4, NTOK], BF16)
            Vv = T([128, NT, 512], BF16)
            keep = top[0]
            w_in = T([128, 8, 2048], BF16)
            wsrc = din["ev_w_in"][0].rearrange("(k p) n -> p k n", p=128)
            for c in range(4):
                wload(w_in[:, :, c * 512:(c + 1) * 512], wsrc[:, :, c * 512:(c + 1) * 512])
            nm = NM(l, 0, 1)
            hT = T([128, 8, 512], BF16)
            ro = rope_tiles() + (7,)
            ub = [T([128, 512]) for _ in range(2)]
            qb = [T([128, 4, 512], BF16) for _ in range(2)]
            for b, (t0, ntl) in enumerate(BLOCKS):
                n = ntl * 128
                col0 = t0 * 128
                rope_load(ro, b)
                for ti in range(ntl):
                    tt = t0 + ti
                    nm.run(x_src(True, tt), tt < 2, hT[:, :, ti * 128:(ti + 1) * 128], "hT", 6)
                nm.flush()
                for ti in range(ntl):
                    tt = t0 + ti
                    bk = 4 + (ti % 2)
                    mmg(ps[bk], [(hT[:, k, ti * 128:(ti + 1) * 128], w_in[:, k, 1536:2048]) for k in range(8)],
                        r=["hT", "wgt"], w=["ps%d" % bk])
                    P.op("act", (lambda e, tt=tt, bk=bk: e.copy(out=Vv[:, tt, :], in_=ps[bk])), r=["ps%d" % bk], w=["Vv"])
                for c in range(4):
                    bk = c % 2
                    mmg(ps[bk][:, 0:n], [(w_in[:, k, c * 128:(c + 1) * 128], hT[:, k, 0:n]) for k in range(8)],
                        r=["hT", "wgt"], w=["ps%d" % bk])
                    u = ub[c % 2]
                    P.op("act", (lambda e, u=u, bk=bk, n=n: e.copy(out=u[:, 0:n], in_=ps[bk][:, 0:n])),
                         r=["ps%d" % bk], w=["ub%d" % (c % 2)])
                    dma("pool", UT[c][:, col0:col0 + n], u[:, 0:n], r=["ub%d" % (c % 2)], w=["UT"])
                qbb = qb[b % 2]
                kq = "qb%d" % (b % 2)
                for c in range(4):
                    bk = 2 + (c % 2)
                    mmg(ps[bk][:, 0:n], [(w_in[:, k, 512 + c * 128:512 + (c + 1) * 128], hT[:, k, 0:n]) for k in range(8)],
                        r=["hT", "wgt"], w=["ps%d" % bk])
                    rope_evict(bk, n, col0, qbb[:, c, 0:n], kq, ro, b == 0)
                dma("pool", QT[b][:, :, 0:n], qbb[:, :, 0:n], r=[kq], w=["QT"])
                for c in range(4):
                    bk = 2 + (c % 2)
                    mmg(ps[bk][:, 0:n], [(w_in[:, k, 1024 + c * 128:1024 + (c + 1) * 128], hT[:, k, 0:n]) for k in range(8)],
                        r=["hT", "wgt"], w=["ps%d" % bk])
                    rope_evict(bk, n, col0, KT[:, c, col0:col0 + n], "KT", ro, b == 0)
            P.barrier()
            top[0] = keep
            if stop == "l0a1":
                return
            pw = T([128, 4, 128], BF16)
            for g in range(4):
                wload(pw[:, g, :], din["pool_w"][0, g])
            psc = T([128, 4])
            dma_nc("sp", psc, din["pool_scale"][0].rearrange("(g p) -> p g", p=128), w=["psc"])
            UP = T([128, PW])
            Aa = T([128, PW])
            Ab = T([128, PW])
            IC = T([128, PW])
            dT = T([128, PW], BF16)
            mpo = [T([128, 512], BF16) for _ in range(2)]
            for g in range(4):
                hw = (1, 2, 4, 8)[g]
                P.op("pool", lambda e: e.memset(UP, 0.0), w=["UP"])
                dma("sp", UP[:, PC0:PC0 + 256], UT[g][:, 0:256], r=["UT"], w=["UP"])
                dma("sp", UP[:, PL0:PL0 + 4096], UT[g][:, 256:NTOK], r=["UT"], w=["UP"])
                dma("pool", IC, din["invcnt"][g].partition_broadcast(128), w=["IC"])
                cur, ck = UP, "UP"
                bufs = [(Aa, "Aa"), (Ab, "Ab")]
                width = PW
                for s in range(g + 1):
                    sh = 1 << s
                    nxt, nk = bufs[s % 2]
                    width -= sh
                    P.op("dve", (lambda e, cur=cur, nxt=nxt, sh=sh, width=width: e.tensor_tensor(
                        out=nxt[:, 0:width], in0=cur[:, 0:width], in1=cur[:, sh:sh + width], op=ALU.add)),
                        r=[ck], w=[nk])
                    cur, ck = nxt, nk
                oth, ok = bufs[(g + 1) % 2]
                P.op("dve", (lambda e, cur=cur, oth=oth, hw=hw: e.tensor_tensor(
                    out=oth[:, 8:PW - 8], in0=cur[:, 8 - hw:PW - 8 - hw], in1=IC[:, 8:PW - 8], op=ALU.mult)),
                    r=[ck, "IC"], w=[ok])
                P.op("pool", (lambda e, oth=oth: e.tensor_tensor(out=dT[:, 8:PW - 8], in0=oth[:, 8:PW - 8],
                                                                 in1=UP[:, 8:PW - 8], op=ALU.subtract)),
                     r=[ok, "UP"], w=["dT"])
                for b, (t0, ntl) in enumerate(BLOCKS):
                    n = ntl * 128
                    col0 = t0 * 128
                    pc = PC0 if b == 0 else PL0 + (b - 1) * 512
                    bk = b % 2
                    mmg(ps[bk][:, 0:n], [(pw[:, g, :], dT[:, pc:pc + n])], r=["dT", "wgt"], w=["ps%d" % bk])
                    mp = mpo[b % 2]
                    P.op("act", (lambda e, bk=bk, n=n, g=g, mp=mp: e.activation(
                        out=mp[:, 0:n], in_=ps[bk][:, 0:n], func=AF.Identity, scale=psc[:, g:g + 1])),
                        r=["ps%d" % bk, "psc"], w=["mpo%d" % (b % 2)])
                    dma("pool", MIXD[g][:, col0:col0 + n], mp[:, 0:n], r=["mpo%d" % (b % 2)], w=["MIXD"])
            P.barrier()
            top[0] = keep
            if stop == "l0a2":
                return
            lamt = T([128, 4, 64])
            lj = T([128, 64])
            lsum = T([128, 2])
            nlam = T([128, 1])
            dma("sp", lamt.rearrange("p a b -> p (a b)"),
                din["diff_lambda"][0].rearrange("a b -> (a b)").partition_broadcast(128), w=["lamt"])
            for i in range(2):
                P.op("dve", (lambda e, i=i: e.scalar_tensor_tensor(out=lj, in0=lamt[:, 2 * i, :], scalar=1.0,
                                                                   in1=lamt[:, 2 * i + 1, :], op0=ALU.mult, op1=ALU.mult,
                                                                   accum_out=lsum[:, i:i + 1])),
                     r=["lamt"], w=["lj", "lsum"])
            P.op("act", lambda e: e.activation(out=lsum, in_=lsum, func=AF.Exp), r=["lsum"], w=["lsum"])
            P.op("dve", lambda e: e.tensor_tensor(out=nlam, in0=lsum[:, 1:2], in1=lsum[:, 0:1], op=ALU.subtract),
                 r=["lsum"], w=["nlam"])
            P.op("dve", lambda e: e.tensor_scalar(out=nlam, in0=nlam, scalar1=-LAM_INIT, scalar2=None, op0=ALU.add),
                 r=["nlam"], w=["nlam"])
            sln = T([128, 1])
            dma_nc("sp", sln, din["diff_subln"][0].rearrange("(p o) -> p o", o=1), w=["sln"])
            P.op("dve", lambda e: e.tensor_scalar(out=sln, in0=sln, scalar1=1.0 - LAM_INIT, scalar2=None, op0=ALU.mult),
                 r=["sln"], w=["sln"])
            qtl = [T([128, 4, 512], BF16) for _ in range(3)]
            MIXA = [T([128, 4, 512], BF16) for _ in range(2)]
            rsum = T([128, 512])
            o2 = [T([128, 512]) for _ in range(2)]
            oc = T([128, 512])
            sq = T([128, 512], BF16)
            rstd = T([128, 512])
            state = {}

            def pre(b):
                if state.get("b") != b:
                    state["b"] = b
                    n = BLOCKS[b][1] * 128
                    col0 = BLOCKS[b][0] * 128
                    dma("sp", qtl[b % 3][:, :, 0:n], QT[b][:, :, 0:n], r=["QT"], w=["qtl%d" % (b % 3)])

            def post(b, h, n, acc, scr):
                mixa = MIXA[b % 2]
                km = "MIXA%d" % (b % 2)
                for j in range(2):
                    ob, sb2 = acc[j]
                    P.op("dve", (lambda e, sb2=sb2: e.reciprocal(out=rsum[:, 0:n], in_=ps[sb2][:, 0:n])),
                         r=["ps%d" % sb2], w=["rsum", "ps%d" % sb2])
                    P.op("dve", (lambda e, ob=ob, j=j: e.tensor_tensor(out=o2[j][:, 0:n], in0=ps[ob][:, 0:n], in1=rsum[:, 0:n], op=ALU.mult)),
                         r=["ps%d" % ob, "rsum"], w=["o2_%d" % j, "ps%d" % ob])
                P.op("dve", lambda e: e.scalar_tensor_tensor(out=oc[:, 0:n], in0=o2[1][:, 0:n], scalar=nlam,
                                                             in1=o2[0][:, 0:n], op0=ALU.mult, op1=ALU.add),
                     r=["o2_0", "o2_1", "nlam"], w=["oc"])
                P.op("act", lambda e: e.activation(out=sq[:, 0:n], in_=oc[:, 0:n], func=AF.Square), r=["oc"], w=["sq"])
                sbk = scr[0]
                mmg(ps[sbk][:, 0:n], [(onesb, sq[:, 0:n])], r=["sq"], w=["ps%d" % sbk])
                P.op("act", lambda e: e.activation(out=rstd[:, 0:n], in_=ps[sbk][:, 0:n], func=AF.Sqrt, scale=1.0 / 128,
                                                   bias=EPS), r=["ps%d" % sbk], w=["rstd", "ps%d" % sbk])
                P.op("dve", lambda e: e.reciprocal(out=rstd[:, 0:n], in_=rstd[:, 0:n]), r=["rstd"], w=["rstd"])
                P.op("dve", lambda e: e.scalar_tensor_tensor(out=mixa[:, h, 0:n], in0=oc[:, 0:n], scalar=sln,
                                                             in1=rstd[:, 0:n], op0=ALU.mult, op1=ALU.mult),
                     r=["oc", "rstd", "sln"], w=[km])
                col0 = BLOCKS[b][0] * 128
                dma("pool", MIXD[4 + h][:, col0:col0 + n], mixa[:, h, 0:n], r=[km], w=["MIXD"])

            groups = []
            for b in range(9):
                n = BLOCKS[b][1] * 128
                kts = list(range(2)) if b == 0 else list(range(NT))
                for h in range(4):
                    for ki, kt in enumerate(kts):
                        mem = []
                        for j in range(2):
                            pr = slice(64 * j, 64 * j + 64)
                            mem.append(dict(qk=[(KT[pr, h, kt * 128:(kt + 1) * 128], qtl[b % 3][pr, h, 0:n])],
                                            v=Vv[:, kt, h * 128:(h + 1) * 128], acc=j, start=(ki == 0), stop=(ki == len(kts) - 1)))
                        g = dict(n=n, members=mem, rk=["KT", "qtl%d" % (b % 3)], rv=["Vv"], accset=0)
                        if h == 0 and ki == 0:
                            g["pre"] = (lambda b=b: (pre(b), pre(b + 1) if b + 1 < 9 else None))
                        if ki == len(kts) - 1:
                            g["post"] = (lambda acc, scr, b=b, h=h, n=n: post(b, h, n, acc, scr))
                        groups.append(g)
            attention(groups, 0.125, [[(4, 5), (6, 7)]])
            phase_reset()
            phase_outproj(0, list(range(9)), True)

        def rownorm(pt, W, NB, dst, dkey, tg):
            junk, ss, rs = tg
            P.op("act", lambda e: e.activation(out=junk[:, 0:W], in_=pt[0][:, 0:W], func=AF.Square, accum_out=ss),
                 r=[pt[1]], w=["rn_junk", "rn_ss"])
            P.op("act", lambda e: e.activation(out=rs, in_=ss, func=AF.Sqrt, scale=1.0 / W, bias=EPS), r=["rn_ss"], w=["rn_rs"])
            P.op("dve", lambda e: e.reciprocal(out=rs, in_=rs), r=["rn_rs"], w=["rn_rs"])
            P.op("dve", lambda e: e.scalar_tensor_tensor(out=dst, in0=pt[0][:, 0:W], scalar=rs, in1=NB[:, 0:W], op0=ALU.mult,
                                                         op1=ALU.mult), r=[pt[1], "rn_rs", "wgt"], w=[dkey, pt[1]])

        def phase_l1():
            l = 1
            KN = T([128, 4, NTOK], BF16)
            VM = T([128, NT, 512], BF16)
            KR2 = T([128, NTOK], BF16)
            keep = top[0]
            WA = T([128, 8, 512], BF16)
            WB = T([128, 8, 256], BF16)
            WKR = T([128, 8, 128], BF16)
            WQN = T([128, 4, 4, 128], BF16)
            WQR = T([128, 4, 256], BF16)
            WKN = T([128, 2, 4, 128], BF16)
            WVV = T([128, 2, 512], BF16)
            wsrc = din["od_w_in"][0].rearrange("(k p) n -> p k n", p=128)
            wload(WA, wsrc[:, :, 0:512])
            wload(WB, wsrc[:, :, 512:768])
            wload(WKR[:, :, 0:64], wsrc[:, :, 768:832])
            wload(WKR[:, :, 64:128], wsrc[:, :, 768:832])
            uq = din["mla_w_uq"][0].rearrange("(k p) (h c) -> p k h c", p=128, c=192)
            ukv = din["mla_w_ukv"][0].rearrange("(k p) (h c) -> p k h c", p=128, c=256)
            for k in range(4):
                wload(WQN[:, k], uq[:, k, :, 0:128])
                wload(WQR[:, k, :].rearrange("p (h c) -> p h c", h=4), uq[:, k, :, 128:192])
            for k in range(2):
                wload(WKN[:, k], ukv[:, k, :, 0:128])
                wload(WVV[:, k, :].rearrange("p (h c) -> p h c", h=4), ukv[:, k, :, 128:256])
            QNb = T([128, 512])
            KVNb = T([128, 256])
            dma("sp", QNb, din["mla_q_norm"][0].partition_broadcast(128), w=["wgt"])
            dma("sp", KVNb, din["mla_kv_norm"][0].partition_broadcast(128), w=["wgt"])
            nm = NM(l, 0, 1)
            hTs = [T([128, 8, 512], BF16)] * 2
            ro = rope_tiles() + (7,)
            tg = (T([128, 512], BF16), T([128, 1]), T([128, 1]))
            cqn = T([128, 512], BF16)
            ckvn = T([128, 256], BF16)
            cqT = T([128, 4, 512], BF16)
            ckvT = T([128, 2, 512], BF16)
            qnb = [T([128, 4, 512], BF16)] * 2
            qrb = [T([128, 2, 512], BF16)] * 2
            for b, (t0, ntl) in enumerate(BLOCKS):
                n = ntl * 128
                col0 = t0 * 128
                hT = hTs[b % 2]
                khT = "hTs0"
                rope_load(ro, b)
                for ti in range(ntl):
                    tt = t0 + ti
                    nm.run(x_src(False, tt), tt < 2, hT[:, :, ti * 128:(ti + 1) * 128], khT, 6)
                nm.flush()
                dma("pool", HT1[b][:, :, 0:n], hT[:, :, 0:n], r=[khT], w=["HT1"])
                for ti in range(ntl):
                    ts_ = slice(ti * 128, (ti + 1) * 128)
                    mmg(ps[0], [(hT[:, k, ts_], WA[:, k, :]) for k in range(8)], r=[khT, "wgt"], w=["ps0"])
                    rownorm((ps[0], "ps0"), 512, QNb, cqn, "cqn", tg)

                    def trq(e):
                        for k in range(4):
                            ins = e.transpose(out=psb[2][:, k * 128:(k + 1) * 128], in_=cqn[:, k * 128:(k + 1) * 128], identity=identb)
                        return ins
                    P.op("pe", trq, r=["cqn"], w=["ps2"])
                    P.op("act", (lambda e, ts_=ts_: e.copy(out=cqT[:, :, ts_], in_=psb[2][:, 0:512].rearrange("p (k n) -> p k n", k=4))),
                         r=["ps2"], w=["cqT"])
                    mmg(ps[1][:, 0:256], [(hT[:, k, ts_], WB[:, k, :]) for k in range(8)], r=[khT, "wgt"], w=["ps1"])
                    rownorm((ps[1], "ps1"), 256, KVNb, ckvn, "ckvn", tg)

                    def trk(e):
                        for k in range(2):
                            ins = e.transpose(out=psb[3][:, k * 128:(k + 1) * 128], in_=ckvn[:, k * 128:(k + 1) * 128], identity=identb)
                        return ins
                    P.op("pe", trk, r=["ckvn"], w=["ps3"])
                    P.op("act", (lambda e, ts_=ts_: e.copy(out=ckvT[:, :, ts_], in_=psb[3][:, 0:256].rearrange("p (k n) -> p k n", k=2))),
                         r=["ps3"], w=["ckvT"])
                for ti in range(ntl):
                    tt = t0 + ti
                    ts_ = slice(ti * 128, (ti + 1) * 128)
                    mmg(ps[0], [(ckvT[:, k, ts_], WVV[:, k, :]) for k in range(2)], r=["ckvT", "wgt"], w=["ps0"])
                    P.op("act", (lambda e, tt=tt: e.copy(out=VM[:, tt, :], in_=ps[0])), r=["ps0"], w=["VM", "ps0"])
                for h in range(4):
                    bk = 4 + (h % 2)
                    mmg(ps[bk][:, 0:n], [(WKN[:, k, h, :], ckvT[:, k, 0:n]) for k in range(2)], r=["ckvT", "wgt"], w=["ps%d" % bk])
                    P.op("act", (lambda e, h=h, bk=bk, n=n, col0=col0: e.copy(out=KN[:, h, col0:col0 + n], in_=ps[bk][:, 0:n])),
                         r=["ps%d" % bk], w=["KN", "ps%d" % bk])
                mmg(ps[4][:, 0:n], [(WKR[:, k, :], hT[:, k, 0:n]) for k in range(8)], r=[khT, "wgt"], w=["ps4"])
                rope_evict(4, n, col0, KR2[:, col0:col0 + n], "KR2", ro, b == 0)
                if b >= 1:
                    qn, qr = qnb[b % 2], qrb[b % 2]
                    for h in range(4):
                        bk = 4 + (h % 2)
                        mmg(ps[bk][:, 0:n], [(WQN[:, k, h, :], cqT[:, k, 0:n]) for k in range(4)], r=["cqT", "wgt"], w=["ps%d" % bk])
                        P.op("act", (lambda e, h=h, bk=bk, n=n, qn=qn: e.copy(out=qn[:, h, 0:n], in_=ps[bk][:, 0:n])),
                             r=["ps%d" % bk], w=["qnb0", "ps%d" % bk])
                    dma("pool", QT[b][:, :, 0:n], qn[:, :, 0:n], r=["qnb0"], w=["QT"])
                    for c in range(2):
                        bk = 4 + (c % 2)
                        mmg(ps[bk][:, 0:n], [(WQR[:, k, c * 128:(c + 1) * 128], cqT[:, k, 0:n]) for k in range(4)],
                            r=["cqT", "wgt"], w=["ps%d" % bk])
                        rope_evict(bk, n, col0, qr[:, c, 0:n], "qrb0", ro, False)
                    dma("pool", QR[b][:, :, 0:n], qr[:, :, 0:n], r=["qrb0"], w=["QR"])
            P.barrier()
            top[0] = keep
            if stop == "l1b1":
                return
            qnl = [T([128, 4, 512], BF16) for _ in range(3)]
            qrl = [T([128, 2, 512], BF16) for _ in range(3)]
            rsum = T([128, 512])
            mo = [T([128, 512], BF16) for _ in range(2)]
            state = {}

            def pre(b):
                if state.get("b") != b:
                    state["b"] = b
                    n = BLOCKS[b][1] * 128
                    dma("sp", qnl[b % 3][:, :, 0:n], QT[b][:, :, 0:n], r=["QT"], w=["qnl%d" % (b % 3)])
                    dma("sp", qrl[b % 3][:, :, 0:n], QR[b][:, :, 0:n], r=["QR"], w=["qnl%d" % (b % 3)])

            def post(b, h, n, acc, scr):
                col0 = BLOCKS[b][0] * 128
                m = mo[h % 2]
                km = "mo%d" % (h % 2)
                ob, sb2 = acc[0]
                P.op("dve", lambda e: e.reciprocal(out=rsum[:, 0:n], in_=ps[sb2][:, 0:n]), r=["ps%d" % sb2], w=["rsum", "ps%d" % sb2])
                P.op("dve", lambda e: e.tensor_tensor(out=m[:, 0:n], in0=ps[ob][:, 0:n], in1=rsum[:, 0:n], op=ALU.mult),
                     r=["ps%d" % ob, "rsum"], w=[km, "ps%d" % ob])
                dma("pool", MIXD[h][:, col0:col0 + n], m[:, 0:n], r=[km], w=["MIXD"])

            groups = []
            gi = 0
            for b in range(1, 9):
                n = 512
                for h in range(4):
                    pr = slice(64 * (h % 2), 64 * (h % 2) + 64)
                    for kp in range(NT // 2):
                        mem = []
                        for mi in range(2):
                            kt = 2 * kp + mi
                            ks = slice(kt * 128, (kt + 1) * 128)
                            mem.append(dict(qk=[(KN[:, h, ks], qnl[b % 3][:, h, 0:n]), (KR2[pr, ks], qrl[b % 3][pr, h // 2, 0:n])],
                                            v=VM[:, kt, h * 128:(h + 1) * 128], acc=0,
                                            start=(kp == 0 and mi == 0), stop=(kp == NT // 2 - 1 and mi == 1)))
                        g = dict(n=n, members=mem, rk=["KN", "KR2", "qnl%d" % (b % 3)], rv=["VM"], accset=gi % 2)
                        if h == 0 and kp == 0:
                            g["pre"] = (lambda b=b: (pre(b), pre(b + 1) if b + 1 < 9 else None))
                        if kp == NT // 2 - 1:
                            g["post"] = (lambda acc, scr, b=b, h=h, n=n: post(b, h, n, acc, scr))
                        groups.append(g)
                    gi += 1
            attention(groups, 192.0 ** -0.5, [[(4, 5)], [(6, 7)]])
            phase_reset()
            if stop == "l1b2":
                return
            LB = T([128, 2, 2, 4])
            lbv = T([128, 2, 4])
            oml = T([128, 2, 4])
            hgn = T([128, 1])
            ones1 = T([128, 1])
            for d in range(2):
                for ll in range(2):
                    dma_nc("sp", LB[:, d, ll, :], din["hgrn_lb"][d, ll].rearrange("(h p) -> p h", p=128), w=["LB"])
            dma_nc("sp", hgn, din["hgrn_norm"][0].rearrange("(p o) -> p o", o=1), w=["hgn"])
            P.op("dve", lambda e: e.tensor_tensor(out=lbv, in0=LB[:, :, 1, :], in1=LB[:, :, 0, :], op=ALU.subtract), r=["LB"], w=["lbv"])
            P.op("act", lambda e: e.activation(out=lbv, in_=lbv, func=AF.Sigmoid), r=["lbv"], w=["lbv"])
            P.op("dve", lambda e: e.tensor_scalar(out=oml, in0=lbv, scalar1=-1.0, scalar2=1.0, op0=ALU.mult, op1=ALU.add),
                 r=["lbv"], w=["oml"])
            P.op("pool", lambda e: e.memset(ones1, 1.0), w=["ones1"])
            WH = T([128, 8, 5, 128], BF16)
            hTl = [T([128, 8, 512], BF16)] * 2
            SG = T([128, NTOK], BF16)
            Vt = T([128, NT, 128], BF16)
            QP = [T([128, NTOK], BF16) for _ in range(2)]
            QPP = [T([128, NTOK], BF16) for _ in range(2)]
            KP = [T([128, NTOK], BF16) for _ in range(2)]
            KPt = [T([128, NT, 128], BF16) for _ in range(2)]
            EL = [T([128, 68]) for _ in range(2)]
            OTa = T([128, NTOK])
            qf = T([128, 512])
            sgm = [T([128, 512]) for _ in range(2)]
            kk = [T([128, 512]) for _ in range(2)]
            lf = [T([128, 512]) for _ in range(2)]
            gb = [T([128, 516]) for _ in range(2)]
            Da = [T([128, 512]) for _ in range(2)]
            Db = [T([128, 512]) for _ in range(2)]
            E1 = [T([128, 512]) for _ in range(2)]
            E2 = [T([128, 512]) for _ in range(2)]
            E3 = [T([128, 512]) for _ in range(2)]
            elt = [T([128, 8]) for _ in range(2)]
            Sf = [[T([128, 128]) for _ in range(2)] for _ in range(2)]
            Sb = [[T([128, 128], BF16) for _ in range(2)] for _ in range(2)]
            Am = [[T([128, 64], BF16) for _ in range(2)] for _ in range(2)]
            rstd, otmp = Db[0], E1[0]
            sq = T([128, 512], BF16)
            mh = [T([128, 512], BF16) for _ in range(2)]
            for d in range(2):
                P.op("pool", (lambda e, d=d: e.memset(gb[d][:, 0:1], 0.0)), w=["gbz"])
            wsrc = din["od_w_in"][0].rearrange("(k p) n -> p k n", p=128)
            for h in range(4):
                for i, c0_ in enumerate((832, 1344, 1856, 2368, 2880)):
                    wload(WH[:, :, i, :], wsrc[:, :, c0_ + h * 128:c0_ + (h + 1) * 128])
                P.op("pool", lambda e: e.memset(OTa, 0.0), w=["OT"])
                for b, (t0, ntl) in enumerate(BLOCKS):
                    n = ntl * 128
                    nch = n // 64
                    col0 = t0 * 128
                    ch0 = col0 // 64
                    hT = hTl[b % 2]
                    khT = "hTl0"
                    dma("sp", hT[:, :, 0:n], HT1[b][:, :, 0:n], r=["HT1"], w=[khT])
                    mmg(ps[0][:, 0:n], [(WH[:, k, 0, :], hT[:, k, 0:n]) for k in range(8)], r=[khT, "wgt"], w=["ps0"])
                    P.op("act", (lambda e, n=n: e.activation(out=qf[:, 0:n], in_=ps[0][:, 0:n], func=AF.Silu)), r=["ps0"], w=["qf", "ps0"])
                    mmg(ps[1][:, 0:n], [(WH[:, k, 4, :], hT[:, k, 0:n]) for k in range(8)], r=[khT, "wgt"], w=["ps1"])
                    P.op("act", (lambda e, n=n, col0=col0: e.activation(out=SG[:, col0:col0 + n], in_=ps[1][:, 0:n], func=AF.Silu)),
                         r=["ps1"], w=["SG", "ps1"])
                    for ti in range(ntl):
                        tt = t0 + ti
                        ts_ = slice(ti * 128, (ti + 1) * 128)
                        mmg(ps[2][:, 0:128], [(hT[:, k, ts_], WH[:, k, 3, :]) for k in range(8)], r=[khT, "wgt"], w=["ps2"])
                        P.op("act", (lambda e, tt=tt: e.copy(out=Vt[:, tt, :], in_=ps[2][:, 0:128])), r=["ps2"], w=["Vt", "ps2"])
                    Gi3 = [gb[d][:, 1:1 + n].rearrange("p (c j) -> p c j", j=64) for d in range(2)]
                    Gs3 = [gb[d][:, 0:n].rearrange("p (c j) -> p c j", j=64) for d in range(2)]
                    S0 = [Gs3[d][:, :, 0:1].to_broadcast([128, nch, 64]) for d in range(2)]
                    I63 = [Gi3[d][:, :, 63:64].to_broadcast([128, nch, 64]) for d in range(2)]
                    Da3 = [Da[d][:, 0:n].rearrange("p (c j) -> p c j", j=64) for d in range(2)]
                    Db3 = [Db[d][:, 0:n].rearrange("p (c j) -> p c j", j=64) for d in range(2)]
                    for d in range(2):
                        mmg(ps[3 + d][:, 0:n], [(WH[:, k, 1 + d, :], hT[:, k, 0:n]) for k in range(8)], r=[khT, "wgt"], w=["ps%d" % (3 + d)])
                    for d in range(2):
                        P.op("act", (lambda e, n=n, d=d: e.activation(out=sgm[d][:, 0:n], in_=ps[3 + d][:, 0:n], func=AF.Sigmoid)),
                             r=["ps%d" % (3 + d)], w=["sgm%d" % d, "ps%d" % (3 + d)])
                    for d in range(2):
                        P.op("dve", (lambda e, n=n, d=d, h=h: e.tensor_scalar(out=sgm[d][:, 0:n], in0=sgm[d][:, 0:n], scalar1=oml[:, d, h:h + 1],
                                                                             scalar2=lbv[:, d, h:h + 1], op0=ALU.mult, op1=ALU.add)),
                             r=["sgm%d" % d, "oml", "lbv"], w=["sgm%d" % d])
                    for d in range(2):
                        P.op("pool", (lambda e, n=n, d=d: e.tensor_scalar(out=kk[d][:, 0:n], in0=sgm[d][:, 0:n], scalar1=-1.0, scalar2=1.0,
                                                                         op0=ALU.mult, op1=ALU.add)), r=["sgm%d" % d], w=["kk%d" % d])
                        P.op("act", (lambda e, n=n, d=d: e.activation(out=lf[d][:, 0:n], in_=sgm[d][:, 0:n], func=AF.Ln)), r=["sgm%d" % d], w=["lf%d" % d])
                    for d in range(2):
                        P.op("dve", (lambda e, n=n, d=d: e.tensor_tensor_scan(out=gb[d][:, 1:1 + n], data0=ones1[:, 0:1].to_broadcast([128, n]),
                                                                             data1=lf[d][:, 0:n], initial=0.0, op0=ALU.mult, op1=ALU.add)),
                             r=["lf%d" % d, "ones1", "gbz"], w=["gb%dk" % d])
                    P.op("dve", (lambda e, a=Gi3[0], b_=S0[0], o=Da3[0]: e.tensor_tensor(out=o, in0=a, in1=b_, op=ALU.subtract)), r=["gb0k"], w=["Da0"])
                    P.op("dve", (lambda e, a=Gs3[1], b_=I63[1], o=Da3[1]: e.tensor_tensor(out=o, in0=a, in1=b_, op=ALU.subtract)), r=["gb1k"], w=["Da1"])
                    P.op("dve", (lambda e, a=Gi3[0], b_=I63[0], o=Db3[0]: e.tensor_tensor(out=o, in0=a, in1=b_, op=ALU.subtract)), r=["gb0k"], w=["Db0"])
                    P.op("dve", (lambda e, a=Gs3[1], b_=S0[1], o=Db3[1]: e.tensor_tensor(out=o, in0=a, in1=b_, op=ALU.subtract)), r=["gb1k"], w=["Db1"])
                    for d in range(2):
                        P.op("dve", (lambda e, d=d, nch=nch, a=Gi3[d], b_=Gs3[d]: e.tensor_tensor(out=elt[d][:, 0:nch], in0=a[:, :, 63], in1=b_[:, :, 0],
                                                                                                  op=ALU.subtract)), r=["gb%dk" % d], w=["elt%d" % d])
                    scs = ((1.0, -1.0, 1.0), (-1.0, 1.0, -1.0))
                    for d in range(2):
                        P.op("act", (lambda e, n=n, d=d: e.activation(out=E1[d][:, 0:n], in_=Da[d][:, 0:n], func=AF.Exp, scale=scs[d][0])), r=["Da%d" % d], w=["E1%d" % d])
                    for d in range(2):
                        P.op("act", (lambda e, n=n, d=d: e.activation(out=E2[d][:, 0:n], in_=Db[d][:, 0:n], func=AF.Exp, scale=scs[d][1])), r=["Db%d" % d], w=["E2%d" % d])
                    for d in range(2):
                        P.op("act", (lambda e, n=n, d=d: e.activation(out=E3[d][:, 0:n], in_=Db[d][:, 0:n], func=AF.Exp, scale=scs[d][2])), r=["Db%d" % d], w=["E3%d" % d])
                        P.op("act", (lambda e, d=d, ch0=ch0, nch=nch: e.activation(out=EL[d][:, ch0:ch0 + nch], in_=elt[d][:, 0:nch], func=AF.Exp)),
                             r=["elt%d" % d], w=["EL%d" % d])
                    for d in range(2):
                        P.op("dve", (lambda e, n=n, d=d, col0=col0: e.tensor_tensor(out=KP[d][:, col0:col0 + n], in0=kk[d][:, 0:n], in1=E2[d][:, 0:n], op=ALU.mult)),
                             r=["kk%d" % d, "E2%d" % d], w=["KP%d" % d])
                    for d in range(2):
                        P.op("dve", (lambda e, n=n, d=d, col0=col0: e.tensor_tensor(out=QP[d][:, col0:col0 + n], in0=qf[:, 0:n], in1=E1[d][:, 0:n], op=ALU.mult)),
                             r=["qf", "E1%d" % d], w=["QP%d" % d])
                        P.op("pool", (lambda e, n=n, d=d, col0=col0: e.tensor_tensor(out=QPP[d][:, col0:col0 + n], in0=qf[:, 0:n], in1=E3[d][:, 0:n], op=ALU.mult)),
                             r=["qf", "E3%d" % d], w=["QPP%d" % d])
                    for d in range(2):
                        def trp(e, d=d, t0=t0, ntl=ntl):
                            for ti in range(ntl):
                                ins = e.transpose(out=psb[6 + d][:, ti * 128:(ti + 1) * 128], in_=KP[d][:, (t0 + ti) * 128:(t0 + ti + 1) * 128], identity=identb)
                            return ins
                        P.op("pe", trp, r=["KP%d" % d], w=["ps%d" % (6 + d)])
                        P.op("act", (lambda e, d=d, t0=t0, ntl=ntl: e.copy(out=KPt[d][:, t0:t0 + ntl, :],
                                                                          in_=psb[6 + d][:, 0:ntl * 128].rearrange("p (t k) -> p t k", k=128))),
                             r=["ps%d" % (6 + d)], w=["KPt%d" % d, "ps%d" % (6 + d)])
                orders = [list(range(68)), [3, 2, 1, 0] + list(range(67, 3, -1))]
                UB = (2, 3)

                def cinfo(c):
                    tt, hh = c // 2, c % 2
                    rows = slice(64 * hh, 64 * hh + 64)
                    cols = slice(64 * c, 64 * c + 64)
                    if c < 4:
                        pos, blk_n, blk_c0 = c, 256, 0
                    else:
                        pos, blk_n, blk_c0 = (c - 4) % 8, 512, 256 + ((c - 4) // 8) * 512
                    return tt, rows, cols, pos, blk_n, blk_c0

                def emitAU(st_):
                    ub = UB[st_ % 2]
                    for d in range(2):
                        c = orders[d][st_]
                        tt, rows, cols, pos, blk_n, blk_c0 = cinfo(c)
                        ab = d
                        mmg(ps[ab][rows, 0:64], [(KP[d][:, cols], QPP[d][:, cols])], r=["KP%d" % d, "QPP%d" % d], w=["ps%d" % ab])
                    for d in range(2):
                        c = orders[d][st_]
                        tt, rows, cols, pos, blk_n, blk_c0 = cinfo(c)
                        ubk = ((2, 6), (3, 7))[d][st_ % 2]
                        mmg(ps[ubk][:, 0:128], [(KPt[d][rows, tt, :], Vt[rows, tt, :])], r=["KPt%d" % d, "Vt"], w=["ps%d" % ubk])

                def emitMask(st_):
                    for d in range(2):
                        c = orders[d][st_]
                        tt, rows, cols, pos, blk_n, blk_c0 = cinfo(c)
                        am = Am[d][st_ % 2]
                        mk = maskf if d == 0 else maskb
                        ab = d
                        pA, kA = ps[ab], "ps%d" % ab
                        P.op("dve", (lambda e, pA=pA, rows=rows, am=am, mk=mk, d=d: e.tensor_tensor(
                            out=am[rows, :], in0=pA[rows, 0:64], in1=mk[rows, :], op=ALU.mult)),
                            r=[kA], w=["Am%d_%d" % (d, st_ % 2), kA])

                def emitO(st_):
                    for d in range(2):
                        c = orders[d][st_]
                        tt, rows, cols, pos, blk_n, blk_c0 = cinfo(c)
                        am = Am[d][st_ % 2]
                        prs = [(Vt[rows, tt, :], am[rows, :])]
                        rr = ["Vt", "Am%d_%d" % (d, st_ % 2)]
                        if st_ > 0:
                            so = (st_ - 1) % 2
                            prs.append((Sb[d][so], QP[d][:, cols]))
                            rr += ["Sb%d_%d" % (d, so), "QP%d" % d]
                        mmg(ps[4 + d][:, pos * 64:(pos + 1) * 64], prs, r=rr, w=["ps%d" % (4 + d)])

                def emitUpd(st_):
                    ub = UB[st_ % 2]
                    kU = "ps%d" % ub
                    sn, so = st_ % 2, (st_ - 1) % 2
                    for d in range(2):
                        c = orders[d][st_]
                        tt, rows, cols, pos, blk_n, blk_c0 = cinfo(c)
                        ubk = ((2, 6), (3, 7))[d][st_ % 2]
                        pU = ps[ubk][:, 0:128]
                        kU = "ps%d" % ubk
                        if st_ == 0:
                            P.op("dve", (lambda e, d=d, pU=pU: e.tensor_copy(out=Sf[d][sn], in_=pU)),
                                 r=[kU], w=["Sf%d_%d" % (d, sn), kU])
                        else:
                            P.op("dve", (lambda e, d=d, pU=pU, c=c: e.scalar_tensor_tensor(
                                out=Sf[d][sn], in0=Sf[d][so], scalar=EL[d][:, c:c + 1], in1=pU, op0=ALU.mult, op1=ALU.add)),
                                r=[kU, "Sf%d_%d" % (d, so), "EL%d" % d], w=["Sf%d_%d" % (d, sn), kU])
                        P.op("act", (lambda e, d=d: e.copy(out=Sb[d][sn], in_=Sf[d][sn])), r=["Sf%d_%d" % (d, sn)], w=["Sb%d_%d" % (d, sn)])
                        last_in_blk = (pos == (blk_n // 64 - 1)) if d == 0 else (pos == 0)
                        if last_in_blk:
                            P.op("dve", (lambda e, d=d, blk_n=blk_n, blk_c0=blk_c0: e.tensor_tensor(
                                out=OTa[:, blk_c0:blk_c0 + blk_n], in0=ps[4 + d][:, 0:blk_n], in1=OTa[:, blk_c0:blk_c0 + blk_n], op=ALU.add)),
                                 r=["ps%d" % (4 + d), "OT"], w=["OT", "ps%d" % (4 + d)])

                LOOK = 1
                if LOOK:
                    emitAU(0)
                    emitMask(0)
                for st_ in range(68):
                    if LOOK:
                        if st_ + 1 < 68:
                            emitAU(st_ + 1)
                            emitMask(st_ + 1)
                    else:
                        emitAU(st_)
                        emitMask(st_)
                    emitO(st_)
                    emitUpd(st_)
                for b in range(1, 9):
                    n = 512
                    col0 = BLOCKS[b][0] * 128
                    cs = slice(col0, col0 + n)
                    m = mh[b % 2]
                    km = "mh%d" % (b % 2)
                    P.op("act", (lambda e, cs=cs: e.activation(out=sq, in_=OTa[:, cs], func=AF.Square)), r=["OT"], w=["sq"])
                    mmg(ps[7], [(onesb, sq)], r=["sq"], w=["ps7"])
                    P.op("act", lambda e: e.activation(out=rstd, in_=ps[7], func=AF.Sqrt, scale=1.0 / 128, bias=EPS), r=["ps7"], w=["Db0", "ps7"])
                    P.op("dve", lambda e: e.reciprocal(out=rstd, in_=rstd), r=["Db0"], w=["Db0"])
                    P.op("dve", (lambda e, cs=cs: e.scalar_tensor_tensor(out=otmp, in0=OTa[:, cs], scalar=hgn, in1=rstd, op0=ALU.mult, op1=ALU.mult)),
                         r=["OT", "Db0", "hgn"], w=["E10"])
                    P.op("dve", (lambda e, cs=cs, m=m: e.tensor_tensor(out=m, in0=otmp, in1=SG[:, cs], op=ALU.mult)), r=["E10", "SG"], w=[km])
                    dma("pool", MIXD[4 + h][:, cs], m, r=[km], w=["MIXD"])
            phase_reset()
            if stop == "l1b3":
                return
            phase_outproj(1, list(range(1, 9)), False)

        def phase_outproj(l, blks, from_inputs):
            wo = T([128, 8, D], BF16)
            wosrc = din["mix_w_out"][l].rearrange("(k p) n -> p k n", p=128)
            for c in range(2):
                wload(wo[:, :, c * 512:(c + 1) * 512], wosrc[:, :, c * 512:(c + 1) * 512])
            res = RES(l, 2)
            mix = [T([128, 8, 512], BF16) for _ in range(2)]
            ypairs = [(0, 1), (2, 3), (4, 5), (6, 7)]
            yi = 0
            for b in blks:
                t0, ntl = BLOCKS[b]
                n = ntl * 128
                col0 = t0 * 128
                mx = mix[b % 2]
                kx = "mix%d" % (b % 2)
                for k in range(8):
                    dma("sp", mx[:, k, 0:n], MIXD[k][:, col0:col0 + n], r=["MIXD"], w=[kx])
                for ti in range(ntl):
                    tt = t0 + ti
                    ls = slice(ti * 128, (ti + 1) * 128)
                    y0, y1 = ypairs[yi % 4]
                    yi += 1
                    for hf, yb in ((0, y0), (1, y1)):
                        mmg(ps[yb], [(mx[:, k, ls], wo[:, k, hf * 512:(hf + 1) * 512]) for k in range(8)],
                            r=[kx, "wgt"], w=["ps%d" % yb])
                    res.run(y0, y1, x_src(from_inputs, tt), XR[tt * 128:(tt + 1) * 128, :], tt < 2, "XR1")

        phase_mod()
        phase_reset()
        S0 = ("mod", "l0a1", "l0a2", "l0mix")
        S1 = S0 + ("l0", "l1b1", "l1b2", "l1b3", "l1mix")
        if stop != "mod":
            phase_l0()
            phase_reset()
        if stop not in S0:
            phase_ffn(0, True, False)
            phase_reset()
        if stop not in S0 + ("l0",):
            phase_l1()
            phase_reset()
        if stop not in S1:
            phase_ffn(1, False, True)
            phase_reset()
        P.emit(st)
    return nc


_CACHE = {}


def kernel(**inputs):
    consts = _consts()
    if "nc" not in _CACHE:
        _CACHE["nc"] = build()
    nc = _CACHE["nc"]
    in_maps = []
    for b in range(8):
        m = {"x": np.ascontiguousarray(inputs["x"][b]), "ctx": np.ascontiguousarray(inputs["ctx"][b]),
             "cvec": np.ascontiguousarray(np.stack([inputs["c"][b], inputs["c_ctx"]], 0))}
        for n in W_NAMES:
            m[n] = np.ascontiguousarray(inputs[n])
        m.update(consts)
        in_maps.append(m)
    res = run_bass_kernel_spmd(nc, in_maps, core_ids=list(range(8)))
    return np.stack([r["out"] for r in res.results], 0).astype(np.float32)
```

```python
import numpy as np
from contextlib import ExitStack
import concourse.bass as bass
import concourse.mybir as mybir
from concourse.bass_utils import run_bass_kernel_spmd

F32 = mybir.dt.float32
BF16 = mybir.dt.bfloat16
ALU = mybir.AluOpType
AF = mybir.ActivationFunctionType

NDSEM = 8
D = 1024
NT = 34
NTOK = 4352
DFF = 2816
NJ = 22
EPS = 1e-6


class Prog:
    ENGS = ("pe", "dve", "act", "pool", "sp")

    def __init__(self, nc):
        self.nc = nc
        self.ops = []
        self.lw = {}
        self.rd = {}
        self.cnt = {e: 0 for e in self.ENGS}
        self.dcnt = {e: 0 for e in self.ENGS}
        self.dslot_last = {e: [None] * NDSEM for e in self.ENGS}
        self.last_nd = {e: None for e in self.ENGS}
        self.pending_bar = {e: [] for e in self.ENGS}

    def op(self, eng, fn, r=(), w=(), dma=False):
        oid = len(self.ops)
        deps = []
        for k in r:
            y = self.lw.get(k)
            if y is not None:
                deps.append((y, "RAW"))
        for k in w:
            y = self.lw.get(k)
            if y is not None:
                deps.append((y, "WAW"))
            for y in self.rd.get(k, ()):
                deps.append((y, "WAR"))
        for y in self.pending_bar[eng]:
            deps.append((y, "RAW"))
        self.pending_bar[eng] = []
        o = dict(id=oid, eng=eng, fn=fn, deps=deps, dma=dma)
        if dma:
            i = self.dcnt[eng]
            self.dcnt[eng] += 1
            slot = i % NDSEM
            o["dslot"] = slot
            o["dval"] = 16 * (i // NDSEM + 1)
            prev = self.dslot_last[eng][slot]
            if prev is not None:
                deps.append((prev, "RAW"))
            self.dslot_last[eng][slot] = oid
        else:
            self.cnt[eng] += 1
            o["val"] = self.cnt[eng]
            self.last_nd[eng] = oid
        self.ops.append(o)
        for k in w:
            self.lw[k] = oid
            self.rd[k] = []
        for k in r:
            if k not in w:
                self.rd.setdefault(k, []).append(oid)
        return oid

    def barrier(self):
        snap = []
        for e in self.ENGS:
            if self.last_nd[e] is not None:
                snap.append(self.last_nd[e])
            for y in self.dslot_last[e]:
                if y is not None:
                    snap.append(y)
        for e in self.ENGS:
            self.pending_bar[e] = list(snap)
        self.lw = {}
        self.rd = {}

    def emit(self, st):
        nc = self.nc
        sems = {e: st.enter_context(nc.semaphore("s_" + e)) for e in self.ENGS}
        dsems = {e: [st.enter_context(nc.semaphore("d_%s%d" % (e, i))) for i in range(NDSEM)]
                 for e in ("sp", "pool", "act") if self.dcnt[e] > 0}
        block = st.enter_context(nc.Block())
        ops = self.ops

        def run(ename, eng):
            seen = {}
            for o in ops:
                if o["eng"] != ename:
                    continue
                need = {}
                for (y, kind) in o["deps"]:
                    Y = ops[y]
                    if Y["dma"]:
                        key = ("d", Y["eng"], Y["dslot"])
                        sem = dsems[Y["eng"]][Y["dslot"]]
                        val = Y["dval"]
                    else:
                        if Y["eng"] == ename and not o["dma"]:
                            if ename == "pe" or kind != "RAW":
                                continue
                        key = ("c", Y["eng"])
                        sem = sems[Y["eng"]]
                        val = Y["val"]
                    if seen.get(key, 0) >= val:
                        continue
                    if key not in need or need[key][1] < val:
                        need[key] = (sem, val)
                for key, (sem, val) in need.items():
                    eng.wait_ge(sem, val)
                    seen[key] = val
                ins = o["fn"](eng)
                if o["dma"]:
                    ins.then_inc(dsems[ename][o["dslot"]], 16)
                else:
                    ins.then_inc(sems[ename], 1)
            if ename in dsems:
                for slot in range(NDSEM):
                    y = self.dslot_last[ename][slot]
                    if y is not None:
                        Y = ops[y]
                        if seen.get(("d", ename, slot), 0) < Y["dval"]:
                            eng.wait_ge(dsems[ename][slot], Y["dval"])

        @block.tensor
        def _(e):
            run("pe", e)

        @block.vector
        def _(e):
            run("dve", e)

        @block.scalar
        def _(e):
            run("act", e)

        @block.gpsimd
        def _(e):
            run("pool", e)

        @block.sync
        def _(e):
            run("sp", e)


PW = 8 + 256 + 16 + 4096 + 8
PC0, PL0 = 8, 280


def _consts():
    c = {}
    c["ident"] = np.eye(128, dtype=np.float32)
    rot = np.zeros((128, 128), np.float32)
    for d in range(128):
        rot[d ^ 16, d] = 1.0
    c["rot"] = rot
    n = 4096
    pos_row = np.repeat(np.arange(n // 64), 64)
    pos_col = np.tile(np.arange(64), n // 64)
    inv_freq = (10000.0 ** (-np.arange(0, 32, 2, dtype=np.float32) / 32)).astype(np.float32)
    ang = np.stack([pos_row, pos_col], -1).astype(np.float32)[..., None] * inv_freq
    cs, sn = np.cos(ang).astype(np.float32), np.sin(ang).astype(np.float32)
    cos_t = np.zeros((128, n), np.float32)
    sin_t = np.zeros((128, n), np.float32)
    for d in range(128):
        dd = d % 64
        a, hf, i = dd // 32, (dd // 16) % 2, dd % 16
        cos_t[d] = cs[:, a, i]
        sin_t[d] = sn[:, a, i] * (-1.0 if hf == 0 else 1.0)
    c["cos_t"] = cos_t
    c["sin_t"] = sin_t
    inv = np.zeros((4, PW), np.float32)
    for g, w in enumerate((2, 4, 8, 16)):
        h = w // 2
        for (n_, off) in ((256, PC0), (4096, PL0)):
            t = np.arange(n_)
            lo = np.clip(t - h, 0, n_)
            hi = np.clip(t + h, 0, n_)
            inv[g, off:off + n_] = 1.0 / (hi - lo).astype(np.float32)
    c["invcnt"] = inv
    p = np.arange(128)[:, None] % 64
    t = np.arange(64)[None, :]
    c["mask_f"] = (p <= t).astype(np.float32)
    c["mask_b"] = (p >= t).astype(np.float32)
    return c


W_NAMES = ["ada_w", "ada_b", "norm_g", "mix_w_out", "ffn_w_gate", "ffn_w_up", "ffn_conv_w",
           "ffn_conv_b", "ffn_w_down", "ev_w_in", "pool_w", "pool_scale", "diff_lambda",
           "diff_subln", "od_w_in", "mla_q_norm", "mla_w_uq", "mla_kv_norm", "mla_w_ukv",
           "hgrn_norm", "hgrn_lb"]
W_SHAPES = {"ada_w": (2, 1024, 6144), "ada_b": (2, 6144), "norm_g": (2, 4, 1024),
            "mix_w_out": (2, 1024, 1024), "ffn_w_gate": (2, 1024, 2816), "ffn_w_up": (2, 1024, 2816),
            "ffn_conv_w": (2, 3, 2816), "ffn_conv_b": (2, 2816), "ffn_w_down": (2, 2816, 1024),
            "ev_w_in": (1, 1024, 2048), "pool_w": (1, 4, 128, 128), "pool_scale": (1, 512),
            "diff_lambda": (1, 4, 64), "diff_subln": (1, 128), "od_w_in": (1, 1024, 3392),
            "mla_q_norm": (1, 512), "mla_w_uq": (1, 512, 768), "mla_kv_norm": (1, 256),
            "mla_w_ukv": (1, 256, 1024), "hgrn_norm": (1, 128), "hgrn_lb": (2, 2, 512)}
C_SHAPES = {"ident": (128, 128), "rot": (128, 128), "cos_t": (128, 4096), "sin_t": (128, 4096),
            "invcnt": (4, PW), "mask_f": (128, 64), "mask_b": (128, 64)}

BLOCKS = [(0, 2)] + [(2 + 4 * i, 4) for i in range(8)]


def build(stop=None, dbg=False):
    nc = bass.Bass("TRN2", target_bir_lowering=False)
    din = {}
    din["x"] = nc.dram_tensor("x", [4096, D], F32, kind="ExternalInput").ap()
    din["ctx"] = nc.dram_tensor("ctx", [256, D], F32, kind="ExternalInput").ap()
    din["cvec"] = nc.dram_tensor("cvec", [2, D], F32, kind="ExternalInput").ap()
    for n in W_NAMES:
        din[n] = nc.dram_tensor(n, list(W_SHAPES[n]), F32, kind="ExternalInput").ap()
    for n in C_SHAPES:
        din[n] = nc.dram_tensor(n, list(C_SHAPES[n]), F32, kind="ExternalInput").ap()
    out = nc.dram_tensor("out", [4096, D], F32, kind="ExternalOutput").ap()
    XR = nc.dram_tensor("XR", [NTOK, D], F32, kind="ExternalOutput" if dbg else "Internal").ap()
    MODV = nc.dram_tensor("MODV", [2, 2, 6, D], F32, kind="Internal").ap()
    QT = nc.dram_tensor("QT", [9, 128, 4, 512], BF16, kind="Internal").ap()
    QR = nc.dram_tensor("QR", [9, 128, 2, 512], BF16, kind="Internal").ap()
    UT = nc.dram_tensor("UT", [4, 128, NTOK], F32, kind="Internal").ap()
    HT1 = nc.dram_tensor("HT1", [9, 128, 8, 512], BF16, kind="Internal").ap()
    H2D = nc.dram_tensor("H2D", [128, 8, 4355], BF16, kind="Internal").ap()
    MIXD = nc.dram_tensor("MIXD", [8, 128, NTOK], BF16, kind="ExternalOutput" if dbg else "Internal").ap()

    st = ExitStack()
    with st:
        P = Prog(nc)
        AW = 52000
        arena = st.enter_context(nc.sbuf_tensor("arena", [128, AW], F32))
        psall = st.enter_context(nc.psum_tensor("psall", [128, 4096], F32))[:]
        ps = [psall[:, i * 512:(i + 1) * 512] for i in range(8)]
        psb = [p.bitcast(BF16) for p in ps]
        top = [0]

        def T(shape, dt=F32):
            n = int(np.prod(shape[1:]))
            cols = n if dt == F32 else (n + 1) // 2
            off = top[0]
            top[0] += cols
            assert top[0] <= AW, "SBUF arena overflow %d" % top[0]
            a = arena[0:shape[0], off:off + cols]
            if dt != F32:
                a = a.bitcast(dt)
            if len(shape) == 3:
                a = a.rearrange("p (a b) -> p a b", a=shape[1])
            elif len(shape) == 4:
                a = a.rearrange("p (a b c) -> p a b c", a=shape[1], b=shape[2])
            return a

        uid = [0]

        def K(s):
            uid[0] += 1
            return "%s#%d" % (s, uid[0])

        def dma(q, o, i, r=(), w=()):
            P.op(q, lambda e: e.dma_start(out=o, in_=i), r=r, w=w, dma=True)

        def dma_nc(q, o, i, r=(), w=()):
            P.op(q, lambda e: e.dma_start(out=o, in_=i, allow_slow_non_contiguous=True), r=r, w=w, dma=True)

        def mmg(o, pairs, r, w):
            def f(e):
                n = len(pairs)
                for i, (l, rh) in enumerate(pairs):
                    ins = e.matmul(o, lhsT=l, rhs=rh, start=(i == 0), stop=(i == n - 1))
                return ins
            P.op("pe", f, r=r, w=w)

        identb = T([128, 128], BF16)
        rotb = T([128, 128], BF16)
        onesb = T([128, 128], BF16)
        maskf = T([128, 64])
        maskb = T([128, 64])
        stg = [T([128, 2048]) for _ in range(2)]
        stgi = [0]
        PERSIST = None

        def wload(dst, src, q=None, ce="pool"):
            i = stgi[0] % 2
            stgi[0] += 1
            shp = list(dst.shape)
            n = int(np.prod(shp[1:]))
            if n > 2048:
                hh = shp[-1] // 2
                if len(shp) == 2:
                    wload(dst[:, 0:hh], src[:, 0:hh], q, ce)
                    wload(dst[:, hh:], src[:, hh:], q, ce)
                else:
                    wload(dst[:, :, 0:hh], src[:, :, 0:hh], q, ce)
                    wload(dst[:, :, hh:], src[:, :, hh:], q, ce)
                return
            s = stg[i][0:shp[0], 0:n]
            if len(shp) == 3:
                s = s.rearrange("p (a b) -> p a b", a=shp[1])
            qq = q or ("sp" if i == 0 else "pool")
            dma(qq, s, src, w=["stg%d" % i])
            kd = "W" + str(id(dst))
            if ce == "pool":
                P.op("pool", lambda e: e.tensor_copy(out=dst, in_=s), r=["stg%d" % i], w=["wgt"])
            elif ce == "dve":
                P.op("dve", lambda e: e.tensor_copy(out=dst, in_=s), r=["stg%d" % i], w=["wgt"])
            else:
                P.op("act", lambda e: e.copy(out=dst, in_=s), r=["stg%d" % i], w=["wgt"])

        for (dst, nm) in ((identb, "ident"), (rotb, "rot")):
            wload(dst, din[nm])
        P.op("pool", lambda e: e.memset(onesb, 1.0), w=["wgt"])
        dma("sp", maskf, din["mask_f"], w=["wgt"])
        dma("sp", maskb, din["mask_b"], w=["wgt"])
        PERSIST = top[0]

        def phase_reset():
            P.barrier()
            top[0] = PERSIST

        def phase_mod():
            cv = T([128, 2, 8])
            cvs = T([128, 2, 8])
            cvb = T([128, 8, 2], BF16)
            for j in range(2):
                dma_nc("sp", cv[:, j, :], din["cvec"][j].rearrange("(k p) -> p k", p=128), w=["cv"])
            P.op("act", lambda e: e.activation(out=cvs, in_=cv, func=AF.Silu), r=["cv"], w=["cvs"])
            P.op("dve", lambda e: e.tensor_copy(out=cvb, in_=cvs.rearrange("p j k -> p k j")), r=["cvs"], w=["cvb"])
            awb = [T([128, 8, 256], BF16) for _ in range(2)]
            Mt = T([2, 6 * D])
            bt = T([2, 6 * D])
            ng = T([2, 4, D])
            V = T([2, 6, D])
            for l in range(2):
                dma("sp", bt, din["ada_b"][l].partition_broadcast(2), w=["bt"])
                dma("sp", ng.rearrange("p a b -> p (a b)"),
                    din["norm_g"][l].rearrange("a b -> (a b)").partition_broadcast(2), w=["ng"])
                for nb in range(24):
                    ab = awb[nb % 2]
                    kab = "awb%d" % (nb % 2)
                    src = din["ada_w"][l].rearrange("(k p) n -> p k n", p=128)[:, :, nb * 256:(nb + 1) * 256]
                    i = stgi[0] % 2
                    stgi[0] += 1
                    s = stg[i][:, :].rearrange("p (a b) -> p a b", a=8)
                    dma("sp" if i == 0 else "pool", s, src, w=["stg%d" % i])
                    if nb % 3 == 2:
                        P.op("dve", (lambda e, ab=ab, s=s: e.tensor_copy(out=ab, in_=s)), r=["stg%d" % i], w=[kab])
                    else:
                        P.op("act", (lambda e, ab=ab, s=s: e.copy(out=ab, in_=s)), r=["stg%d" % i], w=[kab])
                    pb = ps[nb % 2][0:2, 0:256]
                    mmg(pb, [(cvb[:, k, :], ab[:, k, :]) for k in range(8)], r=["cvb", kab], w=["ps%d" % (nb % 2)])
                    P.op("dve", (lambda e, pb=pb, nb=nb: e.tensor_tensor(out=Mt[:, nb * 256:(nb + 1) * 256], in0=pb,
                                                                       in1=bt[:, nb * 256:(nb + 1) * 256], op=ALU.add)),
                         r=["ps%d" % (nb % 2), "bt"], w=["Mt"])
                sl = lambda i: Mt[:, i * D:(i + 1) * D]
                P.op("dve", lambda e: e.scalar_tensor_tensor(out=V[:, 0, :], in0=sl(1), scalar=1.0, in1=ng[:, 0, :],
                                                             op0=ALU.add, op1=ALU.mult), r=["Mt", "ng"], w=["V"])
                P.op("dve", lambda e: e.tensor_copy(out=V[:, 1, :], in_=sl(0)), r=["Mt"], w=["V"])
                P.op("dve", lambda e: e.tensor_tensor(out=V[:, 2, :], in0=sl(2), in1=ng[:, 1, :], op=ALU.mult),
                     r=["Mt", "ng"], w=["V"])
                P.op("dve", lambda e: e.scalar_tensor_tensor(out=V[:, 3, :], in0=sl(4), scalar=1.0, in1=ng[:, 2, :],
                                                             op0=ALU.add, op1=ALU.mult), r=["Mt", "ng"], w=["V"])
                P.op("dve", lambda e: e.tensor_copy(out=V[:, 4, :], in_=sl(3)), r=["Mt"], w=["V"])
                P.op("dve", lambda e: e.tensor_tensor(out=V[:, 5, :], in0=sl(5), in1=ng[:, 3, :], op=ALU.mult),
                     r=["Mt", "ng"], w=["V"])
                dma("sp", MODV[l].rearrange("j a b -> j (a b)"), V.rearrange("p a b -> p (a b)"), r=["V"], w=["MODV"])

        def x_src(layer0_in, tt):
            if layer0_in:
                return din["ctx"][tt * 128:(tt + 1) * 128, :] if tt < 2 else din["x"][(tt - 2) * 128:(tt - 1) * 128, :]
            return XR[tt * 128:(tt + 1) * 128, :]

        class NM:
            def __init__(self, l, gi, si):
                self.xt = [T([128, D]) for _ in range(2)]
                self.junk = T([128, D], BF16)
                self.tmp = [T([128, D]) for _ in range(2)]
                self.pending = None
                self.hb = [T([128, D], BF16) for _ in range(2)]
                self.ss = T([128, 2])
                self.rs = T([128, 2])
                self.G = [T([128, D]) for _ in range(2)]
                self.SH = [T([128, D]) for _ in range(2)]
                for j in range(2):
                    dma("sp", self.G[j], MODV[l, j, gi].partition_broadcast(128), r=["MODV"], w=["nmG"])
                    dma("sp", self.SH[j], MODV[l, j, si].partition_broadcast(128), r=["MODV"], w=["nmG"])
                self.i = 0

            def run(self, src, is_ctx, dst, dkey, bank):
                i = self.i % 2
                self.i += 1
                xt, hb = self.xt[i], self.hb[i]
                kx, kh = "nm_xt%d" % i, "nm_hb%d" % i
                ss, rs = self.ss[:, i:i + 1], self.rs[:, i:i + 1]
                G, SH = self.G[1 if is_ctx else 0], self.SH[1 if is_ctx else 0]
                tmp = self.tmp[i]
                dma("sp", xt, src, r=["XR"], w=[kx])
                P.op("act", lambda e: e.activation(out=self.junk, in_=xt, func=AF.Square, accum_out=ss),
                     r=[kx], w=["nm_junk", "nm_ss%d" % i])
                P.op("act", lambda e: e.activation(out=rs, in_=ss, func=AF.Sqrt, scale=1.0 / D, bias=EPS),
                     r=["nm_ss%d" % i], w=["nm_rs%d" % i])
                P.op("dve", lambda e: e.reciprocal(out=rs, in_=rs), r=["nm_rs%d" % i], w=["nm_rs%d" % i])
                P.op("dve", lambda e: e.scalar_tensor_tensor(out=tmp, in0=xt, scalar=rs, in1=G, op0=ALU.mult,
                                                             op1=ALU.mult), r=[kx, "nm_rs%d" % i, "nmG"], w=["nm_tmp%d" % i])
                P.op("pool", lambda e: e.tensor_tensor(out=hb, in0=tmp, in1=SH, op=ALU.add),
                     r=["nm_tmp%d" % i, "nmG"], w=[kh])
                prev = self.pending
                self.pending = (hb, kh, dst, dkey, bank)
                if prev is not None:
                    self.stage_b(*prev)

            def stage_b(self, hb, kh, dst, dkey, bank):
                pb = psb[bank]

                def tr(e):
                    for k in range(8):
                        ins = e.transpose(out=pb[:, k * 128:(k + 1) * 128], in_=hb[:, k * 128:(k + 1) * 128],
                                          identity=identb)
                    return ins
                P.op("pe", tr, r=[kh], w=["ps%d" % bank])
                P.op("act", lambda e: e.copy(out=dst, in_=pb.rearrange("p (k n) -> p k n", k=8)),
                     r=["ps%d" % bank], w=[dkey])

            def flush(self):
                if self.pending is not None:
                    self.stage_b(*self.pending)
                    self.pending = None

        class RES:
            def __init__(self, l, gidx):
                self.GT = [T([128, D]) for _ in range(2)]
                for j in range(2):
                    dma("sp", self.GT[j], MODV[l, j, gidx].partition_broadcast(128), r=["MODV"], w=["resG"])
                self.xo = [T([128, D]) for _ in range(2)]
                self.tt_ = [T([128, D]) for _ in range(2)]
                self.junk = T([128, 512], BF16)
                self.ss = T([128, 4])
                self.rs = T([128, 2])
                self.i = 0

            def run(self, b0, b1, src, dstd, is_ctx, wkey):
                i = self.i % 2
                self.i += 1
                xo = self.xo[i]
                tbuf = self.tt_[i]
                kt_ = "res_t%d" % i
                kx = "res_x%d" % i
                ss = self.ss[:, 2 * i:2 * i + 2]
                rs = self.rs[:, i:i + 1]
                GT = self.GT[1 if is_ctx else 0]
                dma("pool", xo, src, r=["XR"], w=[kx])
                P.op("act", lambda e: e.activation(out=self.junk, in_=ps[b0], func=AF.Square, accum_out=ss[:, 0:1]),
                     r=["ps%d" % b0], w=["res_junk", "res_ss%d" % i])
                P.op("act", lambda e: e.activation(out=self.junk, in_=ps[b1], func=AF.Square, accum_out=ss[:, 1:2]),
                     r=["ps%d" % b1], w=["res_junk", "res_ss%d" % i])
                P.op("dve", lambda e: e.tensor_tensor(out=rs, in0=ss[:, 0:1], in1=ss[:, 1:2], op=ALU.add),
                     r=["res_ss%d" % i], w=["res_rs%d" % i])
                P.op("act", lambda e: e.activation(out=rs, in_=rs, func=AF.Sqrt, scale=1.0 / D, bias=EPS),
                     r=["res_rs%d" % i], w=["res_rs%d" % i])
                P.op("dve", lambda e: e.reciprocal(out=rs, in_=rs), r=["res_rs%d" % i], w=["res_rs%d" % i])
                for hf, bk in ((0, b0), (1, b1)):
                    P.op("dve", (lambda e, hf=hf, bk=bk: e.scalar_tensor_tensor(
                        out=tbuf[:, hf * 512:(hf + 1) * 512], in0=ps[bk], scalar=rs,
                        in1=GT[:, hf * 512:(hf + 1) * 512], op0=ALU.mult, op1=ALU.mult)),
                        r=["ps%d" % bk, "res_rs%d" % i, "resG"], w=[kt_, "ps%d" % bk])
                P.op("pool", lambda e: e.tensor_tensor(out=xo, in0=tbuf, in1=xo, op=ALU.add),
                     r=[kt_, kx], w=[kx])
                dma("pool", dstd, xo, r=[kx], w=[wkey])

        def rope_evict(pbank, n, t0, dst, dkey, ro, is_ctx):
            src = ps[pbank][:, 0:n]
            if is_ctx:
                P.op("act", lambda e: e.copy(out=dst, in_=src), r=["ps%d" % pbank], w=[dkey])
                return
            qs, t1, t2, cosb, sinb, rb = ro
            P.op("act", lambda e: e.copy(out=qs[:, 0:n], in_=src), r=["ps%d" % pbank], w=["ro_qs"])
            mmg(ps[rb][:, 0:n], [(rotb, qs[:, 0:n])], r=["ro_qs"], w=["ps%d" % rb])
            P.op("dve", lambda e: e.tensor_tensor(out=t1[:, 0:n], in0=src, in1=cosb[:, 0:n], op=ALU.mult),
                 r=["ps%d" % pbank, "ro_cs"], w=["ro_t1", "ps%d" % pbank])
            P.op("dve", lambda e: e.tensor_tensor(out=t2[:, 0:n], in0=ps[rb][:, 0:n], in1=sinb[:, 0:n], op=ALU.mult),
                 r=["ps%d" % rb, "ro_cs"], w=["ro_t2", "ps%d" % rb])
            P.op("pool", lambda e: e.tensor_tensor(out=dst, in0=t1[:, 0:n], in1=t2[:, 0:n], op=ALU.add),
                 r=["ro_t1", "ro_t2"], w=[dkey])

        def rope_tiles():
            return (T([128, 512], BF16), T([128, 512]), T([128, 512]), T([128, 512]), T([128, 512]))

        def rope_load(ro, b):
            if b == 0:
                return
            c0 = (b - 1) * 512
            dma("pool", ro[3], din["cos_t"][:, c0:c0 + 512], w=["ro_cs"])
            dma("pool", ro[4], din["sin_t"][:, c0:c0 + 512], w=["ro_cs"])

        def attention(groups, scale, accsets):
            PT = [T([128, 2, 512], BF16) for _ in range(3)]
            N = len(groups)

            def emitS(i):
                g = groups[i]
                if g.get("pre"):
                    g["pre"]()
                A = 2 * (i % 2)
                n = g["n"]
                for mi, m in enumerate(g["members"]):
                    mmg(ps[A + mi][:, 0:n], m["qk"], r=g["rk"], w=["ps%d" % (A + mi)])

            def emitE(i):
                g = groups[i]
                A = 2 * (i % 2)
                n = g["n"]
                nm_ = len(g["members"])
                pt = PT[i % 3]
                src = psall[:, A * 512:(A + 2) * 512].rearrange("p (j n) -> p j n", j=2)[:, 0:nm_, 0:n]
                P.op("act", (lambda e: e.activation(out=pt[:, 0:nm_, 0:n], in_=src, func=AF.Exp, scale=scale)),
                     r=["ps%d" % A, "ps%d" % (A + 1)], w=["PT%d" % (i % 3), "ps%d" % A, "ps%d" % (A + 1)])

            def emitPV(i):
                g = groups[i]
                n = g["n"]
                pt = PT[i % 3]
                acc = accsets[g["accset"]]
                wk = []
                for m in g["members"]:
                    ob, sb2 = acc[m["acc"]]
                    wk += ["ps%d" % ob, "ps%d" % sb2]

                def pv(e):
                    for mi, m in enumerate(g["members"]):
                        ob, sb2 = acc[m["acc"]]
                        e.matmul(ps[ob][:, 0:n], lhsT=m["v"], rhs=pt[:, mi, 0:n], start=m["start"], stop=m["stop"])
                        ins = e.matmul(ps[sb2][:, 0:n], lhsT=onesb, rhs=pt[:, mi, 0:n], start=m["start"], stop=m["stop"])
                    return ins
                P.op("pe", pv, r=["PT%d" % (i % 3)] + g["rv"], w=list(dict.fromkeys(wk)))
                if g.get("post"):
                    A = 2 * (i % 2)
                    g["post"](acc, (A, A + 1))

            if N:
                emitS(0)
            for i in range(N):
                emitE(i)
                if i + 1 < N:
                    emitS(i + 1)
                emitPV(i)

        def phase_ffn(l, need_ctx, final):
            W2 = 4355
            Wd = T([128, NJ, D], BF16)
            CW = T([128, 4, NJ])
            for i in range(3):
                dma_nc("sp", CW[:, i, :], din["ffn_conv_w"][l, i].rearrange("(j p) -> p j", p=128), w=["CW"])
            dma_nc("sp", CW[:, 3, :], din["ffn_conv_b"][l].rearrange("(j p) -> p j", p=128), w=["CW"])
            wdsrc = din["ffn_w_down"][l].rearrange("(j p) n -> p j n", p=128)
            for j0 in range(0, NJ, 4):
                j1 = min(NJ, j0 + 4)
                wload(Wd[:, j0:j1, :], wdsrc[:, j0:j1, :])
            top_save = top[0]
            nm = NM(l, 3, 4)
            hTs = [T([128, 8, 512], BF16) for _ in range(2)]
            zt = T([128, 8, 1], BF16)
            P.op("pool", lambda e: e.memset(zt, 0.0), w=["zt"])
            for c in (0, 257, 4354):
                dma_nc("pool", H2D[:, :, c:c + 1], zt, r=["zt"], w=["H2D"])
            blks = list(range(0 if need_ctx else 1, 9))
            for b in blks:
                t0, ntl = BLOCKS[b]
                n = ntl * 128
                hT = hTs[b % 2]
                kh = "hTs%d" % (b % 2)
                for ti in range(ntl):
                    tt = t0 + ti
                    nm.run(x_src(False, tt), tt < 2, hT[:, :, ti * 128:(ti + 1) * 128], kh, 6 + tt % 2)
                nm.flush()
                c0 = 1 if b == 0 else 258 + (b - 1) * 512
                dma("pool", H2D[:, :, c0:c0 + n], hT[:, :, 0:n], r=[kh], w=["H2D"])
            P.barrier()
            top[0] = top_save
            res = RES(l, 5)
            GTt = T([128, NJ, 1024], BF16)
            H2P = [T([128, 8, 1026], BF16) for _ in range(2)]
            wgf = [T([128, 8, 128], BF16) for _ in range(2)]
            wuf = [T([128, 8, 128], BF16) for _ in range(2)]
            acc = [T([128, 512]) for _ in range(2)]
            sil = [T([128, 512]) for _ in range(2)]
            parts = [blks[i:i + 2] for i in range(0, len(blks), 2)]
            wgsrc = din["ffn_w_gate"][l].rearrange("(k p) n -> p k n", p=128)
            wusrc = din["ffn_w_up"][l].rearrange("(k p) n -> p k n", p=128)
            it = [0]
            yi = [0]
            bc0 = lambda b: 1 if b == 0 else 258 + (b - 1) * 512
            for pi, part in enumerate(parts):
                cstart = bc0(part[0]) - 1
                cend = bc0(part[-1]) + BLOCKS[part[-1]][1] * 128 + 1
                npc = cend - cstart
                H2T = H2P[pi % 2]
                kH = "H2P%d" % (pi % 2)
                dma("sp", H2T[:, :, 0:npc], H2D[:, :, cstart:cend], r=["H2D"], w=[kH])
                goff = {}
                o = 0
                for b in part:
                    goff[b] = o
                    o += BLOCKS[b][1] * 128
                for j in range(NJ):
                    wg, wu = wgf[j % 2], wuf[j % 2]
                    kw = "ffw%d" % (j % 2)
                    for (dst, srcw) in ((wg, wgsrc), (wu, wusrc)):
                        i = stgi[0] % 2
                        stgi[0] += 1
                        s = stg[i][:, 0:1024].rearrange("p (a b) -> p a b", a=8)
                        dma("sp", s, srcw[:, :, j * 128:(j + 1) * 128], w=["stg%d" % i])
                        P.op("pool", (lambda e, dst=dst, s=s: e.tensor_copy(out=dst, in_=s)), r=["stg%d" % i], w=[kw])
                    for b in part:
                        t0, ntl = BLOCKS[b]
                        n = ntl * 128
                        c0 = bc0(b) - cstart
                        q = it[0] % 2
                        ub_ = (2, 3, 5)[it[0] % 3]
                        it[0] += 1
                        pa, pu, ph = ps[q], ps[ub_], ps[4]
                        ka, ku, kh = "ps%d" % q, "ps%d" % ub_, "ps4"
                        mmg(pa[:, 0:n], [(wg[:, k, :], H2T[:, k, c0:c0 + n]) for k in range(8)], r=[kw, kH], w=[ka])
                        hal = H2T[:, :, c0 - 1:c0 + n + 1:n + 1]
                        mmg(ph[:, 0:2], [(wg[:, k, :], hal[:, k, :]) for k in range(8)], r=[kw, kH], w=[kh])
                        mmg(pu[:, 0:n], [(wu[:, k, :], H2T[:, k, c0:c0 + n]) for k in range(8)], r=[kw, kH], w=[ku])
                        ac, sl_ = acc[q], sil[q]
                        kac, ksl = "acc%d" % q, "sil%d" % q
                        w0, w1, w2, bb = (CW[:, i, j:j + 1] for i in range(4))
                        P.op("dve", (lambda e, ac=ac, pa=pa, w1=w1, bb=bb, n=n: e.tensor_scalar(
                            out=ac[:, 0:n], in0=pa[:, 0:n], scalar1=w1, scalar2=bb, op0=ALU.mult, op1=ALU.add)),
                            r=[ka, "CW"], w=[kac])
                        P.op("dve", (lambda e, ac=ac, ph=ph, w0=w0: e.scalar_tensor_tensor(
                            out=ac[:, 0:1], in0=ph[:, 0:1], scalar=w0, in1=ac[:, 0:1], op0=ALU.mult, op1=ALU.add)),
                            r=[kh, kac], w=[kac])
                        P.op("dve", (lambda e, ac=ac, ph=ph, w2=w2, n=n: e.scalar_tensor_tensor(
                            out=ac[:, n - 1:n], in0=ph[:, 1:2], scalar=w2, in1=ac[:, n - 1:n], op0=ALU.mult, op1=ALU.add)),
                            r=[kh, kac], w=[kac, kh])
                        P.op("dve", (lambda e, ac=ac, pa=pa, w0=w0, n=n: e.scalar_tensor_tensor(
                            out=ac[:, 1:n], in0=pa[:, 0:n - 1], scalar=w0, in1=ac[:, 1:n], op0=ALU.mult, op1=ALU.add)),
                            r=[ka, kac], w=[kac])
                        P.op("dve", (lambda e, ac=ac, pa=pa, w2=w2, n=n: e.scalar_tensor_tensor(
                            out=ac[:, 0:n - 1], in0=pa[:, 1:n], scalar=w2, in1=ac[:, 0:n - 1], op0=ALU.mult, op1=ALU.add)),
                            r=[ka, kac], w=[kac, ka])
                        P.op("act", (lambda e, ac=ac, sl_=sl_, n=n: e.activation(out=sl_[:, 0:n], in_=ac[:, 0:n], func=AF.Silu)),
                             r=[kac], w=[ksl])
                        g0 = goff[b]
                        P.op("dve", (lambda e, sl_=sl_, pu=pu, j=j, g0=g0, n=n: e.tensor_tensor(
                            out=GTt[:, j, g0:g0 + n], in0=sl_[:, 0:n], in1=pu[:, 0:n], op=ALU.mult)),
                            r=[ksl, ku], w=["GT", ku])
                for b in part:
                    t0, ntl = BLOCKS[b]
                    for ti in range(ntl):
                        tt = t0 + ti
                        g0 = goff[b] + ti * 128
                        y0, y1 = ((6, 7), (0, 1), (2, 3))[yi[0] % 3]
                        yi[0] += 1
                        for hf, yb in ((0, y0), (1, y1)):
                            mmg(ps[yb], [(GTt[:, j, g0:g0 + 128], Wd[:, j, hf * 512:(hf + 1) * 512]) for j in range(NJ)],
                                r=["GT", "wgt"], w=["ps%d" % yb])
                        if final:
                            dstd = out[(tt - 2) * 128:(tt - 1) * 128, :]
                            res.run(y0, y1, x_src(False, tt), dstd, tt < 2, "OUT")
                        else:
                            res.run(y0, y1, x_src(False, tt), XR[tt * 128:(tt + 1) * 128, :], tt < 2, "XR2")

        def phase_l0():
            l = 0
            LAM_INIT = 0.2
            KT = T([128, 4, NTOK], BF16)
            Vv = T([128, NT, 512], BF16)
            keep = top[0]
            w_in = T([128, 8, 2048], BF16)
            wsrc = din["ev_w_in"][0].rearrange("(k p) n -> p k n", p=128)
            for c in range(4):
                wload(w_in[:, :, c * 512:(c + 1) * 512], wsrc[:, :, c * 512:(c + 1) * 512])
            nm = NM(l, 0, 1)
            hT = T([128, 8, 512], BF16)
            ro = rope_tiles() + (7,)
            ub = [T([128, 512]) for _ in range(2)]
            qb = [T([128, 4, 512], BF16) for _ in range(2)]
            for b, (t0, ntl) in enumerate(BLOCKS):
                n = ntl * 128
                col0 = t0 * 128
                rope_load(ro, b)
                for ti in range(ntl):
                    tt = t0 + ti
                    nm.run(x_src(True, tt), tt < 2, hT[:, :, ti * 128:(ti + 1) * 128], "hT", 6)
                nm.flush()
                for ti in range(ntl):
                    tt = t0 + ti
                    bk = 4 + (ti % 2)
                    mmg(ps[bk], [(hT[:, k, ti * 128:(ti + 1) * 128], w_in[:, k, 1536:2048]) for k in range(8)],
                        r=["hT", "wgt"], w=["ps%d" % bk])
                    P.op("act", (lambda e, tt=tt, bk=bk: e.copy(out=Vv[:, tt, :], in_=ps[bk])), r=["ps%d" % bk], w=["Vv"])
                for c in range(4):
                    bk = c % 2
                    mmg(ps[bk][:, 0:n], [(w_in[:, k, c * 128:(c + 1) * 128], hT[:, k, 0:n]) for k in range(8)],
                        r=["hT", "wgt"], w=["ps%d" % bk])
                    u = ub[c % 2]
                    P.op("act", (lambda e, u=u, bk=bk, n=n: e.copy(out=u[:, 0:n], in_=ps[bk][:, 0:n])),
                         r=["ps%d" % bk], w=["ub%d" % (c % 2)])
                    dma("pool", UT[c][:, col0:col0 + n], u[:, 0:n], r=["ub%d" % (c % 2)], w=["UT"])
                qbb = qb[b % 2]
                kq = "qb%d" % (b % 2)
                for c in range(4):
                    bk = 2 + (c % 2)
                    mmg(ps[bk][:, 0:n], [(w_in[:, k, 512 + c * 128:512 + (c + 1) * 128], hT[:, k, 0:n]) for k in range(8)],
                        r=["hT", "wgt"], w=["ps%d" % bk])
                    rope_evict(bk, n, col0, qbb[:, c, 0:n], kq, ro, b == 0)
                dma("pool", QT[b][:, :, 0:n], qbb[:, :, 0:n], r=[kq], w=["QT"])
                for c in range(4):
                    bk = 2 + (c % 2)
                    mmg(ps[bk][:, 0:n], [(w_in[:, k, 1024 + c * 128:1024 + (c + 1) * 128], hT[:, k, 0:n]) for k in range(8)],
                        r=["hT", "wgt"], w=["ps%d" % bk])
                    rope_evict(bk, n, col0, KT[:, c, col0:col0 + n], "KT", ro, b == 0)
            P.barrier()
            top[0] = keep
            if stop == "l0a1":
                return
            pw = T([128, 4, 128], BF16)
            for g in range(4):
                wload(pw[:, g, :], din["pool_w"][0, g])
            psc = T([128, 4])
            dma_nc("sp", psc, din["pool_scale"][0].rearrange("(g p) -> p g", p=128), w=["psc"])
            UP = T([128, PW])
            Aa = T([128, PW])
            Ab = T([128, PW])
            IC = T([128, PW])
            dT = T([128, PW], BF16)
            mpo = [T([128, 512], BF16) for _ in range(2)]
            for g in range(4):
                hw = (1, 2, 4, 8)[g]
                P.op("pool", lambda e: e.memset(UP, 0.0), w=["UP"])
                dma("sp", UP[:, PC0:PC0 + 256], UT[g][:, 0:256], r=["UT"], w=["UP"])
                dma("sp", UP[:, PL0:PL0 + 4096], UT[g][:, 256:NTOK], r=["UT"], w=["UP"])
                dma("pool", IC, din["invcnt"][g].partition_broadcast(128), w=["IC"])
                cur, ck = UP, "UP"
                bufs = [(Aa, "Aa"), (Ab, "Ab")]
                width = PW
                for s in range(g + 1):
                    sh = 1 << s
                    nxt, nk = bufs[s % 2]
                    width -= sh
                    P.op("dve", (lambda e, cur=cur, nxt=nxt, sh=sh, width=width: e.tensor_tensor(
                        out=nxt[:, 0:width], in0=cur[:, 0:width], in1=cur[:, sh:sh + width], op=ALU.add)),
                        r=[ck], w=[nk])
                    cur, ck = nxt, nk
                oth, ok = bufs[(g + 1) % 2]
                P.op("dve", (lambda e, cur=cur, oth=oth, hw=hw: e.tensor_tensor(
                    out=oth[:, 8:PW - 8], in0=cur[:, 8 - hw:PW - 8 - hw], in1=IC[:, 8:PW - 8], op=ALU.mult)),
                    r=[ck, "IC"], w=[ok])
                P.op("pool", (lambda e, oth=oth: e.tensor_tensor(out=dT[:, 8:PW - 8], in0=oth[:, 8:PW - 8],
                                                                 in1=UP[:, 8:PW - 8], op=ALU.subtract)),
                     r=[ok, "UP"], w=["dT"])
                for b, (t0, ntl) in enumerate(BLOCKS):
                    n = ntl * 128
                    col0 = t0 * 128
                    pc = PC0 if b == 0 else PL0 + (b - 1) * 512
                    bk = b % 2
                    mmg(ps[bk][:, 0:n], [(pw[:, g, :], dT[:, pc:pc + n])], r=["dT", "wgt"], w=["ps%d" % bk])
                    mp = mpo[b % 2]
                    P.op("act", (lambda e, bk=bk, n=n, g=g, mp=mp: e.activation(
                        out=mp[:, 0:n], in_=ps[bk][:, 0:n], func=AF.Identity, scale=psc[:, g:g + 1])),
                        r=["ps%d" % bk, "psc"], w=["mpo%d" % (b % 2)])
                    dma("pool", MIXD[g][:, col0:col0 + n], mp[:, 0:n], r=["mpo%d" % (b % 2)], w=["MIXD"])
            P.barrier()
            top[0] = keep
            if stop == "l0a2":
                return
            lamt = T([128, 4, 64])
            lj = T([128, 64])
            lsum = T([128, 2])
            nlam = T([128, 1])
            dma("sp", lamt.rearrange("p a b -> p (a b)"),
                din["diff_lambda"][0].rearrange("a b -> (a b)").partition_broadcast(128), w=["lamt"])
            for i in range(2):
                P.op("dve", (lambda e, i=i: e.scalar_tensor_tensor(out=lj, in0=lamt[:, 2 * i, :], scalar=1.0,
                                                                   in1=lamt[:, 2 * i + 1, :], op0=ALU.mult, op1=ALU.mult,
                                                                   accum_out=lsum[:, i:i + 1])),
                     r=["lamt"], w=["lj", "lsum"])
            P.op("act", lambda e: e.activation(out=lsum, in_=lsum, func=AF.Exp), r=["lsum"], w=["lsum"])
            P.op("dve", lambda e: e.tensor_tensor(out=nlam, in0=lsum[:, 1:2], in1=lsum[:, 0:1], op=ALU.subtract),
                 r=["lsum"], w=["nlam"])
            P.op("dve", lambda e: e.tensor_scalar(out=nlam, in0=nlam, scalar1=-LAM_INIT, scalar2=None, op0=ALU.add),
                 r=["nlam"], w=["nlam"])
            sln = T([128, 1])
            dma_nc("sp", sln, din["diff_subln"][0].rearrange("(p o) -> p o", o=1), w=["sln"])
            P.op("dve", lambda e: e.tensor_scalar(out=sln, in0=sln, scalar1=1.0 - LAM_INIT, scalar2=None, op0=ALU.mult),
                 r=["sln"], w=["sln"])
            qtl = [T([128, 4, 512], BF16) for _ in range(3)]
            MIXA = [T([128, 4, 512], BF16) for _ in range(2)]
            rsum = T([128, 512])
            o2 = [T([128, 512]) for _ in range(2)]
            oc = T([128, 512])
            sq = T([128, 512], BF16)
            rstd = T([128, 512])
            state = {}

            def pre(b):
                if state.get("b") != b:
                    state["b"] = b
                    n = BLOCKS[b][1] * 128
                    col0 = BLOCKS[b][0] * 128
                    dma("sp", qtl[b % 3][:, :, 0:n], QT[b][:, :, 0:n], r=["QT"], w=["qtl%d" % (b % 3)])

            def post(b, h, n, acc, scr):
                mixa = MIXA[b % 2]
                km = "MIXA%d" % (b % 2)
                for j in range(2):
                    ob, sb2 = acc[j]
                    P.op("dve", (lambda e, sb2=sb2: e.reciprocal(out=rsum[:, 0:n], in_=ps[sb2][:, 0:n])),
                         r=["ps%d" % sb2], w=["rsum", "ps%d" % sb2])
                    P.op("dve", (lambda e, ob=ob, j=j: e.tensor_tensor(out=o2[j][:, 0:n], in0=ps[ob][:, 0:n], in1=rsum[:, 0:n], op=ALU.mult)),
                         r=["ps%d" % ob, "rsum"], w=["o2_%d" % j, "ps%d" % ob])
                P.op("dve", lambda e: e.scalar_tensor_tensor(out=oc[:, 0:n], in0=o2[1][:, 0:n], scalar=nlam,
                                                             in1=o2[0][:, 0:n], op0=ALU.mult, op1=ALU.add),
                     r=["o2_0", "o2_1", "nlam"], w=["oc"])
                P.op("act", lambda e: e.activation(out=sq[:, 0:n], in_=oc[:, 0:n], func=AF.Square), r=["oc"], w=["sq"])
                sbk = scr[0]
                mmg(ps[sbk][:, 0:n], [(onesb, sq[:, 0:n])], r=["sq"], w=["ps%d" % sbk])
                P.op("act", lambda e: e.activation(out=rstd[:, 0:n], in_=ps[sbk][:, 0:n], func=AF.Sqrt, scale=1.0 / 128,
                                                   bias=EPS), r=["ps%d" % sbk], w=["rstd", "ps%d" % sbk])
                P.op("dve", lambda e: e.reciprocal(out=rstd[:, 0:n], in_=rstd[:, 0:n]), r=["rstd"], w=["rstd"])
                P.op("dve", lambda e: e.scalar_tensor_tensor(out=mixa[:, h, 0:n], in0=oc[:, 0:n], scalar=sln,
                                                             in1=rstd[:, 0:n], op0=ALU.mult, op1=ALU.mult),
                     r=["oc", "rstd", "sln"], w=[km])
                col0 = BLOCKS[b][0] * 128
                dma("pool", MIXD[4 + h][:, col0:col0 + n], mixa[:, h, 0:n], r=[km], w=["MIXD"])

            groups = []
            for b in range(9):
                n = BLOCKS[b][1] * 128
                kts = list(range(2)) if b == 0 else list(range(NT))
                for h in range(4):
                    for ki, kt in enumerate(kts):
                        mem = []
                        for j in range(2):
                            pr = slice(64 * j, 64 * j + 64)
                            mem.append(dict(qk=[(KT[pr, h, kt * 128:(kt + 1) * 128], qtl[b % 3][pr, h, 0:n])],
                                            v=Vv[:, kt, h * 128:(h + 1) * 128], acc=j, start=(ki == 0), stop=(ki == len(kts) - 1)))
                        g = dict(n=n, members=mem, rk=["KT", "qtl%d" % (b % 3)], rv=["Vv"], accset=0)
                        if h == 0 and ki == 0:
                            g["pre"] = (lambda b=b: (pre(b), pre(b + 1) if b + 1 < 9 else None))
                        if ki == len(kts) - 1:
                            g["post"] = (lambda acc, scr, b=b, h=h, n=n: post(b, h, n, acc, scr))
                        groups.append(g)
            attention(groups, 0.125, [[(4, 5), (6, 7)]])
            phase_reset()
            phase_outproj(0, list(range(9)), True)

        def rownorm(pt, W, NB, dst, dkey, tg):
            junk, ss, rs = tg
            P.op("act", lambda e: e.activation(out=junk[:, 0:W], in_=pt[0][:, 0:W], func=AF.Square, accum_out=ss),
                 r=[pt[1]], w=["rn_junk", "rn_ss"])
            P.op("act", lambda e: e.activation(out=rs, in_=ss, func=AF.Sqrt, scale=1.0 / W, bias=EPS), r=["rn_ss"], w=["rn_rs"])
            P.op("dve", lambda e: e.reciprocal(out=rs, in_=rs), r=["rn_rs"], w=["rn_rs"])
            P.op("dve", lambda e: e.scalar_tensor_tensor(out=dst, in0=pt[0][:, 0:W], scalar=rs, in1=NB[:, 0:W], op0=ALU.mult,
                                                         op1=ALU.mult), r=[pt[1], "rn_rs", "wgt"], w=[dkey, pt[1]])

        def phase_l1():
            l = 1
            KN = T([128, 4, NTOK], BF16)
            VM = T([128, NT, 512], BF16)
            KR2 = T([128, NTOK], BF16)
            keep = top[0]
            WA = T([128, 8, 512], BF16)
            WB = T([128, 8, 256], BF16)
            WKR = T([128, 8, 128], BF16)
            WQN = T([128, 4, 4, 128], BF16)
            WQR = T([128, 4, 256], BF16)
            WKN = T([128, 2, 4, 128], BF16)
            WVV = T([128, 2, 512], BF16)
            wsrc = din["od_w_in"][0].rearrange("(k p) n -> p k n", p=128)
            wload(WA, wsrc[:, :, 0:512])
            wload(WB, wsrc[:, :, 512:768])
            wload(WKR[:, :, 0:64], wsrc[:, :, 768:832])
            wload(WKR[:, :, 64:128], wsrc[:, :, 768:832])
            uq = din["mla_w_uq"][0].rearrange("(k p) (h c) -> p k h c", p=128, c=192)
            ukv = din["mla_w_ukv"][0].rearrange("(k p) (h c) -> p k h c", p=128, c=256)
            for k in range(4):
                wload(WQN[:, k], uq[:, k, :, 0:128])
                wload(WQR[:, k, :].rearrange("p (h c) -> p h c", h=4), uq[:, k, :, 128:192])
            for k in range(2):
                wload(WKN[:, k], ukv[:, k, :, 0:128])
                wload(WVV[:, k, :].rearrange("p (h c) -> p h c", h=4), ukv[:, k, :, 128:256])
            QNb = T([128, 512])
            KVNb = T([128, 256])
            dma("sp", QNb, din["mla_q_norm"][0].partition_broadcast(128), w=["wgt"])
            dma("sp", KVNb, din["mla_kv_norm"][0].partition_broadcast(128), w=["wgt"])
            nm = NM(l, 0, 1)
            hTs = [T([128, 8, 512], BF16)] * 2
            ro = rope_tiles() + (7,)
            tg = (T([128, 512], BF16), T([128, 1]), T([128, 1]))
            cqn = T([128, 512], BF16)
            ckvn = T([128, 256], BF16)
            cqT = T([128, 4, 512], BF16)
            ckvT = T([128, 2, 512], BF16)
            qnb = [T([128, 4, 512], BF16)] * 2
            qrb = [T([128, 2, 512], BF16)] * 2
            for b, (t0, ntl) in enumerate(BLOCKS):
                n = ntl * 128
                col0 = t0 * 128
                hT = hTs[b % 2]
                khT = "hTs0"
                rope_load(ro, b)
                for ti in range(ntl):
                    tt = t0 + ti
                    nm.run(x_src(False, tt), tt < 2, hT[:, :, ti * 128:(ti + 1) * 128], khT, 6)
                nm.flush()
                dma("pool", HT1[b][:, :, 0:n], hT[:, :, 0:n], r=[khT], w=["HT1"])
                for ti in range(ntl):
                    ts_ = slice(ti * 128, (ti + 1) * 128)
                    mmg(ps[0], [(hT[:, k, ts_], WA[:, k, :]) for k in range(8)], r=[khT, "wgt"], w=["ps0"])
                    rownorm((ps[0], "ps0"), 512, QNb, cqn, "cqn", tg)

                    def trq(e):
                        for k in range(4):
                            ins = e.transpose(out=psb[2][:, k * 128:(k + 1) * 128], in_=cqn[:, k * 128:(k + 1) * 128], identity=identb)
                        return ins
                    P.op("pe", trq, r=["cqn"], w=["ps2"])
                    P.op("act", (lambda e, ts_=ts_: e.copy(out=cqT[:, :, ts_], in_=psb[2][:, 0:512].rearrange("p (k n) -> p k n", k=4))),
                         r=["ps2"], w=["cqT"])
                    mmg(ps[1][:, 0:256], [(hT[:, k, ts_], WB[:, k, :]) for k in range(8)], r=[khT, "wgt"], w=["ps1"])
                    rownorm((ps[1], "ps1"), 256, KVNb, ckvn, "ckvn", tg)

                    def trk(e):
                        for k in range(2):
                            ins = e.transpose(out=psb[3][:, k * 128:(k + 1) * 128], in_=ckvn[:, k * 128:(k + 1) * 128], identity=identb)
                        return ins
                    P.op("pe", trk, r=["ckvn"], w=["ps3"])
                    P.op("act", (lambda e, ts_=ts_: e.copy(out=ckvT[:, :, ts_], in_=psb[3][:, 0:256].rearrange("p (k n) -> p k n", k=2))),
                         r=["ps3"], w=["ckvT"])
                for ti in range(ntl):
                    tt = t0 + ti
                    ts_ = slice(ti * 128, (ti + 1) * 128)
                    mmg(ps[0], [(ckvT[:, k, ts_], WVV[:, k, :]) for k in range(2)], r=["ckvT", "wgt"], w=["ps0"])
                    P.op("act", (lambda e, tt=tt: e.copy(out=VM[:, tt, :], in_=ps[0])), r=["ps0"], w=["VM", "ps0"])
                for h in range(4):
                    bk = 4 + (h % 2)
                    mmg(ps[bk][:, 0:n], [(WKN[:, k, h, :], ckvT[:, k, 0:n]) for k in range(2)], r=["ckvT", "wgt"], w=["ps%d" % bk])
                    P.op("act", (lambda e, h=h, bk=bk, n=n, col0=col0: e.copy(out=KN[:, h, col0:col0 + n], in_=ps[bk][:, 0:n])),
                         r=["ps%d" % bk], w=["KN", "ps%d" % bk])
                mmg(ps[4][:, 0:n], [(WKR[:, k, :], hT[:, k, 0:n]) for k in range(8)], r=[khT, "wgt"], w=["ps4"])
                rope_evict(4, n, col0, KR2[:, col0:col0 + n], "KR2", ro, b == 0)
                if b >= 1:
                    qn, qr = qnb[b % 2], qrb[b % 2]
                    for h in range(4):
                        bk = 4 + (h % 2)
                        mmg(ps[bk][:, 0:n], [(WQN[:, k, h, :], cqT[:, k, 0:n]) for k in range(4)], r=["cqT", "wgt"], w=["ps%d" % bk])
                        P.op("act", (lambda e, h=h, bk=bk, n=n, qn=qn: e.copy(out=qn[:, h, 0:n], in_=ps[bk][:, 0:n])),
                             r=["ps%d" % bk], w=["qnb0", "ps%d" % bk])
                    dma("pool", QT[b][:, :, 0:n], qn[:, :, 0:n], r=["qnb0"], w=["QT"])
                    for c in range(2):
                        bk = 4 + (c % 2)
                        mmg(ps[bk][:, 0:n], [(WQR[:, k, c * 128:(c + 1) * 128], cqT[:, k, 0:n]) for k in range(4)],
                            r=["cqT", "wgt"], w=["ps%d" % bk])
                        rope_evict(bk, n, col0, qr[:, c, 0:n], "qrb0", ro, False)
                    dma("pool", QR[b][:, :, 0:n], qr[:, :, 0:n], r=["qrb0"], w=["QR"])
            P.barrier()
            top[0] = keep
            if stop == "l1b1":
                return
            qnl = [T([128, 4, 512], BF16) for _ in range(3)]
            qrl = [T([128, 2, 512], BF16) for _ in range(3)]
            rsum = T([128, 512])
            mo = [T([128, 512], BF16) for _ in range(2)]
            state = {}

            def pre(b):
                if state.get("b") != b:
                    state["b"] = b
                    n = BLOCKS[b][1] * 128
                    dma("sp", qnl[b % 3][:, :, 0:n], QT[b][:, :, 0:n], r=["QT"], w=["qnl%d" % (b % 3)])
                    dma("sp", qrl[b % 3][:, :, 0:n], QR[b][:, :, 0:n], r=["QR"], w=["qnl%d" % (b % 3)])

            def post(b, h, n, acc, scr):
                col0 = BLOCKS[b][0] * 128
                m = mo[h % 2]
                km = "mo%d" % (h % 2)
                ob, sb2 = acc[0]
                P.op("dve", lambda e: e.reciprocal(out=rsum[:, 0:n], in_=ps[sb2][:, 0:n]), r=["ps%d" % sb2], w=["rsum", "ps%d" % sb2])
                P.op("dve", lambda e: e.tensor_tensor(out=m[:, 0:n], in0=ps[ob][:, 0:n], in1=rsum[:, 0:n], op=ALU.mult),
                     r=["ps%d" % ob, "rsum"], w=[km, "ps%d" % ob])
                dma("pool", MIXD[h][:, col0:col0 + n], m[:, 0:n], r=[km], w=["MIXD"])

            groups = []
            gi = 0
            for b in range(1, 9):
                n = 512
                for h in range(4):
                    pr = slice(64 * (h % 2), 64 * (h % 2) + 64)
                    for kp in range(NT // 2):
                        mem = []
                        for mi in range(2):
                            kt = 2 * kp + mi
                            ks = slice(kt * 128, (kt + 1) * 128)
                            mem.append(dict(qk=[(KN[:, h, ks], qnl[b % 3][:, h, 0:n]), (KR2[pr, ks], qrl[b % 3][pr, h // 2, 0:n])],
                                            v=VM[:, kt, h * 128:(h + 1) * 128], acc=0,
                                            start=(kp == 0 and mi == 0), stop=(kp == NT // 2 - 1 and mi == 1)))
                        g = dict(n=n, members=mem, rk=["KN", "KR2", "qnl%d" % (b % 3)], rv=["VM"], accset=gi % 2)
                        if h == 0 and kp == 0:
                            g["pre"] = (lambda b=b: (pre(b), pre(b + 1) if b + 1 < 9 else None))
                        if kp == NT // 2 - 1:
                            g["post"] = (lambda acc, scr, b=b, h=h, n=n: post(b, h, n, acc, scr))
                        groups.append(g)
                    gi += 1
            attention(groups, 192.0 ** -0.5, [[(4, 5)], [(6, 7)]])
            phase_reset()
            if stop == "l1b2":
                return
            LB = T([128, 2, 2, 4])
            lbv = T([128, 2, 4])
            oml = T([128, 2, 4])
            hgn = T([128, 1])
            ones1 = T([128, 1])
            for d in range(2):
                for ll in range(2):
                    dma_nc("sp", LB[:, d, ll, :], din["hgrn_lb"][d, ll].rearrange("(h p) -> p h", p=128), w=["LB"])
            dma_nc("sp", hgn, din["hgrn_norm"][0].rearrange("(p o) -> p o", o=1), w=["hgn"])
            P.op("dve", lambda e: e.tensor_tensor(out=lbv, in0=LB[:, :, 1, :], in1=LB[:, :, 0, :], op=ALU.subtract), r=["LB"], w=["lbv"])
            P.op("act", lambda e: e.activation(out=lbv, in_=lbv, func=AF.Sigmoid), r=["lbv"], w=["lbv"])
            P.op("dve", lambda e: e.tensor_scalar(out=oml, in0=lbv, scalar1=-1.0, scalar2=1.0, op0=ALU.mult, op1=ALU.add),
                 r=["lbv"], w=["oml"])
            P.op("pool", lambda e: e.memset(ones1, 1.0), w=["ones1"])
            WH = T([128, 8, 5, 128], BF16)
            hTl = [T([128, 8, 512], BF16)] * 2
            SG = T([128, NTOK], BF16)
            Vt = T([128, NT, 128], BF16)
            QP = [T([128, NTOK], BF16) for _ in range(2)]
            QPP = [T([128, NTOK], BF16) for _ in range(2)]
            KP = [T([128, NTOK], BF16) for _ in range(2)]
            KPt = [T([128, NT, 128], BF16) for _ in range(2)]
            EL = [T([128, 68]) for _ in range(2)]
            OTa = T([128, NTOK])
            qf = T([128, 512])
            sgm = [T([128, 512]) for _ in range(2)]
            kk = [T([128, 512]) for _ in range(2)]
            lf = [T([128, 512]) for _ in range(2)]
            gb = [T([128, 516]) for _ in range(2)]
            Da = [T([128, 512]) for _ in range(2)]
            Db = [T([128, 512]) for _ in range(2)]
            E1 = [T([128, 512]) for _ in range(2)]
            E2 = [T([128, 512]) for _ in range(2)]
            E3 = [T([128, 512]) for _ in range(2)]
            elt = [T([128, 8]) for _ in range(2)]
            Sf = [[T([128, 128]) for _ in range(2)] for _ in range(2)]
            Sb = [[T([128, 128], BF16) for _ in range(2)] for _ in range(2)]
            Am = [[T([128, 64], BF16) for _ in range(2)] for _ in range(2)]
            rstd, otmp = Db[0], E1[0]
            sq = T([128, 512], BF16)
            mh = [T([128, 512], BF16) for _ in range(2)]
            for d in range(2):
                P.op("pool", (lambda e, d=d: e.memset(gb[d][:, 0:1], 0.0)), w=["gbz"])
            wsrc = din["od_w_in"][0].rearrange("(k p) n -> p k n", p=128)
            for h in range(4):
                for i, c0_ in enumerate((832, 1344, 1856, 2368, 2880)):
                    wload(WH[:, :, i, :], wsrc[:, :, c0_ + h * 128:c0_ + (h + 1) * 128])
                P.op("pool", lambda e: e.memset(OTa, 0.0), w=["OT"])
                for b, (t0, ntl) in enumerate(BLOCKS):
                    n = ntl * 128
                    nch = n // 64
                    col0 = t0 * 128
                    ch0 = col0 // 64
                    hT = hTl[b % 2]
                    khT = "hTl0"
                    dma("sp", hT[:, :, 0:n], HT1[b][:, :, 0:n], r=["HT1"], w=[khT])
                    mmg(ps[0][:, 0:n], [(WH[:, k, 0, :], hT[:, k, 0:n]) for k in range(8)], r=[khT, "wgt"], w=["ps0"])
                    P.op("act", (lambda e, n=n: e.activation(out=qf[:, 0:n], in_=ps[0][:, 0:n], func=AF.Silu)), r=["ps0"], w=["qf", "ps0"])
                    mmg(ps[1][:, 0:n], [(WH[:, k, 4, :], hT[:, k, 0:n]) for k in range(8)], r=[khT, "wgt"], w=["ps1"])
                    P.op("act", (lambda e, n=n, col0=col0: e.activation(out=SG[:, col0:col0 + n], in_=ps[1][:, 0:n], func=AF.Silu)),
                         r=["ps1"], w=["SG", "ps1"])
                    for ti in range(ntl):
                        tt = t0 + ti
                        ts_ = slice(ti * 128, (ti + 1) * 128)
                        mmg(ps[2][:, 0:128], [(hT[:, k, ts_], WH[:, k, 3, :]) for k in range(8)], r=[khT, "wgt"], w=["ps2"])
                        P.op("act", (lambda e, tt=tt: e.copy(out=Vt[:, tt, :], in_=ps[2][:, 0:128])), r=["ps2"], w=["Vt", "ps2"])
                    Gi3 = [gb[d][:, 1:1 + n].rearrange("p (c j) -> p c j", j=64) for d in range(2)]
                    Gs3 = [gb[d][:, 0:n].rearrange("p (c j) -> p c j", j=64) for d in range(2)]
                    S0 = [Gs3[d][:, :, 0:1].to_broadcast([128, nch, 64]) for d in range(2)]
                    I63 = [Gi3[d][:, :, 63:64].to_broadcast([128, nch, 64]) for d in range(2)]
                    Da3 = [Da[d][:, 0:n].rearrange("p (c j) -> p c j", j=64) for d in range(2)]
                    Db3 = [Db[d][:, 0:n].rearrange("p (c j) -> p c j", j=64) for d in range(2)]
                    for d in range(2):
                        mmg(ps[3 + d][:, 0:n], [(WH[:, k, 1 + d, :], hT[:, k, 0:n]) for k in range(8)], r=[khT, "wgt"], w=["ps%d" % (3 + d)])
                    for d in range(2):
                        P.op("act", (lambda e, n=n, d=d: e.activation(out=sgm[d][:, 0:n], in_=ps[3 + d][:, 0:n], func=AF.Sigmoid)),
                             r=["ps%d" % (3 + d)], w=["sgm%d" % d, "ps%d" % (3 + d)])
                    for d in range(2):
                        P.op("dve", (lambda e, n=n, d=d, h=h: e.tensor_scalar(out=sgm[d][:, 0:n], in0=sgm[d][:, 0:n], scalar1=oml[:, d, h:h + 1],
                                                                             scalar2=lbv[:, d, h:h + 1], op0=ALU.mult, op1=ALU.add)),
                             r=["sgm%d" % d, "oml", "lbv"], w=["sgm%d" % d])
                    for d in range(2):
                        P.op("pool", (lambda e, n=n, d=d: e.tensor_scalar(out=kk[d][:, 0:n], in0=sgm[d][:, 0:n], scalar1=-1.0, scalar2=1.0,
                                                                         op0=ALU.mult, op1=ALU.add)), r=["sgm%d" % d], w=["kk%d" % d])
                        P.op("act", (lambda e, n=n, d=d: e.activation(out=lf[d][:, 0:n], in_=sgm[d][:, 0:n], func=AF.Ln)), r=["sgm%d" % d], w=["lf%d" % d])
                    for d in range(2):
                        P.op("dve", (lambda e, n=n, d=d: e.tensor_tensor_scan(out=gb[d][:, 1:1 + n], data0=ones1[:, 0:1].to_broadcast([128, n]),
                                                                             data1=lf[d][:, 0:n], initial=0.0, op0=ALU.mult, op1=ALU.add)),
                             r=["lf%d" % d, "ones1", "gbz"], w=["gb%dk" % d])
                    P.op("dve", (lambda e, a=Gi3[0], b_=S0[0], o=Da3[0]: e.tensor_tensor(out=o, in0=a, in1=b_, op=ALU.subtract)), r=["gb0k"], w=["Da0"])
                    P.op("dve", (lambda e, a=Gs3[1], b_=I63[1], o=Da3[1]: e.tensor_tensor(out=o, in0=a, in1=b_, op=ALU.subtract)), r=["gb1k"], w=["Da1"])
                    P.op("dve", (lambda e, a=Gi3[0], b_=I63[0], o=Db3[0]: e.tensor_tensor(out=o, in0=a, in1=b_, op=ALU.subtract)), r=["gb0k"], w=["Db0"])
                    P.op("dve", (lambda e, a=Gs3[1], b_=S0[1], o=Db3[1]: e.tensor_tensor(out=o, in0=a, in1=b_, op=ALU.subtract)), r=["gb1k"], w=["Db1"])
                    for d in range(2):
                        P.op("dve", (lambda e, d=d, nch=nch, a=Gi3[d], b_=Gs3[d]: e.tensor_tensor(out=elt[d][:, 0:nch], in0=a[:, :, 63], in1=b_[:, :, 0],
                                                                                                  op=ALU.subtract)), r=["gb%dk" % d], w=["elt%d" % d])
                    scs = ((1.0, -1.0, 1.0), (-1.0, 1.0, -1.0))
                    for d in range(2):
                        P.op("act", (lambda e, n=n, d=d: e.activation(out=E1[d][:, 0:n], in_=Da[d][:, 0:n], func=AF.Exp, scale=scs[d][0])), r=["Da%d" % d], w=["E1%d" % d])
                    for d in range(2):
                        P.op("act", (lambda e, n=n, d=d: e.activation(out=E2[d][:, 0:n], in_=Db[d][:, 0:n], func=AF.Exp, scale=scs[d][1])), r=["Db%d" % d], w=["E2%d" % d])
                    for d in range(2):
                        P.op("act", (lambda e, n=n, d=d: e.activation(out=E3[d][:, 0:n], in_=Db[d][:, 0:n], func=AF.Exp, scale=scs[d][2])), r=["Db%d" % d], w=["E3%d" % d])
                        P.op("act", (lambda e, d=d, ch0=ch0, nch=nch: e.activation(out=EL[d][:, ch0:ch0 + nch], in_=elt[d][:, 0:nch], func=AF.Exp)),
                             r=["elt%d" % d], w=["EL%d" % d])
                    for d in range(2):
                        P.op("dve", (lambda e, n=n, d=d, col0=col0: e.tensor_tensor(out=KP[d][:, col0:col0 + n], in0=kk[d][:, 0:n], in1=E2[d][:, 0:n], op=ALU.mult)),
                             r=["kk%d" % d, "E2%d" % d], w=["KP%d" % d])
                    for d in range(2):
                        P.op("dve", (lambda e, n=n, d=d, col0=col0: e.tensor_tensor(out=QP[d][:, col0:col0 + n], in0=qf[:, 0:n], in1=E1[d][:, 0:n], op=ALU.mult)),
                             r=["qf", "E1%d" % d], w=["QP%d" % d])
                        P.op("pool", (lambda e, n=n, d=d, col0=col0: e.tensor_tensor(out=QPP[d][:, col0:col0 + n], in0=qf[:, 0:n], in1=E3[d][:, 0:n], op=ALU.mult)),
                             r=["qf", "E3%d" % d], w=["QPP%d" % d])
                    for d in range(2):
                        def trp(e, d=d, t0=t0, ntl=ntl):
                            for ti in range(ntl):
                                ins = e.transpose(out=psb[6 + d][:, ti * 128:(ti + 1) * 128], in_=KP[d][:, (t0 + ti) * 128:(t0 + ti + 1) * 128], identity=identb)
                            return ins
                        P.op("pe", trp, r=["KP%d" % d], w=["ps%d" % (6 + d)])
                        P.op("act", (lambda e, d=d, t0=t0, ntl=ntl: e.copy(out=KPt[d][:, t0:t0 + ntl, :],
                                                                          in_=psb[6 + d][:, 0:ntl * 128].rearrange("p (t k) -> p t k", k=128))),
                             r=["ps%d" % (6 + d)], w=["KPt%d" % d, "ps%d" % (6 + d)])
                orders = [list(range(68)), [3, 2, 1, 0] + list(range(67, 3, -1))]
                UB = (2, 3)

                def cinfo(c):
                    tt, hh = c // 2, c % 2
                    rows = slice(64 * hh, 64 * hh + 64)
                    cols = slice(64 * c, 64 * c + 64)
                    if c < 4:
                        pos, blk_n, blk_c0 = c, 256, 0
                    else:
                        pos, blk_n, blk_c0 = (c - 4) % 8, 512, 256 + ((c - 4) // 8) * 512
                    return tt, rows, cols, pos, blk_n, blk_c0

                def emitAU(st_):
                    ub = UB[st_ % 2]
                    for d in range(2):
                        c = orders[d][st_]
                        tt, rows, cols, pos, blk_n, blk_c0 = cinfo(c)
                        ab = d
                        mmg(ps[ab][rows, 0:64], [(KP[d][:, cols], QPP[d][:, cols])], r=["KP%d" % d, "QPP%d" % d], w=["ps%d" % ab])
                    for d in range(2):
                        c = orders[d][st_]
                        tt, rows, cols, pos, blk_n, blk_c0 = cinfo(c)
                        ubk = ((2, 6), (3, 7))[d][st_ % 2]
                        mmg(ps[ubk][:, 0:128], [(KPt[d][rows, tt, :], Vt[rows, tt, :])], r=["KPt%d" % d, "Vt"], w=["ps%d" % ubk])

                def emitMask(st_):
                    for d in range(2):
                        c = orders[d][st_]
                        tt, rows, cols, pos, blk_n, blk_c0 = cinfo(c)
                        am = Am[d][st_ % 2]
                        mk = maskf if d == 0 else maskb
                        ab = d
                        pA, kA = ps[ab], "ps%d" % ab
                        P.op("dve", (lambda e, pA=pA, rows=rows, am=am, mk=mk, d=d: e.tensor_tensor(
                            out=am[rows, :], in0=pA[rows, 0:64], in1=mk[rows, :], op=ALU.mult)),
                            r=[kA], w=["Am%d_%d" % (d, st_ % 2), kA])

                def emitO(st_):
                    for d in range(2):
                        c = orders[d][st_]
                        tt, rows, cols, pos, blk_n, blk_c0 = cinfo(c)
                        am = Am[d][st_ % 2]
                        prs = [(Vt[rows, tt, :], am[rows, :])]
                        rr = ["Vt", "Am%d_%d" % (d, st_ % 2)]
                        if st_ > 0:
                            so = (st_ - 1) % 2
                            prs.append((Sb[d][so], QP[d][:, cols]))
                            rr += ["Sb%d_%d" % (d, so), "QP%d" % d]
                        mmg(ps[4 + d][:, pos * 64:(pos + 1) * 64], prs, r=rr, w=["ps%d" % (4 + d)])

                def emitUpd(st_):
                    ub = UB[st_ % 2]
                    kU = "ps%d" % ub
                    sn, so = st_ % 2, (st_ - 1) % 2
                    for d in range(2):
                        c = orders[d][st_]
                        tt, rows, cols, pos, blk_n, blk_c0 = cinfo(c)
                        ubk = ((2, 6), (3, 7))[d][st_ % 2]
                        pU = ps[ubk][:, 0:128]
                        kU = "ps%d" % ubk
                        if st_ == 0:
                            P.op("dve", (lambda e, d=d, pU=pU: e.tensor_copy(out=Sf[d][sn], in_=pU)),
                                 r=[kU], w=["Sf%d_%d" % (d, sn), kU])
                        else:
                            P.op("dve", (lambda e, d=d, pU=pU, c=c: e.scalar_tensor_tensor(
                                out=Sf[d][sn], in0=Sf[d][so], scalar=EL[d][:, c:c + 1], in1=pU, op0=ALU.mult, op1=ALU.add)),
                                r=[kU, "Sf%d_%d" % (d, so), "EL%d" % d], w=["Sf%d_%d" % (d, sn), kU])
                        P.op("act", (lambda e, d=d: e.copy(out=Sb[d][sn], in_=Sf[d][sn])), r=["Sf%d_%d" % (d, sn)], w=["Sb%d_%d" % (d, sn)])
                        last_in_blk = (pos == (blk_n // 64 - 1)) if d == 0 else (pos == 0)
                        if last_in_blk:
                            P.op("dve", (lambda e, d=d, blk_n=blk_n, blk_c0=blk_c0: e.tensor_tensor(
                                out=OTa[:, blk_c0:blk_c0 + blk_n], in0=ps[4 + d][:, 0:blk_n], in1=OTa[:, blk_c0:blk_c0 + blk_n], op=ALU.add)),
                                 r=["ps%d" % (4 + d), "OT"], w=["OT", "ps%d" % (4 + d)])

                LOOK = 1
                if LOOK:
                    emitAU(0)
                    emitMask(0)
                for st_ in range(68):
                    if LOOK:
                        if st_ + 1 < 68:
                            emitAU(st_ + 1)
                            emitMask(st_ + 1)
                    else:
                        emitAU(st_)
                        emitMask(st_)
                    emitO(st_)
                    emitUpd(st_)
                for b in range(1, 9):
                    n = 512
                    col0 = BLOCKS[b][0] * 128
                    cs = slice(col0, col0 + n)
                    m = mh[b % 2]
                    km = "mh%d" % (b % 2)
                    P.op("act", (lambda e, cs=cs: e.activation(out=sq, in_=OTa[:, cs], func=AF.Square)), r=["OT"], w=["sq"])
                    mmg(ps[7], [(onesb, sq)], r=["sq"], w=["ps7"])
                    P.op("act", lambda e: e.activation(out=rstd, in_=ps[7], func=AF.Sqrt, scale=1.0 / 128, bias=EPS), r=["ps7"], w=["Db0", "ps7"])
                    P.op("dve", lambda e: e.reciprocal(out=rstd, in_=rstd), r=["Db0"], w=["Db0"])
                    P.op("dve", (lambda e, cs=cs: e.scalar_tensor_tensor(out=otmp, in0=OTa[:, cs], scalar=hgn, in1=rstd, op0=ALU.mult, op1=ALU.mult)),
                         r=["OT", "Db0", "hgn"], w=["E10"])
                    P.op("dve", (lambda e, cs=cs, m=m: e.tensor_tensor(out=m, in0=otmp, in1=SG[:, cs], op=ALU.mult)), r=["E10", "SG"], w=[km])
                    dma("pool", MIXD[4 + h][:, cs], m, r=[km], w=["MIXD"])
            phase_reset()
            if stop == "l1b3":
                return
            phase_outproj(1, list(range(1, 9)), False)

        def phase_outproj(l, blks, from_inputs):
            wo = T([128, 8, D], BF16)
            wosrc = din["mix_w_out"][l].rearrange("(k p) n -> p k n", p=128)
            for c in range(2):
                wload(wo[:, :, c * 512:(c + 1) * 512], wosrc[:, :, c * 512:(c + 1) * 512])
            res = RES(l, 2)
            mix = [T([128, 8, 512], BF16) for _ in range(2)]
            ypairs = [(0, 1), (2, 3), (4, 5), (6, 7)]
            yi = 0
            for b in blks:
                t0, ntl = BLOCKS[b]
                n = ntl * 128
                col0 = t0 * 128
                mx = mix[b % 2]
                kx = "mix%d" % (b % 2)
                for k in range(8):
                    dma("sp", mx[:, k, 0:n], MIXD[k][:, col0:col0 + n], r=["MIXD"], w=[kx])
                for ti in range(ntl):
                    tt = t0 + ti
                    ls = slice(ti * 128, (ti + 1) * 128)
                    y0, y1 = ypairs[yi % 4]
                    yi += 1
                    for hf, yb in ((0, y0), (1, y1)):
                        mmg(ps[yb], [(mx[:, k, ls], wo[:, k, hf * 512:(hf + 1) * 512]) for k in range(8)],
                            r=[kx, "wgt"], w=["ps%d" % yb])
                    res.run(y0, y1, x_src(from_inputs, tt), XR[tt * 128:(tt + 1) * 128, :], tt < 2, "XR1")

        phase_mod()
        phase_reset()
        S0 = ("mod", "l0a1", "l0a2", "l0mix")
        S1 = S0 + ("l0", "l1b1", "l1b2", "l1b3", "l1mix")
        if stop != "mod":
            phase_l0()
            phase_reset()
        if stop not in S0:
            phase_ffn(0, True, False)
            phase_reset()
        if stop not in S0 + ("l0",):
            phase_l1()
            phase_reset()
        if stop not in S1:
            phase_ffn(1, False, True)
            phase_reset()
        P.emit(st)
    return nc


_CACHE = {}


def kernel(**inputs):
    consts = _consts()
    if "nc" not in _CACHE:
        _CACHE["nc"] = build()
    nc = _CACHE["nc"]
    in_maps = []
    for b in range(8):
        m = {"x": np.ascontiguousarray(inputs["x"][b]), "ctx": np.ascontiguousarray(inputs["ctx"][b]),
             "cvec": np.ascontiguousarray(np.stack([inputs["c"][b], inputs["c_ctx"]], 0))}
        for n in W_NAMES:
            m[n] = np.ascontiguousarray(inputs[n])
        m.update(consts)
        in_maps.append(m)
    res = run_bass_kernel_spmd(nc, in_maps, core_ids=list(range(8)))
    return np.stack([r["out"] for r in res.results], 0).astype(np.float32)
```

```python
import numpy as np
from contextlib import ExitStack
import concourse.bass as bass
import concourse.mybir as mybir
from concourse.bass_utils import run_bass_kernel_spmd

F32 = mybir.dt.float32
BF16 = mybir.dt.bfloat16
ALU = mybir.AluOpType
AF = mybir.ActivationFunctionType

NDSEM = 8
D = 1024
NT = 34
NTOK = 4352
DFF = 2816
NJ = 22
EPS = 1e-6


class Prog:
    ENGS = ("pe", "dve", "act", "pool", "sp")

    def __init__(self, nc):
        self.nc = nc
        self.ops = []
        self.lw = {}
        self.rd = {}
        self.cnt = {e: 0 for e in self.ENGS}
        self.dcnt = {e: 0 for e in self.ENGS}
        self.dslot_last = {e: [None] * NDSEM for e in self.ENGS}
        self.last_nd = {e: None for e in self.ENGS}
        self.pending_bar = {e: [] for e in self.ENGS}

    def op(self, eng, fn, r=(), w=(), dma=False):
        oid = len(self.ops)
        deps = []
        for k in r:
            y = self.lw.get(k)
            if y is not None:
                deps.append((y, "RAW"))
        for k in w:
            y = self.lw.get(k)
            if y is not None:
                deps.append((y, "WAW"))
            for y in self.rd.get(k, ()):
                deps.append((y, "WAR"))
        for y in self.pending_bar[eng]:
            deps.append((y, "RAW"))
        self.pending_bar[eng] = []
        o = dict(id=oid, eng=eng, fn=fn, deps=deps, dma=dma)
        if dma:
            i = self.dcnt[eng]
            self.dcnt[eng] += 1
            slot = i % NDSEM
            o["dslot"] = slot
            o["dval"] = 16 * (i // NDSEM + 1)
            prev = self.dslot_last[eng][slot]
            if prev is not None:
                deps.append((prev, "RAW"))
            self.dslot_last[eng][slot] = oid
        else:
            self.cnt[eng] += 1
            o["val"] = self.cnt[eng]
            self.last_nd[eng] = oid
        self.ops.append(o)
        for k in w:
            self.lw[k] = oid
            self.rd[k] = []
        for k in r:
            if k not in w:
                self.rd.setdefault(k, []).append(oid)
        return oid

    def barrier(self):
        snap = []
        for e in self.ENGS:
            if self.last_nd[e] is not None:
                snap.append(self.last_nd[e])
            for y in self.dslot_last[e]:
                if y is not None:
                    snap.append(y)
        for e in self.ENGS:
            self.pending_bar[e] = list(snap)
        self.lw = {}
        self.rd = {}

    def emit(self, st):
        nc = self.nc
        sems = {e: st.enter_context(nc.semaphore("s_" + e)) for e in self.ENGS}
        dsems = {e: [st.enter_context(nc.semaphore("d_%s%d" % (e, i))) for i in range(NDSEM)]
                 for e in ("sp", "pool", "act") if self.dcnt[e] > 0}
        block = st.enter_context(nc.Block())
        ops = self.ops

        def run(ename, eng):
            seen = {}
            for o in ops:
                if o["eng"] != ename:
                    continue
                need = {}
                for (y, kind) in o["deps"]:
                    Y = ops[y]
                    if Y["dma"]:
                        key = ("d", Y["eng"], Y["dslot"])
                        sem = dsems[Y["eng"]][Y["dslot"]]
                        val = Y["dval"]
                    else:
                        if Y["eng"] == ename and not o["dma"]:
                            if ename == "pe":
                                continue
                        key = ("c", Y["eng"])
                        sem = sems[Y["eng"]]
                        val = Y["val"]
                    if seen.get(key, 0) >= val:
                        continue
                    if key not in need or need[key][1] < val:
                        need[key] = (sem, val)
                for key, (sem, val) in need.items():
                    eng.wait_ge(sem, val)
                    seen[key] = val
                ins = o["fn"](eng)
                if o["dma"]:
                    ins.then_inc(dsems[ename][o["dslot"]], 16)
                else:
                    ins.then_inc(sems[ename], 1)
            if ename in dsems:
                for slot in range(NDSEM):
                    y = self.dslot_last[ename][slot]
                    if y is not None:
                        Y = ops[y]
                        if seen.get(("d", ename, slot), 0) < Y["dval"]:
                            eng.wait_ge(dsems[ename][slot], Y["dval"])

        @block.tensor
        def _(e):
            run("pe", e)

        @block.vector
        def _(e):
            run("dve", e)

        @block.scalar
        def _(e):
            run("act", e)

        @block.gpsimd
        def _(e):
            run("pool", e)

        @block.sync
        def _(e):
            run("sp", e)


PW = 8 + 256 + 16 + 4096 + 8
PC0, PL0 = 8, 280


def _consts():
    c = {}
    c["ident"] = np.eye(128, dtype=np.float32)
    rot = np.zeros((128, 128), np.float32)
    for d in range(128):
        rot[d ^ 16, d] = 1.0
    c["rot"] = rot
    n = 4096
    pos_row = np.repeat(np.arange(n // 64), 64)
    pos_col = np.tile(np.arange(64), n // 64)
    inv_freq = (10000.0 ** (-np.arange(0, 32, 2, dtype=np.float32) / 32)).astype(np.float32)
    ang = np.stack([pos_row, pos_col], -1).astype(np.float32)[..., None] * inv_freq
    cs, sn = np.cos(ang).astype(np.float32), np.sin(ang).astype(np.float32)
    cos_t = np.zeros((128, n), np.float32)
    sin_t = np.zeros((128, n), np.float32)
    for d in range(128):
        dd = d % 64
        a, hf, i = dd // 32, (dd // 16) % 2, dd % 16
        cos_t[d] = cs[:, a, i]
        sin_t[d] = sn[:, a, i] * (-1.0 if hf == 0 else 1.0)
    c["cos_t"] = cos_t
    c["sin_t"] = sin_t
    inv = np.zeros((4, PW), np.float32)
    for g, w in enumerate((2, 4, 8, 16)):
        h = w // 2
        for (n_, off) in ((256, PC0), (4096, PL0)):
            t = np.arange(n_)
            lo = np.clip(t - h, 0, n_)
            hi = np.clip(t + h, 0, n_)
            inv[g, off:off + n_] = 1.0 / (hi - lo).astype(np.float32)
    c["invcnt"] = inv
    p = np.arange(128)[:, None] % 64
    t = np.arange(64)[None, :]
    c["mask_f"] = (p <= t).astype(np.float32)
    c["mask_b"] = (p >= t).astype(np.float32)
    return c


W_NAMES = ["ada_w", "ada_b", "norm_g", "mix_w_out", "ffn_w_gate", "ffn_w_up", "ffn_conv_w",
           "ffn_conv_b", "ffn_w_down", "ev_w_in", "pool_w", "pool_scale", "diff_lambda",
           "diff_subln", "od_w_in", "mla_q_norm", "mla_w_uq", "mla_kv_norm", "mla_w_ukv",
           "hgrn_norm", "hgrn_lb"]
W_SHAPES = {"ada_w": (2, 1024, 6144), "ada_b": (2, 6144), "norm_g": (2, 4, 1024),
            "mix_w_out": (2, 1024, 1024), "ffn_w_gate": (2, 1024, 2816), "ffn_w_up": (2, 1024, 2816),
            "ffn_conv_w": (2, 3, 2816), "ffn_conv_b": (2, 2816), "ffn_w_down": (2, 2816, 1024),
            "ev_w_in": (1, 1024, 2048), "pool_w": (1, 4, 128, 128), "pool_scale": (1, 512),
            "diff_lambda": (1, 4, 64), "diff_subln": (1, 128), "od_w_in": (1, 1024, 3392),
            "mla_q_norm": (1, 512), "mla_w_uq": (1, 512, 768), "mla_kv_norm": (1, 256),
            "mla_w_ukv": (1, 256, 1024), "hgrn_norm": (1, 128), "hgrn_lb": (2, 2, 512)}
C_SHAPES = {"ident": (128, 128), "rot": (128, 128), "cos_t": (128, 4096), "sin_t": (128, 4096),
            "invcnt": (4, PW), "mask_f": (128, 64), "mask_b": (128, 64)}

BLOCKS = [(0, 2)] + [(2 + 4 * i, 4) for i in range(8)]


def build(stop=None, dbg=False):
    nc = bass.Bass("TRN2", target_bir_lowering=False)
    din = {}
    din["x"] = nc.dram_tensor("x", [4096, D], F32, kind="ExternalInput").ap()
    din["ctx"] = nc.dram_tensor("ctx", [256, D], F32, kind="ExternalInput").ap()
    din["cvec"] = nc.dram_tensor("cvec", [2, D], F32, kind="ExternalInput").ap()
    for n in W_NAMES:
        din[n] = nc.dram_tensor(n, list(W_SHAPES[n]), F32, kind="ExternalInput").ap()
    for n in C_SHAPES:
        din[n] = nc.dram_tensor(n, list(C_SHAPES[n]), F32, kind="ExternalInput").ap()
    out = nc.dram_tensor("out", [4096, D], F32, kind="ExternalOutput").ap()
    XR = nc.dram_tensor("XR", [NTOK, D], F32, kind="ExternalOutput" if dbg else "Internal").ap()
    MODV = nc.dram_tensor("MODV", [2, 2, 6, D], F32, kind="Internal").ap()
    QT = nc.dram_tensor("QT", [9, 128, 4, 512], BF16, kind="Internal").ap()
    QR = nc.dram_tensor("QR", [9, 128, 2, 512], BF16, kind="Internal").ap()
    UT = nc.dram_tensor("UT", [4, 128, NTOK], F32, kind="Internal").ap()
    HT1 = nc.dram_tensor("HT1", [9, 128, 8, 512], BF16, kind="Internal").ap()
    H2D = nc.dram_tensor("H2D", [128, 8, 4355], BF16, kind="Internal").ap()
    MIXD = nc.dram_tensor("MIXD", [8, 128, NTOK], BF16, kind="ExternalOutput" if dbg else "Internal").ap()

    st = ExitStack()
    with st:
        P = Prog(nc)
        AW = 52000
        arena = st.enter_context(nc.sbuf_tensor("arena", [128, AW], F32))
        psall = st.enter_context(nc.psum_tensor("psall", [128, 4096], F32))[:]
        ps = [psall[:, i * 512:(i + 1) * 512] for i in range(8)]
        psb = [p.bitcast(BF16) for p in ps]
        top = [0]

        def T(shape, dt=F32):
            n = int(np.prod(shape[1:]))
            cols = n if dt == F32 else (n + 1) // 2
            off = top[0]
            top[0] += cols
            assert top[0] <= AW, "SBUF arena overflow %d" % top[0]
            a = arena[0:shape[0], off:off + cols]
            if dt != F32:
                a = a.bitcast(dt)
            if len(shape) == 3:
                a = a.rearrange("p (a b) -> p a b", a=shape[1])
            elif len(shape) == 4:
                a = a.rearrange("p (a b c) -> p a b c", a=shape[1], b=shape[2])
            return a

        uid = [0]

        def K(s):
            uid[0] += 1
            return "%s#%d" % (s, uid[0])

        def dma(q, o, i, r=(), w=()):
            P.op(q, lambda e: e.dma_start(out=o, in_=i), r=r, w=w, dma=True)

        def dma_nc(q, o, i, r=(), w=()):
            P.op(q, lambda e: e.dma_start(out=o, in_=i, allow_slow_non_contiguous=True), r=r, w=w, dma=True)

        def mmg(o, pairs, r, w):
            def f(e):
                n = len(pairs)
                for i, (l, rh) in enumerate(pairs):
                    ins = e.matmul(o, lhsT=l, rhs=rh, start=(i == 0), stop=(i == n - 1))
                return ins
            P.op("pe", f, r=r, w=w)

        identb = T([128, 128], BF16)
        rotb = T([128, 128], BF16)
        onesb = T([128, 128], BF16)
        maskf = T([128, 64])
        maskb = T([128, 64])
        stg = [T([128, 2048]) for _ in range(2)]
        stgi = [0]
        PERSIST = None

        def wload(dst, src, q=None, ce="pool"):
            i = stgi[0] % 2
            stgi[0] += 1
            shp = list(dst.shape)
            n = int(np.prod(shp[1:]))
            if n > 2048:
                hh = shp[-1] // 2
                if len(shp) == 2:
                    wload(dst[:, 0:hh], src[:, 0:hh], q, ce)
                    wload(dst[:, hh:], src[:, hh:], q, ce)
                else:
                    wload(dst[:, :, 0:hh], src[:, :, 0:hh], q, ce)
                    wload(dst[:, :, hh:], src[:, :, hh:], q, ce)
                return
            s = stg[i][0:shp[0], 0:n]
            if len(shp) == 3:
                s = s.rearrange("p (a b) -> p a b", a=shp[1])
            qq = q or ("sp" if i == 0 else "pool")
            dma(qq, s, src, w=["stg%d" % i])
            kd = "W" + str(id(dst))
            if ce == "pool":
                P.op("pool", lambda e: e.tensor_copy(out=dst, in_=s), r=["stg%d" % i], w=["wgt"])
            elif ce == "dve":
                P.op("dve", lambda e: e.tensor_copy(out=dst, in_=s), r=["stg%d" % i], w=["wgt"])
            else:
                P.op("act", lambda e: e.copy(out=dst, in_=s), r=["stg%d" % i], w=["wgt"])

        for (dst, nm) in ((identb, "ident"), (rotb, "rot")):
            wload(dst, din[nm])
        P.op("pool", lambda e: e.memset(onesb, 1.0), w=["wgt"])
        dma("sp", maskf, din["mask_f"], w=["wgt"])
        dma("sp", maskb, din["mask_b"], w=["wgt"])
        PERSIST = top[0]

        def phase_reset():
            P.barrier()
            top[0] = PERSIST

        def phase_mod():
            cv = T([128, 2, 8])
            cvs = T([128, 2, 8])
            cvb = T([128, 8, 2], BF16)
            for j in range(2):
                dma_nc("sp", cv[:, j, :], din["cvec"][j].rearrange("(k p) -> p k", p=128), w=["cv"])
            P.op("act", lambda e: e.activation(out=cvs, in_=cv, func=AF.Silu), r=["cv"], w=["cvs"])
            P.op("dve", lambda e: e.tensor_copy(out=cvb, in_=cvs.rearrange("p j k -> p k j")), r=["cvs"], w=["cvb"])
            awb = [T([128, 8, 256], BF16) for _ in range(2)]
            Mt = T([2, 6 * D])
            bt = T([2, 6 * D])
            ng = T([2, 4, D])
            V = T([2, 6, D])
            for l in range(2):
                dma("sp", bt, din["ada_b"][l].partition_broadcast(2), w=["bt"])
                dma("sp", ng.rearrange("p a b -> p (a b)"),
                    din["norm_g"][l].rearrange("a b -> (a b)").partition_broadcast(2), w=["ng"])
                for nb in range(24):
                    ab = awb[nb % 2]
                    kab = "awb%d" % (nb % 2)
                    src = din["ada_w"][l].rearrange("(k p) n -> p k n", p=128)[:, :, nb * 256:(nb + 1) * 256]
                    i = stgi[0] % 2
                    stgi[0] += 1
                    s = stg[i][:, :].rearrange("p (a b) -> p a b", a=8)
                    dma("sp" if i == 0 else "pool", s, src, w=["stg%d" % i])
                    if nb % 3 == 2:
                        P.op("dve", (lambda e, ab=ab, s=s: e.tensor_copy(out=ab, in_=s)), r=["stg%d" % i], w=[kab])
                    else:
                        P.op("act", (lambda e, ab=ab, s=s: e.copy(out=ab, in_=s)), r=["stg%d" % i], w=[kab])
                    pb = ps[nb % 2][0:2, 0:256]
                    mmg(pb, [(cvb[:, k, :], ab[:, k, :]) for k in range(8)], r=["cvb", kab], w=["ps%d" % (nb % 2)])
                    P.op("dve", (lambda e, pb=pb, nb=nb: e.tensor_tensor(out=Mt[:, nb * 256:(nb + 1) * 256], in0=pb,
                                                                       in1=bt[:, nb * 256:(nb + 1) * 256], op=ALU.add)),
                         r=["ps%d" % (nb % 2), "bt"], w=["Mt"])
                sl = lambda i: Mt[:, i * D:(i + 1) * D]
                P.op("dve", lambda e: e.scalar_tensor_tensor(out=V[:, 0, :], in0=sl(1), scalar=1.0, in1=ng[:, 0, :],
                                                             op0=ALU.add, op1=ALU.mult), r=["Mt", "ng"], w=["V"])
                P.op("dve", lambda e: e.tensor_copy(out=V[:, 1, :], in_=sl(0)), r=["Mt"], w=["V"])
                P.op("dve", lambda e: e.tensor_tensor(out=V[:, 2, :], in0=sl(2), in1=ng[:, 1, :], op=ALU.mult),
                     r=["Mt", "ng"], w=["V"])
                P.op("dve", lambda e: e.scalar_tensor_tensor(out=V[:, 3, :], in0=sl(4), scalar=1.0, in1=ng[:, 2, :],
                                                             op0=ALU.add, op1=ALU.mult), r=["Mt", "ng"], w=["V"])
                P.op("dve", lambda e: e.tensor_copy(out=V[:, 4, :], in_=sl(3)), r=["Mt"], w=["V"])
                P.op("dve", lambda e: e.tensor_tensor(out=V[:, 5, :], in0=sl(5), in1=ng[:, 3, :], op=ALU.mult),
                     r=["Mt", "ng"], w=["V"])
                dma("sp", MODV[l].rearrange("j a b -> j (a b)"), V.rearrange("p a b -> p (a b)"), r=["V"], w=["MODV"])

        def x_src(layer0_in, tt):
            if layer0_in:
                return din["ctx"][tt * 128:(tt + 1) * 128, :] if tt < 2 else din["x"][(tt - 2) * 128:(tt - 1) * 128, :]
            return XR[tt * 128:(tt + 1) * 128, :]

        class NM:
            def __init__(self, l, gi, si):
                self.xt = [T([128, D]) for _ in range(2)]
                self.junk = T([128, D], BF16)
                self.tmp = [T([128, D]) for _ in range(2)]
                self.pending = None
                self.hb = [T([128, D], BF16) for _ in range(2)]
                self.ss = T([128, 2])
                self.rs = T([128, 2])
                self.G = [T([128, D]) for _ in range(2)]
                self.SH = [T([128, D]) for _ in range(2)]
                for j in range(2):
                    dma("sp", self.G[j], MODV[l, j, gi].partition_broadcast(128), r=["MODV"], w=["nmG"])
                    dma("sp", self.SH[j], MODV[l, j, si].partition_broadcast(128), r=["MODV"], w=["nmG"])
                self.i = 0

            def run(self, src, is_ctx, dst, dkey, bank):
                i = self.i % 2
                self.i += 1
                xt, hb = self.xt[i], self.hb[i]
                kx, kh = "nm_xt%d" % i, "nm_hb%d" % i
                ss, rs = self.ss[:, i:i + 1], self.rs[:, i:i + 1]
                G, SH = self.G[1 if is_ctx else 0], self.SH[1 if is_ctx else 0]
                tmp = self.tmp[i]
                dma("sp", xt, src, r=["XR"], w=[kx])
                P.op("act", lambda e: e.activation(out=self.junk, in_=xt, func=AF.Square, accum_out=ss),
                     r=[kx], w=["nm_junk", "nm_ss%d" % i])
                P.op("act", lambda e: e.activation(out=rs, in_=ss, func=AF.Sqrt, scale=1.0 / D, bias=EPS),
                     r=["nm_ss%d" % i], w=["nm_rs%d" % i])
                P.op("dve", lambda e: e.reciprocal(out=rs, in_=rs), r=["nm_rs%d" % i], w=["nm_rs%d" % i])
                P.op("dve", lambda e: e.scalar_tensor_tensor(out=tmp, in0=xt, scalar=rs, in1=G, op0=ALU.mult,
                                                             op1=ALU.mult), r=[kx, "nm_rs%d" % i, "nmG"], w=["nm_tmp%d" % i])
                P.op("pool", lambda e: e.tensor_tensor(out=hb, in0=tmp, in1=SH, op=ALU.add),
                     r=["nm_tmp%d" % i, "nmG"], w=[kh])
                prev = self.pending
                self.pending = (hb, kh, dst, dkey, bank)
                if prev is not None:
                    self.stage_b(*prev)

            def stage_b(self, hb, kh, dst, dkey, bank):
                pb = psb[bank]

                def tr(e):
                    for k in range(8):
                        ins = e.transpose(out=pb[:, k * 128:(k + 1) * 128], in_=hb[:, k * 128:(k + 1) * 128],
                                          identity=identb)
                    return ins
                P.op("pe", tr, r=[kh], w=["ps%d" % bank])
                P.op("act", lambda e: e.copy(out=dst, in_=pb.rearrange("p (k n) -> p k n", k=8)),
                     r=["ps%d" % bank], w=[dkey])

            def flush(self):
                if self.pending is not None:
                    self.stage_b(*self.pending)
                    self.pending = None

        class RES:
            def __init__(self, l, gidx):
                self.GT = [T([128, D]) for _ in range(2)]
                for j in range(2):
                    dma("sp", self.GT[j], MODV[l, j, gidx].partition_broadcast(128), r=["MODV"], w=["resG"])
                self.xo = [T([128, D]) for _ in range(2)]
                self.tt_ = [T([128, D]) for _ in range(2)]
                self.junk = T([128, 512], BF16)
                self.ss = T([128, 4])
                self.rs = T([128, 2])
                self.i = 0

            def run(self, b0, b1, src, dstd, is_ctx, wkey):
                i = self.i % 2
                self.i += 1
                xo = self.xo[i]
                tbuf = self.tt_[i]
                kt_ = "res_t%d" % i
                kx = "res_x%d" % i
                ss = self.ss[:, 2 * i:2 * i + 2]
                rs = self.rs[:, i:i + 1]
                GT = self.GT[1 if is_ctx else 0]
                dma("pool", xo, src, r=["XR"], w=[kx])
                P.op("act", lambda e: e.activation(out=self.junk, in_=ps[b0], func=AF.Square, accum_out=ss[:, 0:1]),
                     r=["ps%d" % b0], w=["res_junk", "res_ss%d" % i])
                P.op("act", lambda e: e.activation(out=self.junk, in_=ps[b1], func=AF.Square, accum_out=ss[:, 1:2]),
                     r=["ps%d" % b1], w=["res_junk", "res_ss%d" % i])
                P.op("dve", lambda e: e.tensor_tensor(out=rs, in0=ss[:, 0:1], in1=ss[:, 1:2], op=ALU.add),
                     r=["res_ss%d" % i], w=["res_rs%d" % i])
                P.op("act", lambda e: e.activation(out=rs, in_=rs, func=AF.Sqrt, scale=1.0 / D, bias=EPS),
                     r=["res_rs%d" % i], w=["res_rs%d" % i])
                P.op("dve", lambda e: e.reciprocal(out=rs, in_=rs), r=["res_rs%d" % i], w=["res_rs%d" % i])
                for hf, bk in ((0, b0), (1, b1)):
                    P.op("dve", (lambda e, hf=hf, bk=bk: e.scalar_tensor_tensor(
                        out=tbuf[:, hf * 512:(hf + 1) * 512], in0=ps[bk], scalar=rs,
                        in1=GT[:, hf * 512:(hf + 1) * 512], op0=ALU.mult, op1=ALU.mult)),
                        r=["ps%d" % bk, "res_rs%d" % i, "resG"], w=[kt_, "ps%d" % bk])
                P.op("pool", lambda e: e.tensor_tensor(out=xo, in0=tbuf, in1=xo, op=ALU.add),
                     r=[kt_, kx], w=[kx])
                dma("pool", dstd, xo, r=[kx], w=[wkey])

        def rope_evict(pbank, n, t0, dst, dkey, ro, is_ctx):
            src = ps[pbank][:, 0:n]
            if is_ctx:
                P.op("act", lambda e: e.copy(out=dst, in_=src), r=["ps%d" % pbank], w=[dkey])
                return
            qs, t1, t2, cosb, sinb, rb = ro
            P.op("act", lambda e: e.copy(out=qs[:, 0:n], in_=src), r=["ps%d" % pbank], w=["ro_qs"])
            mmg(ps[rb][:, 0:n], [(rotb, qs[:, 0:n])], r=["ro_qs"], w=["ps%d" % rb])
            P.op("dve", lambda e: e.tensor_tensor(out=t1[:, 0:n], in0=src, in1=cosb[:, 0:n], op=ALU.mult),
                 r=["ps%d" % pbank, "ro_cs"], w=["ro_t1", "ps%d" % pbank])
            P.op("dve", lambda e: e.tensor_tensor(out=t2[:, 0:n], in0=ps[rb][:, 0:n], in1=sinb[:, 0:n], op=ALU.mult),
                 r=["ps%d" % rb, "ro_cs"], w=["ro_t2", "ps%d" % rb])
            P.op("pool", lambda e: e.tensor_tensor(out=dst, in0=t1[:, 0:n], in1=t2[:, 0:n], op=ALU.add),
                 r=["ro_t1", "ro_t2"], w=[dkey])

        def rope_tiles():
            return (T([128, 512], BF16), T([128, 512]), T([128, 512]), T([128, 512]), T([128, 512]))

        def rope_load(ro, b):
            if b == 0:
                return
            c0 = (b - 1) * 512
            dma("pool", ro[3], din["cos_t"][:, c0:c0 + 512], w=["ro_cs"])
            dma("pool", ro[4], din["sin_t"][:, c0:c0 + 512], w=["ro_cs"])

        def attention(groups, scale, accsets):
            PT = [T([128, 2, 512], BF16) for _ in range(3)]
            N = len(groups)

            def emitS(i):
                g = groups[i]
                if g.get("pre"):
                    g["pre"]()
                A = 2 * (i % 2)
                n = g["n"]
                for mi, m in enumerate(g["members"]):
                    mmg(ps[A + mi][:, 0:n], m["qk"], r=g["rk"], w=["ps%d" % (A + mi)])

            def emitE(i):
                g = groups[i]
                A = 2 * (i % 2)
                n = g["n"]
                nm_ = len(g["members"])
                pt = PT[i % 3]
                src = psall[:, A * 512:(A + 2) * 512].rearrange("p (j n) -> p j n", j=2)[:, 0:nm_, 0:n]
                P.op("act", (lambda e: e.activation(out=pt[:, 0:nm_, 0:n], in_=src, func=AF.Exp, scale=scale)),
                     r=["ps%d" % A, "ps%d" % (A + 1)], w=["PT%d" % (i % 3), "ps%d" % A, "ps%d" % (A + 1)])

            def emitPV(i):
                g = groups[i]
                n = g["n"]
                pt = PT[i % 3]
                acc = accsets[g["accset"]]
                wk = []
                for m in g["members"]:
                    ob, sb2 = acc[m["acc"]]
                    wk += ["ps%d" % ob, "ps%d" % sb2]

                def pv(e):
                    for mi, m in enumerate(g["members"]):
                        ob, sb2 = acc[m["acc"]]
                        e.matmul(ps[ob][:, 0:n], lhsT=m["v"], rhs=pt[:, mi, 0:n], start=m["start"], stop=m["stop"])
                        ins = e.matmul(ps[sb2][:, 0:n], lhsT=onesb, rhs=pt[:, mi, 0:n], start=m["start"], stop=m["stop"])
                    return ins
                P.op("pe", pv, r=["PT%d" % (i % 3)] + g["rv"], w=list(dict.fromkeys(wk)))
                if g.get("post"):
                    A = 2 * (i % 2)
                    g["post"](acc, (A, A + 1))

            if N:
                emitS(0)
            for i in range(N):
                emitE(i)
                if i + 1 < N:
                    emitS(i + 1)
                emitPV(i)

        def phase_ffn(l, need_ctx, final):
            W2 = 4355
            Wd = T([128, NJ, D], BF16)
            CW = T([128, 4, NJ])
            for i in range(3):
                dma_nc("sp", CW[:, i, :], din["ffn_conv_w"][l, i].rearrange("(j p) -> p j", p=128), w=["CW"])
            dma_nc("sp", CW[:, 3, :], din["ffn_conv_b"][l].rearrange("(j p) -> p j", p=128), w=["CW"])
            wdsrc = din["ffn_w_down"][l].rearrange("(j p) n -> p j n", p=128)
            for j0 in range(0, NJ, 4):
                j1 = min(NJ, j0 + 4)
                wload(Wd[:, j0:j1, :], wdsrc[:, j0:j1, :])
            top_save = top[0]
            nm = NM(l, 3, 4)
            hTs = [T([128, 8, 512], BF16) for _ in range(2)]
            zt = T([128, 8, 1], BF16)
            P.op("pool", lambda e: e.memset(zt, 0.0), w=["zt"])
            for c in (0, 257, 4354):
                dma_nc("pool", H2D[:, :, c:c + 1], zt, r=["zt"], w=["H2D"])
            blks = list(range(0 if need_ctx else 1, 9))
            for b in blks:
                t0, ntl = BLOCKS[b]
                n = ntl * 128
                hT = hTs[b % 2]
                kh = "hTs%d" % (b % 2)
                for ti in range(ntl):
                    tt = t0 + ti
                    nm.run(x_src(False, tt), tt < 2, hT[:, :, ti * 128:(ti + 1) * 128], kh, 6 + tt % 2)
                nm.flush()
                c0 = 1 if b == 0 else 258 + (b - 1) * 512
                dma("pool", H2D[:, :, c0:c0 + n], hT[:, :, 0:n], r=[kh], w=["H2D"])
            P.barrier()
            top[0] = top_save
            res = RES(l, 5)
            GTt = T([128, NJ, 1024], BF16)
            H2P = [T([128, 8, 1026], BF16) for _ in range(2)]
            wgf = [T([128, 8, 128], BF16) for _ in range(2)]
            wuf = [T([128, 8, 128], BF16) for _ in range(2)]
            acc = [T([128, 512]) for _ in range(2)]
            sil = [T([128, 512]) for _ in range(2)]
            parts = [blks[i:i + 2] for i in range(0, len(blks), 2)]
            wgsrc = din["ffn_w_gate"][l].rearrange("(k p) n -> p k n", p=128)
            wusrc = din["ffn_w_up"][l].rearrange("(k p) n -> p k n", p=128)
            it = [0]
            yi = [0]
            bc0 = lambda b: 1 if b == 0 else 258 + (b - 1) * 512
            for pi, part in enumerate(parts):
                cstart = bc0(part[0]) - 1
                cend = bc0(part[-1]) + BLOCKS[part[-1]][1] * 128 + 1
                npc = cend - cstart
                H2T = H2P[pi % 2]
                kH = "H2P%d" % (pi % 2)
                dma("sp", H2T[:, :, 0:npc], H2D[:, :, cstart:cend], r=["H2D"], w=[kH])
                goff = {}
                o = 0
                for b in part:
                    goff[b] = o
                    o += BLOCKS[b][1] * 128
                for j in range(NJ):
                    wg, wu = wgf[j % 2], wuf[j % 2]
                    kw = "ffw%d" % (j % 2)
                    for (dst, srcw) in ((wg, wgsrc), (wu, wusrc)):
                        i = stgi[0] % 2
                        stgi[0] += 1
                        s = stg[i][:, 0:1024].rearrange("p (a b) -> p a b", a=8)
                        dma("sp", s, srcw[:, :, j * 128:(j + 1) * 128], w=["stg%d" % i])
                        P.op("pool", (lambda e, dst=dst, s=s: e.tensor_copy(out=dst, in_=s)), r=["stg%d" % i], w=[kw])
                    for b in part:
                        t0, ntl = BLOCKS[b]
                        n = ntl * 128
                        c0 = bc0(b) - cstart
                        q = it[0] % 2
                        ub_ = (2, 3, 5)[it[0] % 3]
                        it[0] += 1
                        pa, pu, ph = ps[q], ps[ub_], ps[4]
                        ka, ku, kh = "ps%d" % q, "ps%d" % ub_, "ps4"
                        mmg(pa[:, 0:n], [(wg[:, k, :], H2T[:, k, c0:c0 + n]) for k in range(8)], r=[kw, kH], w=[ka])
                        hal = H2T[:, :, c0 - 1:c0 + n + 1:n + 1]
                        mmg(ph[:, 0:2], [(wg[:, k, :], hal[:, k, :]) for k in range(8)], r=[kw, kH], w=[kh])
                        mmg(pu[:, 0:n], [(wu[:, k, :], H2T[:, k, c0:c0 + n]) for k in range(8)], r=[kw, kH], w=[ku])
                        ac, sl_ = acc[q], sil[q]
                        kac, ksl = "acc%d" % q, "sil%d" % q
                        w0, w1, w2, bb = (CW[:, i, j:j + 1] for i in range(4))
                        P.op("dve", (lambda e, ac=ac, pa=pa, w1=w1, bb=bb, n=n: e.tensor_scalar(
                            out=ac[:, 0:n], in0=pa[:, 0:n], scalar1=w1, scalar2=bb, op0=ALU.mult, op1=ALU.add)),
                            r=[ka, "CW"], w=[kac])
                        P.op("dve", (lambda e, ac=ac, ph=ph, w0=w0: e.scalar_tensor_tensor(
                            out=ac[:, 0:1], in0=ph[:, 0:1], scalar=w0, in1=ac[:, 0:1], op0=ALU.mult, op1=ALU.add)),
                            r=[kh, kac], w=[kac])
                        P.op("dve", (lambda e, ac=ac, ph=ph, w2=w2, n=n: e.scalar_tensor_tensor(
                            out=ac[:, n - 1:n], in0=ph[:, 1:2], scalar=w2, in1=ac[:, n - 1:n], op0=ALU.mult, op1=ALU.add)),
                            r=[kh, kac], w=[kac, kh])
                        P.op("dve", (lambda e, ac=ac, pa=pa, w0=w0, n=n: e.scalar_tensor_tensor(
                            out=ac[:, 1:n], in0=pa[:, 0:n - 1], scalar=w0, in1=ac[:, 1:n], op0=ALU.mult, op1=ALU.add)),
                            r=[ka, kac], w=[kac])
                        P.op("dve", (lambda e, ac=ac, pa=pa, w2=w2, n=n: e.scalar_tensor_tensor(
                            out=ac[:, 0:n - 1], in0=pa[:, 1:n], scalar=w2, in1=ac[:, 0:n - 1], op0=ALU.mult, op1=ALU.add)),
                            r=[ka, kac], w=[kac, ka])
                        P.op("act", (lambda e, ac=ac, sl_=sl_, n=n: e.activation(out=sl_[:, 0:n], in_=ac[:, 0:n], func=AF.Silu)),
                             r=[kac], w=[ksl])
                        g0 = goff[b]
                        P.op("dve", (lambda e, sl_=sl_, pu=pu, j=j, g0=g0, n=n: e.tensor_tensor(
                            out=GTt[:, j, g0:g0 + n], in0=sl_[:, 0:n], in1=pu[:, 0:n], op=ALU.mult)),
                            r=[ksl, ku], w=["GT", ku])
                for b in part:
                    t0, ntl = BLOCKS[b]
                    for ti in range(ntl):
                        tt = t0 + ti
                        g0 = goff[b] + ti * 128
                        y0, y1 = ((6, 7), (0, 1), (2, 3))[yi[0] % 3]
                        yi[0] += 1
                        for hf, yb in ((0, y0), (1, y1)):
                            mmg(ps[yb], [(GTt[:, j, g0:g0 + 128], Wd[:, j, hf * 512:(hf + 1) * 512]) for j in range(NJ)],
                                r=["GT", "wgt"], w=["ps%d" % yb])
                        if final:
                            dstd = out[(tt - 2) * 128:(tt - 1) * 128, :]
                            res.run(y0, y1, x_src(False, tt), dstd, tt < 2, "OUT")
                        else:
                            res.run(y0, y1, x_src(False, tt), XR[tt * 128:(tt + 1) * 128, :], tt < 2, "XR2")

        def phase_l0():
            l = 0
            LAM_INIT = 0.2
            KT = T([128, 4, NTOK], BF16)
            Vv = T([128, NT, 512], BF16)
            keep = top[0]
            w_in = T([128, 8, 2048], BF16)
            wsrc = din["ev_w_in"][0].rearrange("(k p) n -> p k n", p=128)
            for c in range(4):
                wload(w_in[:, :, c * 512:(c + 1) * 512], wsrc[:, :, c * 512:(c + 1) * 512])
            nm = NM(l, 0, 1)
            hT = T([128, 8, 512], BF16)
            ro = rope_tiles() + (7,)
            ub = [T([128, 512]) for _ in range(2)]
            qb = [T([128, 4, 512], BF16) for _ in range(2)]
            for b, (t0, ntl) in enumerate(BLOCKS):
                n = ntl * 128
                col0 = t0 * 128
                rope_load(ro, b)
                for ti in range(ntl):
                    tt = t0 + ti
                    nm.run(x_src(True, tt), tt < 2, hT[:, :, ti * 128:(ti + 1) * 128], "hT", 6)
                nm.flush()
                for ti in range(ntl):
                    tt = t0 + ti
                    bk = 4 + (ti % 2)
                    mmg(ps[bk], [(hT[:, k, ti * 128:(ti + 1) * 128], w_in[:, k, 1536:2048]) for k in range(8)],
                        r=["hT", "wgt"], w=["ps%d" % bk])
                    P.op("act", (lambda e, tt=tt, bk=bk: e.copy(out=Vv[:, tt, :], in_=ps[bk])), r=["ps%d" % bk], w=["Vv"])
                for c in range(4):
                    bk = c % 2
                    mmg(ps[bk][:, 0:n], [(w_in[:, k, c * 128:(c + 1) * 128], hT[:, k, 0:n]) for k in range(8)],
                        r=["hT", "wgt"], w=["ps%d" % bk])
                    u = ub[c % 2]
                    P.op("act", (lambda e, u=u, bk=bk, n=n: e.copy(out=u[:, 0:n], in_=ps[bk][:, 0:n])),
                         r=["ps%d" % bk], w=["ub%d" % (c % 2)])
                    dma("pool", UT[c][:, col0:col0 + n], u[:, 0:n], r=["ub%d" % (c % 2)], w=["UT"])
                qbb = qb[b % 2]
                kq = "qb%d" % (b % 2)
                for c in range(4):
                    bk = 2 + (c % 2)
                    mmg(ps[bk][:, 0:n], [(w_in[:, k, 512 + c * 128:512 + (c + 1) * 128], hT[:, k, 0:n]) for k in range(8)],
                        r=["hT", "wgt"], w=["ps%d" % bk])
                    rope_evict(bk, n, col0, qbb[:, c, 0:n], kq, ro, b == 0)
                dma("pool", QT[b][:, :, 0:n], qbb[:, :, 0:n], r=[kq], w=["QT"])
                for c in range(4):
                    bk = 2 + (c % 2)
                    mmg(ps[bk][:, 0:n], [(w_in[:, k, 1024 + c * 128:1024 + (c + 1) * 128], hT[:, k, 0:n]) for k in range(8)],
                        r=["hT", "wgt"], w=["ps%d" % bk])
                    rope_evict(bk, n, col0, KT[:, c, col0:col0 + n], "KT", ro, b == 0)
            P.barrier()
            top[0] = keep
            if stop == "l0a1":
                return
            pw = T([128, 4, 128], BF16)
            for g in range(4):
                wload(pw[:, g, :], din["pool_w"][0, g])
            psc = T([128, 4])
            dma_nc("sp", psc, din["pool_scale"][0].rearrange("(g p) -> p g", p=128), w=["psc"])
            UP = T([128, PW])
            Aa = T([128, PW])
            Ab = T([128, PW])
            IC = T([128, PW])
            dT = T([128, PW], BF16)
            mpo = [T([128, 512], BF16) for _ in range(2)]
            for g in range(4):
                hw = (1, 2, 4, 8)[g]
                P.op("pool", lambda e: e.memset(UP, 0.0), w=["UP"])
                dma("sp", UP[:, PC0:PC0 + 256], UT[g][:, 0:256], r=["UT"], w=["UP"])
                dma("sp", UP[:, PL0:PL0 + 4096], UT[g][:, 256:NTOK], r=["UT"], w=["UP"])
                dma("pool", IC, din["invcnt"][g].partition_broadcast(128), w=["IC"])
                cur, ck = UP, "UP"
                bufs = [(Aa, "Aa"), (Ab, "Ab")]
                width = PW
                for s in range(g + 1):
                    sh = 1 << s
                    nxt, nk = bufs[s % 2]
                    width -= sh
                    P.op("dve", (lambda e, cur=cur, nxt=nxt, sh=sh, width=width: e.tensor_tensor(
                        out=nxt[:, 0:width], in0=cur[:, 0:width], in1=cur[:, sh:sh + width], op=ALU.add)),
                        r=[ck], w=[nk])
                    cur, ck = nxt, nk
                oth, ok = bufs[(g + 1) % 2]
                P.op("dve", (lambda e, cur=cur, oth=oth, hw=hw: e.tensor_tensor(
                    out=oth[:, 8:PW - 8], in0=cur[:, 8 - hw:PW - 8 - hw], in1=IC[:, 8:PW - 8], op=ALU.mult)),
                    r=[ck, "IC"], w=[ok])
                P.op("pool", (lambda e, oth=oth: e.tensor_tensor(out=dT[:, 8:PW - 8], in0=oth[:, 8:PW - 8],
                                                                 in1=UP[:, 8:PW - 8], op=ALU.subtract)),
                     r=[ok, "UP"], w=["dT"])
                for b, (t0, ntl) in enumerate(BLOCKS):
                    n = ntl * 128
                    col0 = t0 * 128
                    pc = PC0 if b == 0 else PL0 + (b - 1) * 512
                    bk = b % 2
                    mmg(ps[bk][:, 0:n], [(pw[:, g, :], dT[:, pc:pc + n])], r=["dT", "wgt"], w=["ps%d" % bk])
                    mp = mpo[b % 2]
                    P.op("act", (lambda e, bk=bk, n=n, g=g, mp=mp: e.activation(
                        out=mp[:, 0:n], in_=ps[bk][:, 0:n], func=AF.Identity, scale=psc[:, g:g + 1])),
                        r=["ps%d" % bk, "psc"], w=["mpo%d" % (b % 2)])
                    dma("pool", MIXD[g][:, col0:col0 + n], mp[:, 0:n], r=["mpo%d" % (b % 2)], w=["MIXD"])
            P.barrier()
            top[0] = keep
            if stop == "l0a2":
                return
            lamt = T([128, 4, 64])
            lj = T([128, 64])
            lsum = T([128, 2])
            nlam = T([128, 1])
            dma("sp", lamt.rearrange("p a b -> p (a b)"),
                din["diff_lambda"][0].rearrange("a b -> (a b)").partition_broadcast(128), w=["lamt"])
            for i in range(2):
                P.op("dve", (lambda e, i=i: e.scalar_tensor_tensor(out=lj, in0=lamt[:, 2 * i, :], scalar=1.0,
                                                                   in1=lamt[:, 2 * i + 1, :], op0=ALU.mult, op1=ALU.mult,
                                                                   accum_out=lsum[:, i:i + 1])),
                     r=["lamt"], w=["lj", "lsum"])
            P.op("act", lambda e: e.activation(out=lsum, in_=lsum, func=AF.Exp), r=["lsum"], w=["lsum"])
            P.op("dve", lambda e: e.tensor_tensor(out=nlam, in0=lsum[:, 1:2], in1=lsum[:, 0:1], op=ALU.subtract),
                 r=["lsum"], w=["nlam"])
            P.op("dve", lambda e: e.tensor_scalar(out=nlam, in0=nlam, scalar1=-LAM_INIT, scalar2=None, op0=ALU.add),
                 r=["nlam"], w=["nlam"])
            sln = T([128, 1])
            dma_nc("sp", sln, din["diff_subln"][0].rearrange("(p o) -> p o", o=1), w=["sln"])
            P.op("dve", lambda e: e.tensor_scalar(out=sln, in0=sln, scalar1=1.0 - LAM_INIT, scalar2=None, op0=ALU.mult),
                 r=["sln"], w=["sln"])
            qtl = [T([128, 4, 512], BF16) for _ in range(3)]
            MIXA = [T([128, 4, 512], BF16) for _ in range(2)]
            rsum = T([128, 512])
            o2 = [T([128, 512]) for _ in range(2)]
            oc = T([128, 512])
            sq = T([128, 512], BF16)
            rstd = T([128, 512])
            state = {}

            def pre(b):
                if state.get("b") != b:
                    state["b"] = b
                    n = BLOCKS[b][1] * 128
                    col0 = BLOCKS[b][0] * 128
                    dma("sp", qtl[b % 3][:, :, 0:n], QT[b][:, :, 0:n], r=["QT"], w=["qtl%d" % (b % 3)])

            def post(b, h, n, acc, scr):
                mixa = MIXA[b % 2]
                km = "MIXA%d" % (b % 2)
                for j in range(2):
                    ob, sb2 = acc[j]
                    P.op("dve", (lambda e, sb2=sb2: e.reciprocal(out=rsum[:, 0:n], in_=ps[sb2][:, 0:n])),
                         r=["ps%d" % sb2], w=["rsum", "ps%d" % sb2])
                    P.op("dve", (lambda e, ob=ob, j=j: e.tensor_tensor(out=o2[j][:, 0:n], in0=ps[ob][:, 0:n], in1=rsum[:, 0:n], op=ALU.mult)),
                         r=["ps%d" % ob, "rsum"], w=["o2_%d" % j, "ps%d" % ob])
                P.op("dve", lambda e: e.scalar_tensor_tensor(out=oc[:, 0:n], in0=o2[1][:, 0:n], scalar=nlam,
                                                             in1=o2[0][:, 0:n], op0=ALU.mult, op1=ALU.add),
                     r=["o2_0", "o2_1", "nlam"], w=["oc"])
                P.op("act", lambda e: e.activation(out=sq[:, 0:n], in_=oc[:, 0:n], func=AF.Square), r=["oc"], w=["sq"])
                sbk = scr[0]
                mmg(ps[sbk][:, 0:n], [(onesb, sq[:, 0:n])], r=["sq"], w=["ps%d" % sbk])
                P.op("act", lambda e: e.activation(out=rstd[:, 0:n], in_=ps[sbk][:, 0:n], func=AF.Sqrt, scale=1.0 / 128,
                                                   bias=EPS), r=["ps%d" % sbk], w=["rstd", "ps%d" % sbk])
                P.op("dve", lambda e: e.reciprocal(out=rstd[:, 0:n], in_=rstd[:, 0:n]), r=["rstd"], w=["rstd"])
                P.op("dve", lambda e: e.scalar_tensor_tensor(out=mixa[:, h, 0:n], in0=oc[:, 0:n], scalar=sln,
                                                             in1=rstd[:, 0:n], op0=ALU.mult, op1=ALU.mult),
                     r=["oc", "rstd", "sln"], w=[km])
                col0 = BLOCKS[b][0] * 128
                dma("pool", MIXD[4 + h][:, col0:col0 + n], mixa[:, h, 0:n], r=[km], w=["MIXD"])

            groups = []
            for b in range(9):
                n = BLOCKS[b][1] * 128
                kts = list(range(2)) if b == 0 else list(range(NT))
                for h in range(4):
                    for ki, kt in enumerate(kts):
                        mem = []
                        for j in range(2):
                            pr = slice(64 * j, 64 * j + 64)
                            mem.append(dict(qk=[(KT[pr, h, kt * 128:(kt + 1) * 128], qtl[b % 3][pr, h, 0:n])],
                                            v=Vv[:, kt, h * 128:(h + 1) * 128], acc=j, start=(ki == 0), stop=(ki == len(kts) - 1)))
                        g = dict(n=n, members=mem, rk=["KT", "qtl%d" % (b % 3)], rv=["Vv"], accset=0)
                        if h == 0 and ki == 0:
                            g["pre"] = (lambda b=b: (pre(b), pre(b + 1) if b + 1 < 9 else None))
                        if ki == len(kts) - 1:
                            g["post"] = (lambda acc, scr, b=b, h=h, n=n: post(b, h, n, acc, scr))
                        groups.append(g)
            attention(groups, 0.125, [[(4, 5), (6, 7)]])
            phase_reset()
            phase_outproj(0, list(range(9)), True)

        def rownorm(pt, W, NB, dst, dkey, tg):
            junk, ss, rs = tg
            P.op("act", lambda e: e.activation(out=junk[:, 0:W], in_=pt[0][:, 0:W], func=AF.Square, accum_out=ss),
                 r=[pt[1]], w=["rn_junk", "rn_ss"])
            P.op("act", lambda e: e.activation(out=rs, in_=ss, func=AF.Sqrt, scale=1.0 / W, bias=EPS), r=["rn_ss"], w=["rn_rs"])
            P.op("dve", lambda e: e.reciprocal(out=rs, in_=rs), r=["rn_rs"], w=["rn_rs"])
            P.op("dve", lambda e: e.scalar_tensor_tensor(out=dst, in0=pt[0][:, 0:W], scalar=rs, in1=NB[:, 0:W], op0=ALU.mult,
                                                         op1=ALU.mult), r=[pt[1], "rn_rs", "wgt"], w=[dkey, pt[1]])

        def phase_l1():
            l = 1
            KN = T([128, 4, NTOK], BF16)
            VM = T([128, NT, 512], BF16)
            KR2 = T([128, NTOK], BF16)
            keep = top[0]
            WA = T([128, 8, 512], BF16)
            WB = T([128, 8, 256], BF16)
            WKR = T([128, 8, 128], BF16)
            WQN = T([128, 4, 4, 128], BF16)
            WQR = T([128, 4, 256], BF16)
            WKN = T([128, 2, 4, 128], BF16)
            WVV = T([128, 2, 512], BF16)
            wsrc = din["od_w_in"][0].rearrange("(k p) n -> p k n", p=128)
            wload(WA, wsrc[:, :, 0:512])
            wload(WB, wsrc[:, :, 512:768])
            wload(WKR[:, :, 0:64], wsrc[:, :, 768:832])
            wload(WKR[:, :, 64:128], wsrc[:, :, 768:832])
            uq = din["mla_w_uq"][0].rearrange("(k p) (h c) -> p k h c", p=128, c=192)
            ukv = din["mla_w_ukv"][0].rearrange("(k p) (h c) -> p k h c", p=128, c=256)
            for k in range(4):
                wload(WQN[:, k], uq[:, k, :, 0:128])
                wload(WQR[:, k, :].rearrange("p (h c) -> p h c", h=4), uq[:, k, :, 128:192])
            for k in range(2):
                wload(WKN[:, k], ukv[:, k, :, 0:128])
                wload(WVV[:, k, :].rearrange("p (h c) -> p h c", h=4), ukv[:, k, :, 128:256])
            QNb = T([128, 512])
            KVNb = T([128, 256])
            dma("sp", QNb, din["mla_q_norm"][0].partition_broadcast(128), w=["wgt"])
            dma("sp", KVNb, din["mla_kv_norm"][0].partition_broadcast(128), w=["wgt"])
            nm = NM(l, 0, 1)
            hTs = [T([128, 8, 512], BF16)] * 2
            ro = rope_tiles() + (7,)
            tg = (T([128, 512], BF16), T([128, 1]), T([128, 1]))
            cqn = T([128, 512], BF16)
            ckvn = T([128, 256], BF16)
            cqT = T([128, 4, 512], BF16)
            ckvT = T([128, 2, 512], BF16)
            qnb = [T([128, 4, 512], BF16)] * 2
            qrb = [T([128, 2, 512], BF16)] * 2
            for b, (t0, ntl) in enumerate(BLOCKS):
                n = ntl * 128
                col0 = t0 * 128
                hT = hTs[b % 2]
                khT = "hTs0"
                rope_load(ro, b)
                for ti in range(ntl):
                    tt = t0 + ti
                    nm.run(x_src(False, tt), tt < 2, hT[:, :, ti * 128:(ti + 1) * 128], khT, 6)
                nm.flush()
                dma("pool", HT1[b][:, :, 0:n], hT[:, :, 0:n], r=[khT], w=["HT1"])
                for ti in range(ntl):
                    ts_ = slice(ti * 128, (ti + 1) * 128)
                    mmg(ps[0], [(hT[:, k, ts_], WA[:, k, :]) for k in range(8)], r=[khT, "wgt"], w=["ps0"])
                    rownorm((ps[0], "ps0"), 512, QNb, cqn, "cqn", tg)

                    def trq(e):
                        for k in range(4):
                            ins = e.transpose(out=psb[2][:, k * 128:(k + 1) * 128], in_=cqn[:, k * 128:(k + 1) * 128], identity=identb)
                        return ins
                    P.op("pe", trq, r=["cqn"], w=["ps2"])
                    P.op("act", (lambda e, ts_=ts_: e.copy(out=cqT[:, :, ts_], in_=psb[2][:, 0:512].rearrange("p (k n) -> p k n", k=4))),
                         r=["ps2"], w=["cqT"])
                    mmg(ps[1][:, 0:256], [(hT[:, k, ts_], WB[:, k, :]) for k in range(8)], r=[khT, "wgt"], w=["ps1"])
                    rownorm((ps[1], "ps1"), 256, KVNb, ckvn, "ckvn", tg)

                    def trk(e):
                        for k in range(2):
                            ins = e.transpose(out=psb[3][:, k * 128:(k + 1) * 128], in_=ckvn[:, k * 128:(k + 1) * 128], identity=identb)
                        return ins
                    P.op("pe", trk, r=["ckvn"], w=["ps3"])
                    P.op("act", (lambda e, ts_=ts_: e.copy(out=ckvT[:, :, ts_], in_=psb[3][:, 0:256].rearrange("p (k n) -> p k n", k=2))),
                         r=["ps3"], w=["ckvT"])
                for ti in range(ntl):
                    tt = t0 + ti
                    ts_ = slice(ti * 128, (ti + 1) * 128)
                    mmg(ps[0], [(ckvT[:, k, ts_], WVV[:, k, :]) for k in range(2)], r=["ckvT", "wgt"], w=["ps0"])
                    P.op("act", (lambda e, tt=tt: e.copy(out=VM[:, tt, :], in_=ps[0])), r=["ps0"], w=["VM", "ps0"])
                for h in range(4):
                    bk = 4 + (h % 2)
                    mmg(ps[bk][:, 0:n], [(WKN[:, k, h, :], ckvT[:, k, 0:n]) for k in range(2)], r=["ckvT", "wgt"], w=["ps%d" % bk])
                    P.op("act", (lambda e, h=h, bk=bk, n=n, col0=col0: e.copy(out=KN[:, h, col0:col0 + n], in_=ps[bk][:, 0:n])),
                         r=["ps%d" % bk], w=["KN", "ps%d" % bk])
                mmg(ps[4][:, 0:n], [(WKR[:, k, :], hT[:, k, 0:n]) for k in range(8)], r=[khT, "wgt"], w=["ps4"])
                rope_evict(4, n, col0, KR2[:, col0:col0 + n], "KR2", ro, b == 0)
                if b >= 1:
                    qn, qr = qnb[b % 2], qrb[b % 2]
                    for h in range(4):
                        bk = 4 + (h % 2)
                        mmg(ps[bk][:, 0:n], [(WQN[:, k, h, :], cqT[:, k, 0:n]) for k in range(4)], r=["cqT", "wgt"], w=["ps%d" % bk])
                        P.op("act", (lambda e, h=h, bk=bk, n=n, qn=qn: e.copy(out=qn[:, h, 0:n], in_=ps[bk][:, 0:n])),
                             r=["ps%d" % bk], w=["qnb0", "ps%d" % bk])
                    dma("pool", QT[b][:, :, 0:n], qn[:, :, 0:n], r=["qnb0"], w=["QT"])
                    for c in range(2):
                        bk = 4 + (c % 2)
                        mmg(ps[bk][:, 0:n], [(WQR[:, k, c * 128:(c + 1) * 128], cqT[:, k, 0:n]) for k in range(4)],
                            r=["cqT", "wgt"], w=["ps%d" % bk])
                        rope_evict(bk, n, col0, qr[:, c, 0:n], "qrb0", ro, False)
                    dma("pool", QR[b][:, :, 0:n], qr[:, :, 0:n], r=["qrb0"], w=["QR"])
            P.barrier()
            top[0] = keep
            if stop == "l1b1":
                return
            qnl = [T([128, 4, 512], BF16) for _ in range(3)]
            qrl = [T([128, 2, 512], BF16) for _ in range(3)]
            rsum = T([128, 512])
            mo = [T([128, 512], BF16) for _ in range(2)]
            state = {}

            def pre(b):
                if state.get("b") != b:
                    state["b"] = b
                    n = BLOCKS[b][1] * 128
                    dma("sp", qnl[b % 3][:, :, 0:n], QT[b][:, :, 0:n], r=["QT"], w=["qnl%d" % (b % 3)])
                    dma("sp", qrl[b % 3][:, :, 0:n], QR[b][:, :, 0:n], r=["QR"], w=["qnl%d" % (b % 3)])

            def post(b, h, n, acc, scr):
                col0 = BLOCKS[b][0] * 128
                m = mo[h % 2]
                km = "mo%d" % (h % 2)
                ob, sb2 = acc[0]
                P.op("dve", lambda e: e.reciprocal(out=rsum[:, 0:n], in_=ps[sb2][:, 0:n]), r=["ps%d" % sb2], w=["rsum", "ps%d" % sb2])
                P.op("dve", lambda e: e.tensor_tensor(out=m[:, 0:n], in0=ps[ob][:, 0:n], in1=rsum[:, 0:n], op=ALU.mult),
                     r=["ps%d" % ob, "rsum"], w=[km, "ps%d" % ob])
                dma("pool", MIXD[h][:, col0:col0 + n], m[:, 0:n], r=[km], w=["MIXD"])

            groups = []
            gi = 0
            for b in range(1, 9):
                n = 512
                for h in range(4):
                    pr = slice(64 * (h % 2), 64 * (h % 2) + 64)
                    for kp in range(NT // 2):
                        mem = []
                        for mi in range(2):
                            kt = 2 * kp + mi
                            ks = slice(kt * 128, (kt + 1) * 128)
                            mem.append(dict(qk=[(KN[:, h, ks], qnl[b % 3][:, h, 0:n]), (KR2[pr, ks], qrl[b % 3][pr, h // 2, 0:n])],
                                            v=VM[:, kt, h * 128:(h + 1) * 128], acc=0,
                                            start=(kp == 0 and mi == 0), stop=(kp == NT // 2 - 1 and mi == 1)))
                        g = dict(n=n, members=mem, rk=["KN", "KR2", "qnl%d" % (b % 3)], rv=["VM"], accset=gi % 2)
                        if h == 0 and kp == 0:
                            g["pre"] = (lambda b=b: (pre(b), pre(b + 1) if b + 1 < 9 else None))
                        if kp == NT // 2 - 1:
                            g["post"] = (lambda acc, scr, b=b, h=h, n=n: post(b, h, n, acc, scr))
                        groups.append(g)
                    gi += 1
            attention(groups, 192.0 ** -0.5, [[(4, 5)], [(6, 7)]])
            phase_reset()
            if stop == "l1b2":
                return
            LB = T([128, 2, 2, 4])
            lbv = T([128, 2, 4])
            oml = T([128, 2, 4])
            hgn = T([128, 1])
            ones1 = T([128, 1])
            for d in range(2):
                for ll in range(2):
                    dma_nc("sp", LB[:, d, ll, :], din["hgrn_lb"][d, ll].rearrange("(h p) -> p h", p=128), w=["LB"])
            dma_nc("sp", hgn, din["hgrn_norm"][0].rearrange("(p o) -> p o", o=1), w=["hgn"])
            P.op("dve", lambda e: e.tensor_tensor(out=lbv, in0=LB[:, :, 1, :], in1=LB[:, :, 0, :], op=ALU.subtract), r=["LB"], w=["lbv"])
            P.op("act", lambda e: e.activation(out=lbv, in_=lbv, func=AF.Sigmoid), r=["lbv"], w=["lbv"])
            P.op("dve", lambda e: e.tensor_scalar(out=oml, in0=lbv, scalar1=-1.0, scalar2=1.0, op0=ALU.mult, op1=ALU.add),
                 r=["lbv"], w=["oml"])
            P.op("pool", lambda e: e.memset(ones1, 1.0), w=["ones1"])
            WH = T([128, 8, 5, 128], BF16)
            hTl = [T([128, 8, 512], BF16)] * 2
            SG = T([128, NTOK], BF16)
            Vt = T([128, NT, 128], BF16)
            QP = [T([128, NTOK], BF16) for _ in range(2)]
            QPP = [T([128, NTOK], BF16) for _ in range(2)]
            KP = [T([128, NTOK], BF16) for _ in range(2)]
            KPt = [T([128, NT, 128], BF16) for _ in range(2)]
            EL = [T([128, 68]) for _ in range(2)]
            OTa = T([128, NTOK])
            qf = T([128, 512])
            sgm = [T([128, 512]) for _ in range(2)]
            kk = [T([128, 512]) for _ in range(2)]
            lf = [T([128, 512]) for _ in range(2)]
            gb = [T([128, 516]) for _ in range(2)]
            Da = [T([128, 512]) for _ in range(2)]
            Db = [T([128, 512]) for _ in range(2)]
            E1 = [T([128, 512]) for _ in range(2)]
            E2 = [T([128, 512]) for _ in range(2)]
            E3 = [T([128, 512]) for _ in range(2)]
            elt = [T([128, 8]) for _ in range(2)]
            Sf = [[T([128, 128]) for _ in range(2)] for _ in range(2)]
            Sb = [[T([128, 128], BF16) for _ in range(2)] for _ in range(2)]
            Am = [[T([128, 64], BF16) for _ in range(2)] for _ in range(2)]
            rstd, otmp = Db[0], E1[0]
            sq = T([128, 512], BF16)
            mh = [T([128, 512], BF16) for _ in range(2)]
            for d in range(2):
                P.op("pool", (lambda e, d=d: e.memset(gb[d][:, 0:1], 0.0)), w=["gbz"])
            wsrc = din["od_w_in"][0].rearrange("(k p) n -> p k n", p=128)
            for h in range(4):
                for i, c0_ in enumerate((832, 1344, 1856, 2368, 2880)):
                    wload(WH[:, :, i, :], wsrc[:, :, c0_ + h * 128:c0_ + (h + 1) * 128])
                P.op("pool", lambda e: e.memset(OTa, 0.0), w=["OT"])
                for b, (t0, ntl) in enumerate(BLOCKS):
                    n = ntl * 128
                    nch = n // 64
                    col0 = t0 * 128
                    ch0 = col0 // 64
                    hT = hTl[b % 2]
                    khT = "hTl0"
                    dma("sp", hT[:, :, 0:n], HT1[b][:, :, 0:n], r=["HT1"], w=[khT])
                    mmg(ps[0][:, 0:n], [(WH[:, k, 0, :], hT[:, k, 0:n]) for k in range(8)], r=[khT, "wgt"], w=["ps0"])
                    P.op("act", (lambda e, n=n: e.activation(out=qf[:, 0:n], in_=ps[0][:, 0:n], func=AF.Silu)), r=["ps0"], w=["qf", "ps0"])
                    mmg(ps[1][:, 0:n], [(WH[:, k, 4, :], hT[:, k, 0:n]) for k in range(8)], r=[khT, "wgt"], w=["ps1"])
                    P.op("act", (lambda e, n=n, col0=col0: e.activation(out=SG[:, col0:col0 + n], in_=ps[1][:, 0:n], func=AF.Silu)),
                         r=["ps1"], w=["SG", "ps1"])
                    for ti in range(ntl):
                        tt = t0 + ti
                        ts_ = slice(ti * 128, (ti + 1) * 128)
                        mmg(ps[2][:, 0:128], [(hT[:, k, ts_], WH[:, k, 3, :]) for k in range(8)], r=[khT, "wgt"], w=["ps2"])
                        P.op("act", (lambda e, tt=tt: e.copy(out=Vt[:, tt, :], in_=ps[2][:, 0:128])), r=["ps2"], w=["Vt", "ps2"])
                    Gi3 = [gb[d][:, 1:1 + n].rearrange("p (c j) -> p c j", j=64) for d in range(2)]
                    Gs3 = [gb[d][:, 0:n].rearrange("p (c j) -> p c j", j=64) for d in range(2)]
                    S0 = [Gs3[d][:, :, 0:1].to_broadcast([128, nch, 64]) for d in range(2)]
                    I63 = [Gi3[d][:, :, 63:64].to_broadcast([128, nch, 64]) for d in range(2)]
                    Da3 = [Da[d][:, 0:n].rearrange("p (c j) -> p c j", j=64) for d in range(2)]
                    Db3 = [Db[d][:, 0:n].rearrange("p (c j) -> p c j", j=64) for d in range(2)]
                    for d in range(2):
                        mmg(ps[3 + d][:, 0:n], [(WH[:, k, 1 + d, :], hT[:, k, 0:n]) for k in range(8)], r=[khT, "wgt"], w=["ps%d" % (3 + d)])
                    for d in range(2):
                        P.op("act", (lambda e, n=n, d=d: e.activation(out=sgm[d][:, 0:n], in_=ps[3 + d][:, 0:n], func=AF.Sigmoid)),
                             r=["ps%d" % (3 + d)], w=["sgm%d" % d, "ps%d" % (3 + d)])
                    for d in range(2):
                        P.op("dve", (lambda e, n=n, d=d, h=h: e.tensor_scalar(out=sgm[d][:, 0:n], in0=sgm[d][:, 0:n], scalar1=oml[:, d, h:h + 1],
                                                                             scalar2=lbv[:, d, h:h + 1], op0=ALU.mult, op1=ALU.add)),
                             r=["sgm%d" % d, "oml", "lbv"], w=["sgm%d" % d])
                    for d in range(2):
                        P.op("pool", (lambda e, n=n, d=d: e.tensor_scalar(out=kk[d][:, 0:n], in0=sgm[d][:, 0:n], scalar1=-1.0, scalar2=1.0,
                                                                         op0=ALU.mult, op1=ALU.add)), r=["sgm%d" % d], w=["kk%d" % d])
                        P.op("act", (lambda e, n=n, d=d: e.activation(out=lf[d][:, 0:n], in_=sgm[d][:, 0:n], func=AF.Ln)), r=["sgm%d" % d], w=["lf%d" % d])
                    for d in range(2):
                        P.op("dve", (lambda e, n=n, d=d: e.tensor_tensor_scan(out=gb[d][:, 1:1 + n], data0=ones1[:, 0:1].to_broadcast([128, n]),
                                                                             data1=lf[d][:, 0:n], initial=0.0, op0=ALU.mult, op1=ALU.add)),
                             r=["lf%d" % d, "ones1", "gbz"], w=["gb%dk" % d])
                    P.op("dve", (lambda e, a=Gi3[0], b_=S0[0], o=Da3[0]: e.tensor_tensor(out=o, in0=a, in1=b_, op=ALU.subtract)), r=["gb0k"], w=["Da0"])
                    P.op("dve", (lambda e, a=Gs3[1], b_=I63[1], o=Da3[1]: e.tensor_tensor(out=o, in0=a, in1=b_, op=ALU.subtract)), r=["gb1k"], w=["Da1"])
                    P.op("dve", (lambda e, a=Gi3[0], b_=I63[0], o=Db3[0]: e.tensor_tensor(out=o, in0=a, in1=b_, op=ALU.subtract)), r=["gb0k"], w=["Db0"])
                    P.op("dve", (lambda e, a=Gs3[1], b_=S0[1], o=Db3[1]: e.tensor_tensor(out=o, in0=a, in1=b_, op=ALU.subtract)), r=["gb1k"], w=["Db1"])
                    for d in range(2):
                        P.op("dve", (lambda e, d=d, nch=nch, a=Gi3[d], b_=Gs3[d]: e.tensor_tensor(out=elt[d][:, 0:nch], in0=a[:, :, 63], in1=b_[:, :, 0],
                                                                                                  op=ALU.subtract)), r=["gb%dk" % d], w=["elt%d" % d])
                    scs = ((1.0, -1.0, 1.0), (-1.0, 1.0, -1.0))
                    for d in range(2):
                        P.op("act", (lambda e, n=n, d=d: e.activation(out=E1[d][:, 0:n], in_=Da[d][:, 0:n], func=AF.Exp, scale=scs[d][0])), r=["Da%d" % d], w=["E1%d" % d])
                    for d in range(2):
                        P.op("act", (lambda e, n=n, d=d: e.activation(out=E2[d][:, 0:n], in_=Db[d][:, 0:n], func=AF.Exp, scale=scs[d][1])), r=["Db%d" % d], w=["E2%d" % d])
                    for d in range(2):
                        P.op("act", (lambda e, n=n, d=d: e.activation(out=E3[d][:, 0:n], in_=Db[d][:, 0:n], func=AF.Exp, scale=scs[d][2])), r=["Db%d" % d], w=["E3%d" % d])
                        P.op("act", (lambda e, d=d, ch0=ch0, nch=nch: e.activation(out=EL[d][:, ch0:ch0 + nch], in_=elt[d][:, 0:nch], func=AF.Exp)),
                             r=["elt%d" % d], w=["EL%d" % d])
                    for d in range(2):
                        P.op("dve", (lambda e, n=n, d=d, col0=col0: e.tensor_tensor(out=KP[d][:, col0:col0 + n], in0=kk[d][:, 0:n], in1=E2[d][:, 0:n], op=ALU.mult)),
                             r=["kk%d" % d, "E2%d" % d], w=["KP%d" % d])
                    for d in range(2):
                        P.op("dve", (lambda e, n=n, d=d, col0=col0: e.tensor_tensor(out=QP[d][:, col0:col0 + n], in0=qf[:, 0:n], in1=E1[d][:, 0:n], op=ALU.mult)),
                             r=["qf", "E1%d" % d], w=["QP%d" % d])
                        P.op("pool", (lambda e, n=n, d=d, col0=col0: e.tensor_tensor(out=QPP[d][:, col0:col0 + n], in0=qf[:, 0:n], in1=E3[d][:, 0:n], op=ALU.mult)),
                             r=["qf", "E3%d" % d], w=["QPP%d" % d])
                    for d in range(2):
                        def trp(e, d=d, t0=t0, ntl=ntl):
                            for ti in range(ntl):
                                ins = e.transpose(out=psb[6 + d][:, ti * 128:(ti + 1) * 128], in_=KP[d][:, (t0 + ti) * 128:(t0 + ti + 1) * 128], identity=identb)
                            return ins
                        P.op("pe", trp, r=["KP%d" % d], w=["ps%d" % (6 + d)])
                        P.op("act", (lambda e, d=d, t0=t0, ntl=ntl: e.copy(out=KPt[d][:, t0:t0 + ntl, :],
                                                                          in_=psb[6 + d][:, 0:ntl * 128].rearrange("p (t k) -> p t k", k=128))),
                             r=["ps%d" % (6 + d)], w=["KPt%d" % d, "ps%d" % (6 + d)])
                orders = [list(range(68)), [3, 2, 1, 0] + list(range(67, 3, -1))]
                UB = (2, 3)

                def cinfo(c):
                    tt, hh = c // 2, c % 2
                    rows = slice(64 * hh, 64 * hh + 64)
                    cols = slice(64 * c, 64 * c + 64)
                    if c < 4:
                        pos, blk_n, blk_c0 = c, 256, 0
                    else:
                        pos, blk_n, blk_c0 = (c - 4) % 8, 512, 256 + ((c - 4) // 8) * 512
                    return tt, rows, cols, pos, blk_n, blk_c0

                def emitAU(st_):
                    ub = UB[st_ % 2]
                    for d in range(2):
                        c = orders[d][st_]
                        tt, rows, cols, pos, blk_n, blk_c0 = cinfo(c)
                        ab = d
                        mmg(ps[ab][rows, 0:64], [(KP[d][:, cols], QPP[d][:, cols])], r=["KP%d" % d, "QPP%d" % d], w=["ps%d" % ab])
                    for d in range(2):
                        c = orders[d][st_]
                        tt, rows, cols, pos, blk_n, blk_c0 = cinfo(c)
                        ubk = ((2, 6), (3, 7))[d][st_ % 2]
                        mmg(ps[ubk][:, 0:128], [(KPt[d][rows, tt, :], Vt[rows, tt, :])], r=["KPt%d" % d, "Vt"], w=["ps%d" % ubk])

                def emitMask(st_):
                    for d in range(2):
                        c = orders[d][st_]
                        tt, rows, cols, pos, blk_n, blk_c0 = cinfo(c)
                        am = Am[d][st_ % 2]
                        mk = maskf if d == 0 else maskb
                        ab = d
                        pA, kA = ps[ab], "ps%d" % ab
                        P.op("dve", (lambda e, pA=pA, rows=rows, am=am, mk=mk, d=d: e.tensor_tensor(
                            out=am[rows, :], in0=pA[rows, 0:64], in1=mk[rows, :], op=ALU.mult)),
                            r=[kA], w=["Am%d_%d" % (d, st_ % 2), kA])

                def emitO(st_):
                    for d in range(2):
                        c = orders[d][st_]
                        tt, rows, cols, pos, blk_n, blk_c0 = cinfo(c)
                        am = Am[d][st_ % 2]
                        prs = [(Vt[rows, tt, :], am[rows, :])]
                        rr = ["Vt", "Am%d_%d" % (d, st_ % 2)]
                        if st_ > 0:
                            so = (st_ - 1) % 2
                            prs.append((Sb[d][so], QP[d][:, cols]))
                            rr += ["Sb%d_%d" % (d, so), "QP%d" % d]
                        mmg(ps[4 + d][:, pos * 64:(pos + 1) * 64], prs, r=rr, w=["ps%d" % (4 + d)])

                def emitUpd(st_):
                    ub = UB[st_ % 2]
                    kU = "ps%d" % ub
                    sn, so = st_ % 2, (st_ - 1) % 2
                    for d in range(2):
                        c = orders[d][st_]
                        tt, rows, cols, pos, blk_n, blk_c0 = cinfo(c)
                        ubk = ((2, 6), (3, 7))[d][st_ % 2]
                        pU = ps[ubk][:, 0:128]
                        kU = "ps%d" % ubk
                        if st_ == 0:
                            P.op("dve", (lambda e, d=d, pU=pU: e.tensor_copy(out=Sf[d][sn], in_=pU)),
                                 r=[kU], w=["Sf%d_%d" % (d, sn), kU])
                        else:
                            P.op("dve", (lambda e, d=d, pU=pU, c=c: e.scalar_tensor_tensor(
                                out=Sf[d][sn], in0=Sf[d][so], scalar=EL[d][:, c:c + 1], in1=pU, op0=ALU.mult, op1=ALU.add)),
                                r=[kU, "Sf%d_%d" % (d, so), "EL%d" % d], w=["Sf%d_%d" % (d, sn), kU])
                        P.op("act", (lambda e, d=d: e.copy(out=Sb[d][sn], in_=Sf[d][sn])), r=["Sf%d_%d" % (d, sn)], w=["Sb%d_%d" % (d, sn)])
                        last_in_blk = (pos == (blk_n // 64 - 1)) if d == 0 else (pos == 0)
                        if last_in_blk:
                            P.op("dve", (lambda e, d=d, blk_n=blk_n, blk_c0=blk_c0: e.tensor_tensor(
                                out=OTa[:, blk_c0:blk_c0 + blk_n], in0=ps[4 + d][:, 0:blk_n], in1=OTa[:, blk_c0:blk_c0 + blk_n], op=ALU.add)),
                                 r=["ps%d" % (4 + d), "OT"], w=["OT", "ps%d" % (4 + d)])

                LOOK = 1
                if LOOK:
                    emitAU(0)
                    emitMask(0)
                for st_ in range(68):
                    if LOOK:
                        if st_ + 1 < 68:
                            emitAU(st_ + 1)
                            emitMask(st_ + 1)
                    else:
                        emitAU(st_)
                        emitMask(st_)
                    emitO(st_)
                    emitUpd(st_)
                for b in range(1, 9):
                    n = 512
                    col0 = BLOCKS[b][0] * 128
                    cs = slice(col0, col0 + n)
                    m = mh[b % 2]
                    km = "mh%d" % (b % 2)
                    P.op("act", (lambda e, cs=cs: e.activation(out=sq, in_=OTa[:, cs], func=AF.Square)), r=["OT"], w=["sq"])
                    mmg(ps[7], [(onesb, sq)], r=["sq"], w=["ps7"])
                    P.op("act", lambda e: e.activation(out=rstd, in_=ps[7], func=AF.Sqrt, scale=1.0 / 128, bias=EPS), r=["ps7"], w=["Db0", "ps7"])
                    P.op("dve", lambda e: e.reciprocal(out=rstd, in_=rstd), r=["Db0"], w=["Db0"])
                    P.op("dve", (lambda e, cs=cs: e.scalar_tensor_tensor(out=otmp, in0=OTa[:, cs], scalar=hgn, in1=rstd, op0=ALU.mult, op1=ALU.mult)),
                         r=["OT", "Db0", "hgn"], w=["E10"])
                    P.op("dve", (lambda e, cs=cs, m=m: e.tensor_tensor(out=m, in0=otmp, in1=SG[:, cs], op=ALU.mult)), r=["E10", "SG"], w=[km])
                    dma("pool", MIXD[4 + h][:, cs], m, r=[km], w=["MIXD"])
            phase_reset()
            if stop == "l1b3":
                return
            phase_outproj(1, list(range(1, 9)), False)

        def phase_outproj(l, blks, from_inputs):
            wo = T([128, 8, D], BF16)
            wosrc = din["mix_w_out"][l].rearrange("(k p) n -> p k n", p=128)
            for c in range(2):
                wload(wo[:, :, c * 512:(c + 1) * 512], wosrc[:, :, c * 512:(c + 1) * 512])
            res = RES(l, 2)
            mix = [T([128, 8, 512], BF16) for _ in range(2)]
            ypairs = [(0, 1), (2, 3), (4, 5), (6, 7)]
            yi = 0
            for b in blks:
                t0, ntl = BLOCKS[b]
                n = ntl * 128
                col0 = t0 * 128
                mx = mix[b % 2]
                kx = "mix%d" % (b % 2)
                for k in range(8):
                    dma("sp", mx[:, k, 0:n], MIXD[k][:, col0:col0 + n], r=["MIXD"], w=[kx])
                for ti in range(ntl):
                    tt = t0 + ti
                    ls = slice(ti * 128, (ti + 1) * 128)
                    y0, y1 = ypairs[yi % 4]
                    yi += 1
                    for hf, yb in ((0, y0), (1, y1)):
                        mmg(ps[yb], [(mx[:, k, ls], wo[:, k, hf * 512:(hf + 1) * 512]) for k in range(8)],
                            r=[kx, "wgt"], w=["ps%d" % yb])
                    res.run(y0, y1, x_src(from_inputs, tt), XR[tt * 128:(tt + 1) * 128, :], tt < 2, "XR1")

        phase_mod()
        phase_reset()
        S0 = ("mod", "l0a1", "l0a2", "l0mix")
        S1 = S0 + ("l0", "l1b1", "l1b2", "l1b3", "l1mix")
        if stop != "mod":
            phase_l0()
            phase_reset()
        if stop not in S0:
            phase_ffn(0, True, False)
            phase_reset()
        if stop not in S0 + ("l0",):
            phase_l1()
            phase_reset()
        if stop not in S1:
            phase_ffn(1, False, True)
            phase_reset()
        P.emit(st)
    return nc


_CACHE = {}


def kernel(**inputs):
    consts = _consts()
    if "nc" not in _CACHE:
        _CACHE["nc"] = build()
    nc = _CACHE["nc"]
    in_maps = []
    for b in range(8):
        m = {"x": np.ascontiguousarray(inputs["x"][b]), "ctx": np.ascontiguousarray(inputs["ctx"][b]),
             "cvec": np.ascontiguousarray(np.stack([inputs["c"][b], inputs["c_ctx"]], 0))}
        for n in W_NAMES:
            m[n] = np.ascontiguousarray(inputs[n])
        m.update(consts)
        in_maps.append(m)
    res = run_bass_kernel_spmd(nc, in_maps, core_ids=list(range(8)))
    return np.stack([r["out"] for r in res.results], 0).astype(np.float32)
```

```python
import numpy as np
from contextlib import ExitStack
import concourse.bass as bass
import concourse.mybir as mybir
from concourse.bass_utils import run_bass_kernel_spmd

F32 = mybir.dt.float32
BF16 = mybir.dt.bfloat16
ALU = mybir.AluOpType
AF = mybir.ActivationFunctionType

NDSEM = 8
D = 1024
NT = 34
NTOK = 4352
DFF = 2816
NJ = 22
EPS = 1e-6


class Prog:
    ENGS = ("pe", "dve", "act", "pool", "sp")

    def __init__(self, nc):
        self.nc = nc
        self.ops = []
        self.lw = {}
        self.rd = {}
        self.cnt = {e: 0 for e in self.ENGS}
        self.dcnt = {e: 0 for e in self.ENGS}
        self.dslot_last = {e: [None] * NDSEM for e in self.ENGS}
        self.last_nd = {e: None for e in self.ENGS}
        self.pending_bar = {e: [] for e in self.ENGS}

    def op(self, eng, fn, r=(), w=(), dma=False):
        oid = len(self.ops)
        deps = []
        for k in r:
            y = self.lw.get(k)
            if y is not None:
                deps.append((y, "RAW"))
        for k in w:
            y = self.lw.get(k)
            if y is not None:
                deps.append((y, "WAW"))
            for y in self.rd.get(k, ()):
                deps.append((y, "WAR"))
        for y in self.pending_bar[eng]:
            deps.append((y, "RAW"))
        self.pending_bar[eng] = []
        o = dict(id=oid, eng=eng, fn=fn, deps=deps, dma=dma)
        if dma:
            i = self.dcnt[eng]
            self.dcnt[eng] += 1
            slot = i % NDSEM
            o["dslot"] = slot
            o["dval"] = 16 * (i // NDSEM + 1)
            prev = self.dslot_last[eng][slot]
            if prev is not None:
                deps.append((prev, "RAW"))
            self.dslot_last[eng][slot] = oid
        else:
            self.cnt[eng] += 1
            o["val"] = self.cnt[eng]
            self.last_nd[eng] = oid
        self.ops.append(o)
        for k in w:
            self.lw[k] = oid
            self.rd[k] = []
        for k in r:
            if k not in w:
                self.rd.setdefault(k, []).append(oid)
        return oid

    def barrier(self):
        snap = []
        for e in self.ENGS:
            if self.last_nd[e] is not None:
                snap.append(self.last_nd[e])
            for y in self.dslot_last[e]:
                if y is not None:
                    snap.append(y)
        for e in self.ENGS:
            self.pending_bar[e] = list(snap)
        self.lw = {}
        self.rd = {}

    def emit(self, st):
        nc = self.nc
        sems = {e: st.enter_context(nc.semaphore("s_" + e)) for e in self.ENGS}
        dsems = {e: [st.enter_context(nc.semaphore("d_%s%d" % (e, i))) for i in range(NDSEM)]
                 for e in ("sp", "pool", "act") if self.dcnt[e] > 0}
        block = st.enter_context(nc.Block())
        ops = self.ops

        def run(ename, eng):
            seen = {}
            for o in ops:
                if o["eng"] != ename:
                    continue
                need = {}
                for (y, kind) in o["deps"]:
                    Y = ops[y]
                    if Y["dma"]:
                        key = ("d", Y["eng"], Y["dslot"])
                        sem = dsems[Y["eng"]][Y["dslot"]]
                        val = Y["dval"]
                    else:
                        if Y["eng"] == ename and not o["dma"]:
                            if ename == "pe":
                                continue
                        key = ("c", Y["eng"])
                        sem = sems[Y["eng"]]
                        val = Y["val"]
                    if seen.get(key, 0) >= val:
                        continue
                    if key not in need or need[key][1] < val:
                        need[key] = (sem, val)
                for key, (sem, val) in need.items():
                    eng.wait_ge(sem, val)
                    seen[key] = val
                ins = o["fn"](eng)
                if o["dma"]:
                    ins.then_inc(dsems[ename][o["dslot"]], 16)
                else:
                    ins.then_inc(sems[ename], 1)
            if ename in dsems:
                for slot in range(NDSEM):
                    y = self.dslot_last[ename][slot]
                    if y is not None:
                        Y = ops[y]
                        if seen.get(("d", ename, slot), 0) < Y["dval"]:
                            eng.wait_ge(dsems[ename][slot], Y["dval"])

        @block.tensor
        def _(e):
            run("pe", e)

        @block.vector
        def _(e):
            run("dve", e)

        @block.scalar
        def _(e):
            run("act", e)

        @block.gpsimd
        def _(e):
            run("pool", e)

        @block.sync
        def _(e):
            run("sp", e)


PW = 8 + 256 + 16 + 4096 + 8
PC0, PL0 = 8, 280


def _consts():
    c = {}
    c["ident"] = np.eye(128, dtype=np.float32)
    rot = np.zeros((128, 128), np.float32)
    for d in range(128):
        rot[d ^ 16, d] = 1.0
    c["rot"] = rot
    n = 4096
    pos_row = np.repeat(np.arange(n // 64), 64)
    pos_col = np.tile(np.arange(64), n // 64)
    inv_freq = (10000.0 ** (-np.arange(0, 32, 2, dtype=np.float32) / 32)).astype(np.float32)
    ang = np.stack([pos_row, pos_col], -1).astype(np.float32)[..., None] * inv_freq
    cs, sn = np.cos(ang).astype(np.float32), np.sin(ang).astype(np.float32)
    cos_t = np.zeros((128, n), np.float32)
    sin_t = np.zeros((128, n), np.float32)
    for d in range(128):
        dd = d % 64
        a, hf, i = dd // 32, (dd // 16) % 2, dd % 16
        cos_t[d] = cs[:, a, i]
        sin_t[d] = sn[:, a, i] * (-1.0 if hf == 0 else 1.0)
    c["cos_t"] = cos_t
    c["sin_t"] = sin_t
    c["sin2_t"] = np.ascontiguousarray(sin_t[np.arange(128) ^ 16])
    inv = np.zeros((4, PW), np.float32)
    for g, w in enumerate((2, 4, 8, 16)):
        h = w // 2
        for (n_, off) in ((256, PC0), (4096, PL0)):
            t = np.arange(n_)
            lo = np.clip(t - h, 0, n_)
            hi = np.clip(t + h, 0, n_)
            inv[g, off:off + n_] = 1.0 / (hi - lo).astype(np.float32)
    c["invcnt"] = inv
    p = np.arange(128)[:, None] % 64
    t = np.arange(64)[None, :]
    c["mask_f"] = (p <= t).astype(np.float32)
    c["mask_b"] = (p >= t).astype(np.float32)
    return c


W_NAMES = ["ada_w", "ada_b", "norm_g", "mix_w_out", "ffn_w_gate", "ffn_w_up", "ffn_conv_w",
           "ffn_conv_b", "ffn_w_down", "ev_w_in", "pool_w", "pool_scale", "diff_lambda",
           "diff_subln", "od_w_in", "mla_q_norm", "mla_w_uq", "mla_kv_norm", "mla_w_ukv",
           "hgrn_norm", "hgrn_lb"]
W_SHAPES = {"ada_w": (2, 1024, 6144), "ada_b": (2, 6144), "norm_g": (2, 4, 1024),
            "mix_w_out": (2, 1024, 1024), "ffn_w_gate": (2, 1024, 2816), "ffn_w_up": (2, 1024, 2816),
            "ffn_conv_w": (2, 3, 2816), "ffn_conv_b": (2, 2816), "ffn_w_down": (2, 2816, 1024),
            "ev_w_in": (1, 1024, 2048), "pool_w": (1, 4, 128, 128), "pool_scale": (1, 512),
            "diff_lambda": (1, 4, 64), "diff_subln": (1, 128), "od_w_in": (1, 1024, 3392),
            "mla_q_norm": (1, 512), "mla_w_uq": (1, 512, 768), "mla_kv_norm": (1, 256),
            "mla_w_ukv": (1, 256, 1024), "hgrn_norm": (1, 128), "hgrn_lb": (2, 2, 512)}
C_SHAPES = {"ident": (128, 128), "rot": (128, 128), "cos_t": (128, 4096), "sin_t": (128, 4096), "sin2_t": (128, 4096),
            "invcnt": (4, PW), "mask_f": (128, 64), "mask_b": (128, 64)}

BLOCKS = [(0, 2)] + [(2 + 4 * i, 4) for i in range(8)]


def build(stop=None, dbg=False):
    nc = bass.Bass("TRN2", target_bir_lowering=False)
    din = {}
    din["x"] = nc.dram_tensor("x", [4096, D], F32, kind="ExternalInput").ap()
    din["ctx"] = nc.dram_tensor("ctx", [256, D], F32, kind="ExternalInput").ap()
    din["cvec"] = nc.dram_tensor("cvec", [2, D], F32, kind="ExternalInput").ap()
    for n in W_NAMES:
        din[n] = nc.dram_tensor(n, list(W_SHAPES[n]), F32, kind="ExternalInput").ap()
    for n in C_SHAPES:
        din[n] = nc.dram_tensor(n, list(C_SHAPES[n]), F32, kind="ExternalInput").ap()
    out = nc.dram_tensor("out", [4096, D], F32, kind="ExternalOutput").ap()
    XR = nc.dram_tensor("XR", [NTOK, D], F32, kind="ExternalOutput" if dbg else "Internal").ap()
    MODV = nc.dram_tensor("MODV", [2, 2, 6, D], F32, kind="Internal").ap()
    QT = nc.dram_tensor("QT", [9, 128, 4, 512], BF16, kind="Internal").ap()
    QR = nc.dram_tensor("QR", [9, 128, 2, 512], BF16, kind="Internal").ap()
    UT = nc.dram_tensor("UT", [4, 128, NTOK], F32, kind="Internal").ap()
    HT1 = nc.dram_tensor("HT1", [9, 128, 8, 512], BF16, kind="Internal").ap()
    H2D = nc.dram_tensor("H2D", [128, 8, 4355], BF16, kind="Internal").ap()
    MIXD = nc.dram_tensor("MIXD", [8, 128, NTOK], BF16, kind="ExternalOutput" if dbg else "Internal").ap()

    st = ExitStack()
    with st:
        P = Prog(nc)
        AW = 52000
        arena = st.enter_context(nc.sbuf_tensor("arena", [128, AW], F32))
        psall = st.enter_context(nc.psum_tensor("psall", [128, 4096], F32))[:]
        ps = [psall[:, i * 512:(i + 1) * 512] for i in range(8)]
        psb = [p.bitcast(BF16) for p in ps]
        top = [0]

        def T(shape, dt=F32):
            n = int(np.prod(shape[1:]))
            cols = n if dt == F32 else (n + 1) // 2
            off = top[0]
            top[0] += cols
            assert top[0] <= AW, "SBUF arena overflow %d" % top[0]
            a = arena[0:shape[0], off:off + cols]
            if dt != F32:
                a = a.bitcast(dt)
            if len(shape) == 3:
                a = a.rearrange("p (a b) -> p a b", a=shape[1])
            elif len(shape) == 4:
                a = a.rearrange("p (a b c) -> p a b c", a=shape[1], b=shape[2])
            return a

        uid = [0]

        def K(s):
            uid[0] += 1
            return "%s#%d" % (s, uid[0])

        def dma(q, o, i, r=(), w=()):
            P.op(q, lambda e: e.dma_start(out=o, in_=i), r=r, w=w, dma=True)

        def dma_nc(q, o, i, r=(), w=()):
            P.op(q, lambda e: e.dma_start(out=o, in_=i, allow_slow_non_contiguous=True), r=r, w=w, dma=True)

        def mmg(o, pairs, r, w):
            def f(e):
                n = len(pairs)
                for i, (l, rh) in enumerate(pairs):
                    ins = e.matmul(o, lhsT=l, rhs=rh, start=(i == 0), stop=(i == n - 1))
                return ins
            P.op("pe", f, r=r, w=w)

        identb = T([128, 128], BF16)
        rotb = T([128, 128], BF16)
        onesb = T([128, 128], BF16)
        maskf = T([128, 64])
        maskb = T([128, 64])
        stg = [T([128, 2048]) for _ in range(2)]
        stgi = [0]
        PERSIST = None

        def wload(dst, src, q=None, ce="act"):
            i = stgi[0] % 2
            stgi[0] += 1
            shp = list(dst.shape)
            n = int(np.prod(shp[1:]))
            if n > 2048:
                hh = shp[-1] // 2
                if len(shp) == 2:
                    wload(dst[:, 0:hh], src[:, 0:hh], q, ce)
                    wload(dst[:, hh:], src[:, hh:], q, ce)
                else:
                    wload(dst[:, :, 0:hh], src[:, :, 0:hh], q, ce)
                    wload(dst[:, :, hh:], src[:, :, hh:], q, ce)
                return
            s = stg[i][0:shp[0], 0:n]
            if len(shp) == 3:
                s = s.rearrange("p (a b) -> p a b", a=shp[1])
            qq = q or ("sp" if i == 0 else "pool")
            dma(qq, s, src, w=["stg%d" % i])
            kd = "W" + str(id(dst))
            if ce == "pool":
                P.op("pool", lambda e: e.tensor_copy(out=dst, in_=s), r=["stg%d" % i], w=["wgt"])
            elif ce == "dve":
                P.op("dve", lambda e: e.tensor_copy(out=dst, in_=s), r=["stg%d" % i], w=["wgt"])
            else:
                P.op("act", lambda e: e.copy(out=dst, in_=s), r=["stg%d" % i], w=["wgt"])

        for (dst, nm) in ((identb, "ident"), (rotb, "rot")):
            wload(dst, din[nm])
        P.op("pool", lambda e: e.memset(onesb, 1.0), w=["wgt"])
        dma("sp", maskf, din["mask_f"], w=["wgt"])
        dma("sp", maskb, din["mask_b"], w=["wgt"])
        PERSIST = top[0]

        def phase_reset():
            P.barrier()
            top[0] = PERSIST

        def phase_mod():
            cv = T([128, 2, 8])
            cvs = T([128, 2, 8])
            cvb = T([128, 8, 2], BF16)
            for j in range(2):
                dma_nc("sp", cv[:, j, :], din["cvec"][j].rearrange("(k p) -> p k", p=128), w=["cv"])
            P.op("act", lambda e: e.activation(out=cvs, in_=cv, func=AF.Silu), r=["cv"], w=["cvs"])
            P.op("dve", lambda e: e.tensor_copy(out=cvb, in_=cvs.rearrange("p j k -> p k j")), r=["cvs"], w=["cvb"])
            awb = [T([128, 8, 256], BF16) for _ in range(2)]
            Mt = T([2, 6 * D])
            bt = T([2, 6 * D])
            ng = T([2, 4, D])
            V = T([2, 6, D])
            for l in range(2):
                dma("sp", bt, din["ada_b"][l].partition_broadcast(2), w=["bt"])
                dma("sp", ng.rearrange("p a b -> p (a b)"),
                    din["norm_g"][l].rearrange("a b -> (a b)").partition_broadcast(2), w=["ng"])
                for nb in range(24):
                    ab = awb[nb % 2]
                    kab = "awb%d" % (nb % 2)
                    src = din["ada_w"][l].rearrange("(k p) n -> p k n", p=128)[:, :, nb * 256:(nb + 1) * 256]
                    i = stgi[0] % 2
                    stgi[0] += 1
                    s = stg[i][:, :].rearrange("p (a b) -> p a b", a=8)
                    dma("sp" if i == 0 else "pool", s, src, w=["stg%d" % i])
                    if nb % 3 == 2:
                        P.op("dve", (lambda e, ab=ab, s=s: e.tensor_copy(out=ab, in_=s)), r=["stg%d" % i], w=[kab])
                    else:
                        P.op("act", (lambda e, ab=ab, s=s: e.copy(out=ab, in_=s)), r=["stg%d" % i], w=[kab])
                    pb = ps[nb % 2][0:2, 0:256]
                    mmg(pb, [(cvb[:, k, :], ab[:, k, :]) for k in range(8)], r=["cvb", kab], w=["ps%d" % (nb % 2)])
                    P.op("dve", (lambda e, pb=pb, nb=nb: e.tensor_tensor(out=Mt[:, nb * 256:(nb + 1) * 256], in0=pb,
                                                                       in1=bt[:, nb * 256:(nb + 1) * 256], op=ALU.add)),
                         r=["ps%d" % (nb % 2), "bt"], w=["Mt"])
                sl = lambda i: Mt[:, i * D:(i + 1) * D]
                P.op("dve", lambda e: e.scalar_tensor_tensor(out=V[:, 0, :], in0=sl(1), scalar=1.0, in1=ng[:, 0, :],
                                                             op0=ALU.add, op1=ALU.mult), r=["Mt", "ng"], w=["V"])
                P.op("dve", lambda e: e.tensor_copy(out=V[:, 1, :], in_=sl(0)), r=["Mt"], w=["V"])
                P.op("dve", lambda e: e.tensor_tensor(out=V[:, 2, :], in0=sl(2), in1=ng[:, 1, :], op=ALU.mult),
                     r=["Mt", "ng"], w=["V"])
                P.op("dve", lambda e: e.scalar_tensor_tensor(out=V[:, 3, :], in0=sl(4), scalar=1.0, in1=ng[:, 2, :],
                                                             op0=ALU.add, op1=ALU.mult), r=["Mt", "ng"], w=["V"])
                P.op("dve", lambda e: e.tensor_copy(out=V[:, 4, :], in_=sl(3)), r=["Mt"], w=["V"])
                P.op("dve", lambda e: e.tensor_tensor(out=V[:, 5, :], in0=sl(5), in1=ng[:, 3, :], op=ALU.mult),
                     r=["Mt", "ng"], w=["V"])
                dma("sp", MODV[l].rearrange("j a b -> j (a b)"), V.rearrange("p a b -> p (a b)"), r=["V"], w=["MODV"])

        def x_src(layer0_in, tt):
            if layer0_in:
                return din["ctx"][tt * 128:(tt + 1) * 128, :] if tt < 2 else din["x"][(tt - 2) * 128:(tt - 1) * 128, :]
            return XR[tt * 128:(tt + 1) * 128, :]

        class NM:
            def __init__(self, l, gi, si):
                self.xt = [T([128, D]) for _ in range(2)]
                self.junk = T([128, D], BF16)
                self.tmp = [T([128, D]) for _ in range(2)]
                self.pending = None
                self.hb = [T([128, D], BF16) for _ in range(2)]
                self.ss = T([128, 2])
                self.rs = T([128, 2])
                self.G = [T([128, D]) for _ in range(2)]
                self.SH = [T([128, D]) for _ in range(2)]
                for j in range(2):
                    dma("sp", self.G[j], MODV[l, j, gi].partition_broadcast(128), r=["MODV"], w=["nmG"])
                    dma("sp", self.SH[j], MODV[l, j, si].partition_broadcast(128), r=["MODV"], w=["nmG"])
                self.i = 0

            def run(self, src, is_ctx, dst, dkey, bank):
                i = self.i % 2
                self.i += 1
                xt, hb = self.xt[i], self.hb[i]
                kx, kh = "nm_xt%d" % i, "nm_hb%d" % i
                ss, rs = self.ss[:, i:i + 1], self.rs[:, i:i + 1]
                G, SH = self.G[1 if is_ctx else 0], self.SH[1 if is_ctx else 0]
                tmp = self.tmp[i]
                dma("sp", xt, src, r=["XR"], w=[kx])
                P.op("act", lambda e: e.activation(out=self.junk, in_=xt, func=AF.Square, accum_out=ss),
                     r=[kx], w=["nm_junk", "nm_ss%d" % i])
                P.op("act", lambda e: e.activation(out=rs, in_=ss, func=AF.Sqrt, scale=1.0 / D, bias=EPS),
                     r=["nm_ss%d" % i], w=["nm_rs%d" % i])
                P.op("dve", lambda e: e.reciprocal(out=rs, in_=rs), r=["nm_rs%d" % i], w=["nm_rs%d" % i])
                P.op("dve", lambda e: e.scalar_tensor_tensor(out=tmp, in0=xt, scalar=rs, in1=G, op0=ALU.mult,
                                                             op1=ALU.mult), r=[kx, "nm_rs%d" % i, "nmG"], w=["nm_tmp%d" % i])
                P.op("pool", lambda e: e.tensor_tensor(out=hb, in0=tmp, in1=SH, op=ALU.add),
                     r=["nm_tmp%d" % i, "nmG"], w=[kh])
                prev = self.pending
                self.pending = (hb, kh, dst, dkey, bank)
                if prev is not None:
                    self.stage_b(*prev)

            def stage_b(self, hb, kh, dst, dkey, bank):
                pb = psb[bank]

                def tr(e):
                    for k in range(8):
                        ins = e.transpose(out=pb[:, k * 128:(k + 1) * 128], in_=hb[:, k * 128:(k + 1) * 128],
                                          identity=identb)
                    return ins
                P.op("pe", tr, r=[kh], w=["ps%d" % bank])
                P.op("act", lambda e: e.copy(out=dst, in_=pb.rearrange("p (k n) -> p k n", k=8)),
                     r=["ps%d" % bank], w=[dkey])

            def flush(self):
                if self.pending is not None:
                    self.stage_b(*self.pending)
                    self.pending = None

        class RES:
            def __init__(self, l, gidx):
                self.GT = [T([128, D]) for _ in range(2)]
                for j in range(2):
                    dma("sp", self.GT[j], MODV[l, j, gidx].partition_broadcast(128), r=["MODV"], w=["resG"])
                self.xo = [T([128, D]) for _ in range(2)]
                self.tt_ = [T([128, D]) for _ in range(2)]
                self.junk = T([128, 512], BF16)
                self.ss = T([128, 4])
                self.rs = T([128, 2])
                self.i = 0

            def run(self, b0, b1, src, dstd, is_ctx, wkey):
                i = self.i % 2
                self.i += 1
                xo = self.xo[i]
                tbuf = self.tt_[i]
                kt_ = "res_t%d" % i
                kx = "res_x%d" % i
                ss = self.ss[:, 2 * i:2 * i + 2]
                rs = self.rs[:, i:i + 1]
                GT = self.GT[1 if is_ctx else 0]
                dma("pool", xo, src, r=["XR"], w=[kx])
                P.op("act", lambda e: e.activation(out=self.junk, in_=ps[b0], func=AF.Square, accum_out=ss[:, 0:1]),
                     r=["ps%d" % b0], w=["res_junk", "res_ss%d" % i])
                P.op("act", lambda e: e.activation(out=self.junk, in_=ps[b1], func=AF.Square, accum_out=ss[:, 1:2]),
                     r=["ps%d" % b1], w=["res_junk", "res_ss%d" % i])
                P.op("dve", lambda e: e.tensor_tensor(out=rs, in0=ss[:, 0:1], in1=ss[:, 1:2], op=ALU.add),
                     r=["res_ss%d" % i], w=["res_rs%d" % i])
                P.op("act", lambda e: e.activation(out=rs, in_=rs, func=AF.Sqrt, scale=1.0 / D, bias=EPS),
                     r=["res_rs%d" % i], w=["res_rs%d" % i])
                P.op("dve", lambda e: e.reciprocal(out=rs, in_=rs), r=["res_rs%d" % i], w=["res_rs%d" % i])
                for hf, bk in ((0, b0), (1, b1)):
                    P.op("dve", (lambda e, hf=hf, bk=bk: e.scalar_tensor_tensor(
                        out=tbuf[:, hf * 512:(hf + 1) * 512], in0=ps[bk], scalar=rs,
                        in1=GT[:, hf * 512:(hf + 1) * 512], op0=ALU.mult, op1=ALU.mult)),
                        r=["ps%d" % bk, "res_rs%d" % i, "resG"], w=[kt_, "ps%d" % bk])
                P.op("pool", lambda e: e.tensor_tensor(out=xo, in0=tbuf, in1=xo, op=ALU.add),
                     r=[kt_, kx], w=[kx])
                dma("pool", dstd, xo, r=[kx], w=[wkey])

        def rope_flush(ro):
            pend = ro[6]
            if pend:
                pend.pop()()

        def rope_evict(pbank, n, t0, dst, dkey, ro, is_ctx):
            src = ps[pbank][:, 0:n]
            if is_ctx:
                rope_flush(ro)
                P.op("act", lambda e: e.copy(out=dst, in_=src), r=["ps%d" % pbank], w=[dkey])
                return
            t1s, t2s, cosb, sinb, rbs, cnt, pend = ro
            i = cnt[0] % 2
            cnt[0] += 1
            t1, t2, rb = t1s[i], t2s[i], rbs[i]
            rope_flush(ro)
            P.op("dve", lambda e: e.tensor_tensor(out=t1[:, 0:n], in0=src, in1=cosb[:, 0:n], op=ALU.mult),
                 r=["ps%d" % pbank, "ro_cs"], w=["ro_t1%d" % i])
            P.op("dve", lambda e: e.tensor_tensor(out=t2[:, 0:n], in0=src, in1=sinb[:, 0:n], op=ALU.mult),
                 r=["ps%d" % pbank, "ro_cs"], w=["ro_t2%d" % i, "ps%d" % pbank])

            def part2():
                mmg(ps[rb][:, 0:n], [(identb, t1[:, 0:n]), (rotb, t2[:, 0:n])], r=["ro_t1%d" % i, "ro_t2%d" % i], w=["ps%d" % rb])
                P.op("act", lambda e: e.copy(out=dst, in_=ps[rb][:, 0:n]), r=["ps%d" % rb], w=[dkey, "ps%d" % rb])
            pend.append(part2)

        def rope_tiles(rbs):
            return ([T([128, 512], BF16) for _ in range(2)], [T([128, 512], BF16) for _ in range(2)],
                    T([128, 512]), T([128, 512]), rbs, [0], [])

        def rope_load(ro, b):
            if b == 0:
                return
            c0 = (b - 1) * 512
            dma("pool", ro[2], din["cos_t"][:, c0:c0 + 512], w=["ro_cs"])
            dma("pool", ro[3], din["sin2_t"][:, c0:c0 + 512], w=["ro_cs"])

        def attention(groups, scale, accsets):
            PT = [T([128, 2, 512], BF16) for _ in range(3)]
            N = len(groups)

            def emitS(i):
                g = groups[i]
                if g.get("pre"):
                    g["pre"]()
                A = 2 * (i % 2)
                n = g["n"]
                for mi, m in enumerate(g["members"]):
                    mmg(ps[A + mi][:, 0:n], m["qk"], r=g["rk"], w=["ps%d" % (A + mi)])

            def emitE(i):
                g = groups[i]
                A = 2 * (i % 2)
                n = g["n"]
                nm_ = len(g["members"])
                pt = PT[i % 3]
                src = psall[:, A * 512:(A + 2) * 512].rearrange("p (j n) -> p j n", j=2)[:, 0:nm_, 0:n]
                P.op("act", (lambda e: e.activation(out=pt[:, 0:nm_, 0:n], in_=src, func=AF.Exp, scale=scale)),
                     r=["ps%d" % A, "ps%d" % (A + 1)], w=["PT%d" % (i % 3), "ps%d" % A, "ps%d" % (A + 1)])

            def emitPV(i):
                g = groups[i]
                n = g["n"]
                pt = PT[i % 3]
                acc = accsets[g["accset"]]
                wk = []
                for m in g["members"]:
                    ob, sb2 = acc[m["acc"]]
                    wk += ["ps%d" % ob, "ps%d" % sb2]

                def pv(e):
                    for mi, m in enumerate(g["members"]):
                        ob, sb2 = acc[m["acc"]]
                        e.matmul(ps[ob][:, 0:n], lhsT=m["v"], rhs=pt[:, mi, 0:n], start=m["start"], stop=m["stop"])
                        ins = e.matmul(ps[sb2][:, 0:n], lhsT=onesb, rhs=pt[:, mi, 0:n], start=m["start"], stop=m["stop"])
                    return ins
                P.op("pe", pv, r=["PT%d" % (i % 3)] + g["rv"], w=list(dict.fromkeys(wk)))
                if g.get("post"):
                    A = 2 * (i % 2)
                    g["post"](acc, (A, A + 1))

            if N:
                emitS(0)
            for i in range(N):
                emitE(i)
                if i + 1 < N:
                    emitS(i + 1)
                emitPV(i)

        def phase_ffn(l, need_ctx, final):
            W2 = 4355
            Wd = T([128, NJ, D], BF16)
            CW = T([128, 4, NJ])
            for i in range(3):
                dma_nc("sp", CW[:, i, :], din["ffn_conv_w"][l, i].rearrange("(j p) -> p j", p=128), w=["CW"])
            dma_nc("sp", CW[:, 3, :], din["ffn_conv_b"][l].rearrange("(j p) -> p j", p=128), w=["CW"])
            wdsrc = din["ffn_w_down"][l].rearrange("(j p) n -> p j n", p=128)
            for j0 in range(0, NJ, 4):
                j1 = min(NJ, j0 + 4)
                wload(Wd[:, j0:j1, :], wdsrc[:, j0:j1, :], None, "pool")
            top_save = top[0]
            nm = NM(l, 3, 4)
            hTs = [T([128, 8, 512], BF16) for _ in range(2)]
            zt = T([128, 8, 1], BF16)
            P.op("pool", lambda e: e.memset(zt, 0.0), w=["zt"])
            for c in (0, 257, 4354):
                dma_nc("pool", H2D[:, :, c:c + 1], zt, r=["zt"], w=["H2D"])
            blks = list(range(0 if need_ctx else 1, 9))
            for b in blks:
                t0, ntl = BLOCKS[b]
                n = ntl * 128
                hT = hTs[b % 2]
                kh = "hTs%d" % (b % 2)
                for ti in range(ntl):
                    tt = t0 + ti
                    nm.run(x_src(False, tt), tt < 2, hT[:, :, ti * 128:(ti + 1) * 128], kh, 6 + tt % 2)
                nm.flush()
                c0 = 1 if b == 0 else 258 + (b - 1) * 512
                dma("pool", H2D[:, :, c0:c0 + n], hT[:, :, 0:n], r=[kh], w=["H2D"])
            P.barrier()
            top[0] = top_save
            res = RES(l, 5)
            GTt = T([128, NJ, 1024], BF16)
            H2P = [T([128, 8, 1026], BF16) for _ in range(2)]
            wgf = [T([128, 8, 128], BF16) for _ in range(2)]
            wuf = [T([128, 8, 128], BF16) for _ in range(2)]
            acc = [T([128, 512]) for _ in range(2)]
            sil = [T([128, 512]) for _ in range(2)]
            parts = [blks[i:i + 2] for i in range(0, len(blks), 2)]
            wgsrc = din["ffn_w_gate"][l].rearrange("(k p) n -> p k n", p=128)
            wusrc = din["ffn_w_up"][l].rearrange("(k p) n -> p k n", p=128)
            it = [0]
            yi = [0]
            bc0 = lambda b: 1 if b == 0 else 258 + (b - 1) * 512
            for pi, part in enumerate(parts):
                cstart = bc0(part[0]) - 1
                cend = bc0(part[-1]) + BLOCKS[part[-1]][1] * 128 + 1
                npc = cend - cstart
                H2T = H2P[pi % 2]
                kH = "H2P%d" % (pi % 2)
                dma("sp", H2T[:, :, 0:npc], H2D[:, :, cstart:cend], r=["H2D"], w=[kH])
                goff = {}
                o = 0
                for b in part:
                    goff[b] = o
                    o += BLOCKS[b][1] * 128
                for j in range(NJ):
                    wg, wu = wgf[j % 2], wuf[j % 2]
                    kw = "ffw%d" % (j % 2)
                    for (dst, srcw) in ((wg, wgsrc), (wu, wusrc)):
                        i = stgi[0] % 2
                        stgi[0] += 1
                        s = stg[i][:, 0:1024].rearrange("p (a b) -> p a b", a=8)
                        dma("sp", s, srcw[:, :, j * 128:(j + 1) * 128], w=["stg%d" % i])
                        P.op("pool", (lambda e, dst=dst, s=s: e.tensor_copy(out=dst, in_=s)), r=["stg%d" % i], w=[kw])
                    for b in part:
                        t0, ntl = BLOCKS[b]
                        n = ntl * 128
                        c0 = bc0(b) - cstart
                        q = it[0] % 2
                        ub_ = (2, 3, 5)[it[0] % 3]
                        it[0] += 1
                        pa, pu, ph = ps[q], ps[ub_], ps[4]
                        ka, ku, kh = "ps%d" % q, "ps%d" % ub_, "ps4"
                        mmg(pa[:, 0:n], [(wg[:, k, :], H2T[:, k, c0:c0 + n]) for k in range(8)], r=[kw, kH], w=[ka])
                        hal = H2T[:, :, c0 - 1:c0 + n + 1:n + 1]
                        mmg(ph[:, 0:2], [(wg[:, k, :], hal[:, k, :]) for k in range(8)], r=[kw, kH], w=[kh])
                        mmg(pu[:, 0:n], [(wu[:, k, :], H2T[:, k, c0:c0 + n]) for k in range(8)], r=[kw, kH], w=[ku])
                        ac, sl_ = acc[q], sil[q]
                        kac, ksl = "acc%d" % q, "sil%d" % q
                        w0, w1, w2, bb = (CW[:, i, j:j + 1] for i in range(4))
                        P.op("dve", (lambda e, ac=ac, pa=pa, w1=w1, bb=bb, n=n: e.tensor_scalar(
                            out=ac[:, 0:n], in0=pa[:, 0:n], scalar1=w1, scalar2=bb, op0=ALU.mult, op1=ALU.add)),
                            r=[ka, "CW"], w=[kac])
                        P.op("dve", (lambda e, ac=ac, ph=ph, w0=w0: e.scalar_tensor_tensor(
                            out=ac[:, 0:1], in0=ph[:, 0:1], scalar=w0, in1=ac[:, 0:1], op0=ALU.mult, op1=ALU.add)),
                            r=[kh, kac], w=[kac])
                        P.op("dve", (lambda e, ac=ac, ph=ph, w2=w2, n=n: e.scalar_tensor_tensor(
                            out=ac[:, n - 1:n], in0=ph[:, 1:2], scalar=w2, in1=ac[:, n - 1:n], op0=ALU.mult, op1=ALU.add)),
                            r=[kh, kac], w=[kac, kh])
                        P.op("dve", (lambda e, ac=ac, pa=pa, w0=w0, n=n: e.scalar_tensor_tensor(
                            out=ac[:, 1:n], in0=pa[:, 0:n - 1], scalar=w0, in1=ac[:, 1:n], op0=ALU.mult, op1=ALU.add)),
                            r=[ka, kac], w=[kac])
                        P.op("dve", (lambda e, ac=ac, pa=pa, w2=w2, n=n: e.scalar_tensor_tensor(
                            out=ac[:, 0:n - 1], in0=pa[:, 1:n], scalar=w2, in1=ac[:, 0:n - 1], op0=ALU.mult, op1=ALU.add)),
                            r=[ka, kac], w=[kac, ka])
                        P.op("act", (lambda e, ac=ac, sl_=sl_, n=n: e.activation(out=sl_[:, 0:n], in_=ac[:, 0:n], func=AF.Silu)),
                             r=[kac], w=[ksl])
                        g0 = goff[b]
                        P.op("dve", (lambda e, sl_=sl_, pu=pu, j=j, g0=g0, n=n: e.tensor_tensor(
                            out=GTt[:, j, g0:g0 + n], in0=sl_[:, 0:n], in1=pu[:, 0:n], op=ALU.mult)),
                            r=[ksl, ku], w=["GT", ku])
                for b in part:
                    t0, ntl = BLOCKS[b]
                    for ti in range(ntl):
                        tt = t0 + ti
                        g0 = goff[b] + ti * 128
                        y0, y1 = ((6, 7), (0, 1), (2, 3))[yi[0] % 3]
                        yi[0] += 1
                        for hf, yb in ((0, y0), (1, y1)):
                            mmg(ps[yb], [(GTt[:, j, g0:g0 + 128], Wd[:, j, hf * 512:(hf + 1) * 512]) for j in range(NJ)],
                                r=["GT", "wgt"], w=["ps%d" % yb])
                        if final:
                            dstd = out[(tt - 2) * 128:(tt - 1) * 128, :]
                            res.run(y0, y1, x_src(False, tt), dstd, tt < 2, "OUT")
                        else:
                            res.run(y0, y1, x_src(False, tt), XR[tt * 128:(tt + 1) * 128, :], tt < 2, "XR2")

        def phase_l0():
            l = 0
            LAM_INIT = 0.2
            KT = T([128, 4, NTOK], BF16)
            Vv = T([128, NT, 512], BF16)
            keep = top[0]
            w_in = T([128, 8, 2048], BF16)
            wsrc = din["ev_w_in"][0].rearrange("(k p) n -> p k n", p=128)
            for c in range(4):
                wload(w_in[:, :, c * 512:(c + 1) * 512], wsrc[:, :, c * 512:(c + 1) * 512])
            nm = NM(l, 0, 1)
            hT = T([128, 8, 512], BF16)
            ro = rope_tiles((7, 5))
            ub = [T([128, 512]) for _ in range(2)]
            qb = [T([128, 4, 512], BF16) for _ in range(2)]
            for b, (t0, ntl) in enumerate(BLOCKS):
                n = ntl * 128
                col0 = t0 * 128
                rope_load(ro, b)
                for ti in range(ntl):
                    tt = t0 + ti
                    nm.run(x_src(True, tt), tt < 2, hT[:, :, ti * 128:(ti + 1) * 128], "hT", 6)
                nm.flush()
                for ti in range(ntl):
                    tt = t0 + ti
                    bk = 4 + (ti % 2)
                    mmg(ps[bk], [(hT[:, k, ti * 128:(ti + 1) * 128], w_in[:, k, 1536:2048]) for k in range(8)],
                        r=["hT", "wgt"], w=["ps%d" % bk])
                    P.op("act", (lambda e, tt=tt, bk=bk: e.copy(out=Vv[:, tt, :], in_=ps[bk])), r=["ps%d" % bk], w=["Vv"])
                for c in range(4):
                    bk = c % 2
                    mmg(ps[bk][:, 0:n], [(w_in[:, k, c * 128:(c + 1) * 128], hT[:, k, 0:n]) for k in range(8)],
                        r=["hT", "wgt"], w=["ps%d" % bk])
                    u = ub[c % 2]
                    P.op("act", (lambda e, u=u, bk=bk, n=n: e.copy(out=u[:, 0:n], in_=ps[bk][:, 0:n])),
                         r=["ps%d" % bk], w=["ub%d" % (c % 2)])
                    dma("pool", UT[c][:, col0:col0 + n], u[:, 0:n], r=["ub%d" % (c % 2)], w=["UT"])
                qbb = qb[b % 2]
                kq = "qb%d" % (b % 2)
                for c in range(4):
                    bk = 2 + (c % 2)
                    mmg(ps[bk][:, 0:n], [(w_in[:, k, 512 + c * 128:512 + (c + 1) * 128], hT[:, k, 0:n]) for k in range(8)],
                        r=["hT", "wgt"], w=["ps%d" % bk])
                    rope_evict(bk, n, col0, qbb[:, c, 0:n], kq, ro, b == 0)
                for c in range(4):
                    bk = 2 + (c % 2)
                    mmg(ps[bk][:, 0:n], [(w_in[:, k, 1024 + c * 128:1024 + (c + 1) * 128], hT[:, k, 0:n]) for k in range(8)],
                        r=["hT", "wgt"], w=["ps%d" % bk])
                    rope_evict(bk, n, col0, KT[:, c, col0:col0 + n], "KT", ro, b == 0)
                rope_flush(ro)
                dma("pool", QT[b][:, :, 0:n], qbb[:, :, 0:n], r=[kq], w=["QT"])
            P.barrier()
            top[0] = keep
            if stop == "l0a1":
                return
            pw = T([128, 4, 128], BF16)
            for g in range(4):
                wload(pw[:, g, :], din["pool_w"][0, g])
            psc = T([128, 4])
            dma_nc("sp", psc, din["pool_scale"][0].rearrange("(g p) -> p g", p=128), w=["psc"])
            UP = T([128, PW])
            Aa = T([128, PW])
            Ab = T([128, PW])
            IC = T([128, PW])
            dT = T([128, PW], BF16)
            mpo = [T([128, 512], BF16) for _ in range(2)]
            for g in range(4):
                hw = (1, 2, 4, 8)[g]
                P.op("pool", lambda e: e.memset(UP, 0.0), w=["UP"])
                dma("sp", UP[:, PC0:PC0 + 256], UT[g][:, 0:256], r=["UT"], w=["UP"])
                dma("sp", UP[:, PL0:PL0 + 4096], UT[g][:, 256:NTOK], r=["UT"], w=["UP"])
                dma("pool", IC, din["invcnt"][g].partition_broadcast(128), w=["IC"])
                cur, ck = UP, "UP"
                bufs = [(Aa, "Aa"), (Ab, "Ab")]
                width = PW
                for s in range(g + 1):
                    sh = 1 << s
                    nxt, nk = bufs[s % 2]
                    width -= sh
                    P.op("dve", (lambda e, cur=cur, nxt=nxt, sh=sh, width=width: e.tensor_tensor(
                        out=nxt[:, 0:width], in0=cur[:, 0:width], in1=cur[:, sh:sh + width], op=ALU.add)),
                        r=[ck], w=[nk])
                    cur, ck = nxt, nk
                oth, ok = bufs[(g + 1) % 2]
                P.op("dve", (lambda e, cur=cur, oth=oth, hw=hw: e.tensor_tensor(
                    out=oth[:, 8:PW - 8], in0=cur[:, 8 - hw:PW - 8 - hw], in1=IC[:, 8:PW - 8], op=ALU.mult)),
                    r=[ck, "IC"], w=[ok])
                P.op("pool", (lambda e, oth=oth: e.tensor_tensor(out=dT[:, 8:PW - 8], in0=oth[:, 8:PW - 8],
                                                                 in1=UP[:, 8:PW - 8], op=ALU.subtract)),
                     r=[ok, "UP"], w=["dT"])
                for b, (t0, ntl) in enumerate(BLOCKS):
                    n = ntl * 128
                    col0 = t0 * 128
                    pc = PC0 if b == 0 else PL0 + (b - 1) * 512
                    bk = b % 2
                    mmg(ps[bk][:, 0:n], [(pw[:, g, :], dT[:, pc:pc + n])], r=["dT", "wgt"], w=["ps%d" % bk])
                    mp = mpo[b % 2]
                    P.op("act", (lambda e, bk=bk, n=n, g=g, mp=mp: e.activation(
                        out=mp[:, 0:n], in_=ps[bk][:, 0:n], func=AF.Identity, scale=psc[:, g:g + 1])),
                        r=["ps%d" % bk, "psc"], w=["mpo%d" % (b % 2)])
                    dma("pool", MIXD[g][:, col0:col0 + n], mp[:, 0:n], r=["mpo%d" % (b % 2)], w=["MIXD"])
            P.barrier()
            top[0] = keep
            if stop == "l0a2":
                return
            lamt = T([128, 4, 64])
            lj = T([128, 64])
            lsum = T([128, 2])
            nlam = T([128, 1])
            dma("sp", lamt.rearrange("p a b -> p (a b)"),
                din["diff_lambda"][0].rearrange("a b -> (a b)").partition_broadcast(128), w=["lamt"])
            for i in range(2):
                P.op("dve", (lambda e, i=i: e.scalar_tensor_tensor(out=lj, in0=lamt[:, 2 * i, :], scalar=1.0,
                                                                   in1=lamt[:, 2 * i + 1, :], op0=ALU.mult, op1=ALU.mult,
                                                                   accum_out=lsum[:, i:i + 1])),
                     r=["lamt"], w=["lj", "lsum"])
            P.op("act", lambda e: e.activation(out=lsum, in_=lsum, func=AF.Exp), r=["lsum"], w=["lsum"])
            P.op("dve", lambda e: e.tensor_tensor(out=nlam, in0=lsum[:, 1:2], in1=lsum[:, 0:1], op=ALU.subtract),
                 r=["lsum"], w=["nlam"])
            P.op("dve", lambda e: e.tensor_scalar(out=nlam, in0=nlam, scalar1=-LAM_INIT, scalar2=None, op0=ALU.add),
                 r=["nlam"], w=["nlam"])
            sln = T([128, 1])
            dma_nc("sp", sln, din["diff_subln"][0].rearrange("(p o) -> p o", o=1), w=["sln"])
            P.op("dve", lambda e: e.tensor_scalar(out=sln, in0=sln, scalar1=1.0 - LAM_INIT, scalar2=None, op0=ALU.mult),
                 r=["sln"], w=["sln"])
            qtl = [T([128, 4, 512], BF16) for _ in range(3)]
            MIXA = [T([128, 4, 512], BF16) for _ in range(2)]
            rsum = T([128, 512])
            o2 = [T([128, 512]) for _ in range(2)]
            oc = T([128, 512])
            sq = T([128, 512], BF16)
            rstd = T([128, 512])
            state = {}

            def pre(b):
                if state.get("b") != b:
                    state["b"] = b
                    n = BLOCKS[b][1] * 128
                    col0 = BLOCKS[b][0] * 128
                    dma("sp", qtl[b % 3][:, :, 0:n], QT[b][:, :, 0:n], r=["QT"], w=["qtl%d" % (b % 3)])

            def post(b, h, n, acc, scr):
                mixa = MIXA[b % 2]
                km = "MIXA%d" % (b % 2)
                for j in range(2):
                    ob, sb2 = acc[j]
                    P.op("dve", (lambda e, sb2=sb2: e.reciprocal(out=rsum[:, 0:n], in_=ps[sb2][:, 0:n])),
                         r=["ps%d" % sb2], w=["rsum", "ps%d" % sb2])
                    P.op("dve", (lambda e, ob=ob, j=j: e.tensor_tensor(out=o2[j][:, 0:n], in0=ps[ob][:, 0:n], in1=rsum[:, 0:n], op=ALU.mult)),
                         r=["ps%d" % ob, "rsum"], w=["o2_%d" % j, "ps%d" % ob])
                P.op("dve", lambda e: e.scalar_tensor_tensor(out=oc[:, 0:n], in0=o2[1][:, 0:n], scalar=nlam,
                                                             in1=o2[0][:, 0:n], op0=ALU.mult, op1=ALU.add),
                     r=["o2_0", "o2_1", "nlam"], w=["oc"])
                P.op("act", lambda e: e.activation(out=sq[:, 0:n], in_=oc[:, 0:n], func=AF.Square), r=["oc"], w=["sq"])
                sbk = scr[0]
                mmg(ps[sbk][:, 0:n], [(onesb, sq[:, 0:n])], r=["sq"], w=["ps%d" % sbk])
                P.op("act", lambda e: e.activation(out=rstd[:, 0:n], in_=ps[sbk][:, 0:n], func=AF.Sqrt, scale=1.0 / 128,
                                                   bias=EPS), r=["ps%d" % sbk], w=["rstd", "ps%d" % sbk])
                P.op("dve", lambda e: e.reciprocal(out=rstd[:, 0:n], in_=rstd[:, 0:n]), r=["rstd"], w=["rstd"])
                P.op("dve", lambda e: e.scalar_tensor_tensor(out=mixa[:, h, 0:n], in0=oc[:, 0:n], scalar=sln,
                                                             in1=rstd[:, 0:n], op0=ALU.mult, op1=ALU.mult),
                     r=["oc", "rstd", "sln"], w=[km])
                col0 = BLOCKS[b][0] * 128
                dma("pool", MIXD[4 + h][:, col0:col0 + n], mixa[:, h, 0:n], r=[km], w=["MIXD"])

            groups = []
            for b in range(9):
                n = BLOCKS[b][1] * 128
                kts = list(range(2)) if b == 0 else list(range(NT))
                for h in range(4):
                    for ki, kt in enumerate(kts):
                        mem = []
                        for j in range(2):
                            pr = slice(64 * j, 64 * j + 64)
                            mem.append(dict(qk=[(KT[pr, h, kt * 128:(kt + 1) * 128], qtl[b % 3][pr, h, 0:n])],
                                            v=Vv[:, kt, h * 128:(h + 1) * 128], acc=j, start=(ki == 0), stop=(ki == len(kts) - 1)))
                        g = dict(n=n, members=mem, rk=["KT", "qtl%d" % (b % 3)], rv=["Vv"], accset=0)
                        if h == 0 and ki == 0:
                            g["pre"] = (lambda b=b: (pre(b), pre(b + 1) if b + 1 < 9 else None))
                        if ki == len(kts) - 1:
                            g["post"] = (lambda acc, scr, b=b, h=h, n=n: post(b, h, n, acc, scr))
                        groups.append(g)
            attention(groups, 0.125, [[(4, 5), (6, 7)]])
            phase_reset()
            phase_outproj(0, list(range(9)), True)

        def rownorm(pt, W, NB, dst, dkey, tg):
            junk, ss, rs = tg
            P.op("act", lambda e: e.activation(out=junk[:, 0:W], in_=pt[0][:, 0:W], func=AF.Square, accum_out=ss),
                 r=[pt[1]], w=["rn_junk", "rn_ss"])
            P.op("act", lambda e: e.activation(out=rs, in_=ss, func=AF.Sqrt, scale=1.0 / W, bias=EPS), r=["rn_ss"], w=["rn_rs"])
            P.op("dve", lambda e: e.reciprocal(out=rs, in_=rs), r=["rn_rs"], w=["rn_rs"])
            P.op("dve", lambda e: e.scalar_tensor_tensor(out=dst, in0=pt[0][:, 0:W], scalar=rs, in1=NB[:, 0:W], op0=ALU.mult,
                                                         op1=ALU.mult), r=[pt[1], "rn_rs", "wgt"], w=[dkey, pt[1]])

        def phase_l1():
            l = 1
            KN = T([128, 4, NTOK], BF16)
            VM = T([128, NT, 512], BF16)
            KR2 = T([128, NTOK], BF16)
            keep = top[0]
            WA = T([128, 8, 512], BF16)
            WB = T([128, 8, 256], BF16)
            WKR = T([128, 8, 128], BF16)
            WQN = T([128, 4, 4, 128], BF16)
            WQR = T([128, 4, 256], BF16)
            WKN = T([128, 2, 4, 128], BF16)
            WVV = T([128, 2, 512], BF16)
            wsrc = din["od_w_in"][0].rearrange("(k p) n -> p k n", p=128)
            wload(WA, wsrc[:, :, 0:512])
            wload(WB, wsrc[:, :, 512:768])
            wload(WKR[:, :, 0:64], wsrc[:, :, 768:832])
            wload(WKR[:, :, 64:128], wsrc[:, :, 768:832])
            uq = din["mla_w_uq"][0].rearrange("(k p) (h c) -> p k h c", p=128, c=192)
            ukv = din["mla_w_ukv"][0].rearrange("(k p) (h c) -> p k h c", p=128, c=256)
            for k in range(4):
                wload(WQN[:, k], uq[:, k, :, 0:128])
                wload(WQR[:, k, :].rearrange("p (h c) -> p h c", h=4), uq[:, k, :, 128:192])
            for k in range(2):
                wload(WKN[:, k], ukv[:, k, :, 0:128])
                wload(WVV[:, k, :].rearrange("p (h c) -> p h c", h=4), ukv[:, k, :, 128:256])
            QNb = T([128, 512])
            KVNb = T([128, 256])
            dma("sp", QNb, din["mla_q_norm"][0].partition_broadcast(128), w=["wgt"])
            dma("sp", KVNb, din["mla_kv_norm"][0].partition_broadcast(128), w=["wgt"])
            nm = NM(l, 0, 1)
            hTs = [T([128, 8, 512], BF16)] * 2
            ro = rope_tiles((7, 1))
            tg = (T([128, 512], BF16), T([128, 1]), T([128, 1]))
            cqn = T([128, 512], BF16)
            ckvn = T([128, 256], BF16)
            cqT = T([128, 4, 512], BF16)
            ckvT = T([128, 2, 512], BF16)
            qnb = [T([128, 4, 512], BF16)] * 2
            qrb = [T([128, 2, 512], BF16)] * 2
            for b, (t0, ntl) in enumerate(BLOCKS):
                n = ntl * 128
                col0 = t0 * 128
                hT = hTs[b % 2]
                khT = "hTs0"
                rope_load(ro, b)
                for ti in range(ntl):
                    tt = t0 + ti
                    nm.run(x_src(False, tt), tt < 2, hT[:, :, ti * 128:(ti + 1) * 128], khT, 6)
                nm.flush()
                dma("pool", HT1[b][:, :, 0:n], hT[:, :, 0:n], r=[khT], w=["HT1"])
                for ti in range(ntl):
                    ts_ = slice(ti * 128, (ti + 1) * 128)
                    mmg(ps[0], [(hT[:, k, ts_], WA[:, k, :]) for k in range(8)], r=[khT, "wgt"], w=["ps0"])
                    rownorm((ps[0], "ps0"), 512, QNb, cqn, "cqn", tg)

                    def trq(e):
                        for k in range(4):
                            ins = e.transpose(out=psb[2][:, k * 128:(k + 1) * 128], in_=cqn[:, k * 128:(k + 1) * 128], identity=identb)
                        return ins
                    P.op("pe", trq, r=["cqn"], w=["ps2"])
                    P.op("act", (lambda e, ts_=ts_: e.copy(out=cqT[:, :, ts_], in_=psb[2][:, 0:512].rearrange("p (k n) -> p k n", k=4))),
                         r=["ps2"], w=["cqT"])
                    mmg(ps[1][:, 0:256], [(hT[:, k, ts_], WB[:, k, :]) for k in range(8)], r=[khT, "wgt"], w=["ps1"])
                    rownorm((ps[1], "ps1"), 256, KVNb, ckvn, "ckvn", tg)

                    def trk(e):
                        for k in range(2):
                            ins = e.transpose(out=psb[3][:, k * 128:(k + 1) * 128], in_=ckvn[:, k * 128:(k + 1) * 128], identity=identb)
                        return ins
                    P.op("pe", trk, r=["ckvn"], w=["ps3"])
                    P.op("act", (lambda e, ts_=ts_: e.copy(out=ckvT[:, :, ts_], in_=psb[3][:, 0:256].rearrange("p (k n) -> p k n", k=2))),
                         r=["ps3"], w=["ckvT"])
                for ti in range(ntl):
                    tt = t0 + ti
                    ts_ = slice(ti * 128, (ti + 1) * 128)
                    mmg(ps[0], [(ckvT[:, k, ts_], WVV[:, k, :]) for k in range(2)], r=["ckvT", "wgt"], w=["ps0"])
                    P.op("act", (lambda e, tt=tt: e.copy(out=VM[:, tt, :], in_=ps[0])), r=["ps0"], w=["VM", "ps0"])
                for h in range(4):
                    bk = 4 + (h % 2)
                    mmg(ps[bk][:, 0:n], [(WKN[:, k, h, :], ckvT[:, k, 0:n]) for k in range(2)], r=["ckvT", "wgt"], w=["ps%d" % bk])
                    P.op("act", (lambda e, h=h, bk=bk, n=n, col0=col0: e.copy(out=KN[:, h, col0:col0 + n], in_=ps[bk][:, 0:n])),
                         r=["ps%d" % bk], w=["KN", "ps%d" % bk])
                mmg(ps[4][:, 0:n], [(WKR[:, k, :], hT[:, k, 0:n]) for k in range(8)], r=[khT, "wgt"], w=["ps4"])
                rope_evict(4, n, col0, KR2[:, col0:col0 + n], "KR2", ro, b == 0)
                if b == 0:
                    rope_flush(ro)
                if b >= 1:
                    qn, qr = qnb[b % 2], qrb[b % 2]
                    for h in range(4):
                        bk = 4 + (h % 2)
                        mmg(ps[bk][:, 0:n], [(WQN[:, k, h, :], cqT[:, k, 0:n]) for k in range(4)], r=["cqT", "wgt"], w=["ps%d" % bk])
                        P.op("act", (lambda e, h=h, bk=bk, n=n, qn=qn: e.copy(out=qn[:, h, 0:n], in_=ps[bk][:, 0:n])),
                             r=["ps%d" % bk], w=["qnb0", "ps%d" % bk])
                    dma("pool", QT[b][:, :, 0:n], qn[:, :, 0:n], r=["qnb0"], w=["QT"])
                    for c in range(2):
                        bk = 4 + (c % 2)
                        mmg(ps[bk][:, 0:n], [(WQR[:, k, c * 128:(c + 1) * 128], cqT[:, k, 0:n]) for k in range(4)],
                            r=["cqT", "wgt"], w=["ps%d" % bk])
                        rope_evict(bk, n, col0, qr[:, c, 0:n], "qrb0", ro, False)
                    rope_flush(ro)
                    dma("pool", QR[b][:, :, 0:n], qr[:, :, 0:n], r=["qrb0"], w=["QR"])
            P.barrier()
            top[0] = keep
            if stop == "l1b1":
                return
            qnl = [T([128, 4, 512], BF16) for _ in range(3)]
            qrl = [T([128, 2, 512], BF16) for _ in range(3)]
            rsum = T([128, 512])
            mo = [T([128, 512], BF16) for _ in range(2)]
            state = {}

            def pre(b):
                if state.get("b") != b:
                    state["b"] = b
                    n = BLOCKS[b][1] * 128
                    dma("sp", qnl[b % 3][:, :, 0:n], QT[b][:, :, 0:n], r=["QT"], w=["qnl%d" % (b % 3)])
                    dma("sp", qrl[b % 3][:, :, 0:n], QR[b][:, :, 0:n], r=["QR"], w=["qnl%d" % (b % 3)])

            def post(b, h, n, acc, scr):
                col0 = BLOCKS[b][0] * 128
                m = mo[h % 2]
                km = "mo%d" % (h % 2)
                ob, sb2 = acc[0]
                P.op("dve", lambda e: e.reciprocal(out=rsum[:, 0:n], in_=ps[sb2][:, 0:n]), r=["ps%d" % sb2], w=["rsum", "ps%d" % sb2])
                P.op("dve", lambda e: e.tensor_tensor(out=m[:, 0:n], in0=ps[ob][:, 0:n], in1=rsum[:, 0:n], op=ALU.mult),
                     r=["ps%d" % ob, "rsum"], w=[km, "ps%d" % ob])
                dma("pool", MIXD[h][:, col0:col0 + n], m[:, 0:n], r=[km], w=["MIXD"])

            groups = []
            gi = 0
            for b in range(1, 9):
                n = 512
                for h in range(4):
                    pr = slice(64 * (h % 2), 64 * (h % 2) + 64)
                    for kp in range(NT // 2):
                        mem = []
                        for mi in range(2):
                            kt = 2 * kp + mi
                            ks = slice(kt * 128, (kt + 1) * 128)
                            mem.append(dict(qk=[(KN[:, h, ks], qnl[b % 3][:, h, 0:n]), (KR2[pr, ks], qrl[b % 3][pr, h // 2, 0:n])],
                                            v=VM[:, kt, h * 128:(h + 1) * 128], acc=0,
                                            start=(kp == 0 and mi == 0), stop=(kp == NT // 2 - 1 and mi == 1)))
                        g = dict(n=n, members=mem, rk=["KN", "KR2", "qnl%d" % (b % 3)], rv=["VM"], accset=gi % 2)
                        if h == 0 and kp == 0:
                            g["pre"] = (lambda b=b: (pre(b), pre(b + 1) if b + 1 < 9 else None))
                        if kp == NT // 2 - 1:
                            g["post"] = (lambda acc, scr, b=b, h=h, n=n: post(b, h, n, acc, scr))
                        groups.append(g)
                    gi += 1
            attention(groups, 192.0 ** -0.5, [[(4, 5)], [(6, 7)]])
            phase_reset()
            if stop == "l1b2":
                return
            LB = T([128, 2, 2, 4])
            lbv = T([128, 2, 4])
            oml = T([128, 2, 4])
            hgn = T([128, 1])
            ones1 = T([128, 1])
            for d in range(2):
                for ll in range(2):
                    dma_nc("sp", LB[:, d, ll, :], din["hgrn_lb"][d, ll].rearrange("(h p) -> p h", p=128), w=["LB"])
            dma_nc("sp", hgn, din["hgrn_norm"][0].rearrange("(p o) -> p o", o=1), w=["hgn"])
            P.op("dve", lambda e: e.tensor_tensor(out=lbv, in0=LB[:, :, 1, :], in1=LB[:, :, 0, :], op=ALU.subtract), r=["LB"], w=["lbv"])
            P.op("act", lambda e: e.activation(out=lbv, in_=lbv, func=AF.Sigmoid), r=["lbv"], w=["lbv"])
            P.op("dve", lambda e: e.tensor_scalar(out=oml, in0=lbv, scalar1=-1.0, scalar2=1.0, op0=ALU.mult, op1=ALU.add),
                 r=["lbv"], w=["oml"])
            P.op("pool", lambda e: e.memset(ones1, 1.0), w=["ones1"])
            WH = T([128, 8, 5, 128], BF16)
            hTl = [T([128, 8, 512], BF16)] * 2
            SG = T([128, NTOK], BF16)
            Vt = T([128, NT, 128], BF16)
            QP = [T([128, NTOK], BF16) for _ in range(2)]
            QPP = [T([128, NTOK], BF16) for _ in range(2)]
            KP = [T([128, NTOK], BF16) for _ in range(2)]
            KPt = [T([128, NT, 128], BF16) for _ in range(2)]
            EL = [T([128, 68]) for _ in range(2)]
            OTa = T([128, NTOK])
            qf = T([128, 512])
            sgm = [T([128, 512]) for _ in range(2)]
            kk = [T([128, 512]) for _ in range(2)]
            lf = [T([128, 512]) for _ in range(2)]
            gb = [T([128, 516]) for _ in range(2)]
            Da = [T([128, 512]) for _ in range(2)]
            Db = [T([128, 512]) for _ in range(2)]
            E1 = [T([128, 512]) for _ in range(2)]
            E2 = [T([128, 512]) for _ in range(2)]
            E3 = [T([128, 512]) for _ in range(2)]
            elt = [T([128, 8]) for _ in range(2)]
            Sf = [[T([128, 128]) for _ in range(2)] for _ in range(2)]
            Sb = [[T([128, 128], BF16) for _ in range(2)] for _ in range(2)]
            Am = [[T([128, 64], BF16) for _ in range(2)] for _ in range(2)]
            rstd, otmp = Db[0], E1[0]
            sq = T([128, 512], BF16)
            mh = [T([128, 512], BF16) for _ in range(2)]
            for d in range(2):
                P.op("pool", (lambda e, d=d: e.memset(gb[d][:, 0:1], 0.0)), w=["gbz"])
            wsrc = din["od_w_in"][0].rearrange("(k p) n -> p k n", p=128)
            for h in range(4):
                for i, c0_ in enumerate((832, 1344, 1856, 2368, 2880)):
                    wload(WH[:, :, i, :], wsrc[:, :, c0_ + h * 128:c0_ + (h + 1) * 128], None, "pool")
                P.op("pool", lambda e: e.memset(OTa, 0.0), w=["OT"])
                for b, (t0, ntl) in enumerate(BLOCKS):
                    n = ntl * 128
                    nch = n // 64
                    col0 = t0 * 128
                    ch0 = col0 // 64
                    hT = hTl[b % 2]
                    khT = "hTl0"
                    dma("sp", hT[:, :, 0:n], HT1[b][:, :, 0:n], r=["HT1"], w=[khT])
                    mmg(ps[0][:, 0:n], [(WH[:, k, 0, :], hT[:, k, 0:n]) for k in range(8)], r=[khT, "wgt"], w=["ps0"])
                    P.op("act", (lambda e, n=n: e.activation(out=qf[:, 0:n], in_=ps[0][:, 0:n], func=AF.Silu)), r=["ps0"], w=["qf", "ps0"])
                    mmg(ps[1][:, 0:n], [(WH[:, k, 4, :], hT[:, k, 0:n]) for k in range(8)], r=[khT, "wgt"], w=["ps1"])
                    P.op("act", (lambda e, n=n, col0=col0: e.activation(out=SG[:, col0:col0 + n], in_=ps[1][:, 0:n], func=AF.Silu)),
                         r=["ps1"], w=["SG", "ps1"])
                    for ti in range(ntl):
                        tt = t0 + ti
                        ts_ = slice(ti * 128, (ti + 1) * 128)
                        mmg(ps[2][:, 0:128], [(hT[:, k, ts_], WH[:, k, 3, :]) for k in range(8)], r=[khT, "wgt"], w=["ps2"])
                        P.op("act", (lambda e, tt=tt: e.copy(out=Vt[:, tt, :], in_=ps[2][:, 0:128])), r=["ps2"], w=["Vt", "ps2"])
                    Gi3 = [gb[d][:, 1:1 + n].rearrange("p (c j) -> p c j", j=64) for d in range(2)]
                    Gs3 = [gb[d][:, 0:n].rearrange("p (c j) -> p c j", j=64) for d in range(2)]
                    S0 = [Gs3[d][:, :, 0:1].to_broadcast([128, nch, 64]) for d in range(2)]
                    I63 = [Gi3[d][:, :, 63:64].to_broadcast([128, nch, 64]) for d in range(2)]
                    Da3 = [Da[d][:, 0:n].rearrange("p (c j) -> p c j", j=64) for d in range(2)]
                    Db3 = [Db[d][:, 0:n].rearrange("p (c j) -> p c j", j=64) for d in range(2)]
                    for d in range(2):
                        mmg(ps[3 + d][:, 0:n], [(WH[:, k, 1 + d, :], hT[:, k, 0:n]) for k in range(8)], r=[khT, "wgt"], w=["ps%d" % (3 + d)])
                    for d in range(2):
                        P.op("act", (lambda e, n=n, d=d: e.activation(out=sgm[d][:, 0:n], in_=ps[3 + d][:, 0:n], func=AF.Sigmoid)),
                             r=["ps%d" % (3 + d)], w=["sgm%d" % d, "ps%d" % (3 + d)])
                    for d in range(2):
                        P.op("dve", (lambda e, n=n, d=d, h=h: e.tensor_scalar(out=sgm[d][:, 0:n], in0=sgm[d][:, 0:n], scalar1=oml[:, d, h:h + 1],
                                                                             scalar2=lbv[:, d, h:h + 1], op0=ALU.mult, op1=ALU.add)),
                             r=["sgm%d" % d, "oml", "lbv"], w=["sgm%d" % d])
                    for d in range(2):
                        P.op("pool", (lambda e, n=n, d=d: e.tensor_scalar(out=kk[d][:, 0:n], in0=sgm[d][:, 0:n], scalar1=-1.0, scalar2=1.0,
                                                                         op0=ALU.mult, op1=ALU.add)), r=["sgm%d" % d], w=["kk%d" % d])
                        P.op("act", (lambda e, n=n, d=d: e.activation(out=lf[d][:, 0:n], in_=sgm[d][:, 0:n], func=AF.Ln)), r=["sgm%d" % d], w=["lf%d" % d])
                    for d in range(2):
                        P.op("dve", (lambda e, n=n, d=d: e.tensor_tensor_scan(out=gb[d][:, 1:1 + n], data0=ones1[:, 0:1].to_broadcast([128, n]),
                                                                             data1=lf[d][:, 0:n], initial=0.0, op0=ALU.mult, op1=ALU.add)),
                             r=["lf%d" % d, "ones1", "gbz"], w=["gb%dk" % d])
                    P.op("dve", (lambda e, a=Gi3[0], b_=S0[0], o=Da3[0]: e.tensor_tensor(out=o, in0=a, in1=b_, op=ALU.subtract)), r=["gb0k"], w=["Da0"])
                    P.op("dve", (lambda e, a=Gs3[1], b_=I63[1], o=Da3[1]: e.tensor_tensor(out=o, in0=a, in1=b_, op=ALU.subtract)), r=["gb1k"], w=["Da1"])
                    P.op("dve", (lambda e, a=Gi3[0], b_=I63[0], o=Db3[0]: e.tensor_tensor(out=o, in0=a, in1=b_, op=ALU.subtract)), r=["gb0k"], w=["Db0"])
                    P.op("dve", (lambda e, a=Gs3[1], b_=S0[1], o=Db3[1]: e.tensor_tensor(out=o, in0=a, in1=b_, op=ALU.subtract)), r=["gb1k"], w=["Db1"])
                    for d in range(2):
                        P.op("dve", (lambda e, d=d, nch=nch, a=Gi3[d], b_=Gs3[d]: e.tensor_tensor(out=elt[d][:, 0:nch], in0=a[:, :, 63], in1=b_[:, :, 0],
                                                                                                  op=ALU.subtract)), r=["gb%dk" % d], w=["elt%d" % d])
                    scs = ((1.0, -1.0, 1.0), (-1.0, 1.0, -1.0))
                    for d in range(2):
                        P.op("act", (lambda e, n=n, d=d: e.activation(out=E1[d][:, 0:n], in_=Da[d][:, 0:n], func=AF.Exp, scale=scs[d][0])), r=["Da%d" % d], w=["E1%d" % d])
                    for d in range(2):
                        P.op("act", (lambda e, n=n, d=d: e.activation(out=E2[d][:, 0:n], in_=Db[d][:, 0:n], func=AF.Exp, scale=scs[d][1])), r=["Db%d" % d], w=["E2%d" % d])
                    for d in range(2):
                        P.op("act", (lambda e, n=n, d=d: e.activation(out=E3[d][:, 0:n], in_=Db[d][:, 0:n], func=AF.Exp, scale=scs[d][2])), r=["Db%d" % d], w=["E3%d" % d])
                        P.op("act", (lambda e, d=d, ch0=ch0, nch=nch: e.activation(out=EL[d][:, ch0:ch0 + nch], in_=elt[d][:, 0:nch], func=AF.Exp)),
                             r=["elt%d" % d], w=["EL%d" % d])
                    for d in range(2):
                        P.op("dve", (lambda e, n=n, d=d, col0=col0: e.tensor_tensor(out=KP[d][:, col0:col0 + n], in0=kk[d][:, 0:n], in1=E2[d][:, 0:n], op=ALU.mult)),
                             r=["kk%d" % d, "E2%d" % d], w=["KP%d" % d])
                    for d in range(2):
                        P.op("dve", (lambda e, n=n, d=d, col0=col0: e.tensor_tensor(out=QP[d][:, col0:col0 + n], in0=qf[:, 0:n], in1=E1[d][:, 0:n], op=ALU.mult)),
                             r=["qf", "E1%d" % d], w=["QP%d" % d])
                        P.op("pool", (lambda e, n=n, d=d, col0=col0: e.tensor_tensor(out=QPP[d][:, col0:col0 + n], in0=qf[:, 0:n], in1=E3[d][:, 0:n], op=ALU.mult)),
                             r=["qf", "E3%d" % d], w=["QPP%d" % d])
                    for d in range(2):
                        def trp(e, d=d, t0=t0, ntl=ntl):
                            for ti in range(ntl):
                                ins = e.transpose(out=psb[6 + d][:, ti * 128:(ti + 1) * 128], in_=KP[d][:, (t0 + ti) * 128:(t0 + ti + 1) * 128], identity=identb)
                            return ins
                        P.op("pe", trp, r=["KP%d" % d], w=["ps%d" % (6 + d)])
                        P.op("act", (lambda e, d=d, t0=t0, ntl=ntl: e.copy(out=KPt[d][:, t0:t0 + ntl, :],
                                                                          in_=psb[6 + d][:, 0:ntl * 128].rearrange("p (t k) -> p t k", k=128))),
                             r=["ps%d" % (6 + d)], w=["KPt%d" % d, "ps%d" % (6 + d)])
                orders = [list(range(68)), [3, 2, 1, 0] + list(range(67, 3, -1))]
                UB = (2, 3)

                def cinfo(c):
                    tt, hh = c // 2, c % 2
                    rows = slice(64 * hh, 64 * hh + 64)
                    cols = slice(64 * c, 64 * c + 64)
                    if c < 4:
                        pos, blk_n, blk_c0 = c, 256, 0
                    else:
                        pos, blk_n, blk_c0 = (c - 4) % 8, 512, 256 + ((c - 4) // 8) * 512
                    return tt, rows, cols, pos, blk_n, blk_c0

                def emitAU(st_):
                    ub = UB[st_ % 2]
                    for d in range(2):
                        c = orders[d][st_]
                        tt, rows, cols, pos, blk_n, blk_c0 = cinfo(c)
                        ab = d
                        mmg(ps[ab][rows, 0:64], [(KP[d][:, cols], QPP[d][:, cols])], r=["KP%d" % d, "QPP%d" % d], w=["ps%d" % ab])
                    for d in range(2):
                        c = orders[d][st_]
                        tt, rows, cols, pos, blk_n, blk_c0 = cinfo(c)
                        ubk = ((2, 6), (3, 7))[d][st_ % 2]
                        mmg(ps[ubk][:, 0:128], [(KPt[d][rows, tt, :], Vt[rows, tt, :])], r=["KPt%d" % d, "Vt"], w=["ps%d" % ubk])

                def emitMask(st_):
                    for d in range(2):
                        c = orders[d][st_]
                        tt, rows, cols, pos, blk_n, blk_c0 = cinfo(c)
                        am = Am[d][st_ % 2]
                        mk = maskf if d == 0 else maskb
                        ab = d
                        pA, kA = ps[ab], "ps%d" % ab
                        P.op("dve", (lambda e, pA=pA, rows=rows, am=am, mk=mk, d=d: e.tensor_tensor(
                            out=am[rows, :], in0=pA[rows, 0:64], in1=mk[rows, :], op=ALU.mult)),
                            r=[kA], w=["Am%d_%d" % (d, st_ % 2), kA])

                def emitO(st_):
                    for d in range(2):
                        c = orders[d][st_]
                        tt, rows, cols, pos, blk_n, blk_c0 = cinfo(c)
                        am = Am[d][st_ % 2]
                        prs = [(Vt[rows, tt, :], am[rows, :])]
                        rr = ["Vt", "Am%d_%d" % (d, st_ % 2)]
                        if st_ > 0:
                            so = (st_ - 1) % 2
                            prs.append((Sb[d][so], QP[d][:, cols]))
                            rr += ["Sb%d_%d" % (d, so), "QP%d" % d]
                        mmg(ps[4 + d][:, pos * 64:(pos + 1) * 64], prs, r=rr, w=["ps%d" % (4 + d)])

                def emitUpd(st_):
                    ub = UB[st_ % 2]
                    kU = "ps%d" % ub
                    sn, so = st_ % 2, (st_ - 1) % 2
                    for d in range(2):
                        c = orders[d][st_]
                        tt, rows, cols, pos, blk_n, blk_c0 = cinfo(c)
                        ubk = ((2, 6), (3, 7))[d][st_ % 2]
                        pU = ps[ubk][:, 0:128]
                        kU = "ps%d" % ubk
                        if st_ == 0:
                            P.op("dve", (lambda e, d=d, pU=pU: e.tensor_copy(out=Sf[d][sn], in_=pU)),
                                 r=[kU], w=["Sf%d_%d" % (d, sn), kU])
                        else:
                            P.op("dve", (lambda e, d=d, pU=pU, c=c: e.scalar_tensor_tensor(
                                out=Sf[d][sn], in0=Sf[d][so], scalar=EL[d][:, c:c + 1], in1=pU, op0=ALU.mult, op1=ALU.add)),
                                r=[kU, "Sf%d_%d" % (d, so), "EL%d" % d], w=["Sf%d_%d" % (d, sn), kU])
                        P.op("act", (lambda e, d=d: e.copy(out=Sb[d][sn], in_=Sf[d][sn])), r=["Sf%d_%d" % (d, sn)], w=["Sb%d_%d" % (d, sn)])
                        last_in_blk = (pos == (blk_n // 64 - 1)) if d == 0 else (pos == 0)
                        if last_in_blk:
                            P.op("dve", (lambda e, d=d, blk_n=blk_n, blk_c0=blk_c0: e.tensor_tensor(
                                out=OTa[:, blk_c0:blk_c0 + blk_n], in0=ps[4 + d][:, 0:blk_n], in1=OTa[:, blk_c0:blk_c0 + blk_n], op=ALU.add)),
                                 r=["ps%d" % (4 + d), "OT"], w=["OT", "ps%d" % (4 + d)])

                LOOK = 1
                if LOOK:
                    emitAU(0)
                    emitMask(0)
                for st_ in range(68):
                    if LOOK:
                        if st_ + 1 < 68:
                            emitAU(st_ + 1)
                            emitMask(st_ + 1)
                    else:
                        emitAU(st_)
                        emitMask(st_)
                    emitO(st_)
                    emitUpd(st_)
                for b in range(1, 9):
                    n = 512
                    col0 = BLOCKS[b][0] * 128
                    cs = slice(col0, col0 + n)
                    m = mh[b % 2]
                    km = "mh%d" % (b % 2)
                    P.op("act", (lambda e, cs=cs: e.activation(out=sq, in_=OTa[:, cs], func=AF.Square)), r=["OT"], w=["sq"])
                    mmg(ps[7], [(onesb, sq)], r=["sq"], w=["ps7"])
                    P.op("act", lambda e: e.activation(out=rstd, in_=ps[7], func=AF.Sqrt, scale=1.0 / 128, bias=EPS), r=["ps7"], w=["Db0", "ps7"])
                    P.op("dve", lambda e: e.reciprocal(out=rstd, in_=rstd), r=["Db0"], w=["Db0"])
                    P.op("dve", (lambda e, cs=cs: e.scalar_tensor_tensor(out=otmp, in0=OTa[:, cs], scalar=hgn, in1=rstd, op0=ALU.mult, op1=ALU.mult)),
                         r=["OT", "Db0", "hgn"], w=["E10"])
                    P.op("dve", (lambda e, cs=cs, m=m: e.tensor_tensor(out=m, in0=otmp, in1=SG[:, cs], op=ALU.mult)), r=["E10", "SG"], w=[km])
                    dma("pool", MIXD[4 + h][:, cs], m, r=[km], w=["MIXD"])
            phase_reset()
            if stop == "l1b3":
                return
            phase_outproj(1, list(range(1, 9)), False)

        def phase_outproj(l, blks, from_inputs):
            wo = T([128, 8, D], BF16)
            wosrc = din["mix_w_out"][l].rearrange("(k p) n -> p k n", p=128)
            for c in range(2):
                wload(wo[:, :, c * 512:(c + 1) * 512], wosrc[:, :, c * 512:(c + 1) * 512])
            res = RES(l, 2)
            mix = [T([128, 8, 512], BF16) for _ in range(2)]
            ypairs = [(0, 1), (2, 3), (4, 5), (6, 7)]
            yi = 0
            for b in blks:
                t0, ntl = BLOCKS[b]
                n = ntl * 128
                col0 = t0 * 128
                mx = mix[b % 2]
                kx = "mix%d" % (b % 2)
                for k in range(8):
                    dma("sp", mx[:, k, 0:n], MIXD[k][:, col0:col0 + n], r=["MIXD"], w=[kx])
                for ti in range(ntl):
                    tt = t0 + ti
                    ls = slice(ti * 128, (ti + 1) * 128)
                    y0, y1 = ypairs[yi % 4]
                    yi += 1
                    for hf, yb in ((0, y0), (1, y1)):
                        mmg(ps[yb], [(mx[:, k, ls], wo[:, k, hf * 512:(hf + 1) * 512]) for k in range(8)],
                            r=[kx, "wgt"], w=["ps%d" % yb])
                    res.run(y0, y1, x_src(from_inputs, tt), XR[tt * 128:(tt + 1) * 128, :], tt < 2, "XR1")

        phase_mod()
        phase_reset()
        S0 = ("mod", "l0a1", "l0a2", "l0mix")
        S1 = S0 + ("l0", "l1b1", "l1b2", "l1b3", "l1mix")
        if stop != "mod":
            phase_l0()
            phase_reset()
        if stop not in S0:
            phase_ffn(0, True, False)
            phase_reset()
        if stop not in S0 + ("l0",):
            phase_l1()
            phase_reset()
        if stop not in S1:
            phase_ffn(1, False, True)
            phase_reset()
        P.emit(st)
    return nc


_CACHE = {}


def kernel(**inputs):
    consts = _consts()
    if "nc" not in _CACHE:
        _CACHE["nc"] = build()
    nc = _CACHE["nc"]
    in_maps = []
    for b in range(8):
        m = {"x": np.ascontiguousarray(inputs["x"][b]), "ctx": np.ascontiguousarray(inputs["ctx"][b]),
             "cvec": np.ascontiguousarray(np.stack([inputs["c"][b], inputs["c_ctx"]], 0))}
        for n in W_NAMES:
            m[n] = np.ascontiguousarray(inputs[n])
        m.update(consts)
        in_maps.append(m)
    res = run_bass_kernel_spmd(nc, in_maps, core_ids=list(range(8)))
    return np.stack([r["out"] for r in res.results], 0).astype(np.float32)
```

```python
import numpy as np
from contextlib import ExitStack
import concourse.bass as bass
import concourse.mybir as mybir
from concourse.bass_utils import run_bass_kernel_spmd

F32 = mybir.dt.float32
BF16 = mybir.dt.bfloat16
ALU = mybir.AluOpType
AF = mybir.ActivationFunctionType

NDSEM = 8
D = 1024
NT = 34
NTOK = 4352
DFF = 2816
NJ = 22
EPS = 1e-6


class Prog:
    ENGS = ("pe", "dve", "act", "pool", "sp")

    def __init__(self, nc):
        self.nc = nc
        self.ops = []
        self.lw = {}
        self.rd = {}
        self.cnt = {e: 0 for e in self.ENGS}
        self.dcnt = {e: 0 for e in self.ENGS}
        self.dslot_last = {e: [None] * NDSEM for e in self.ENGS}
        self.last_nd = {e: None for e in self.ENGS}
        self.pending_bar = {e: [] for e in self.ENGS}

    def op(self, eng, fn, r=(), w=(), dma=False):
        oid = len(self.ops)
        deps = []
        for k in r:
            y = self.lw.get(k)
            if y is not None:
                deps.append((y, "RAW"))
        for k in w:
            y = self.lw.get(k)
            if y is not None:
                deps.append((y, "WAW"))
            for y in self.rd.get(k, ()):
                deps.append((y, "WAR"))
        for y in self.pending_bar[eng]:
            deps.append((y, "RAW"))
        self.pending_bar[eng] = []
        o = dict(id=oid, eng=eng, fn=fn, deps=deps, dma=dma)
        if dma:
            i = self.dcnt[eng]
            self.dcnt[eng] += 1
            slot = i % NDSEM
            o["dslot"] = slot
            o["dval"] = 16 * (i // NDSEM + 1)
            prev = self.dslot_last[eng][slot]
            if prev is not None:
                deps.append((prev, "RAW"))
            self.dslot_last[eng][slot] = oid
        else:
            self.cnt[eng] += 1
            o["val"] = self.cnt[eng]
            self.last_nd[eng] = oid
        self.ops.append(o)
        for k in w:
            self.lw[k] = oid
            self.rd[k] = []
        for k in r:
            if k not in w:
                self.rd.setdefault(k, []).append(oid)
        return oid

    def barrier(self):
        snap = []
        for e in self.ENGS:
            if self.last_nd[e] is not None:
                snap.append(self.last_nd[e])
            for y in self.dslot_last[e]:
                if y is not None:
                    snap.append(y)
        for e in self.ENGS:
            self.pending_bar[e] = list(snap)
        self.lw = {}
        self.rd = {}

    def emit(self, st):
        nc = self.nc
        sems = {e: st.enter_context(nc.semaphore("s_" + e)) for e in self.ENGS}
        dsems = {e: [st.enter_context(nc.semaphore("d_%s%d" % (e, i))) for i in range(NDSEM)]
                 for e in ("sp", "pool", "act") if self.dcnt[e] > 0}
        block = st.enter_context(nc.Block())
        ops = self.ops

        def run(ename, eng):
            seen = {}
            for o in ops:
                if o["eng"] != ename:
                    continue
                need = {}
                for (y, kind) in o["deps"]:
                    Y = ops[y]
                    if Y["dma"]:
                        key = ("d", Y["eng"], Y["dslot"])
                        sem = dsems[Y["eng"]][Y["dslot"]]
                        val = Y["dval"]
                    else:
                        if Y["eng"] == ename and not o["dma"]:
                            if ename == "pe":
                                continue
                        key = ("c", Y["eng"])
                        sem = sems[Y["eng"]]
                        val = Y["val"]
                    if seen.get(key, 0) >= val:
                        continue
                    if key not in need or need[key][1] < val:
                        need[key] = (sem, val)
                for key, (sem, val) in need.items():
                    eng.wait_ge(sem, val)
                    seen[key] = val
                ins = o["fn"](eng)
                if o["dma"]:
                    ins.then_inc(dsems[ename][o["dslot"]], 16)
                else:
                    ins.then_inc(sems[ename], 1)
            if ename in dsems:
                for slot in range(NDSEM):
                    y = self.dslot_last[ename][slot]
                    if y is not None:
                        Y = ops[y]
                        if seen.get(("d", ename, slot), 0) < Y["dval"]:
                            eng.wait_ge(dsems[ename][slot], Y["dval"])

        @block.tensor
        def _(e):
            run("pe", e)

        @block.vector
        def _(e):
            run("dve", e)

        @block.scalar
        def _(e):
            run("act", e)

        @block.gpsimd
        def _(e):
            run("pool", e)

        @block.sync
        def _(e):
            run("sp", e)


PW = 8 + 256 + 16 + 4096 + 8
PC0, PL0 = 8, 280


def _consts():
    c = {}
    c["ident"] = np.eye(128, dtype=np.float32)
    rot = np.zeros((128, 128), np.float32)
    for d in range(128):
        rot[d ^ 16, d] = 1.0
    c["rot"] = rot
    n = 4096
    pos_row = np.repeat(np.arange(n // 64), 64)
    pos_col = np.tile(np.arange(64), n // 64)
    inv_freq = (10000.0 ** (-np.arange(0, 32, 2, dtype=np.float32) / 32)).astype(np.float32)
    ang = np.stack([pos_row, pos_col], -1).astype(np.float32)[..., None] * inv_freq
    cs, sn = np.cos(ang).astype(np.float32), np.sin(ang).astype(np.float32)
    cos_t = np.zeros((128, n), np.float32)
    sin_t = np.zeros((128, n), np.float32)
    for d in range(128):
        dd = d % 64
        a, hf, i = dd // 32, (dd // 16) % 2, dd % 16
        cos_t[d] = cs[:, a, i]
        sin_t[d] = sn[:, a, i] * (-1.0 if hf == 0 else 1.0)
    c["cos_t"] = cos_t
    c["sin_t"] = sin_t
    c["sin2_t"] = np.ascontiguousarray(sin_t[np.arange(128) ^ 16])
    inv = np.zeros((4, PW), np.float32)
    for g, w in enumerate((2, 4, 8, 16)):
        h = w // 2
        for (n_, off) in ((256, PC0), (4096, PL0)):
            t = np.arange(n_)
            lo = np.clip(t - h, 0, n_)
            hi = np.clip(t + h, 0, n_)
            inv[g, off:off + n_] = 1.0 / (hi - lo).astype(np.float32)
    c["invcnt"] = inv
    p = np.arange(128)[:, None] % 64
    t = np.arange(64)[None, :]
    c["mask_f"] = (p <= t).astype(np.float32)
    c["mask_b"] = (p >= t).astype(np.float32)
    return c


W_NAMES = ["ada_w", "ada_b", "norm_g", "mix_w_out", "ffn_w_gate", "ffn_w_up", "ffn_conv_w",
           "ffn_conv_b", "ffn_w_down", "ev_w_in", "pool_w", "pool_scale", "diff_lambda",
           "diff_subln", "od_w_in", "mla_q_norm", "mla_w_uq", "mla_kv_norm", "mla_w_ukv",
           "hgrn_norm", "hgrn_lb"]
W_SHAPES = {"ada_w": (2, 1024, 6144), "ada_b": (2, 6144), "norm_g": (2, 4, 1024),
            "mix_w_out": (2, 1024, 1024), "ffn_w_gate": (2, 1024, 2816), "ffn_w_up": (2, 1024, 2816),
            "ffn_conv_w": (2, 3, 2816), "ffn_conv_b": (2, 2816), "ffn_w_down": (2, 2816, 1024),
            "ev_w_in": (1, 1024, 2048), "pool_w": (1, 4, 128, 128), "pool_scale": (1, 512),
            "diff_lambda": (1, 4, 64), "diff_subln": (1, 128), "od_w_in": (1, 1024, 3392),
            "mla_q_norm": (1, 512), "mla_w_uq": (1, 512, 768), "mla_kv_norm": (1, 256),
            "mla_w_ukv": (1, 256, 1024), "hgrn_norm": (1, 128), "hgrn_lb": (2, 2, 512)}
C_SHAPES = {"ident": (128, 128), "rot": (128, 128), "cos_t": (128, 4096), "sin_t": (128, 4096), "sin2_t": (128, 4096),
            "invcnt": (4, PW), "mask_f": (128, 64), "mask_b": (128, 64)}

BLOCKS = [(0, 2)] + [(2 + 4 * i, 4) for i in range(8)]


def build(stop=None, dbg=False):
    nc = bass.Bass("TRN2", target_bir_lowering=False)
    din = {}
    din["x"] = nc.dram_tensor("x", [4096, D], F32, kind="ExternalInput").ap()
    din["ctx"] = nc.dram_tensor("ctx", [256, D], F32, kind="ExternalInput").ap()
    din["cvec"] = nc.dram_tensor("cvec", [2, D], F32, kind="ExternalInput").ap()
    for n in W_NAMES:
        din[n] = nc.dram_tensor(n, list(W_SHAPES[n]), F32, kind="ExternalInput").ap()
    for n in C_SHAPES:
        din[n] = nc.dram_tensor(n, list(C_SHAPES[n]), F32, kind="ExternalInput").ap()
    out = nc.dram_tensor("out", [4096, D], F32, kind="ExternalOutput").ap()
    XR = nc.dram_tensor("XR", [NTOK, D], F32, kind="ExternalOutput" if dbg else "Internal").ap()
    MODV = nc.dram_tensor("MODV", [2, 2, 6, D], F32, kind="Internal").ap()
    QT = nc.dram_tensor("QT", [9, 128, 4, 512], BF16, kind="Internal").ap()
    QR = nc.dram_tensor("QR", [9, 128, 2, 512], BF16, kind="Internal").ap()
    UT = nc.dram_tensor("UT", [4, 128, NTOK], F32, kind="Internal").ap()
    HT1 = nc.dram_tensor("HT1", [9, 128, 8, 512], BF16, kind="Internal").ap()
    H2D = nc.dram_tensor("H2D", [128, 8, 4355], BF16, kind="Internal").ap()
    MIXD = nc.dram_tensor("MIXD", [8, 128, NTOK], BF16, kind="ExternalOutput" if dbg else "Internal").ap()

    st = ExitStack()
    with st:
        P = Prog(nc)
        AW = 52000
        arena = st.enter_context(nc.sbuf_tensor("arena", [128, AW], F32))
        psall = st.enter_context(nc.psum_tensor("psall", [128, 4096], F32))[:]
        ps = [psall[:, i * 512:(i + 1) * 512] for i in range(8)]
        psb = [p.bitcast(BF16) for p in ps]
        top = [0]

        def T(shape, dt=F32):
            n = int(np.prod(shape[1:]))
            cols = n if dt == F32 else (n + 1) // 2
            off = top[0]
            top[0] += cols
            assert top[0] <= AW, "SBUF arena overflow %d" % top[0]
            a = arena[0:shape[0], off:off + cols]
            if dt != F32:
                a = a.bitcast(dt)
            if len(shape) == 3:
                a = a.rearrange("p (a b) -> p a b", a=shape[1])
            elif len(shape) == 4:
                a = a.rearrange("p (a b c) -> p a b c", a=shape[1], b=shape[2])
            return a

        uid = [0]

        def K(s):
            uid[0] += 1
            return "%s#%d" % (s, uid[0])

        def dma(q, o, i, r=(), w=()):
            P.op(q, lambda e: e.dma_start(out=o, in_=i), r=r, w=w, dma=True)

        def dma_nc(q, o, i, r=(), w=()):
            P.op(q, lambda e: e.dma_start(out=o, in_=i, allow_slow_non_contiguous=True), r=r, w=w, dma=True)

        def mmg(o, pairs, r, w):
            def f(e):
                n = len(pairs)
                for i, (l, rh) in enumerate(pairs):
                    ins = e.matmul(o, lhsT=l, rhs=rh, start=(i == 0), stop=(i == n - 1))
                return ins
            P.op("pe", f, r=r, w=w)

        identb = T([128, 128], BF16)
        rotb = T([128, 128], BF16)
        onesb = T([128, 128], BF16)
        maskf = T([128, 64])
        maskb = T([128, 64])
        stg = [T([128, 2048]) for _ in range(2)]
        stgi = [0]
        PERSIST = None

        def wload(dst, src, q=None, ce="act"):
            i = stgi[0] % 2
            stgi[0] += 1
            shp = list(dst.shape)
            n = int(np.prod(shp[1:]))
            if n > 2048:
                hh = shp[-1] // 2
                if len(shp) == 2:
                    wload(dst[:, 0:hh], src[:, 0:hh], q, ce)
                    wload(dst[:, hh:], src[:, hh:], q, ce)
                else:
                    wload(dst[:, :, 0:hh], src[:, :, 0:hh], q, ce)
                    wload(dst[:, :, hh:], src[:, :, hh:], q, ce)
                return
            s = stg[i][0:shp[0], 0:n]
            if len(shp) == 3:
                s = s.rearrange("p (a b) -> p a b", a=shp[1])
            qq = q or ("sp" if i == 0 else "pool")
            dma(qq, s, src, w=["stg%d" % i])
            kd = "W" + str(id(dst))
            if ce == "pool":
                P.op("pool", lambda e: e.tensor_copy(out=dst, in_=s), r=["stg%d" % i], w=["wgt"])
            elif ce == "dve":
                P.op("dve", lambda e: e.tensor_copy(out=dst, in_=s), r=["stg%d" % i], w=["wgt"])
            else:
                P.op("act", lambda e: e.copy(out=dst, in_=s), r=["stg%d" % i], w=["wgt"])

        for (dst, nm) in ((identb, "ident"), (rotb, "rot")):
            wload(dst, din[nm])
        P.op("pool", lambda e: e.memset(onesb, 1.0), w=["wgt"])
        dma("sp", maskf, din["mask_f"], w=["wgt"])
        dma("sp", maskb, din["mask_b"], w=["wgt"])
        PERSIST = top[0]

        def phase_reset():
            P.barrier()
            top[0] = PERSIST

        def phase_mod():
            cv = T([128, 2, 8])
            cvs = T([128, 2, 8])
            cvb = T([128, 8, 2], BF16)
            for j in range(2):
                dma_nc("sp", cv[:, j, :], din["cvec"][j].rearrange("(k p) -> p k", p=128), w=["cv"])
            P.op("act", lambda e: e.activation(out=cvs, in_=cv, func=AF.Silu), r=["cv"], w=["cvs"])
            P.op("dve", lambda e: e.tensor_copy(out=cvb, in_=cvs.rearrange("p j k -> p k j")), r=["cvs"], w=["cvb"])
            awb = [T([128, 8, 256], BF16) for _ in range(2)]
            Mt = T([2, 6 * D])
            bt = T([2, 6 * D])
            ng = T([2, 4, D])
            V = T([2, 6, D])
            for l in range(2):
                dma("sp", bt, din["ada_b"][l].partition_broadcast(2), w=["bt"])
                dma("sp", ng.rearrange("p a b -> p (a b)"),
                    din["norm_g"][l].rearrange("a b -> (a b)").partition_broadcast(2), w=["ng"])
                for nb in range(24):
                    ab = awb[nb % 2]
                    kab = "awb%d" % (nb % 2)
                    src = din["ada_w"][l].rearrange("(k p) n -> p k n", p=128)[:, :, nb * 256:(nb + 1) * 256]
                    i = stgi[0] % 2
                    stgi[0] += 1
                    s = stg[i][:, :].rearrange("p (a b) -> p a b", a=8)
                    dma("sp" if i == 0 else "pool", s, src, w=["stg%d" % i])
                    if nb % 3 == 2:
                        P.op("dve", (lambda e, ab=ab, s=s: e.tensor_copy(out=ab, in_=s)), r=["stg%d" % i], w=[kab])
                    else:
                        P.op("act", (lambda e, ab=ab, s=s: e.copy(out=ab, in_=s)), r=["stg%d" % i], w=[kab])
                    pb = ps[nb % 2][0:2, 0:256]
                    mmg(pb, [(cvb[:, k, :], ab[:, k, :]) for k in range(8)], r=["cvb", kab], w=["ps%d" % (nb % 2)])
                    P.op("dve", (lambda e, pb=pb, nb=nb: e.tensor_tensor(out=Mt[:, nb * 256:(nb + 1) * 256], in0=pb,
                                                                       in1=bt[:, nb * 256:(nb + 1) * 256], op=ALU.add)),
                         r=["ps%d" % (nb % 2), "bt"], w=["Mt"])
                sl = lambda i: Mt[:, i * D:(i + 1) * D]
                P.op("dve", lambda e: e.scalar_tensor_tensor(out=V[:, 0, :], in0=sl(1), scalar=1.0, in1=ng[:, 0, :],
                                                             op0=ALU.add, op1=ALU.mult), r=["Mt", "ng"], w=["V"])
                P.op("dve", lambda e: e.tensor_copy(out=V[:, 1, :], in_=sl(0)), r=["Mt"], w=["V"])
                P.op("dve", lambda e: e.tensor_tensor(out=V[:, 2, :], in0=sl(2), in1=ng[:, 1, :], op=ALU.mult),
                     r=["Mt", "ng"], w=["V"])
                P.op("dve", lambda e: e.scalar_tensor_tensor(out=V[:, 3, :], in0=sl(4), scalar=1.0, in1=ng[:, 2, :],
                                                             op0=ALU.add, op1=ALU.mult), r=["Mt", "ng"], w=["V"])
                P.op("dve", lambda e: e.tensor_copy(out=V[:, 4, :], in_=sl(3)), r=["Mt"], w=["V"])
                P.op("dve", lambda e: e.tensor_tensor(out=V[:, 5, :], in0=sl(5), in1=ng[:, 3, :], op=ALU.mult),
                     r=["Mt", "ng"], w=["V"])
                dma("sp", MODV[l].rearrange("j a b -> j (a b)"), V.rearrange("p a b -> p (a b)"), r=["V"], w=["MODV"])

        def x_src(layer0_in, tt):
            if layer0_in:
                return din["ctx"][tt * 128:(tt + 1) * 128, :] if tt < 2 else din["x"][(tt - 2) * 128:(tt - 1) * 128, :]
            return XR[tt * 128:(tt + 1) * 128, :]

        class NM:
            def __init__(self, l, gi, si):
                self.xt = [T([128, D]) for _ in range(2)]
                self.junk = T([128, D], BF16)
                self.tmp = [T([128, D]) for _ in range(2)]
                self.pending = None
                self.hb = [T([128, D], BF16) for _ in range(2)]
                self.ss = T([128, 2])
                self.rs = T([128, 2])
                self.G = [T([128, D]) for _ in range(2)]
                self.SH = [T([128, D]) for _ in range(2)]
                for j in range(2):
                    dma("sp", self.G[j], MODV[l, j, gi].partition_broadcast(128), r=["MODV"], w=["nmG"])
                    dma("sp", self.SH[j], MODV[l, j, si].partition_broadcast(128), r=["MODV"], w=["nmG"])
                self.i = 0

            def run(self, src, is_ctx, dst, dkey, bank):
                i = self.i % 2
                self.i += 1
                xt, hb = self.xt[i], self.hb[i]
                kx, kh = "nm_xt%d" % i, "nm_hb%d" % i
                ss, rs = self.ss[:, i:i + 1], self.rs[:, i:i + 1]
                G, SH = self.G[1 if is_ctx else 0], self.SH[1 if is_ctx else 0]
                tmp = self.tmp[i]
                dma("sp", xt, src, r=["XR"], w=[kx])
                P.op("act", lambda e: e.activation(out=self.junk, in_=xt, func=AF.Square, accum_out=ss),
                     r=[kx], w=["nm_junk", "nm_ss%d" % i])
                P.op("act", lambda e: e.activation(out=rs, in_=ss, func=AF.Sqrt, scale=1.0 / D, bias=EPS),
                     r=["nm_ss%d" % i], w=["nm_rs%d" % i])
                P.op("dve", lambda e: e.reciprocal(out=rs, in_=rs), r=["nm_rs%d" % i], w=["nm_rs%d" % i])
                P.op("dve", lambda e: e.scalar_tensor_tensor(out=tmp, in0=xt, scalar=rs, in1=G, op0=ALU.mult,
                                                             op1=ALU.mult), r=[kx, "nm_rs%d" % i, "nmG"], w=["nm_tmp%d" % i])
                P.op("dve", lambda e: e.tensor_tensor(out=hb, in0=tmp, in1=SH, op=ALU.add),
                     r=["nm_tmp%d" % i, "nmG"], w=[kh])
                prev = self.pending
                self.pending = (hb, kh, dst, dkey, bank)
                if prev is not None:
                    self.stage_b(*prev)

            def stage_b(self, hb, kh, dst, dkey, bank):
                pb = psb[bank]

                def tr(e):
                    for k in range(8):
                        ins = e.transpose(out=pb[:, k * 128:(k + 1) * 128], in_=hb[:, k * 128:(k + 1) * 128],
                                          identity=identb)
                    return ins
                P.op("pe", tr, r=[kh], w=["ps%d" % bank])
                P.op("act", lambda e: e.copy(out=dst, in_=pb.rearrange("p (k n) -> p k n", k=8)),
                     r=["ps%d" % bank], w=[dkey])

            def flush(self):
                if self.pending is not None:
                    self.stage_b(*self.pending)
                    self.pending = None

        class RES:
            def __init__(self, l, gidx):
                self.GT = [T([128, D]) for _ in range(2)]
                for j in range(2):
                    dma("sp", self.GT[j], MODV[l, j, gidx].partition_broadcast(128), r=["MODV"], w=["resG"])
                self.xo = [T([128, D]) for _ in range(2)]
                self.tt_ = [T([128, D]) for _ in range(2)]
                self.junk = T([128, 512], BF16)
                self.ss = T([128, 4])
                self.rs = T([128, 2])
                self.i = 0

            def run(self, b0, b1, src, dstd, is_ctx, wkey):
                i = self.i % 2
                self.i += 1
                xo = self.xo[i]
                tbuf = self.tt_[i]
                kt_ = "res_t%d" % i
                kx = "res_x%d" % i
                ss = self.ss[:, 2 * i:2 * i + 2]
                rs = self.rs[:, i:i + 1]
                GT = self.GT[1 if is_ctx else 0]
                dma("sp", xo, src, r=["XR"], w=[kx])
                P.op("act", lambda e: e.activation(out=self.junk, in_=ps[b0], func=AF.Square, accum_out=ss[:, 0:1]),
                     r=["ps%d" % b0], w=["res_junk", "res_ss%d" % i])
                P.op("act", lambda e: e.activation(out=self.junk, in_=ps[b1], func=AF.Square, accum_out=ss[:, 1:2]),
                     r=["ps%d" % b1], w=["res_junk", "res_ss%d" % i])
                P.op("dve", lambda e: e.tensor_tensor(out=rs, in0=ss[:, 0:1], in1=ss[:, 1:2], op=ALU.add),
                     r=["res_ss%d" % i], w=["res_rs%d" % i])
                P.op("act", lambda e: e.activation(out=rs, in_=rs, func=AF.Sqrt, scale=1.0 / D, bias=EPS),
                     r=["res_rs%d" % i], w=["res_rs%d" % i])
                P.op("dve", lambda e: e.reciprocal(out=rs, in_=rs), r=["res_rs%d" % i], w=["res_rs%d" % i])
                for hf, bk in ((0, b0), (1, b1)):
                    P.op("dve", (lambda e, hf=hf, bk=bk: e.scalar_tensor_tensor(
                        out=tbuf[:, hf * 512:(hf + 1) * 512], in0=ps[bk], scalar=rs,
                        in1=GT[:, hf * 512:(hf + 1) * 512], op0=ALU.mult, op1=ALU.mult)),
                        r=["ps%d" % bk, "res_rs%d" % i, "resG"], w=[kt_, "ps%d" % bk])
                P.op("dve", lambda e: e.tensor_tensor(out=xo, in0=tbuf, in1=xo, op=ALU.add),
                     r=[kt_, kx], w=[kx])
                dma("pool", dstd, xo, r=[kx], w=[wkey])

        def rope_flush(ro):
            pend = ro[6]
            if pend:
                pend.pop()()

        def rope_evict(pbank, n, t0, dst, dkey, ro, is_ctx):
            src = ps[pbank][:, 0:n]
            if is_ctx:
                rope_flush(ro)
                P.op("act", lambda e: e.copy(out=dst, in_=src), r=["ps%d" % pbank], w=[dkey])
                return
            t1s, t2s, cosb, sinb, rbs, cnt, pend = ro
            i = cnt[0] % 2
            cnt[0] += 1
            t1, t2, rb = t1s[i], t2s[i], rbs[i]
            rope_flush(ro)
            P.op("dve", lambda e: e.tensor_tensor(out=t1[:, 0:n], in0=src, in1=cosb[:, 0:n], op=ALU.mult),
                 r=["ps%d" % pbank, "ro_cs"], w=["ro_t1%d" % i])
            P.op("dve", lambda e: e.tensor_tensor(out=t2[:, 0:n], in0=src, in1=sinb[:, 0:n], op=ALU.mult),
                 r=["ps%d" % pbank, "ro_cs"], w=["ro_t2%d" % i, "ps%d" % pbank])

            def part2():
                mmg(ps[rb][:, 0:n], [(identb, t1[:, 0:n]), (rotb, t2[:, 0:n])], r=["ro_t1%d" % i, "ro_t2%d" % i], w=["ps%d" % rb])
                P.op("act", lambda e: e.copy(out=dst, in_=ps[rb][:, 0:n]), r=["ps%d" % rb], w=[dkey, "ps%d" % rb])
            pend.append(part2)

        def rope_tiles(rbs):
            return ([T([128, 512], BF16) for _ in range(2)], [T([128, 512], BF16) for _ in range(2)],
                    T([128, 512]), T([128, 512]), rbs, [0], [])

        def rope_load(ro, b):
            if b == 0:
                return
            c0 = (b - 1) * 512
            dma("pool", ro[2], din["cos_t"][:, c0:c0 + 512], w=["ro_cs"])
            dma("pool", ro[3], din["sin2_t"][:, c0:c0 + 512], w=["ro_cs"])

        def attention(groups, scale, accsets):
            PT = [T([128, 2, 512], BF16) for _ in range(3)]
            N = len(groups)

            def emitS(i):
                g = groups[i]
                if g.get("pre"):
                    g["pre"]()
                A = 2 * (i % 2)
                n = g["n"]
                for mi, m in enumerate(g["members"]):
                    mmg(ps[A + mi][:, 0:n], m["qk"], r=g["rk"], w=["ps%d" % (A + mi)])

            def emitE(i):
                g = groups[i]
                A = 2 * (i % 2)
                n = g["n"]
                nm_ = len(g["members"])
                pt = PT[i % 3]
                src = psall[:, A * 512:(A + 2) * 512].rearrange("p (j n) -> p j n", j=2)[:, 0:nm_, 0:n]
                P.op("act", (lambda e: e.activation(out=pt[:, 0:nm_, 0:n], in_=src, func=AF.Exp, scale=scale)),
                     r=["ps%d" % A, "ps%d" % (A + 1)], w=["PT%d" % (i % 3), "ps%d" % A, "ps%d" % (A + 1)])

            def emitPV(i):
                g = groups[i]
                n = g["n"]
                pt = PT[i % 3]
                acc = accsets[g["accset"]]
                wk = []
                for m in g["members"]:
                    ob, sb2 = acc[m["acc"]]
                    wk += ["ps%d" % ob, "ps%d" % sb2]

                def pv(e):
                    for mi, m in enumerate(g["members"]):
                        ob, sb2 = acc[m["acc"]]
                        e.matmul(ps[ob][:, 0:n], lhsT=m["v"], rhs=pt[:, mi, 0:n], start=m["start"], stop=m["stop"])
                        ins = e.matmul(ps[sb2][:, 0:n], lhsT=onesb, rhs=pt[:, mi, 0:n], start=m["start"], stop=m["stop"])
                    return ins
                P.op("pe", pv, r=["PT%d" % (i % 3)] + g["rv"], w=list(dict.fromkeys(wk)))
                if g.get("post"):
                    A = 2 * (i % 2)
                    g["post"](acc, (A, A + 1))

            if N:
                emitS(0)
            for i in range(N):
                emitE(i)
                if i + 1 < N:
                    emitS(i + 1)
                emitPV(i)

        def phase_ffn(l, need_ctx, final):
            W2 = 4355
            Wd = T([128, NJ, D], BF16)
            CW = T([128, 4, NJ])
            for i in range(3):
                dma_nc("sp", CW[:, i, :], din["ffn_conv_w"][l, i].rearrange("(j p) -> p j", p=128), w=["CW"])
            dma_nc("sp", CW[:, 3, :], din["ffn_conv_b"][l].rearrange("(j p) -> p j", p=128), w=["CW"])
            wdsrc = din["ffn_w_down"][l].rearrange("(j p) n -> p j n", p=128)
            for j0 in range(0, NJ, 4):
                j1 = min(NJ, j0 + 4)
                wload(Wd[:, j0:j1, :], wdsrc[:, j0:j1, :], None, "pool")
            top_save = top[0]
            nm = NM(l, 3, 4)
            hTs = [T([128, 8, 512], BF16) for _ in range(2)]
            zt = T([128, 8, 1], BF16)
            P.op("pool", lambda e: e.memset(zt, 0.0), w=["zt"])
            for c in (0, 257, 4354):
                dma_nc("pool", H2D[:, :, c:c + 1], zt, r=["zt"], w=["H2D"])
            blks = list(range(0 if need_ctx else 1, 9))
            for b in blks:
                t0, ntl = BLOCKS[b]
                n = ntl * 128
                hT = hTs[b % 2]
                kh = "hTs%d" % (b % 2)
                for ti in range(ntl):
                    tt = t0 + ti
                    nm.run(x_src(False, tt), tt < 2, hT[:, :, ti * 128:(ti + 1) * 128], kh, 6 + tt % 2)
                nm.flush()
                c0 = 1 if b == 0 else 258 + (b - 1) * 512
                dma("pool", H2D[:, :, c0:c0 + n], hT[:, :, 0:n], r=[kh], w=["H2D"])
            P.barrier()
            top[0] = top_save
            res = RES(l, 5)
            GTt = T([128, NJ, 1024], BF16)
            H2P = [T([128, 8, 1026], BF16) for _ in range(2)]
            wgf = [T([128, 8, 128], BF16) for _ in range(2)]
            wuf = [T([128, 8, 128], BF16) for _ in range(2)]
            acc = [T([128, 512]) for _ in range(2)]
            sil = [T([128, 512]) for _ in range(2)]
            parts = [blks[i:i + 2] for i in range(0, len(blks), 2)]
            wgsrc = din["ffn_w_gate"][l].rearrange("(k p) n -> p k n", p=128)
            wusrc = din["ffn_w_up"][l].rearrange("(k p) n -> p k n", p=128)
            it = [0]
            yi = [0]
            bc0 = lambda b: 1 if b == 0 else 258 + (b - 1) * 512
            for pi, part in enumerate(parts):
                cstart = bc0(part[0]) - 1
                cend = bc0(part[-1]) + BLOCKS[part[-1]][1] * 128 + 1
                npc = cend - cstart
                H2T = H2P[pi % 2]
                kH = "H2P%d" % (pi % 2)
                dma("sp", H2T[:, :, 0:npc], H2D[:, :, cstart:cend], r=["H2D"], w=[kH])
                goff = {}
                o = 0
                for b in part:
                    goff[b] = o
                    o += BLOCKS[b][1] * 128
                for j in range(NJ):
                    wg, wu = wgf[j % 2], wuf[j % 2]
                    kw = "ffw%d" % (j % 2)
                    for (dst, srcw) in ((wg, wgsrc), (wu, wusrc)):
                        i = stgi[0] % 2
                        stgi[0] += 1
                        s = stg[i][:, 0:1024].rearrange("p (a b) -> p a b", a=8)
                        dma("sp", s, srcw[:, :, j * 128:(j + 1) * 128], w=["stg%d" % i])
                        P.op("pool", (lambda e, dst=dst, s=s: e.tensor_copy(out=dst, in_=s)), r=["stg%d" % i], w=[kw])
                    for b in part:
                        t0, ntl = BLOCKS[b]
                        n = ntl * 128
                        c0 = bc0(b) - cstart
                        q = it[0] % 2
                        ub_ = (2, 3, 5)[it[0] % 3]
                        it[0] += 1
                        pa, pu, ph = ps[q], ps[ub_], ps[4]
                        ka, ku, kh = "ps%d" % q, "ps%d" % ub_, "ps4"
                        mmg(pa[:, 0:n], [(wg[:, k, :], H2T[:, k, c0:c0 + n]) for k in range(8)], r=[kw, kH], w=[ka])
                        hal = H2T[:, :, c0 - 1:c0 + n + 1:n + 1]
                        mmg(ph[:, 0:2], [(wg[:, k, :], hal[:, k, :]) for k in range(8)], r=[kw, kH], w=[kh])
                        mmg(pu[:, 0:n], [(wu[:, k, :], H2T[:, k, c0:c0 + n]) for k in range(8)], r=[kw, kH], w=[ku])
                        ac, sl_ = acc[q], sil[q]
                        kac, ksl = "acc%d" % q, "sil%d" % q
                        w0, w1, w2, bb = (CW[:, i, j:j + 1] for i in range(4))
                        P.op("dve", (lambda e, ac=ac, pa=pa, w1=w1, bb=bb, n=n: e.tensor_scalar(
                            out=ac[:, 0:n], in0=pa[:, 0:n], scalar1=w1, scalar2=bb, op0=ALU.mult, op1=ALU.add)),
                            r=[ka, "CW"], w=[kac])
                        P.op("dve", (lambda e, ac=ac, ph=ph, w0=w0: e.scalar_tensor_tensor(
                            out=ac[:, 0:1], in0=ph[:, 0:1], scalar=w0, in1=ac[:, 0:1], op0=ALU.mult, op1=ALU.add)),
                            r=[kh, kac], w=[kac])
                        P.op("dve", (lambda e, ac=ac, ph=ph, w2=w2, n=n: e.scalar_tensor_tensor(
                            out=ac[:, n - 1:n], in0=ph[:, 1:2], scalar=w2, in1=ac[:, n - 1:n], op0=ALU.mult, op1=ALU.add)),
                            r=[kh, kac], w=[kac, kh])
                        P.op("dve", (lambda e, ac=ac, pa=pa, w0=w0, n=n: e.scalar_tensor_tensor(
                            out=ac[:, 1:n], in0=pa[:, 0:n - 1], scalar=w0, in1=ac[:, 1:n], op0=ALU.mult, op1=ALU.add)),
                            r=[ka, kac], w=[kac])
                        P.op("dve", (lambda e, ac=ac, pa=pa, w2=w2, n=n: e.scalar_tensor_tensor(
                            out=ac[:, 0:n - 1], in0=pa[:, 1:n], scalar=w2, in1=ac[:, 0:n - 1], op0=ALU.mult, op1=ALU.add)),
                            r=[ka, kac], w=[kac, ka])
                        P.op("act", (lambda e, ac=ac, sl_=sl_, n=n: e.activation(out=sl_[:, 0:n], in_=ac[:, 0:n], func=AF.Silu)),
                             r=[kac], w=[ksl])
                        g0 = goff[b]
                        P.op("dve", (lambda e, sl_=sl_, pu=pu, j=j, g0=g0, n=n: e.tensor_tensor(
                            out=GTt[:, j, g0:g0 + n], in0=sl_[:, 0:n], in1=pu[:, 0:n], op=ALU.mult)),
                            r=[ksl, ku], w=["GT", ku])
                for b in part:
                    t0, ntl = BLOCKS[b]
                    for ti in range(ntl):
                        tt = t0 + ti
                        g0 = goff[b] + ti * 128
                        y0, y1 = ((6, 7), (0, 1), (2, 3))[yi[0] % 3]
                        yi[0] += 1
                        for hf, yb in ((0, y0), (1, y1)):
                            mmg(ps[yb], [(GTt[:, j, g0:g0 + 128], Wd[:, j, hf * 512:(hf + 1) * 512]) for j in range(NJ)],
                                r=["GT", "wgt"], w=["ps%d" % yb])
                        if final:
                            dstd = out[(tt - 2) * 128:(tt - 1) * 128, :]
                            res.run(y0, y1, x_src(False, tt), dstd, tt < 2, "OUT")
                        else:
                            res.run(y0, y1, x_src(False, tt), XR[tt * 128:(tt + 1) * 128, :], tt < 2, "XR2")

        def phase_l0():
            l = 0
            LAM_INIT = 0.2
            KT = T([128, 4, NTOK], BF16)
            Vv = T([128, NT, 512], BF16)
            keep = top[0]
            w_in = T([128, 8, 2048], BF16)
            wsrc = din["ev_w_in"][0].rearrange("(k p) n -> p k n", p=128)
            for c in range(4):
                wload(w_in[:, :, c * 512:(c + 1) * 512], wsrc[:, :, c * 512:(c + 1) * 512])
            nm = NM(l, 0, 1)
            hT = T([128, 8, 512], BF16)
            ro = rope_tiles((7, 5))
            ub = [T([128, 512]) for _ in range(2)]
            qb = [T([128, 4, 512], BF16) for _ in range(2)]
            for b, (t0, ntl) in enumerate(BLOCKS):
                n = ntl * 128
                col0 = t0 * 128
                rope_load(ro, b)
                for ti in range(ntl):
                    tt = t0 + ti
                    nm.run(x_src(True, tt), tt < 2, hT[:, :, ti * 128:(ti + 1) * 128], "hT", 6)
                nm.flush()
                for ti in range(ntl):
                    tt = t0 + ti
                    bk = 4 + (ti % 2)
                    mmg(ps[bk], [(hT[:, k, ti * 128:(ti + 1) * 128], w_in[:, k, 1536:2048]) for k in range(8)],
                        r=["hT", "wgt"], w=["ps%d" % bk])
                    P.op("act", (lambda e, tt=tt, bk=bk: e.copy(out=Vv[:, tt, :], in_=ps[bk])), r=["ps%d" % bk], w=["Vv"])
                for c in range(4):
                    bk = c % 2
                    mmg(ps[bk][:, 0:n], [(w_in[:, k, c * 128:(c + 1) * 128], hT[:, k, 0:n]) for k in range(8)],
                        r=["hT", "wgt"], w=["ps%d" % bk])
                    u = ub[c % 2]
                    P.op("act", (lambda e, u=u, bk=bk, n=n: e.copy(out=u[:, 0:n], in_=ps[bk][:, 0:n])),
                         r=["ps%d" % bk], w=["ub%d" % (c % 2)])
                    dma("pool", UT[c][:, col0:col0 + n], u[:, 0:n], r=["ub%d" % (c % 2)], w=["UT"])
                qbb = qb[b % 2]
                kq = "qb%d" % (b % 2)
                for c in range(4):
                    bk = 2 + (c % 2)
                    mmg(ps[bk][:, 0:n], [(w_in[:, k, 512 + c * 128:512 + (c + 1) * 128], hT[:, k, 0:n]) for k in range(8)],
                        r=["hT", "wgt"], w=["ps%d" % bk])
                    rope_evict(bk, n, col0, qbb[:, c, 0:n], kq, ro, b == 0)
                for c in range(4):
                    bk = 2 + (c % 2)
                    mmg(ps[bk][:, 0:n], [(w_in[:, k, 1024 + c * 128:1024 + (c + 1) * 128], hT[:, k, 0:n]) for k in range(8)],
                        r=["hT", "wgt"], w=["ps%d" % bk])
                    rope_evict(bk, n, col0, KT[:, c, col0:col0 + n], "KT", ro, b == 0)
                rope_flush(ro)
                dma("pool", QT[b][:, :, 0:n], qbb[:, :, 0:n], r=[kq], w=["QT"])
            P.barrier()
            top[0] = keep
            if stop == "l0a1":
                return
            pw = T([128, 4, 128], BF16)
            for g in range(4):
                wload(pw[:, g, :], din["pool_w"][0, g])
            psc = T([128, 4])
            dma_nc("sp", psc, din["pool_scale"][0].rearrange("(g p) -> p g", p=128), w=["psc"])
            UP = T([128, PW])
            Aa = T([128, PW])
            Ab = T([128, PW])
            IC = T([128, PW])
            dT = T([128, PW], BF16)
            mpo = [T([128, 512], BF16) for _ in range(2)]
            for g in range(4):
                hw = (1, 2, 4, 8)[g]
                P.op("pool", lambda e: e.memset(UP, 0.0), w=["UP"])
                dma("sp", UP[:, PC0:PC0 + 256], UT[g][:, 0:256], r=["UT"], w=["UP"])
                dma("sp", UP[:, PL0:PL0 + 4096], UT[g][:, 256:NTOK], r=["UT"], w=["UP"])
                dma("pool", IC, din["invcnt"][g].partition_broadcast(128), w=["IC"])
                cur, ck = UP, "UP"
                bufs = [(Aa, "Aa"), (Ab, "Ab")]
                width = PW
                for s in range(g + 1):
                    sh = 1 << s
                    nxt, nk = bufs[s % 2]
                    width -= sh
                    P.op("dve", (lambda e, cur=cur, nxt=nxt, sh=sh, width=width: e.tensor_tensor(
                        out=nxt[:, 0:width], in0=cur[:, 0:width], in1=cur[:, sh:sh + width], op=ALU.add)),
                        r=[ck], w=[nk])
                    cur, ck = nxt, nk
                oth, ok = bufs[(g + 1) % 2]
                P.op("dve", (lambda e, cur=cur, oth=oth, hw=hw: e.tensor_tensor(
                    out=oth[:, 8:PW - 8], in0=cur[:, 8 - hw:PW - 8 - hw], in1=IC[:, 8:PW - 8], op=ALU.mult)),
                    r=[ck, "IC"], w=[ok])
                P.op("pool", (lambda e, oth=oth: e.tensor_tensor(out=dT[:, 8:PW - 8], in0=oth[:, 8:PW - 8],
                                                                 in1=UP[:, 8:PW - 8], op=ALU.subtract)),
                     r=[ok, "UP"], w=["dT"])
                for b, (t0, ntl) in enumerate(BLOCKS):
                    n = ntl * 128
                    col0 = t0 * 128
                    pc = PC0 if b == 0 else PL0 + (b - 1) * 512
                    bk = b % 2
                    mmg(ps[bk][:, 0:n], [(pw[:, g, :], dT[:, pc:pc + n])], r=["dT", "wgt"], w=["ps%d" % bk])
                    mp = mpo[b % 2]
                    P.op("act", (lambda e, bk=bk, n=n, g=g, mp=mp: e.activation(
                        out=mp[:, 0:n], in_=ps[bk][:, 0:n], func=AF.Identity, scale=psc[:, g:g + 1])),
                        r=["ps%d" % bk, "psc"], w=["mpo%d" % (b % 2)])
                    dma("pool", MIXD[g][:, col0:col0 + n], mp[:, 0:n], r=["mpo%d" % (b % 2)], w=["MIXD"])
            P.barrier()
            top[0] = keep
            if stop == "l0a2":
                return
            lamt = T([128, 4, 64])
            lj = T([128, 64])
            lsum = T([128, 2])
            nlam = T([128, 1])
            dma("sp", lamt.rearrange("p a b -> p (a b)"),
                din["diff_lambda"][0].rearrange("a b -> (a b)").partition_broadcast(128), w=["lamt"])
            for i in range(2):
                P.op("dve", (lambda e, i=i: e.scalar_tensor_tensor(out=lj, in0=lamt[:, 2 * i, :], scalar=1.0,
                                                                   in1=lamt[:, 2 * i + 1, :], op0=ALU.mult, op1=ALU.mult,
                                                                   accum_out=lsum[:, i:i + 1])),
                     r=["lamt"], w=["lj", "lsum"])
            P.op("act", lambda e: e.activation(out=lsum, in_=lsum, func=AF.Exp), r=["lsum"], w=["lsum"])
            P.op("dve", lambda e: e.tensor_tensor(out=nlam, in0=lsum[:, 1:2], in1=lsum[:, 0:1], op=ALU.subtract),
                 r=["lsum"], w=["nlam"])
            P.op("dve", lambda e: e.tensor_scalar(out=nlam, in0=nlam, scalar1=-LAM_INIT, scalar2=None, op0=ALU.add),
                 r=["nlam"], w=["nlam"])
            sln = T([128, 1])
            dma_nc("sp", sln, din["diff_subln"][0].rearrange("(p o) -> p o", o=1), w=["sln"])
            P.op("dve", lambda e: e.tensor_scalar(out=sln, in0=sln, scalar1=1.0 - LAM_INIT, scalar2=None, op0=ALU.mult),
                 r=["sln"], w=["sln"])
            qtl = [T([128, 4, 512], BF16) for _ in range(3)]
            MIXA = [T([128, 4, 512], BF16) for _ in range(2)]
            rsum = T([128, 512])
            o2 = [T([128, 512]) for _ in range(2)]
            oc = T([128, 512])
            sq = T([128, 512], BF16)
            rstd = T([128, 512])
            state = {}

            def pre(b):
                if state.get("b") != b:
                    state["b"] = b
                    n = BLOCKS[b][1] * 128
                    col0 = BLOCKS[b][0] * 128
                    dma("sp", qtl[b % 3][:, :, 0:n], QT[b][:, :, 0:n], r=["QT"], w=["qtl%d" % (b % 3)])

            def post(b, h, n, acc, scr):
                mixa = MIXA[b % 2]
                km = "MIXA%d" % (b % 2)
                for j in range(2):
                    ob, sb2 = acc[j]
                    P.op("dve", (lambda e, sb2=sb2: e.reciprocal(out=rsum[:, 0:n], in_=ps[sb2][:, 0:n])),
                         r=["ps%d" % sb2], w=["rsum", "ps%d" % sb2])
                    P.op("dve", (lambda e, ob=ob, j=j: e.tensor_tensor(out=o2[j][:, 0:n], in0=ps[ob][:, 0:n], in1=rsum[:, 0:n], op=ALU.mult)),
                         r=["ps%d" % ob, "rsum"], w=["o2_%d" % j, "ps%d" % ob])
                P.op("dve", lambda e: e.scalar_tensor_tensor(out=oc[:, 0:n], in0=o2[1][:, 0:n], scalar=nlam,
                                                             in1=o2[0][:, 0:n], op0=ALU.mult, op1=ALU.add),
                     r=["o2_0", "o2_1", "nlam"], w=["oc"])
                P.op("act", lambda e: e.activation(out=sq[:, 0:n], in_=oc[:, 0:n], func=AF.Square), r=["oc"], w=["sq"])
                sbk = scr[0]
                mmg(ps[sbk][:, 0:n], [(onesb, sq[:, 0:n])], r=["sq"], w=["ps%d" % sbk])
                P.op("act", lambda e: e.activation(out=rstd[:, 0:n], in_=ps[sbk][:, 0:n], func=AF.Sqrt, scale=1.0 / 128,
                                                   bias=EPS), r=["ps%d" % sbk], w=["rstd", "ps%d" % sbk])
                P.op("dve", lambda e: e.reciprocal(out=rstd[:, 0:n], in_=rstd[:, 0:n]), r=["rstd"], w=["rstd"])
                P.op("dve", lambda e: e.scalar_tensor_tensor(out=mixa[:, h, 0:n], in0=oc[:, 0:n], scalar=sln,
                                                             in1=rstd[:, 0:n], op0=ALU.mult, op1=ALU.mult),
                     r=["oc", "rstd", "sln"], w=[km])
                col0 = BLOCKS[b][0] * 128
                dma("pool", MIXD[4 + h][:, col0:col0 + n], mixa[:, h, 0:n], r=[km], w=["MIXD"])

            groups = []
            for b in range(9):
                n = BLOCKS[b][1] * 128
                kts = list(range(2)) if b == 0 else list(range(NT))
                for h in range(4):
                    for ki, kt in enumerate(kts):
                        mem = []
                        for j in range(2):
                            pr = slice(64 * j, 64 * j + 64)
                            mem.append(dict(qk=[(KT[pr, h, kt * 128:(kt + 1) * 128], qtl[b % 3][pr, h, 0:n])],
                                            v=Vv[:, kt, h * 128:(h + 1) * 128], acc=j, start=(ki == 0), stop=(ki == len(kts) - 1)))
                        g = dict(n=n, members=mem, rk=["KT", "qtl%d" % (b % 3)], rv=["Vv"], accset=0)
                        if h == 0 and ki == 0:
                            g["pre"] = (lambda b=b: (pre(b), pre(b + 1) if b + 1 < 9 else None))
                        if ki == len(kts) - 1:
                            g["post"] = (lambda acc, scr, b=b, h=h, n=n: post(b, h, n, acc, scr))
                        groups.append(g)
            attention(groups, 0.125, [[(4, 5), (6, 7)]])
            phase_reset()
            phase_outproj(0, list(range(9)), True)

        def rownorm(pt, W, NB, dst, dkey, tg):
            junk, ss, rs = tg
            P.op("act", lambda e: e.activation(out=junk[:, 0:W], in_=pt[0][:, 0:W], func=AF.Square, accum_out=ss),
                 r=[pt[1]], w=["rn_junk", "rn_ss"])
            P.op("act", lambda e: e.activation(out=rs, in_=ss, func=AF.Sqrt, scale=1.0 / W, bias=EPS), r=["rn_ss"], w=["rn_rs"])
            P.op("dve", lambda e: e.reciprocal(out=rs, in_=rs), r=["rn_rs"], w=["rn_rs"])
            P.op("dve", lambda e: e.scalar_tensor_tensor(out=dst, in0=pt[0][:, 0:W], scalar=rs, in1=NB[:, 0:W], op0=ALU.mult,
                                                         op1=ALU.mult), r=[pt[1], "rn_rs", "wgt"], w=[dkey, pt[1]])

        def phase_l1():
            l = 1
            KN = T([128, 4, NTOK], BF16)
            VM = T([128, NT, 512], BF16)
            KR2 = T([128, NTOK], BF16)
            keep = top[0]
            WA = T([128, 8, 512], BF16)
            WB = T([128, 8, 256], BF16)
            WKR = T([128, 8, 128], BF16)
            WQN = T([128, 4, 4, 128], BF16)
            WQR = T([128, 4, 256], BF16)
            WKN = T([128, 2, 4, 128], BF16)
            WVV = T([128, 2, 512], BF16)
            wsrc = din["od_w_in"][0].rearrange("(k p) n -> p k n", p=128)
            wload(WA, wsrc[:, :, 0:512])
            wload(WB, wsrc[:, :, 512:768])
            wload(WKR[:, :, 0:64], wsrc[:, :, 768:832])
            wload(WKR[:, :, 64:128], wsrc[:, :, 768:832])
            uq = din["mla_w_uq"][0].rearrange("(k p) (h c) -> p k h c", p=128, c=192)
            ukv = din["mla_w_ukv"][0].rearrange("(k p) (h c) -> p k h c", p=128, c=256)
            for k in range(4):
                wload(WQN[:, k], uq[:, k, :, 0:128])
                wload(WQR[:, k, :].rearrange("p (h c) -> p h c", h=4), uq[:, k, :, 128:192])
            for k in range(2):
                wload(WKN[:, k], ukv[:, k, :, 0:128])
                wload(WVV[:, k, :].rearrange("p (h c) -> p h c", h=4), ukv[:, k, :, 128:256])
            QNb = T([128, 512])
            KVNb = T([128, 256])
            dma("sp", QNb, din["mla_q_norm"][0].partition_broadcast(128), w=["wgt"])
            dma("sp", KVNb, din["mla_kv_norm"][0].partition_broadcast(128), w=["wgt"])
            nm = NM(l, 0, 1)
            hTs = [T([128, 8, 512], BF16)] * 2
            ro = rope_tiles((7, 1))
            tg = (T([128, 512], BF16), T([128, 1]), T([128, 1]))
            cqn = T([128, 512], BF16)
            ckvn = T([128, 256], BF16)
            cqT = T([128, 4, 512], BF16)
            ckvT = T([128, 2, 512], BF16)
            qnb = [T([128, 4, 512], BF16)] * 2
            qrb = [T([128, 2, 512], BF16)] * 2
            for b, (t0, ntl) in enumerate(BLOCKS):
                n = ntl * 128
                col0 = t0 * 128
                hT = hTs[b % 2]
                khT = "hTs0"
                rope_load(ro, b)
                for ti in range(ntl):
                    tt = t0 + ti
                    nm.run(x_src(False, tt), tt < 2, hT[:, :, ti * 128:(ti + 1) * 128], khT, 6)
                nm.flush()
                dma("pool", HT1[b][:, :, 0:n], hT[:, :, 0:n], r=[khT], w=["HT1"])
                for ti in range(ntl):
                    ts_ = slice(ti * 128, (ti + 1) * 128)
                    mmg(ps[0], [(hT[:, k, ts_], WA[:, k, :]) for k in range(8)], r=[khT, "wgt"], w=["ps0"])
                    rownorm((ps[0], "ps0"), 512, QNb, cqn, "cqn", tg)

                    def trq(e):
                        for k in range(4):
                            ins = e.transpose(out=psb[2][:, k * 128:(k + 1) * 128], in_=cqn[:, k * 128:(k + 1) * 128], identity=identb)
                        return ins
                    P.op("pe", trq, r=["cqn"], w=["ps2"])
                    P.op("act", (lambda e, ts_=ts_: e.copy(out=cqT[:, :, ts_], in_=psb[2][:, 0:512].rearrange("p (k n) -> p k n", k=4))),
                         r=["ps2"], w=["cqT"])
                    mmg(ps[1][:, 0:256], [(hT[:, k, ts_], WB[:, k, :]) for k in range(8)], r=[khT, "wgt"], w=["ps1"])
                    rownorm((ps[1], "ps1"), 256, KVNb, ckvn, "ckvn", tg)

                    def trk(e):
                        for k in range(2):
                            ins = e.transpose(out=psb[3][:, k * 128:(k + 1) * 128], in_=ckvn[:, k * 128:(k + 1) * 128], identity=identb)
                        return ins
                    P.op("pe", trk, r=["ckvn"], w=["ps3"])
                    P.op("act", (lambda e, ts_=ts_: e.copy(out=ckvT[:, :, ts_], in_=psb[3][:, 0:256].rearrange("p (k n) -> p k n", k=2))),
                         r=["ps3"], w=["ckvT"])
                for ti in range(ntl):
                    tt = t0 + ti
                    ts_ = slice(ti * 128, (ti + 1) * 128)
                    mmg(ps[0], [(ckvT[:, k, ts_], WVV[:, k, :]) for k in range(2)], r=["ckvT", "wgt"], w=["ps0"])
                    P.op("act", (lambda e, tt=tt: e.copy(out=VM[:, tt, :], in_=ps[0])), r=["ps0"], w=["VM", "ps0"])
                for h in range(4):
                    bk = 4 + (h % 2)
                    mmg(ps[bk][:, 0:n], [(WKN[:, k, h, :], ckvT[:, k, 0:n]) for k in range(2)], r=["ckvT", "wgt"], w=["ps%d" % bk])
                    P.op("act", (lambda e, h=h, bk=bk, n=n, col0=col0: e.copy(out=KN[:, h, col0:col0 + n], in_=ps[bk][:, 0:n])),
                         r=["ps%d" % bk], w=["KN", "ps%d" % bk])
                mmg(ps[4][:, 0:n], [(WKR[:, k, :], hT[:, k, 0:n]) for k in range(8)], r=[khT, "wgt"], w=["ps4"])
                rope_evict(4, n, col0, KR2[:, col0:col0 + n], "KR2", ro, b == 0)
                if b == 0:
                    rope_flush(ro)
                if b >= 1:
                    qn, qr = qnb[b % 2], qrb[b % 2]
                    for h in range(4):
                        bk = 4 + (h % 2)
                        mmg(ps[bk][:, 0:n], [(WQN[:, k, h, :], cqT[:, k, 0:n]) for k in range(4)], r=["cqT", "wgt"], w=["ps%d" % bk])
                        P.op("act", (lambda e, h=h, bk=bk, n=n, qn=qn: e.copy(out=qn[:, h, 0:n], in_=ps[bk][:, 0:n])),
                             r=["ps%d" % bk], w=["qnb0", "ps%d" % bk])
                    dma("pool", QT[b][:, :, 0:n], qn[:, :, 0:n], r=["qnb0"], w=["QT"])
                    for c in range(2):
                        bk = 4 + (c % 2)
                        mmg(ps[bk][:, 0:n], [(WQR[:, k, c * 128:(c + 1) * 128], cqT[:, k, 0:n]) for k in range(4)],
                            r=["cqT", "wgt"], w=["ps%d" % bk])
                        rope_evict(bk, n, col0, qr[:, c, 0:n], "qrb0", ro, False)
                    rope_flush(ro)
                    dma("pool", QR[b][:, :, 0:n], qr[:, :, 0:n], r=["qrb0"], w=["QR"])
            P.barrier()
            top[0] = keep
            if stop == "l1b1":
                return
            qnl = [T([128, 4, 512], BF16) for _ in range(3)]
            qrl = [T([128, 2, 512], BF16) for _ in range(3)]
            rsum = T([128, 512])
            mo = [T([128, 512], BF16) for _ in range(2)]
            state = {}

            def pre(b):
                if state.get("b") != b:
                    state["b"] = b
                    n = BLOCKS[b][1] * 128
                    dma("sp", qnl[b % 3][:, :, 0:n], QT[b][:, :, 0:n], r=["QT"], w=["qnl%d" % (b % 3)])
                    dma("pool", qrl[b % 3][:, :, 0:n], QR[b][:, :, 0:n], r=["QR"], w=["qrl%d" % (b % 3)])

            def post(b, h, n, acc, scr):
                col0 = BLOCKS[b][0] * 128
                m = mo[h % 2]
                km = "mo%d" % (h % 2)
                ob, sb2 = acc[0]
                P.op("dve", lambda e: e.reciprocal(out=rsum[:, 0:n], in_=ps[sb2][:, 0:n]), r=["ps%d" % sb2], w=["rsum", "ps%d" % sb2])
                P.op("dve", lambda e: e.tensor_tensor(out=m[:, 0:n], in0=ps[ob][:, 0:n], in1=rsum[:, 0:n], op=ALU.mult),
                     r=["ps%d" % ob, "rsum"], w=[km, "ps%d" % ob])
                dma("pool", MIXD[h][:, col0:col0 + n], m[:, 0:n], r=[km], w=["MIXD"])

            groups = []
            gi = 0
            for b in range(1, 9):
                n = 512
                for h in range(4):
                    pr = slice(64 * (h % 2), 64 * (h % 2) + 64)
                    for kp in range(NT // 2):
                        mem = []
                        for mi in range(2):
                            kt = 2 * kp + mi
                            ks = slice(kt * 128, (kt + 1) * 128)
                            mem.append(dict(qk=[(KN[:, h, ks], qnl[b % 3][:, h, 0:n]), (KR2[pr, ks], qrl[b % 3][pr, h // 2, 0:n])],
                                            v=VM[:, kt, h * 128:(h + 1) * 128], acc=0,
                                            start=(kp == 0 and mi == 0), stop=(kp == NT // 2 - 1 and mi == 1)))
                        g = dict(n=n, members=mem, rk=["KN", "KR2", "qnl%d" % (b % 3), "qrl%d" % (b % 3)], rv=["VM"], accset=gi % 2)
                        if h == 0 and kp == 0:
                            g["pre"] = (lambda b=b: (pre(b), pre(b + 1) if b + 1 < 9 else None))
                        if kp == NT // 2 - 1:
                            g["post"] = (lambda acc, scr, b=b, h=h, n=n: post(b, h, n, acc, scr))
                        groups.append(g)
                    gi += 1
            attention(groups, 192.0 ** -0.5, [[(4, 5)], [(6, 7)]])
            phase_reset()
            if stop == "l1b2":
                return
            LB = T([128, 2, 2, 4])
            lbv = T([128, 2, 4])
            oml = T([128, 2, 4])
            hgn = T([128, 1])
            ones1 = T([128, 1])
            for d in range(2):
                for ll in range(2):
                    dma_nc("sp", LB[:, d, ll, :], din["hgrn_lb"][d, ll].rearrange("(h p) -> p h", p=128), w=["LB"])
            dma_nc("sp", hgn, din["hgrn_norm"][0].rearrange("(p o) -> p o", o=1), w=["hgn"])
            P.op("dve", lambda e: e.tensor_tensor(out=lbv, in0=LB[:, :, 1, :], in1=LB[:, :, 0, :], op=ALU.subtract), r=["LB"], w=["lbv"])
            P.op("act", lambda e: e.activation(out=lbv, in_=lbv, func=AF.Sigmoid), r=["lbv"], w=["lbv"])
            P.op("dve", lambda e: e.tensor_scalar(out=oml, in0=lbv, scalar1=-1.0, scalar2=1.0, op0=ALU.mult, op1=ALU.add),
                 r=["lbv"], w=["oml"])
            P.op("pool", lambda e: e.memset(ones1, 1.0), w=["ones1"])
            WH = T([128, 8, 5, 128], BF16)
            hTl = [T([128, 8, 512], BF16)] * 2
            SG = T([128, NTOK], BF16)
            Vt = T([128, NT, 128], BF16)
            QP = [T([128, NTOK], BF16) for _ in range(2)]
            QPP = [T([128, NTOK], BF16) for _ in range(2)]
            KP = [T([128, NTOK], BF16) for _ in range(2)]
            KPt = [T([128, NT, 128], BF16) for _ in range(2)]
            EL = [T([128, 68]) for _ in range(2)]
            OTa = T([128, NTOK])
            qf = T([128, 512])
            sgm = [T([128, 512]) for _ in range(2)]
            kk = [T([128, 512]) for _ in range(2)]
            lf = [T([128, 512]) for _ in range(2)]
            gb = [T([128, 516]) for _ in range(2)]
            Da = [T([128, 512]) for _ in range(2)]
            Db = [T([128, 512]) for _ in range(2)]
            E1 = [T([128, 512]) for _ in range(2)]
            E2 = [T([128, 512]) for _ in range(2)]
            E3 = [T([128, 512]) for _ in range(2)]
            elt = [T([128, 8]) for _ in range(2)]
            Sf = [[T([128, 128]) for _ in range(2)] for _ in range(2)]
            Sb = [[T([128, 128], BF16) for _ in range(2)] for _ in range(2)]
            Am = [[T([128, 64], BF16) for _ in range(2)] for _ in range(2)]
            rstd, otmp = Db[0], E1[0]
            sq = T([128, 512], BF16)
            mh = [T([128, 512], BF16) for _ in range(2)]
            for d in range(2):
                P.op("pool", (lambda e, d=d: e.memset(gb[d][:, 0:1], 0.0)), w=["gbz"])
            wsrc = din["od_w_in"][0].rearrange("(k p) n -> p k n", p=128)
            for h in range(4):
                for i, c0_ in enumerate((832, 1344, 1856, 2368, 2880)):
                    wload(WH[:, :, i, :], wsrc[:, :, c0_ + h * 128:c0_ + (h + 1) * 128], None, "pool")
                P.op("pool", lambda e: e.memset(OTa, 0.0), w=["OT"])
                for b, (t0, ntl) in enumerate(BLOCKS):
                    n = ntl * 128
                    nch = n // 64
                    col0 = t0 * 128
                    ch0 = col0 // 64
                    hT = hTl[b % 2]
                    khT = "hTl0"
                    dma("sp", hT[:, :, 0:n], HT1[b][:, :, 0:n], r=["HT1"], w=[khT])
                    mmg(ps[0][:, 0:n], [(WH[:, k, 0, :], hT[:, k, 0:n]) for k in range(8)], r=[khT, "wgt"], w=["ps0"])
                    P.op("act", (lambda e, n=n: e.activation(out=qf[:, 0:n], in_=ps[0][:, 0:n], func=AF.Silu)), r=["ps0"], w=["qf", "ps0"])
                    mmg(ps[1][:, 0:n], [(WH[:, k, 4, :], hT[:, k, 0:n]) for k in range(8)], r=[khT, "wgt"], w=["ps1"])
                    P.op("act", (lambda e, n=n, col0=col0: e.activation(out=SG[:, col0:col0 + n], in_=ps[1][:, 0:n], func=AF.Silu)),
                         r=["ps1"], w=["SG", "ps1"])
                    for ti in range(ntl):
                        tt = t0 + ti
                        ts_ = slice(ti * 128, (ti + 1) * 128)
                        mmg(ps[2][:, 0:128], [(hT[:, k, ts_], WH[:, k, 3, :]) for k in range(8)], r=[khT, "wgt"], w=["ps2"])
                        P.op("act", (lambda e, tt=tt: e.copy(out=Vt[:, tt, :], in_=ps[2][:, 0:128])), r=["ps2"], w=["Vt", "ps2"])
                    Gi3 = [gb[d][:, 1:1 + n].rearrange("p (c j) -> p c j", j=64) for d in range(2)]
                    Gs3 = [gb[d][:, 0:n].rearrange("p (c j) -> p c j", j=64) for d in range(2)]
                    S0 = [Gs3[d][:, :, 0:1].to_broadcast([128, nch, 64]) for d in range(2)]
                    I63 = [Gi3[d][:, :, 63:64].to_broadcast([128, nch, 64]) for d in range(2)]
                    Da3 = [Da[d][:, 0:n].rearrange("p (c j) -> p c j", j=64) for d in range(2)]
                    Db3 = [Db[d][:, 0:n].rearrange("p (c j) -> p c j", j=64) for d in range(2)]
                    for d in range(2):
                        mmg(ps[3 + d][:, 0:n], [(WH[:, k, 1 + d, :], hT[:, k, 0:n]) for k in range(8)], r=[khT, "wgt"], w=["ps%d" % (3 + d)])
                    for d in range(2):
                        P.op("act", (lambda e, n=n, d=d: e.activation(out=sgm[d][:, 0:n], in_=ps[3 + d][:, 0:n], func=AF.Sigmoid)),
                             r=["ps%d" % (3 + d)], w=["sgm%d" % d, "ps%d" % (3 + d)])
                    for d in range(2):
                        P.op("dve", (lambda e, n=n, d=d, h=h: e.tensor_scalar(out=sgm[d][:, 0:n], in0=sgm[d][:, 0:n], scalar1=oml[:, d, h:h + 1],
                                                                             scalar2=lbv[:, d, h:h + 1], op0=ALU.mult, op1=ALU.add)),
                             r=["sgm%d" % d, "oml", "lbv"], w=["sgm%d" % d])
                    for d in range(2):
                        P.op("pool", (lambda e, n=n, d=d: e.tensor_scalar(out=kk[d][:, 0:n], in0=sgm[d][:, 0:n], scalar1=-1.0, scalar2=1.0,
                                                                         op0=ALU.mult, op1=ALU.add)), r=["sgm%d" % d], w=["kk%d" % d])
                        P.op("act", (lambda e, n=n, d=d: e.activation(out=lf[d][:, 0:n], in_=sgm[d][:, 0:n], func=AF.Ln)), r=["sgm%d" % d], w=["lf%d" % d])
                    for d in range(2):
                        P.op("dve", (lambda e, n=n, d=d: e.tensor_tensor_scan(out=gb[d][:, 1:1 + n], data0=ones1[:, 0:1].to_broadcast([128, n]),
                                                                             data1=lf[d][:, 0:n], initial=0.0, op0=ALU.mult, op1=ALU.add)),
                             r=["lf%d" % d, "ones1", "gbz"], w=["gb%dk" % d])
                    P.op("dve", (lambda e, a=Gi3[0], b_=S0[0], o=Da3[0]: e.tensor_tensor(out=o, in0=a, in1=b_, op=ALU.subtract)), r=["gb0k"], w=["Da0"])
                    P.op("dve", (lambda e, a=Gs3[1], b_=I63[1], o=Da3[1]: e.tensor_tensor(out=o, in0=a, in1=b_, op=ALU.subtract)), r=["gb1k"], w=["Da1"])
                    P.op("dve", (lambda e, a=Gi3[0], b_=I63[0], o=Db3[0]: e.tensor_tensor(out=o, in0=a, in1=b_, op=ALU.subtract)), r=["gb0k"], w=["Db0"])
                    P.op("dve", (lambda e, a=Gs3[1], b_=S0[1], o=Db3[1]: e.tensor_tensor(out=o, in0=a, in1=b_, op=ALU.subtract)), r=["gb1k"], w=["Db1"])
                    for d in range(2):
                        P.op("dve", (lambda e, d=d, nch=nch, a=Gi3[d], b_=Gs3[d]: e.tensor_tensor(out=elt[d][:, 0:nch], in0=a[:, :, 63], in1=b_[:, :, 0],
                                                                                                  op=ALU.subtract)), r=["gb%dk" % d], w=["elt%d" % d])
                    scs = ((1.0, -1.0, 1.0), (-1.0, 1.0, -1.0))
                    for d in range(2):
                        P.op("act", (lambda e, n=n, d=d: e.activation(out=E1[d][:, 0:n], in_=Da[d][:, 0:n], func=AF.Exp, scale=scs[d][0])), r=["Da%d" % d], w=["E1%d" % d])
                    for d in range(2):
                        P.op("act", (lambda e, n=n, d=d: e.activation(out=E2[d][:, 0:n], in_=Db[d][:, 0:n], func=AF.Exp, scale=scs[d][1])), r=["Db%d" % d], w=["E2%d" % d])
                    for d in range(2):
                        P.op("act", (lambda e, n=n, d=d: e.activation(out=E3[d][:, 0:n], in_=Db[d][:, 0:n], func=AF.Exp, scale=scs[d][2])), r=["Db%d" % d], w=["E3%d" % d])
                        P.op("act", (lambda e, d=d, ch0=ch0, nch=nch: e.activation(out=EL[d][:, ch0:ch0 + nch], in_=elt[d][:, 0:nch], func=AF.Exp)),
                             r=["elt%d" % d], w=["EL%d" % d])
                    for d in range(2):
                        P.op("dve", (lambda e, n=n, d=d, col0=col0: e.tensor_tensor(out=KP[d][:, col0:col0 + n], in0=kk[d][:, 0:n], in1=E2[d][:, 0:n], op=ALU.mult)),
                             r=["kk%d" % d, "E2%d" % d], w=["KP%d" % d])
                    for d in range(2):
                        P.op("dve", (lambda e, n=n, d=d, col0=col0: e.tensor_tensor(out=QP[d][:, col0:col0 + n], in0=qf[:, 0:n], in1=E1[d][:, 0:n], op=ALU.mult)),
                             r=["qf", "E1%d" % d], w=["QP%d" % d])
                        P.op("pool", (lambda e, n=n, d=d, col0=col0: e.tensor_tensor(out=QPP[d][:, col0:col0 + n], in0=qf[:, 0:n], in1=E3[d][:, 0:n], op=ALU.mult)),
                             r=["qf", "E3%d" % d], w=["QPP%d" % d])
                    for d in range(2):
                        def trp(e, d=d, t0=t0, ntl=ntl):
                            for ti in range(ntl):
                                ins = e.transpose(out=psb[6 + d][:, ti * 128:(ti + 1) * 128], in_=KP[d][:, (t0 + ti) * 128:(t0 + ti + 1) * 128], identity=identb)
                            return ins
                        P.op("pe", trp, r=["KP%d" % d], w=["ps%d" % (6 + d)])
                        P.op("act", (lambda e, d=d, t0=t0, ntl=ntl: e.copy(out=KPt[d][:, t0:t0 + ntl, :],
                                                                          in_=psb[6 + d][:, 0:ntl * 128].rearrange("p (t k) -> p t k", k=128))),
                             r=["ps%d" % (6 + d)], w=["KPt%d" % d, "ps%d" % (6 + d)])
                orders = [list(range(68)), [3, 2, 1, 0] + list(range(67, 3, -1))]
                UB = (2, 3)

                def cinfo(c):
                    tt, hh = c // 2, c % 2
                    rows = slice(64 * hh, 64 * hh + 64)
                    cols = slice(64 * c, 64 * c + 64)
                    if c < 4:
                        pos, blk_n, blk_c0 = c, 256, 0
                    else:
                        pos, blk_n, blk_c0 = (c - 4) % 8, 512, 256 + ((c - 4) // 8) * 512
                    return tt, rows, cols, pos, blk_n, blk_c0

                def emitAU(st_):
                    ub = UB[st_ % 2]
                    for d in range(2):
                        c = orders[d][st_]
                        tt, rows, cols, pos, blk_n, blk_c0 = cinfo(c)
                        ab = d
                        mmg(ps[ab][rows, 0:64], [(KP[d][:, cols], QPP[d][:, cols])], r=["KP%d" % d, "QPP%d" % d], w=["ps%d" % ab])
                    for d in range(2):
                        c = orders[d][st_]
                        tt, rows, cols, pos, blk_n, blk_c0 = cinfo(c)
                        ubk = ((2, 6), (3, 7))[d][st_ % 2]
                        mmg(ps[ubk][:, 0:128], [(KPt[d][rows, tt, :], Vt[rows, tt, :])], r=["KPt%d" % d, "Vt"], w=["ps%d" % ubk])

                def emitMask(st_):
                    for d in range(2):
                        c = orders[d][st_]
                        tt, rows, cols, pos, blk_n, blk_c0 = cinfo(c)
                        am = Am[d][st_ % 2]
                        mk = maskf if d == 0 else maskb
                        ab = d
                        pA, kA = ps[ab], "ps%d" % ab
                        P.op("dve", (lambda e, pA=pA, rows=rows, am=am, mk=mk, d=d: e.tensor_tensor(
                            out=am[rows, :], in0=pA[rows, 0:64], in1=mk[rows, :], op=ALU.mult)),
                            r=[kA], w=["Am%d_%d" % (d, st_ % 2), kA])

                def emitO(st_):
                    for d in range(2):
                        c = orders[d][st_]
                        tt, rows, cols, pos, blk_n, blk_c0 = cinfo(c)
                        am = Am[d][st_ % 2]
                        prs = [(Vt[rows, tt, :], am[rows, :])]
                        rr = ["Vt", "Am%d_%d" % (d, st_ % 2)]
                        if st_ > 0:
                            so = (st_ - 1) % 2
                            prs.append((Sb[d][so], QP[d][:, cols]))
                            rr += ["Sb%d_%d" % (d, so), "QP%d" % d]
                        mmg(ps[4 + d][:, pos * 64:(pos + 1) * 64], prs, r=rr, w=["ps%d" % (4 + d)])

                def emitUpd(st_):
                    ub = UB[st_ % 2]
                    kU = "ps%d" % ub
                    sn, so = st_ % 2, (st_ - 1) % 2
                    for d in range(2):
                        c = orders[d][st_]
                        tt, rows, cols, pos, blk_n, blk_c0 = cinfo(c)
                        ubk = ((2, 6), (3, 7))[d][st_ % 2]
                        pU = ps[ubk][:, 0:128]
                        kU = "ps%d" % ubk
                        if st_ == 0:
                            P.op("dve", (lambda e, d=d, pU=pU: e.tensor_copy(out=Sf[d][sn], in_=pU)),
                                 r=[kU], w=["Sf%d_%d" % (d, sn), kU])
                        else:
                            P.op("dve", (lambda e, d=d, pU=pU, c=c: e.scalar_tensor_tensor(
                                out=Sf[d][sn], in0=Sf[d][so], scalar=EL[d][:, c:c + 1], in1=pU, op0=ALU.mult, op1=ALU.add)),
                                r=[kU, "Sf%d_%d" % (d, so), "EL%d" % d], w=["Sf%d_%d" % (d, sn), kU])
                        P.op("act", (lambda e, d=d: e.copy(out=Sb[d][sn], in_=Sf[d][sn])), r=["Sf%d_%d" % (d, sn)], w=["Sb%d_%d" % (d, sn)])
                        last_in_blk = (pos == (blk_n // 64 - 1)) if d == 0 else (pos == 0)
                        if last_in_blk:
                            P.op("dve", (lambda e, d=d, blk_n=blk_n, blk_c0=blk_c0: e.tensor_tensor(
                                out=OTa[:, blk_c0:blk_c0 + blk_n], in0=ps[4 + d][:, 0:blk_n], in1=OTa[:, blk_c0:blk_c0 + blk_n], op=ALU.add)),
                                 r=["ps%d" % (4 + d), "OT"], w=["OT", "ps%d" % (4 + d)])

                LOOK = 1
                if LOOK:
                    emitAU(0)
                    emitMask(0)
                for st_ in range(68):
                    if LOOK:
                        if st_ + 1 < 68:
                            emitAU(st_ + 1)
                            emitMask(st_ + 1)
                    else:
                        emitAU(st_)
                        emitMask(st_)
                    emitO(st_)
                    emitUpd(st_)
                for b in range(1, 9):
                    n = 512
                    col0 = BLOCKS[b][0] * 128
                    cs = slice(col0, col0 + n)
                    m = mh[b % 2]
                    km = "mh%d" % (b % 2)
                    P.op("act", (lambda e, cs=cs: e.activation(out=sq, in_=OTa[:, cs], func=AF.Square)), r=["OT"], w=["sq"])
                    mmg(ps[7], [(onesb, sq)], r=["sq"], w=["ps7"])
                    P.op("act", lambda e: e.activation(out=rstd, in_=ps[7], func=AF.Sqrt, scale=1.0 / 128, bias=EPS), r=["ps7"], w=["Db0", "ps7"])
                    P.op("dve", lambda e: e.reciprocal(out=rstd, in_=rstd), r=["Db0"], w=["Db0"])
                    P.op("dve", (lambda e, cs=cs: e.scalar_tensor_tensor(out=otmp, in0=OTa[:, cs], scalar=hgn, in1=rstd, op0=ALU.mult, op1=ALU.mult)),
                         r=["OT", "Db0", "hgn"], w=["E10"])
                    P.op("dve", (lambda e, cs=cs, m=m: e.tensor_tensor(out=m, in0=otmp, in1=SG[:, cs], op=ALU.mult)), r=["E10", "SG"], w=[km])
                    dma("pool", MIXD[4 + h][:, cs], m, r=[km], w=["MIXD"])
            phase_reset()
            if stop == "l1b3":
                return
            phase_outproj(1, list(range(1, 9)), False)

        def phase_outproj(l, blks, from_inputs):
            wo = T([128, 8, D], BF16)
            wosrc = din["mix_w_out"][l].rearrange("(k p) n -> p k n", p=128)
            for c in range(2):
                wload(wo[:, :, c * 512:(c + 1) * 512], wosrc[:, :, c * 512:(c + 1) * 512])
            res = RES(l, 2)
            mix = [T([128, 8, 512], BF16) for _ in range(2)]
            ypairs = [(0, 1), (2, 3), (4, 5), (6, 7)]
            yi = 0
            for b in blks:
                t0, ntl = BLOCKS[b]
                n = ntl * 128
                col0 = t0 * 128
                mx = mix[b % 2]
                kx = "mix%d" % (b % 2)
                for k in range(8):
                    dma("sp" if k % 2 == 0 else "pool", mx[:, k, 0:n], MIXD[k][:, col0:col0 + n], r=["MIXD"], w=["%s_%d" % (kx, k)])
                for ti in range(ntl):
                    tt = t0 + ti
                    ls = slice(ti * 128, (ti + 1) * 128)
                    y0, y1 = ypairs[yi % 4]
                    yi += 1
                    for hf, yb in ((0, y0), (1, y1)):
                        mmg(ps[yb], [(mx[:, k, ls], wo[:, k, hf * 512:(hf + 1) * 512]) for k in range(8)],
                            r=["%s_%d" % (kx, k) for k in range(8)] + ["wgt"], w=["ps%d" % yb])
                    res.run(y0, y1, x_src(from_inputs, tt), XR[tt * 128:(tt + 1) * 128, :], tt < 2, "XR1")

        phase_mod()
        phase_reset()
        S0 = ("mod", "l0a1", "l0a2", "l0mix")
        S1 = S0 + ("l0", "l1b1", "l1b2", "l1b3", "l1mix")
        if stop != "mod":
            phase_l0()
            phase_reset()
        if stop not in S0:
            phase_ffn(0, True, False)
            phase_reset()
        if stop not in S0 + ("l0",):
            phase_l1()
            phase_reset()
        if stop not in S1:
            phase_ffn(1, False, True)
            phase_reset()
        P.emit(st)
    return nc


_CACHE = {}


def kernel(**inputs):
    consts = _consts()
    if "nc" not in _CACHE:
        _CACHE["nc"] = build()
    nc = _CACHE["nc"]
    in_maps = []
    for b in range(8):
        m = {"x": np.ascontiguousarray(inputs["x"][b]), "ctx": np.ascontiguousarray(inputs["ctx"][b]),
             "cvec": np.ascontiguousarray(np.stack([inputs["c"][b], inputs["c_ctx"]], 0))}
        for n in W_NAMES:
            m[n] = np.ascontiguousarray(inputs[n])
        m.update(consts)
        in_maps.append(m)
    res = run_bass_kernel_spmd(nc, in_maps, core_ids=list(range(8)))
    return np.stack([r["out"] for r in res.results], 0).astype(np.float32)
```
